# Optimizing a Trainium2 kernel written in Bass

```python
import math
import jax, jax.numpy as jnp
from jax import lax
import numpy as np

D_MODEL = 1024
BATCH = 4
SEQ = 4096
DEPTH = 2

GRID_W = 64
CTX_LEN = 256
N_MIXERS = 4
D_MIX = D_MODEL
GROUP_W = D_MIX // N_MIXERS
M_HEADS = 4
M_HD = GROUP_W // M_HEADS
M_CHUNK = 64
M_GATES = 2 * 2 * M_HEADS
A_HEADS = 4
A_HD = GROUP_W // (2 * A_HEADS)
N_FREQ = A_HD // 4
ROPE_THETA = 10000.0
Q_BLOCK = 128
POOL_WINDOWS = (2, 4, 8, 16)
POOL_GW = GROUP_W // len(POOL_WINDOWS)
CONV_K = 31
FF_RAW = -(-8 * D_MODEL // 3)
D_FF = -(-FF_RAW // 256) * 256
IN_SPLITS = (GROUP_W, GROUP_W, GROUP_W, GROUP_W, M_GATES, GROUP_W, GROUP_W, GROUP_W, GROUP_W, 2 * GROUP_W)
IN_COLS = sum(IN_SPLITS)

kernel_name = "hybrid_parallel_group_flow_block"


def rms_norm(x, g, eps=1e-6):
    xf = x.astype(jnp.float32)
    y = xf * lax.rsqrt(jnp.mean(xf * xf, axis=-1, keepdims=True) + eps)
    return (y * g).astype(x.dtype)


def layer_norm(x, g, b, eps=1e-5):
    xf = x.astype(jnp.float32)
    mu = jnp.mean(xf, axis=-1, keepdims=True)
    var = jnp.mean(jnp.square(xf - mu), axis=-1, keepdims=True)
    return ((xf - mu) * lax.rsqrt(var + eps) * g + b).astype(x.dtype)


def modulate(x, shift, scale):
    return x * (1 + scale) + shift


def split_columns(p):
    idx, acc = [], 0
    for s in IN_SPLITS[:-1]:
        acc += s
        idx.append(acc)
    return jnp.split(p, idx, axis=-1)


def flip(a):
    return jnp.flip(a, axis=1)


def axial_rope_tables(n_tokens):
    n_rows = n_tokens // GRID_W
    rows = jnp.repeat(jnp.arange(n_rows), GRID_W).astype(jnp.float32)
    cols = jnp.tile(jnp.arange(GRID_W), n_rows).astype(jnp.float32)
    freqs = ROPE_THETA ** (-jnp.arange(N_FREQ, dtype=jnp.float32) / N_FREQ)
    ang = jnp.stack([rows[:, None] * freqs, cols[:, None] * freqs], axis=1)
    return jnp.cos(ang), jnp.sin(ang)


def apply_axial_rope(x, cos, sin):
    shp = x.shape
    xr = x.reshape(shp[:-1] + (2, 2, N_FREQ))
    x0, x1 = xr[..., 0, :], xr[..., 1, :]
    c = cos[None, :, None, None].astype(x.dtype)
    s = sin[None, :, None, None].astype(x.dtype)
    out = jnp.stack([x0 * c - x1 * s, x1 * c + x0 * s], axis=-2)
    return out.reshape(shp)


def mlstm_inputs(mq, mk, mv, mg, gate_b):
    B, L, _ = mq.shape
    q = mq.reshape(B, L, M_HEADS, M_HD).astype(jnp.float32)
    k = mk.reshape(B, L, M_HEADS, M_HD).astype(jnp.float32) * (M_HD ** -0.5)
    v = mv.reshape(B, L, M_HEADS, M_HD).astype(jnp.float32)
    gates = mg.reshape(B, L, 2, 2, M_HEADS).astype(jnp.float32) + gate_b
    log_i = gates[:, :, :, 0]
    log_f = jax.nn.log_sigmoid(gates[:, :, :, 1])
    return q, k, v, log_i, log_f


def mlstm_scan(q, k, v, log_i, log_f, state):
    B, L, H, d = q.shape
    nc = L // M_CHUNK

    def to_chunks(a):
        return jnp.moveaxis(a.reshape((B, nc, M_CHUNK) + a.shape[2:]), 1, 0)

    causal = jnp.tril(jnp.ones((M_CHUNK, M_CHUNK), dtype=bool))[None, :, :, None]

    def step(carry, inp):
        C, n, m = carry
        qc, kc, vc, ic, fc = inp
        b = jnp.cumsum(fc, axis=1)
        b_tot = b[:, -1]
        inter = b + m[:, None]
        log_d = b[:, :, None, :] - b[:, None, :, :] + ic[:, None, :, :]
        log_d = jnp.where(causal, log_d, -jnp.inf)
        m_t = jnp.maximum(inter, jnp.max(log_d, axis=2))
        sw = jnp.exp(log_d - m_t[:, :, None, :]) * jnp.einsum('bthd,bshd->btsh', qc, kc)
        e_inter = jnp.exp(inter - m_t)
        num = e_inter[..., None] * jnp.einsum('bhvk,bthk->bthv', C, qc) + jnp.einsum('btsh,bshv->bthv', sw, vc)
        den = e_inter * jnp.einsum('bhk,bthk->bth', n, qc) + jnp.sum(sw, axis=2)
        h = num / jnp.maximum(jnp.abs(den), jnp.exp(-m_t))[..., None]
        g = b_tot[:, None] - b + ic
        m_new = jnp.maximum(b_tot + m, jnp.max(g, axis=1))
        e_old = jnp.exp(b_tot + m - m_new)
        e_s = jnp.exp(g - m_new[:, None])
        C_new = e_old[..., None, None] * C + jnp.einsum('bsh,bshv,bshk->bhvk', e_s, vc, kc)
        n_new = e_old[..., None] * n + jnp.einsum('bsh,bshk->bhk', e_s, kc)
        return (C_new, n_new, m_new), h

    final, hs = lax.scan(step, state, (to_chunks(q), to_chunks(k), to_chunks(v), to_chunks(log_i), to_chunks(log_f)))
    return jnp.moveaxis(hs, 0, 1).reshape(B, L, H, d), final


def mlstm_out(h, o_pre, norm_g):
    B, L = o_pre.shape[:2]
    o = jax.nn.sigmoid(o_pre.reshape(B, L, M_HEADS, M_HD).astype(jnp.float32))
    return (o * rms_norm(h, norm_g)).reshape(B, L, GROUP_W).astype(o_pre.dtype)


def attn_inputs(aq, ak, av, q_norm_g, k_norm_g, rope):
    B, L, _ = aq.shape
    q = rms_norm(aq.reshape(B, L, A_HEADS, 2, A_HD), q_norm_g)
    k = rms_norm(ak.reshape(B, L, A_HEADS, 2, A_HD), k_norm_g)
    v = av.reshape(B, L, A_HEADS, 2 * A_HD)
    if rope is not None:
        q = apply_axial_rope(q, rope[0], rope[1])
        k = apply_axial_rope(k, rope[0], rope[1])
    return q, k, v


def diff_softmax_attend(q, k, v, lam):
    s = jnp.einsum('bqhmd,bkhmd->bhmqk', q, k).astype(jnp.float32) * (A_HD ** -0.5)
    p = jax.nn.softmax(s, axis=-1)
    a = p[:, :, 0] - lam * p[:, :, 1]
    return jnp.einsum('bhqk,bkhe->bqhe', a.astype(v.dtype), v)


def blocked_latent_attention(q, k_all, v_all, lam):
    B, L = q.shape[:2]
    nb = L // Q_BLOCK
    qb = jnp.moveaxis(q.reshape((B, nb, Q_BLOCK) + q.shape[2:]), 1, 0)
    ob = lax.map(lambda qq: diff_softmax_attend(qq, k_all, v_all, lam), qb)
    return jnp.moveaxis(ob, 0, 1).reshape((B, L) + ob.shape[3:])


def attn_out(o, subln_g, lam_init):
    B, L = o.shape[:2]
    return (rms_norm(o, subln_g) * (1 - lam_init)).reshape(B, L, GROUP_W)


def multiscale_pool(u):
    B, L, _ = u.shape
    uf = u.astype(jnp.float32)
    cs = jnp.concatenate([jnp.zeros((B, 1, GROUP_W), jnp.float32), jnp.cumsum(uf, axis=1)], axis=1)
    t = jnp.arange(L)
    outs = []
    for gi, w in enumerate(POOL_WINDOWS):
        lo = jnp.clip(t - w // 2, 0, L - 1)
        hi = jnp.clip(t + (w - 1 - w // 2), 0, L - 1)
        csg = cs[:, :, gi * POOL_GW:(gi + 1) * POOL_GW]
        mean = (csg[:, hi + 1] - csg[:, lo]) / (hi - lo + 1).astype(jnp.float32)[None, :, None]
        outs.append(mean - uf[:, :, gi * POOL_GW:(gi + 1) * POOL_GW])
    return jnp.concatenate(outs, axis=-1).astype(u.dtype)


def pool_branch(u, pool_w, pool_scale):
    B, L, _ = u.shape
    y = multiscale_pool(u).reshape(B, L, len(POOL_WINDOWS), POOL_GW)
    y = jnp.einsum('blgc,gcd->blgd', y, pool_w).reshape(B, L, GROUP_W)
    return y * pool_scale


def conformer_conv(u, dw_w, dw_b, ln_g, ln_b, pw_w):
    a, g = jnp.split(u, 2, axis=-1)
    y = a * jax.nn.sigmoid(g)
    y = lax.conv_general_dilated(y, dw_w[:, None, :], window_strides=(1,),
                                 padding=((CONV_K // 2, CONV_K // 2),),
                                 dimension_numbers=('NWC', 'WIO', 'NWC'),
                                 feature_group_count=GROUP_W) + dw_b
    y = jax.nn.silu(layer_norm(y, ln_g, ln_b))
    return y @ pw_w


def swiglu(u, w_in, w_out):
    g, up = jnp.split(u @ w_in, 2, axis=-1)
    return (jax.nn.silu(g) * up) @ w_out


def hybrid_mixer(u_lat, u_ctx, rope, w_in, gate_b, m_norm_g, q_norm_g, k_norm_g, lam_q, lam_k, subln_g,
                 pool_w, pool_scale, dw_w, dw_b, ln_g, ln_b, pw_w, lam_init, need_ctx):
    pl = split_columns(u_lat @ w_in)
    pc = split_columns(u_ctx @ w_in)
    B = u_lat.shape[0]
    ql, kl, vl, il, fl = mlstm_inputs(pl[0], pl[1], pl[2], pl[4], gate_b)
    qc, kc, vc, ic, fc = mlstm_inputs(pc[0], pc[1], pc[2], pc[4], gate_b)
    zero = (jnp.zeros((B, M_HEADS, M_HD, M_HD), jnp.float32),
            jnp.zeros((B, M_HEADS, M_HD), jnp.float32),
            jnp.zeros((B, M_HEADS), jnp.float32))
    hcf, st_f = mlstm_scan(qc, kc, vc, ic[:, :, 0], fc[:, :, 0], zero)
    hcb, st_b = mlstm_scan(flip(qc), flip(kc), flip(vc), flip(ic[:, :, 1]), flip(fc[:, :, 1]), zero)
    hlf, _ = mlstm_scan(ql, kl, vl, il[:, :, 0], fl[:, :, 0], st_f)
    hlb, _ = mlstm_scan(flip(ql), flip(kl), flip(vl), flip(il[:, :, 1]), flip(fl[:, :, 1]), st_b)
    m_lat = mlstm_out(hlf + flip(hlb), pl[3], m_norm_g)
    lam = (jnp.exp(jnp.sum(lam_q[0] * lam_k[0])) - jnp.exp(jnp.sum(lam_q[1] * lam_k[1])) + lam_init).astype(jnp.float32)
    aql, akl, avl = attn_inputs(pl[5], pl[6], pl[7], q_norm_g, k_norm_g, rope)
    aqc, akc, avc = attn_inputs(pc[5], pc[6], pc[7], q_norm_g, k_norm_g, None)
    k_all = jnp.concatenate([akc, akl], axis=1)
    v_all = jnp.concatenate([avc, avl], axis=1)
    a_lat = attn_out(blocked_latent_attention(aql, k_all, v_all, lam), subln_g, lam_init)
    p_lat = pool_branch(pl[8], pool_w, pool_scale)
    c_lat = conformer_conv(pl[9], dw_w, dw_b, ln_g, ln_b, pw_w)
    y_lat = jnp.concatenate([m_lat, a_lat, p_lat, c_lat], axis=-1)
    if not need_ctx:
        return y_lat, None
    m_ctx = mlstm_out(hcf + flip(hcb), pc[3], m_norm_g)
    a_ctx = attn_out(diff_softmax_attend(aqc, akc, avc, lam), subln_g, lam_init)
    p_ctx = pool_branch(pc[8], pool_w, pool_scale)
    c_ctx_out = conformer_conv(pc[9], dw_w, dw_b, ln_g, ln_b, pw_w)
    y_ctx = jnp.concatenate([m_ctx, a_ctx, p_ctx, c_ctx_out], axis=-1)
    return y_lat, y_ctx


def setup_inputs(seed: int = 0) -> dict:
    key = jax.random.key(seed)
    ks = iter(jax.random.split(key, 32))
    nrm = lambda shape, s: jax.random.normal(next(ks), shape, jnp.float32) * s
    gate_b = jnp.stack([nrm((DEPTH, 2, M_HEADS), 0.1),
                        3.0 + 3.0 * jax.random.uniform(next(ks), (DEPTH, 2, M_HEADS), jnp.float32)], axis=2)
    return {
        "x": nrm((BATCH, SEQ, D_MODEL), 1.0),
        "c": nrm((BATCH, D_MODEL), 1.0),
        "ctx": nrm((BATCH, CTX_LEN, D_MODEL), 1.0),
        "c_ctx": nrm((D_MODEL,), 1.0),
        "norm1_g": 1.0 + nrm((DEPTH, D_MODEL), 0.02),
        "norm2_g": 1.0 + nrm((DEPTH, D_MODEL), 0.02),
        "mod_w": nrm((DEPTH, D_MODEL, 6 * D_MODEL), D_MODEL ** -0.5),
        "mod_b": nrm((DEPTH, 6 * D_MODEL), 0.02),
        "w_in": nrm((DEPTH, D_MODEL, IN_COLS), D_MODEL ** -0.5),
        "mlstm_gate_b": gate_b,
        "mlstm_norm_g": 1.0 + nrm((DEPTH, M_HD), 0.02),
        "attn_q_norm_g": 1.0 + nrm((DEPTH, A_HD), 0.02),
        "attn_k_norm_g": 1.0 + nrm((DEPTH, A_HD), 0.02),
        "lambda_q": nrm((DEPTH, 2, A_HD), 0.1),
        "lambda_k": nrm((DEPTH, 2, A_HD), 0.1),
        "attn_subln_g": 1.0 + nrm((DEPTH, 2 * A_HD), 0.02),
        "pool_w": nrm((DEPTH, len(POOL_WINDOWS), POOL_GW, POOL_GW), POOL_GW ** -0.5),
        "pool_scale": 1.0 + nrm((DEPTH, GROUP_W), 0.1),
        "conv_dw_w": nrm((DEPTH, CONV_K, GROUP_W), CONV_K ** -0.5),
        "conv_dw_b": nrm((DEPTH, GROUP_W), 0.02),
        "conv_ln_g": 1.0 + nrm((DEPTH, GROUP_W), 0.02),
        "conv_ln_b": nrm((DEPTH, GROUP_W), 0.02),
        "conv_pw_w": nrm((DEPTH, GROUP_W, GROUP_W), GROUP_W ** -0.5),
        "w_out": nrm((DEPTH, D_MIX, D_MODEL), D_MIX ** -0.5),
        "ffn_w_in": nrm((DEPTH, D_MODEL, 2 * D_FF), D_MODEL ** -0.5),
        "ffn_w_out": nrm((DEPTH, D_FF, D_MODEL), D_FF ** -0.5),
    }


def reference(x, c, ctx, c_ctx, norm1_g, norm2_g, mod_w, mod_b, w_in, mlstm_gate_b, mlstm_norm_g,
              attn_q_norm_g, attn_k_norm_g, lambda_q, lambda_k, attn_subln_g, pool_w, pool_scale,
              conv_dw_w, conv_dw_b, conv_ln_g, conv_ln_b, conv_pw_w, w_out, ffn_w_in, ffn_w_out):
    L = x.shape[1]
    rope = axial_rope_tables(L)
    h, hc = x, ctx
    s_c = jax.nn.silu(c)
    s_cc = jax.nn.silu(c_ctx)
    for l in range(DEPTH):
        need_ctx = l < DEPTH - 1
        lam_init = 0.8 - 0.6 * math.exp(-0.3 * l)
        mod_lat = (s_c @ mod_w[l] + mod_b[l])[:, None, :]
        mod_ctx = (s_cc @ mod_w[l] + mod_b[l])[None, None, :]
        sh1, sc1, g1, sh2, sc2, g2 = jnp.split(mod_lat, 6, axis=-1)
        csh1, csc1, cg1, csh2, csc2, cg2 = jnp.split(mod_ctx, 6, axis=-1)
        u_lat = modulate(rms_norm(h, norm1_g[l]), sh1, sc1)
        u_ctx = modulate(rms_norm(hc, norm1_g[l]), csh1, csc1)
        y_lat, y_ctx = hybrid_mixer(u_lat, u_ctx, rope, w_in[l], mlstm_gate_b[l], mlstm_norm_g[l],
                                    attn_q_norm_g[l], attn_k_norm_g[l], lambda_q[l], lambda_k[l],
                                    attn_subln_g[l], pool_w[l], pool_scale[l], conv_dw_w[l], conv_dw_b[l],
                                    conv_ln_g[l], conv_ln_b[l], conv_pw_w[l], lam_init, need_ctx)
        h = h + g1 * (y_lat @ w_out[l])
        h = h + g2 * swiglu(modulate(rms_norm(h, norm2_g[l]), sh2, sc2), ffn_w_in[l], ffn_w_out[l])
        if need_ctx:
            hc = hc + cg1 * (y_ctx @ w_out[l])
            hc = hc + cg2 * swiglu(modulate(rms_norm(hc, norm2_g[l]), csh2, csc2), ffn_w_in[l], ffn_w_out[l])
    return h
```

```python
import math
import numpy as np
import concourse.bass as bass
import concourse.mybir as mybir
from concourse.bass_utils import run_bass_kernel_spmd

F32 = mybir.dt.float32
BF16 = mybir.dt.bfloat16
ALU = mybir.AluOpType
AF = mybir.ActivationFunctionType
AX = mybir.AxisListType

COMPUTE = ("pe", "act", "dve", "pool")
N_DMA_SEMS = 12
N_BG_SEMS = 24


class _Op:
    __slots__ = ("eng", "fn", "deps", "is_dma", "eidx", "marked", "sem", "semval", "prev_on_sem", "bg")

    def __init__(self, eng, fn, deps, is_dma):
        self.eng, self.fn, self.deps, self.is_dma = eng, fn, deps, is_dma
        self.marked = False
        self.bg = False
        self.sem = None
        self.semval = 0
        self.prev_on_sem = None


class Sched:
    def __init__(self, nc, same_engine_sync=True):
        self.nc = nc
        self.ops = []
        self.last_w = {}
        self.readers = {}
        self.same_engine_sync = same_engine_sync
        self.engs = {"pe": nc.tensor, "act": nc.scalar, "dve": nc.vector, "pool": nc.gpsimd, "sp": nc.sync}
        self.ecount = {e: 0 for e in self.engs}
        self._bar_tile = nc.alloc_sbuf_tensor("bar_tile", [128, 8], F32).ap()

    def barrier(self):
        self.op("pool", lambda e: e.memset(self._bar_tile, 0.0), r=(), w=["__phase__"])

    def op(self, eng, fn, r=(), w=(), dma=False):
        r = list(r) + ["__phase__"]
        deps = set()
        for k in r:
            if k in self.last_w:
                deps.add(self.last_w[k])
        for k in w:
            if k in self.last_w:
                deps.add(self.last_w[k])
            deps.update(self.readers.get(k, ()))
        idx = len(self.ops)
        o = _Op(eng, fn, deps, dma)
        o.eidx = self.ecount[eng]
        self.ecount[eng] += 1
        self.ops.append(o)
        for k in r:
            self.readers.setdefault(k, []).append(idx)
        for k in w:
            self.last_w[k] = idx
            self.readers[k] = []
        return idx

    def dma(self, eng, out, in_, r=(), w=(), bg=False):
        i = self.op(eng, lambda e: e.dma_start(out=out, in_=in_), r, w, dma=True)
        self.ops[i].bg = bg
        return i

    def emit(self, final_wait_eng="sp"):
        nc = self.nc
        import os
        if os.environ.get("TRUNC"):
            self.ops = self.ops[:int(os.environ["TRUNC"])]
        ops = self.ops
        known = {e: {f: -1 for f in COMPUTE} for e in self.engs}
        dma_known = {e: {} for e in self.engs}
        waits = [None] * len(ops)
        dma_pool = {e: [] for e in self.engs}
        dma_rr = {e: 0 for e in self.engs}
        dma_last = {}
        for i, o in enumerate(ops):
            wl = []
            for d in sorted(o.deps):
                p = ops[d]
                if p.is_dma:
                    if dma_known[o.eng].get(p.sem, 0) < p.semval:
                        dma_known[o.eng][p.sem] = p.semval
                        wl.append(("dma", d))
                else:
                    if p.eng == o.eng and (o.eng == 'pe' or not self.same_engine_sync) and not o.is_dma:
                        continue
                    if known[o.eng][p.eng] < p.eidx:
                        known[o.eng][p.eng] = p.eidx
                        wl.append(("eng", d))
            best = {}
            for kind, d in wl:
                if kind == "eng":
                    e = ops[d].eng
                    if e not in best or ops[best[e]].eidx < ops[d].eidx:
                        best[e] = d
            bestd = {}
            for kind, d in wl:
                if kind == "dma":
                    sl = ops[d].sem
                    if sl not in bestd or ops[bestd[sl]].semval < ops[d].semval:
                        bestd[sl] = d
            wl = [("dma", d) for d in bestd.values()] + [("eng", d) for d in best.values()]
            if o.is_dma:
                qn = o.eng + ("_bg" if o.bg else "")
                dma_rr.setdefault(qn, 0)
                slot = dma_rr[qn] % (N_BG_SEMS if o.bg else N_DMA_SEMS)
                dma_rr[qn] += 1
                o.sem = (qn, slot)
                prev = dma_last.get(o.sem)
                o.prev_on_sem = prev
                o.semval = (ops[prev].semval if prev is not None else 0) + 16
                dma_last[o.sem] = i
                if prev is not None and dma_known[o.eng].get(o.sem, 0) < ops[prev].semval:
                    wl.append(("dma", prev))
                    dma_known[o.eng][o.sem] = ops[prev].semval
            for kind, d in wl:
                if kind == "eng":
                    ops[d].marked = True
            waits[i] = wl
        cnt = {e: 0 for e in COMPUTE}
        ordinal = {}
        for i, o in enumerate(ops):
            if not o.is_dma and o.marked:
                cnt[o.eng] += 1
                ordinal[i] = cnt[o.eng]
        esem = {e: nc.alloc_semaphore("sem_" + e) for e in COMPUTE}
        dsem = {}
        for o in ops:
            if o.is_dma and o.sem not in dsem:
                dsem[o.sem] = nc.alloc_semaphore("dsem_%s_%d" % o.sem)
        for i, o in enumerate(ops):
            eng = self.engs[o.eng]
            for kind, d in waits[i]:
                p = ops[d]
                if kind == "dma":
                    eng.wait_ge(dsem[p.sem], p.semval)
                else:
                    eng.wait_ge(esem[p.eng], ordinal[d])
            ins = o.fn(eng)
            if o.is_dma:
                ins.then_inc(dsem[o.sem], 16)
            elif o.marked:
                ins.then_inc(esem[o.eng], 1)
        fe = self.engs[final_wait_eng]
        for key, i in dma_last.items():
            fe.wait_ge(dsem[key], ops[i].semval)
        self.stats = {e: self.ecount[e] for e in self.engs}
        self.stats["marked"] = dict(cnt)


EPS_RMS = 1e-6
EPS_LN = 1e-5
D = 1024
DFF = 2816


class KB:
    def __init__(self):
        self.nc = bass.Bass("TRN2", target_bir_lowering=False)
        self.S = Sched(self.nc)
        self.rr = {}
        self.dram = {}
        self.pfx = {"inst": "", "layer": "", "global": ""}
        self.uid = 0
        self.scope = None
        self.psums = {}

    def inp(self, name, shape, dt=F32, lvl="inst"):
        full = self.pfx[lvl] + name
        if full not in self.dram:
            self.dram[full] = self.nc.dram_tensor(full, list(shape), dt, kind="ExternalInput").ap()
        return self.dram[full]

    def outp(self, name, shape, dt=F32):
        if name not in self.dram:
            self.dram[name] = self.nc.dram_tensor(name, list(shape), dt, kind="ExternalOutput").ap()
        return self.dram[name]

    def internal(self, name, shape, dt=F32):
        if name not in self.dram:
            self.dram[name] = self.nc.dram_tensor(name, list(shape), dt).ap()
        return self.dram[name]

    def scope_begin(self, inst, layer):
        from contextlib import ExitStack
        self.scope = ExitStack()
        self.pfx = {"inst": inst + "_", "layer": layer + "_", "global": ""}
        self.rr = {}
        self._bufs = {}

    def scope_end(self):
        self.S.barrier()
        self.scope.close()
        self.scope = None

    def sb(self, name, shape, dt=F32):
        self.uid += 1
        return self.scope.enter_context(self.nc.sbuf_tensor("s%d_%s" % (self.uid, name), list(shape), dt)).ap()

    def phase_begin(self):
        from contextlib import ExitStack
        self.stack = ExitStack()

    def sbt(self, name, shape, dt=F32):
        self.uid += 1
        return self.stack.enter_context(self.nc.sbuf_tensor("s%d_%s" % (self.uid, name), list(shape), dt)).ap()

    def phase_end(self):
        self.S.barrier()
        self.stack.close()
        self.stack = None

    def ps(self, name, shape, dt=F32):
        if name not in self.psums:
            self.psums[name] = self.nc.alloc_psum_tensor("p_" + name, list(shape), dt).ap()
        return self.psums[name]

    def rot(self, name, n):
        i = self.rr.get(name, 0)
        self.rr[name] = i + 1
        return i % n

    def dma(self, eng, out, in_, r=(), w=(), bg=False):
        self.S.dma(eng, out, in_, r=r, w=w, bg=bg)

    def mm(self, out, lhsT, rhs, start, stop, r, w):
        self.S.op("pe", lambda e: e.matmul(out, lhsT=lhsT, rhs=rhs, start=start, stop=stop), r=r, w=w)

    def tr(self, out, in_, ident, r, w):
        self.S.op("pe", lambda e: e.transpose(out, in_, ident), r=r, w=w)

    def act(self, out, in_, func, r, w, bias=0.0, scale=1.0, accum=None, eng="act"):
        if accum is None:
            self.S.op(eng, lambda e: e.activation(out=out, in_=in_, func=func, bias=bias, scale=scale), r=r, w=w)
        else:
            self.S.op(eng, lambda e: e.activation(out=out, in_=in_, func=func, bias=bias, scale=scale, accum_out=accum), r=r, w=w)

    def tt(self, eng, out, in0, in1, op, r, w):
        self.S.op(eng, lambda e: e.tensor_tensor(out=out, in0=in0, in1=in1, op=op), r=r, w=w)

    def ts(self, eng, out, in0, s1, s2, op0, op1, r, w):
        if s2 is None:
            self.S.op(eng, lambda e: e.tensor_scalar(out=out, in0=in0, scalar1=s1, scalar2=None, op0=op0), r=r, w=w)
        else:
            self.S.op(eng, lambda e: e.tensor_scalar(out=out, in0=in0, scalar1=s1, scalar2=s2, op0=op0, op1=op1), r=r, w=w)

    def stt(self, eng, out, in0, scalar, in1, op0, op1, r, w):
        self.S.op(eng, lambda e: e.scalar_tensor_tensor(out=out, in0=in0, scalar=scalar, in1=in1, op0=op0, op1=op1), r=r, w=w)

    def copy(self, eng, out, in_, r, w):
        if eng == "act":
            self.S.op("act", lambda e: e.copy(out=out, in_=in_), r=r, w=w)
        else:
            self.S.op(eng, lambda e: e.tensor_copy(out=out, in_=in_), r=r, w=w)

    def recip(self, out, in_, r, w):
        self.S.op("dve", lambda e: e.reciprocal(out=out, in_=in_), r=r, w=w)

    def memset(self, eng, ap, val, w):
        self.S.op(eng, lambda e: e.memset(ap, val), w=w)


def mod_columns(kb, modw, modb_col, ccol, col0, nchunks, name, stg, psm):
    sc = kb.sbt(name + "_sc", [128, 8, 2])
    kb.dma("sp", sc, ccol, w=[name + "_sc"])
    kb.act(sc, sc, AF.Silu, r=[name + "_sc"], w=[name + "_sc"])
    mb = kb.sbt(name + "_mb", [128, nchunks])
    kb.dma("sp", mb, modb_col, w=[name + "_mb"])
    for jg in range(nchunks // 4):
        b = kb.rot("mstg", 2)
        kb.dma("sp", stg[b].rearrange("p (kt n) -> p kt n", kt=8),
               modw[:, col0 + jg * 512:col0 + (jg + 1) * 512].rearrange("(kt p) n -> p kt n", p=128), w=["mstg%d" % b])
        for jj in range(4):
            j = jg * 4 + jj
            for kt in range(8):
                kb.mm(psm[:, j, :], stg[b][:, kt * 512 + jj * 128:kt * 512 + (jj + 1) * 128], sc[:, kt, :], kt == 0, kt == 7,
                      r=["mstg%d" % b, name + "_sc"], w=["psm"])
    modc = kb.sbt(name + "_modc", [128, nchunks, 2])
    for r_ in range(2):
        kb.tt("dve", modc[:, :, r_], psm[:, :, r_], mb, ALU.add, r=["psm", name + "_mb"], w=[name + "_modc"])
    return modc


def emit_mod(kb, l, modc_d, grow_d, mkey):
    kb.scope_begin("MOD%d" % l, "L%d" % l)
    ccol = kb.inp("ccol", [128, 8, 2], lvl="global")
    modw = kb.inp("modw", [D, 6 * D], lvl="layer")
    modb_col = kb.inp("modb_col", [128, 48], lvl="layer")
    modb_row = kb.inp("modb_row", [6 * D], lvl="layer")
    pA = kb.ps("pA", [128, 512]); pB = kb.ps("pB", [128, 512]); pC = kb.ps("pC", [128, 512])
    pD = kb.ps("pD", [128, 512]); pE = kb.ps("pE", [128, 512])
    kb.phase_begin()
    mstg = [kb.sbt("mstg%d" % i, [128, 4096]) for i in range(2)]
    psm = pE.rearrange("p (c r) -> p c r", r=2)[:, 0:16, :]
    for i in range(3):
        mc = mod_columns(kb, modw, modb_col[:, 16 * i:16 * (i + 1)], ccol, 2048 * i, 16, "mc%d" % i, mstg, psm)
        kb.dma("sp", modc_d[:, 16 * i:16 * (i + 1), :], mc, r=["mc%d_modc" % i], w=[mkey])
    scr = kb.sbt("scr", [128, 8, 2])
    kb.dma("sp", scr, ccol, w=["scr"])
    kb.act(scr, scr, AF.Silu, r=["scr"], w=["scr"])
    grow = kb.sbt("grow", [2, 2048])
    gb2 = kb.sbt("gb2", [2, 2048])
    kb.dma("sp", gb2[:, 0:1024], modb_row[2 * D:3 * D].partition_broadcast(2), w=["gb2"])
    kb.dma("sp", gb2[:, 1024:2048], modb_row[5 * D:6 * D].partition_broadcast(2), w=["gb2"])
    psr_l = [pA, pB, pC, pD]; psr_k = ["pA", "pB", "pC", "pD"]
    for kt in range(8):
        b = kt % 2
        kb.dma("sp", mstg[b][:, 0:1024], modw[kt * 128:(kt + 1) * 128, 2 * D:3 * D], w=["mstg%d" % b])
        kb.dma("sp", mstg[b][:, 1024:2048], modw[kt * 128:(kt + 1) * 128, 5 * D:6 * D], w=["mstg%d" % b])
        for j in range(4):
            kb.mm(psr_l[j][0:2, :], scr[:, kt, :], mstg[b][:, j * 512:(j + 1) * 512], kt == 0, kt == 7,
                  r=["mstg%d" % b, "scr"], w=[psr_k[j]])
    for j in range(4):
        kb.tt("dve", grow[:, j * 512:(j + 1) * 512], psr_l[j][0:2, :], gb2[:, j * 512:(j + 1) * 512], ALU.add,
              r=[psr_k[j], "gb2"], w=["grow"])
    kb.dma("sp", grow_d, grow, r=["grow"], w=[mkey])
    kb.phase_end()
    kb.scope_end()


def norm_transpose_tile(kb, x_dram_rows, uT_out, SC, SH, r_idx, ident_b, pst, tag, okey):
    i = kb.rot(tag + "_x", 3)
    xk = "%s_x%d" % (tag, i)
    x = kb._bufs[xk]
    kb.dma("sp", x, x_dram_rows, w=[xk])
    return norm_transpose_sb(kb, x, xk, uT_out, SC, SH, r_idx, ident_b, pst, tag, okey)


NXH = 4


def norm_part(kb, x, xk, tag):
    j = kb.rot(tag + "_s", NXH)
    junk, ss, xh = kb._bufs[tag + "_junk"], kb._bufs["%s_ss%d" % (tag, j)], kb._bufs["%s_xh%d" % (tag, j)]
    ssk, xhk = "%s_ss%d" % (tag, j), "%s_xh%d" % (tag, j)
    kb.memset("pool", ss, 0.0, w=[ssk])
    kb.act(junk, x, AF.Square, r=[xk, ssk], w=[tag + "_junk", ssk], accum=ss)
    kb.act(ss, ss, AF.Sqrt, r=[ssk], w=[ssk], bias=EPS_RMS, scale=1.0 / D)
    kb.recip(ss, ss, r=[ssk], w=[ssk])
    kb.ts("dve", xh, x, ss, None, ALU.mult, None, r=[xk, ssk], w=[xhk])
    return xh, xhk


def tr_part(kb, xh, xhk, uT_out, SC, SH, r_idx, ident_b, pst, okey):
    pk, psT = pst
    for kt in range(8):
        kb.tr(psT[:, kt * 128:(kt + 1) * 128], xh[:, kt * 128:(kt + 1) * 128], ident_b, r=[xhk, "ident_b"], w=[pk])
    for kt in range(8):
        eng = "act" if kt % 2 == 0 else "dve"
        if eng == "act":
            kb.act(uT_out[:, kt, :], psT[:, kt * 128:(kt + 1) * 128], AF.Identity, r=[pk, "SCSH"], w=[okey],
                   bias=SH[:, kt, r_idx:r_idx + 1], scale=SC[:, kt, r_idx:r_idx + 1])
        else:
            kb.ts("dve", uT_out[:, kt, :], psT[:, kt * 128:(kt + 1) * 128], SC[:, kt, r_idx:r_idx + 1],
                  SH[:, kt, r_idx:r_idx + 1], ALU.mult, ALU.add, r=[pk, "SCSH"], w=[okey])


def norm_transpose_sb(kb, x, xk, uT_out, SC, SH, r_idx, ident_b, pst, tag, okey):
    xh, xhk = norm_part(kb, x, xk, tag)
    tr_part(kb, xh, xhk, uT_out, SC, SH, r_idx, ident_b, pst, okey)


def alloc_norm_xbufs(kb, tag):
    for i in range(3):
        kb._bufs["%s_x%d" % (tag, i)] = kb.sbt("%s_x%d" % (tag, i), [128, D])


def alloc_norm_bufs(kb, tag):
    kb._bufs[tag + "_junk"] = kb.sb(tag + "_junk", [128, D], BF16)
    for j in range(NXH):
        kb._bufs["%s_ss%d" % (tag, j)] = kb.sb("%s_ss%d" % (tag, j), [128, 1])
        kb._bufs["%s_xh%d" % (tag, j)] = kb.sb("%s_xh%d" % (tag, j), [128, D], BF16)


def load_rows(kb, x, xk, hsrc, hkey, row0, nrows_total):
    lo, hi = max(row0, 0), min(row0 + 128, nrows_total)
    if lo > row0 or hi < row0 + 128:
        kb.memset("pool", x, 0.0, w=[xk])
    if hi > lo:
        kb.dma("sp", x[lo - row0:hi - row0, :], hsrc[lo:hi, :], r=[hkey], w=[xk])


def emit_F(kb, l, g, hsrc, hkey, yint, ykey, hdst, hdkey, dst_is_final, wb16, modc_d, grow_d, mkey, uTd):
    layer0 = (l == 0)
    kb.scope_begin("F%d%d" % (l, g), "L%d" % l)
    nc = kb.nc
    NT = 17 if layer0 else 16
    NP = 19 if layer0 else 17
    TP = NP * 128
    TT = NT * 128
    lat0 = 256 + g * 2048
    ctx0 = g * 128
    maskp = kb.inp("maskp%d" % g, [19 * 128], lvl="global")
    ccol = kb.inp("ccol", [128, 8, 2], lvl="global")
    modw = kb.inp("modw", [D, 6 * D], lvl="layer")
    modb_col = kb.inp("modb_col", [128, 48], lvl="layer")
    modb_row = kb.inp("modb_row", [6 * D], lvl="layer")
    n1g = kb.inp("n1g", [128, 8], lvl="layer")
    n2g = kb.inp("n2g", [128, 8], lvl="layer")
    w_pc, wpc_key = wb16["w_pc"]
    bands = kb.inp("bands%d" % g, [128, 32, 128], lvl="global")
    poolw2 = kb.inp("poolw2", [64, 4, 128], lvl="layer")
    pscale = kb.inp("pscale", [128, 2], lvl="layer")
    dww = kb.inp("dww", [128, 2, 31], lvl="layer")
    dwb = kb.inp("dwb", [128, 2], lvl="layer")
    lng = kb.inp("lng", [128, 2], lvl="layer")
    lnb = kb.inp("lnb", [128, 2], lvl="layer")
    pww = kb.inp("pww", [256, 256], lvl="layer")
    w_out, wout_key = wb16["w_out"]
    fwi, fwi_key = wb16["fwi"]
    fwo, fwo_key = wb16["fwo"]
    identf_d = kb.inp("identf", [128, 128], lvl="global")
    sel2_d = kb.inp("sel2", [2, 2, 128], lvl="global")

    identf = kb.sb("identf", [128, 128])
    kb.dma("sp", identf, identf_d, w=["identf"])
    ident_b = kb.sb("ident_b", [128, 128], BF16)
    kb.copy("dve", ident_b, identf, r=["identf"], w=["ident_b"])
    onesf = kb.sb("onesf", [128, 128])
    kb.memset("pool", onesf, 1.0, w=["onesf"])
    sel2 = kb.sb("sel2", [2, 2, 128])
    kb.dma("sp", sel2, sel2_d, w=["sel2"])
    SC1 = kb.sb("SC1", [128, 8, 2]); SH1 = kb.sb("SH1", [128, 8, 2])
    SC2 = kb.sb("SC2", [128, 8, 2]); SH2 = kb.sb("SH2", [128, 8, 2])
    Gbc = [kb.sb("Gbc%d" % r_, [128, 2048]) for r_ in range(2 if layer0 else 1)]
    yT = kb.sb("yT", [128, 8, TT], BF16)
    alloc_norm_bufs(kb, "n1")
    pT = kb.ps("pT", [128, 1024], BF16)
    pA = kb.ps("pA", [128, 512]); pB = kb.ps("pB", [128, 512]); pC = kb.ps("pC", [128, 512])
    pD = kb.ps("pD", [128, 512]); pE = kb.ps("pE", [128, 512])
    for kt in range(4):
        kb.dma("sp", yT[:, kt, 0:2048], yint[kt * 128:(kt + 1) * 128, lat0:lat0 + 2048], r=[ykey], w=["yT%d" % kt])
        if layer0:
            kb.dma("sp", yT[:, kt, 2048:2176], yint[kt * 128:(kt + 1) * 128, ctx0:ctx0 + 128], r=[ykey], w=["yT%d" % kt])

    kb.phase_begin()
    mc1 = kb.sbt("mc1", [128, 16, 2]); mc2 = kb.sbt("mc2", [128, 16, 2])
    kb.dma("sp", mc1, modc_d[:, 0:16, :], r=[mkey], w=["mc1_modc"])
    kb.dma("sp", mc2, modc_d[:, 24:40, :], r=[mkey], w=["mc2_modc"])
    g1n = kb.sbt("g1n", [128, 8]); g2n = kb.sbt("g2n", [128, 8])
    kb.dma("sp", g1n, n1g, w=["g1n"]); kb.dma("sp", g2n, n2g, w=["g2n"])
    for (mc, gn, SC, SH, nm) in ((mc1, g1n, SC1, SH1, "1"), (mc2, g2n, SC2, SH2, "2")):
        for r_ in range(2):
            kb.stt("dve", SC[:, :, r_], mc[:, 8:16, r_], 1.0, gn, ALU.add, ALU.mult,
                   r=["mc%s_modc" % nm, "g%sn" % nm], w=["SCSH"])
            kb.copy("dve", SH[:, :, r_], mc[:, 0:8, r_], r=["mc%s_modc" % nm], w=["SCSH"])
    grow = kb.sbt("grow", [2, 2048])
    kb.dma("sp", grow, grow_d, r=[mkey], w=["grow"])
    psr_l = [pA, pB, pC, pD]; psr_k = ["pA", "pB", "pC", "pD"]
    for r_ in range(len(Gbc)):
        for j in range(4):
            kb.mm(psr_l[j], sel2[:, r_, :], grow[:, j * 512:(j + 1) * 512], True, True, r=["sel2", "grow"], w=[psr_k[j]])
        for j in range(4):
            kb.copy("act", Gbc[r_][:, j * 512:(j + 1) * 512], psr_l[j], r=[psr_k[j]], w=["Gbc%d" % r_])

    wpc_sb = kb.sbt("wpc", [128, 8, 768], BF16)
    kb.dma("sp", wpc_sb, w_pc.rearrange("(kt p) n -> p kt n", p=128), r=[wpc_key], w=["wpc"])
    pww_sb = kb.sbt("pww", [128, 2, 256], BF16)
    kb.dma("pool", pww_sb, pww.rearrange("(ct p) n -> p ct n", p=128), w=["pww"])
    poolw_sb = kb.sbt("poolw", [64, 4, 128], BF16)
    kb.dma("pool", poolw_sb, poolw2, w=["poolw"])
    bands_sb = kb.sbt("bands", [128, 32, 128])
    kb.dma("sp", bands_sb, bands, w=["bands"])
    small = {}
    for nm, src, shp in (("pscale", pscale, [128, 2]), ("dww", dww, [128, 2, 31]), ("dwb", dwb, [128, 2]),
                         ("lng", lng, [128, 2]), ("lnb", lnb, [128, 2])):
        small[nm] = kb.sbt(nm, shp)
        kb.dma("sp", small[nm], src, w=[nm])
    maskb = kb.sbt("maskb", [128, TP], BF16)
    kb.dma("pool", maskb, maskp[0:TP].partition_broadcast(128), w=["maskb"])
    Dg = kb.sbt("Dg", [128, 2, 31, 128], BF16)
    for ct in range(2):
        for k in range(31):
            if k % 3 == 0:
                kb.act(Dg[:, ct, k, :], identf, AF.Copy, r=["identf", "dww"], w=["Dg"], scale=small["dww"][:, ct, k:k + 1])
            else:
                kb.ts("pool" if k % 3 == 1 else "dve", Dg[:, ct, k, :], identf, small["dww"][:, ct, k:k + 1], None, ALU.mult, None,
                      r=["identf", "dww"], w=["Dg"])
    alloc_norm_xbufs(kb, "n1")
    u_p = kb.sbt("u_p", [128, NP, 256])
    GLU = kb.sbt("GLU", [128, 2, TP], BF16)
    uTt = [kb.sbt("uTt%d" % i, [128, 8, 128], BF16) for i in range(4)]
    sg = kb.sbt("sg", [128, 2, 128]); gl = kb.sbt("gl", [128, 2, 128])
    for i in range(NP):
        r_idx = 0 if i < 17 else 1
        b = i % 4
        if i < 17:
            t0_, tbase, tlen = g * 2048 - 16 + i * 128, 256, 4096
        else:
            t0_, tbase, tlen = g * 128 - 16 + (i - 17) * 128, 0, 256
        lo_, hi_ = max(t0_, 0), min(t0_ + 128, tlen)
        if lo_ > t0_ or hi_ < t0_ + 128:
            kb.memset("pool", uTt[b], 0.0, w=["uTt%d" % b])
        if hi_ > lo_:
            kb.dma("sp", uTt[b][:, :, lo_ - t0_:hi_ - t0_], uTd[:, :, tbase + lo_:tbase + hi_], r=["uTd%d" % l], w=["uTt%d" % b])
        pcv = pA.rearrange("p (c t) -> p c t", c=4)
        for cti in range(4):
            for kt in range(8):
                kb.mm(pcv[:, cti, :], wpc_sb[:, kt, 256 + cti * 128:256 + (cti + 1) * 128], uTt[b][:, kt, :], kt == 0, kt == 7,
                      r=["wpc", "uTt%d" % b], w=["pA"])
        for kt in range(8):
            kb.mm(pB[:, 0:256], uTt[b][:, kt, :], wpc_sb[:, kt, 0:256], kt == 0, kt == 7, r=["wpc", "uTt%d" % b], w=["pB"])
        kb.act(sg, pcv[:, 2:4, :], AF.Sigmoid, r=["pA"], w=["sg"])
        kb.tt("dve", gl, pcv[:, 0:2, :], sg, ALU.mult, r=["pA", "sg"], w=["gl"])
        kb.tt("pool", GLU[:, :, i * 128:(i + 1) * 128], gl, maskb[:, i * 128:(i + 1) * 128].unsqueeze(1).to_broadcast([128, 2, 128]),
              ALU.mult, r=["gl", "maskb"], w=["GLU%d" % i])
        kb.copy("act", u_p[:, i, :], pB[:, 0:256], r=["pB"], w=["u_p%d" % i])

    pooled = kb.sbt("pooled", [64, 4, 128], BF16)
    blocks = [(j, j, (0 if j == 0 else (2 if j == 15 else 1))) for j in range(16)]
    if layer0:
        blocks.append((16, 17, 3))
    for (ot, pt, cls) in blocks:
        ppv = pC.rearrange("p (g t) -> p g t", g=4)
        for gi in range(4):
            for half in range(2):
                kb.mm(ppv[0:64, gi, :], u_p[:, pt + half, gi * 64:(gi + 1) * 64], bands_sb[:, (cls * 4 + gi) * 2 + half, :],
                      half == 0, half == 1, r=["u_p%d" % (pt + half), "bands"], w=["pC"])
        kb.copy("act", pooled, ppv[0:64, :, :], r=["pC"], w=["pooled"])
        pqv = pD.rearrange("p (g t) -> p g t", g=4)
        for pr in range(2):
            for gl_ in range(2):
                gi = pr * 2 + gl_
                kb.mm(pqv[:, pr, :], poolw_sb[:, gi, :], pooled[:, gi, :], gl_ == 0, gl_ == 1, r=["poolw", "pooled"], w=["pD"])
        for pr in range(2):
            kb.ts("dve", yT[:, 4 + pr, ot * 128:(ot + 1) * 128], pqv[:, pr, :], small["pscale"][:, pr:pr + 1], None, ALU.mult, None,
                  r=["pD", "pscale"], w=["yT%d" % (4 + pr)])

    xc = kb.sbt("xc", [128, 2, 512]); sq = kb.sbt("sq", [128, 2, 512])
    mean = kb.sbt("mean", [128, 512]); var = kb.sbt("var", [128, 512]); t1 = kb.sbt("t1", [128, 2, 512])
    zs = kb.sbt("zs", [128, 2, 512], BF16)
    cblocks = [(jb * 512, jb * 512 + 1, 512) for jb in range(4)]
    if layer0:
        cblocks.append((16 * 128, 17 * 128 + 1, 128))
    for (oc, gc, n) in cblocks:
        for ct in range(2):
            pc_ = pA if ct == 0 else pB
            for k in range(31):
                kb.mm(pc_[:, 0:n], Dg[:, ct, k, :], GLU[:, ct, gc + k:gc + k + n], k == 0, k == 30,
                      r=["Dg"] + ["GLU%d" % i for i in range((gc + k) // 128, (gc + k + n - 1) // 128 + 1)], w=["pA" if ct == 0 else "pB"])
            kb.act(xc[:, ct, 0:n], pc_[:, 0:n], AF.Identity, r=["pA" if ct == 0 else "pB", "dwb"], w=["xc"], bias=small["dwb"][:, ct:ct + 1])
        kb.tt("pool", sq[:, :, 0:n], xc[:, :, 0:n], xc[:, :, 0:n], ALU.mult, r=["xc"], w=["sq"])
        for ct in range(2):
            kb.mm(pC[:, 0:n], onesf, xc[:, ct, 0:n], ct == 0, ct == 1, r=["onesf", "xc"], w=["pC"])
        for ct in range(2):
            kb.mm(pD[:, 0:n], onesf, sq[:, ct, 0:n], ct == 0, ct == 1, r=["onesf", "sq"], w=["pD"])
        kb.act(mean[:, 0:n], pC[:, 0:n], AF.Copy, r=["pC"], w=["mean"], scale=1.0 / 256)
        kb.tt("dve", var[:, 0:n], mean[:, 0:n], mean[:, 0:n], ALU.mult, r=["mean"], w=["var"])
        kb.stt("dve", var[:, 0:n], pD[:, 0:n], 1.0 / 256, var[:, 0:n], ALU.mult, ALU.subtract, r=["pD", "var"], w=["var"])
        kb.act(var[:, 0:n], var[:, 0:n], AF.Sqrt, r=["var"], w=["var"], bias=EPS_LN)
        kb.recip(var[:, 0:n], var[:, 0:n], r=["var"], w=["var"])
        for ct in range(2):
            kb.tt("dve", t1[:, ct, 0:n], xc[:, ct, 0:n], mean[:, 0:n], ALU.subtract, r=["xc", "mean"], w=["t1"])
            kb.tt("pool", t1[:, ct, 0:n], t1[:, ct, 0:n], var[:, 0:n], ALU.mult, r=["t1", "var"], w=["t1"])
            kb.act(zs[:, ct, 0:n], t1[:, ct, 0:n], AF.Silu, r=["t1", "lng", "lnb"], w=["zs"],
                   bias=small["lnb"][:, ct:ct + 1], scale=small["lng"][:, ct:ct + 1])
        for dt_ in range(2):
            for ct in range(2):
                kb.mm(pE[:, 0:n], pww_sb[:, ct, dt_ * 128:(dt_ + 1) * 128], zs[:, ct, 0:n], ct == 0, ct == 1, r=["pww", "zs"], w=["pE"])
            kb.copy("act", yT[:, 6 + dt_, oc:oc + n], pE[:, 0:n], r=["pE"], w=["yT%d" % (6 + dt_)])
    kb.phase_end()

    kb.phase_begin()
    wout_sb = kb.sbt("wout", [128, 8, D], BF16)
    for kt in range(8):
        kb.dma("sp", wout_sb[:, kt, :], w_out[kt * 128:(kt + 1) * 128, :], r=[wout_key], w=["wout"])
    fwo_sb = kb.sbt("fwo", [128, 22, D], BF16)
    for fc in range(22):
        kb.dma("act", fwo_sb[:, fc, :], fwo[fc * 128:(fc + 1) * 128, :], r=[fwo_key], w=["fwo"])
    h0 = [kb.sbt("h0_%d" % i, [128, D]) for i in range(1)]
    h1 = kb.sbt("h1", [128, 4, D])
    u2T = kb.sbt("u2T", [128, 8, 512], BF16)
    actT = kb.sbt("actT", [128, 22, 512], BF16)
    wch = [kb.sbt("wch%d" % i, [128, 8, 256], BF16) for i in range(5)]
    sgf = [kb.sbt("sgf%d" % i, [128, 512]) for i in range(2)]
    tmpo = [kb.sbt("tmpo%d" % i, [128, 512]) for i in range(2)]
    obuf = [kb.sbt("obuf%d" % i, [128, D]) for i in range(1)]
    fwi_v = fwi.rearrange("(kt p) n -> p kt n", p=128)
    ykeys = ["yT%d" % k for k in range(8)]
    groups = [list(range(g * 4, min(g * 4 + 4, NT))) for g in range((NT + 3) // 4)]
    for grp in groups:
        n = len(grp) * 128
        for li, ti in enumerate(grp):
            r_idx = 1 if ti == 16 else 0
            g_bc = Gbc[r_idx]
            hb = kb.rot("h0", 1)
            hr0 = (lat0 + ti * 128) if ti < 16 else ctx0
            kb.dma("sp", h0[hb], hsrc[hr0:hr0 + 128, :], r=[hkey], w=["h0_%d" % hb])
            for nh in range(2):
                pp = pA if nh == 0 else pB
                pk = "pA" if nh == 0 else "pB"
                for kt in range(8):
                    kb.mm(pp, yT[:, kt, ti * 128:(ti + 1) * 128], wout_sb[:, kt, nh * 512:(nh + 1) * 512], kt == 0, kt == 7,
                          r=["wout", ykeys[kt]], w=[pk])
                tb = kb.rot("tmpo", 2)
                kb.tt("dve", tmpo[tb], pp, g_bc[:, nh * 512:(nh + 1) * 512], ALU.mult, r=[pk, "Gbc%d" % r_idx], w=["tmpo%d" % tb])
                kb.tt("pool", h1[:, li, nh * 512:(nh + 1) * 512], tmpo[tb], h0[hb][:, nh * 512:(nh + 1) * 512], ALU.add,
                      r=["tmpo%d" % tb, "h0_%d" % hb], w=["h1_%d" % li])
            norm_transpose_sb(kb, h1[:, li, :], "h1_%d" % li, u2T[:, :, li * 128:(li + 1) * 128], SC2, SH2, r_idx, ident_b,
                              ("pT", pT), "n1", "u2T")
        pF = kb.ps("pF", [128, 512]); pG = kb.ps("pG", [128, 512])
        for fc in range(22):
            wb = kb.rot("wch", 5)
            (pg_, pgk), (pu_, puk) = ((pC, "pC"), (pD, "pD")) if fc % 2 == 0 else ((pF, "pF"), (pG, "pG"))
            kb.dma("sp", wch[wb][:, :, 0:128], fwi_v[:, :, fc * 128:(fc + 1) * 128], r=[fwi_key], w=["wch%d" % wb])
            kb.dma("sp", wch[wb][:, :, 128:256], fwi_v[:, :, DFF + fc * 128:DFF + (fc + 1) * 128], r=[fwi_key], w=["wch%d" % wb])
            for kt in range(8):
                kb.mm(pg_[:, 0:n], wch[wb][:, kt, 0:128], u2T[:, kt, 0:n], kt == 0, kt == 7, r=["wch%d" % wb, "u2T"], w=[pgk])
            for kt in range(8):
                kb.mm(pu_[:, 0:n], wch[wb][:, kt, 128:256], u2T[:, kt, 0:n], kt == 0, kt == 7, r=["wch%d" % wb, "u2T"], w=[puk])
            sb_ = kb.rot("sgf", 2)
            kb.act(sgf[sb_][:, 0:n], pg_[:, 0:n], AF.Silu, r=[pgk], w=["sgf%d" % sb_])
            kb.tt("dve", actT[:, fc, 0:n], sgf[sb_][:, 0:n], pu_[:, 0:n], ALU.mult, r=["sgf%d" % sb_, puk], w=["actT"])
        for li, ti in enumerate(grp):
            r_idx = 1 if ti == 16 else 0
            g_bc = Gbc[r_idx]
            ob = kb.rot("obuf", 1)
            for nh in range(2):
                pp = pA if nh == 0 else pB
                pk = "pA" if nh == 0 else "pB"
                for fc in range(22):
                    kb.mm(pp, actT[:, fc, li * 128:(li + 1) * 128], fwo_sb[:, fc, nh * 512:(nh + 1) * 512], fc == 0, fc == 21,
                          r=["actT", "fwo"], w=[pk])
                tb = kb.rot("tmpo", 2)
                kb.tt("dve", tmpo[tb], pp, g_bc[:, 1024 + nh * 512:1024 + (nh + 1) * 512], ALU.mult,
                      r=[pk, "Gbc%d" % r_idx], w=["tmpo%d" % tb])
                kb.tt("pool", obuf[ob][:, nh * 512:(nh + 1) * 512], tmpo[tb], h1[:, li, nh * 512:(nh + 1) * 512], ALU.add,
                      r=["tmpo%d" % tb, "h1_%d" % li], w=["obuf%d" % ob])
            if dst_is_final:
                od0 = g * 2048 + ti * 128
            else:
                od0 = (lat0 + ti * 128) if ti < 16 else ctx0
            kb.dma("sp", hdst[od0:od0 + 128, :], obuf[ob], r=["obuf%d" % ob], w=[hdkey])
    kb.phase_end()
    kb.scope_end()

import math

NTK = 34
TTK = NTK * 128


def c_of_pos(d, pos):
    if d == 0:
        return pos
    return 1 - pos if pos < 2 else 35 - pos


def emit_M(kb, l, p, hsrc, hkey, yint, ykey, modc_d, mkey, uTd, hook=None):
    layer0 = (l == 0)
    lam_init = 0.8 - 0.6 * math.exp(-0.3 * l)
    stop_after = None
    kb.scope_begin("M%d%d" % (l, p), "L%d" % l)
    nc = kb.nc
    ccol = kb.inp("ccol", [128, 8, 2], lvl="global")
    modw = kb.inp("modw", [D, 6 * D], lvl="layer")
    modb_col = kb.inp("modb_col", [128, 48], lvl="layer")[:, 0:16]
    n1g = kb.inp("n1g", [128, 8], lvl="layer")
    w_fm = kb.inp("w_fm", [D, 512])
    w_tm = kb.inp("w_tm", [D, 528])
    gate_b = kb.inp("gate_b", [8])
    mng = kb.inp("mng", [64], lvl="layer")
    qkng = kb.inp("qkng", [128, 2], lvl="layer")
    lamqk = kb.inp("lamqk", [2, 2, 32], lvl="layer")
    sublng = kb.inp("sublng", [64, 1], lvl="layer")
    identf_d = kb.inp("identf", [128, 128], lvl="global")
    tri_d = kb.inp("tri", [128, 2, 128], lvl="global")
    mask_d = kb.inp("mask", [128, 2, 128], lvl="global")
    bones_d = kb.inp("bones", [128, 128], lvl="global")
    perm_d = kb.inp("perm", [128, 128], lvl="global")
    mcol_d = kb.inp("mcol", [128, 4], lvl="global")
    permB_d = kb.inp("permB", [68, 68], lvl="global")
    sgn_d = kb.inp("sgn", [2, 64], lvl="global")
    cos_d = kb.inp("cost", [128, TTK], lvl="global")
    sin_d = kb.inp("sint", [128, TTK], lvl="global")
    selZ_d0 = kb.inp("selZ", [128, 64], lvl="global")

    identf = kb.sb("identf", [128, 128]); kb.dma("sp", identf, identf_d, w=["identf"])
    ident_b = kb.sb("ident_b", [128, 128], BF16); kb.copy("dve", ident_b, identf, r=["identf"], w=["ident_b"])
    onesf = kb.sb("onesf", [128, 128]); kb.memset("pool", onesf, 1.0, w=["onesf"])
    tri = kb.sb("tri", [128, 2, 128]); kb.dma("sp", tri, tri_d, w=["tri"])
    mask = kb.sb("mask", [128, 2, 128]); kb.dma("sp", mask, mask_d, w=["mask"])
    bones = kb.sb("bones", [128, 128], BF16); kb.dma("pool", bones, bones_d, w=["bones"])
    perm = kb.sb("perm", [128, 128], BF16); kb.dma("pool", perm, perm_d, w=["perm"])
    mcol = kb.sb("mcol", [128, 4]); kb.dma("sp", mcol, mcol_d, w=["mcol"])
    permB = kb.sb("permB", [68, 68]); kb.dma("sp", permB, permB_d, w=["permB"])
    sgn = kb.sb("sgn", [2, 64]); kb.dma("sp", sgn, sgn_d, w=["sgn"])
    qkg = kb.sb("qkg", [128, 2]); kb.dma("sp", qkg, qkng, w=["qkg"])
    sublg = kb.sb("sublg", [64, 1]); kb.dma("sp", sublg, sublng, w=["sublg"])
    gbb = kb.sb("gbb", [128, 8]); kb.dma("sp", gbb, gate_b.partition_broadcast(128), w=["gbb"])
    mngb = kb.sb("mngb", [128, 64]); kb.dma("sp", mngb, mng.partition_broadcast(128), w=["mngb"])
    SC1 = kb.sb("SC1", [128, 8, 2]); SH1 = kb.sb("SH1", [128, 8, 2])
    negLam = kb.sb("negLam", [64, 1])
    qT = kb.sb("qT", [64, 2, TTK], BF16); kT = kb.sb("kT", [64, 2, TTK], BF16)
    k_tm = kb.sb("k_tm", [128, NTK, 128], BF16)
    vext = kb.sb("vext", [128, NTK, 2, 72], BF16)
    Vx = kb.sb("Vx", [128, NTK, 2, 128], BF16)
    sigo = kb.sb("sigo", [128, NTK, 128], BF16)
    G = kb.sb("G", [128, NTK, 8])
    QT = kb.sb("QT", [128, TTK], BF16)
    KTm = kb.sb("KTm", [128, 4, TTK], BF16)
    selZ_d = selZ_d0
    selZ = kb.sb("selZ", [128, 64]); kb.dma("sp", selZ, selZ_d, w=["selZ"])
    kb.memset("pool", vext, 1.0, w=["vext"]); kb.memset("pool", Vx, 0.0, w=["Vx"])
    kb.memset("pool", Vx[:, :, :, 64:65], 1.0, w=["Vx"])
    alloc_norm_bufs(kb, "n1")
    pT = kb.ps("pT", [128, 1024], BF16)
    pA = kb.ps("pA", [128, 512]); pB = kb.ps("pB", [128, 512]); pC = kb.ps("pC", [128, 512])
    pD = kb.ps("pD", [128, 512]); pE = kb.ps("pE", [128, 512]); pF = kb.ps("pF", [128, 512]); pG = kb.ps("pG", [128, 512])

    kb.phase_begin()
    mc1 = kb.sbt("mc1", [128, 16, 2])
    kb.dma("sp", mc1, modc_d[:, 0:16, :], r=[mkey], w=["mc1_modc"])
    g1n = kb.sbt("g1n", [128, 8]); kb.dma("sp", g1n, n1g, w=["g1n"])
    for r_ in range(2):
        kb.stt("dve", SC1[:, :, r_], mc1[:, 8:16, r_], 1.0, g1n, ALU.add, ALU.mult, r=["mc1_modc", "g1n"], w=["SCSH"])
        kb.copy("dve", SH1[:, :, r_], mc1[:, 0:8, r_], r=["mc1_modc"], w=["SCSH"])
    lqk = kb.sbt("lqk", [2, 2, 32]); kb.dma("sp", lqk, lamqk, w=["lqk"])
    lpr = kb.sbt("lpr", [2, 2, 32]); lsm = kb.sbt("lsm", [2, 2])
    for j_ in range(2):
        kb.tt("dve", lpr[:, j_, :], lqk[:, 0, :], lqk[:, 1, :], ALU.mult, r=["lqk"], w=["lpr"])
    kb.S.op("dve", lambda e: e.tensor_reduce(out=lsm, in_=lpr, axis=AX.X, op=ALU.add), r=["lpr"], w=["lsm"])
    kb.act(lsm, lsm, AF.Exp, r=["lsm"], w=["lsm"])
    kb.mm(pF[0:64, 0:2], sgn, lsm, True, True, r=["sgn", "lsm"], w=["pF"])
    kb.ts("dve", negLam, pF[0:64, 0:1], -float(lam_init), None, ALU.add, None, r=["pF"], w=["negLam"])

    wfm = kb.sbt("wfm", [128, 8, 512], BF16)
    kb.dma("pool", wfm, w_fm.rearrange("(kt p) n -> p kt n", p=128), w=["wfm"])
    wtm = kb.sbt("wtm", [128, 8, 528], BF16)
    kb.dma("pool", wtm, w_tm.rearrange("(kt p) n -> p kt n", p=128), w=["wtm"])
    if hook is not None:
        hook()
    alloc_norm_xbufs(kb, "n1")
    uTb = [kb.sbt("uTb%d" % i, [128, 8, 512], BF16) for i in range(2)]
    cosb = [kb.sbt("cosb%d" % i, [128, 512]) for i in range(2)]
    sinb = [kb.sbt("sinb%d" % i, [128, 512]) for i in range(2)]
    Ab = kb.sbt("Ab", [128, 512], BF16); sqb = kb.sbt("sqb", [128, 512], BF16)
    rs = kb.sbt("rs", [128, 512]); t1 = kb.sbt("t1", [128, 512]); t2 = kb.sbt("t2", [128, 512])
    blocks = [(0, 256)] + [(256 + 512 * i, 512) for i in range(8)]
    Abq = [kb.sbt("Abq%d" % i, [128, 512], BF16) for i in range(2)]
    sqq = [kb.sbt("sqq%d" % i, [128, 512], BF16) for i in range(2)]

    def stage1(bi):
        c0, n = blocks[bi]
        ub = bi % 2
        uk = "uTb%d" % ub
        kb.dma("sp", cosb[ub][:, 0:n], cos_d[:, c0:c0 + n], w=["cosb%d" % ub])
        kb.dma("sp", sinb[ub][:, 0:n], sin_d[:, c0:c0 + n], w=["sinb%d" % ub])
        if p == 0:
            nq_ = []
            for li in range(n // 128):
                c = c0 // 128 + li
                xi = kb.rot("n1_x", 3)
                xk_ = "n1_x%d" % xi
                kb.dma("sp", kb._bufs[xk_], hsrc[c * 128:(c + 1) * 128, :], r=[hkey], w=[xk_])
                nq_.append(norm_part(kb, kb._bufs[xk_], xk_, "n1"))
            for li in range(n // 128):
                c = c0 // 128 + li
                tr_part(kb, nq_[li][0], nq_[li][1], uTb[ub][:, :, li * 128:(li + 1) * 128], SC1, SH1,
                        1 if c < 2 else 0, ident_b, ("pT", pT), uk)
            kb.dma("sp", uTd[:, :, c0:c0 + n], uTb[ub][:, :, 0:n], r=[uk], w=["uTd%d" % l])
        else:
            kb.dma("sp", uTb[ub][:, :, 0:n], uTd[:, :, c0:c0 + n], r=["uTd%d" % l], w=[uk])
        for which, dst, dk in ((0, qT, "qT"), (1, kT, "kT")):
            for h in range(2):
                pp, pk = (pA, "pA") if h == 0 else (pB, "pB")
                cs = which * 128 + h * 64
                for kt in range(8):
                    kb.mm(pp[0:64, 0:n], wfm[:, kt, cs:cs + 64], uTb[ub][:, kt, 0:n], kt == 0, kt == 7, r=["wfm", uk], w=[pk])
                kb.copy("act" if h == 0 else "dve", dst[:, h, c0:c0 + n], pp[0:64, 0:n], r=[pk], w=[dk])
        for which in (0, 1):
            cs = 256 + which * 128
            for kt in range(8):
                kb.mm(pC[:, 0:n], wfm[:, kt, cs:cs + 128], uTb[ub][:, kt, 0:n], kt == 0, kt == 7, r=["wfm", uk], w=["pC"])
            kb.act(Abq[which][:, 0:n], pC[:, 0:n], AF.Copy, r=["pC", "qkg"], w=["Abq%d_%d" % (which, bi % 2)], scale=qkg[:, which:which + 1])
            kb.act(sqq[which][:, 0:n], pC[:, 0:n], AF.Square, r=["pC"], w=["sqq%d_%d" % (which, bi % 2)])
        for li in range(n // 128):
            c = c0 // 128 + li
            lt = uTb[ub][:, :, li * 128:(li + 1) * 128]
            for kt in range(8):
                kb.mm(pF, lt[:, kt, :], wtm[:, kt, 0:512], kt == 0, kt == 7, r=["wtm", uk], w=["pF"])
            for kt in range(8):
                kb.mm(pG[:, 0:16], lt[:, kt, :], wtm[:, kt, 512:528], kt == 0, kt == 7, r=["wtm", uk], w=["pG"])
            kb.copy("act", k_tm[:, c, :], pF[:, 0:128], r=["pF"], w=["k_tm"])
            kb.copy("act", vext[:, c, :, 0:64], pF[:, 128:256].rearrange("p (h v) -> p h v", h=2), r=["pF"], w=["vext"])
            kb.act(sigo[:, c, :], pF[:, 256:384], AF.Sigmoid, r=["pF"], w=["sigo"])
            kb.copy("act", Vx[:, c, :, 0:64], pF[:, 384:512].rearrange("p (h v) -> p h v", h=2), r=["pF"], w=["Vx"])
            kb.tt("dve", G[:, c, :], pG[:, 0:8], gbb, ALU.add, r=["pG", "gbb"], w=["G"])

    def stage2(bi):
        c0, n = blocks[bi]
        ub = bi % 2
        for which in (0, 1):
            Ab, sqb = Abq[which], sqq[which]
            kb.mm(pD[:, 0:n], bones, sqb[:, 0:n], True, True, r=["bones", "sqq%d_%d" % (which, bi % 2)], w=["pD"])
            kb.mm(pE[:, 0:n], perm, Ab[:, 0:n], True, True, r=["perm", "Abq%d_%d" % (which, bi % 2)], w=["pE"])
            kb.act(rs[:, 0:n], pD[:, 0:n], AF.Sqrt, r=["pD"], w=["rs"], bias=EPS_RMS, scale=1.0 / 32)
            kb.recip(rs[:, 0:n], rs[:, 0:n], r=["rs"], w=["rs"])
            kb.tt("dve", t1[:, 0:n], Ab[:, 0:n], cosb[ub][:, 0:n], ALU.mult, r=["Abq%d_%d" % (which, bi % 2), "cosb%d" % ub], w=["t1"])
            kb.tt("dve", t2[:, 0:n], pE[:, 0:n], sinb[ub][:, 0:n], ALU.mult, r=["pE", "sinb%d" % ub], w=["t2"])
            kb.tt("pool", t1[:, 0:n], t1[:, 0:n], t2[:, 0:n], ALU.add, r=["t1", "t2"], w=["t1"])
            if which == 0:
                kb.tt("pool", QT[:, c0:c0 + n], t1[:, 0:n], rs[:, 0:n], ALU.mult, r=["t1", "rs"], w=["QT"])
            else:
                kb.tt("pool", t2[:, 0:n], t1[:, 0:n], rs[:, 0:n], ALU.mult, r=["t1", "rs", "t2"], w=["t2"])
                for j in range(4):
                    kb.ts("dve" if j % 2 == 0 else "pool", KTm[:, j, c0:c0 + n], t2[:, 0:n], mcol[:, j:j + 1], None, ALU.mult, None,
                          r=["t2", "mcol"], w=["KT"])

    Abq2 = [[Abq[0], Abq[1]], [kb.sbt("Abq2", [128, 512], BF16), kb.sbt("Abq3", [128, 512], BF16)]]
    sqq2 = [[sqq[0], sqq[1]], [kb.sbt("sqq2", [128, 512], BF16), kb.sbt("sqq3", [128, 512], BF16)]]
    for bi in range(len(blocks)):
        Abq[0], Abq[1] = Abq2[bi % 2]
        sqq[0], sqq[1] = sqq2[bi % 2]
        _names = {0: ("Abq0", "Abq1", "sqq0", "sqq1"), 1: ("Abq2", "Abq3", "sqq2", "sqq3")}
        stage1(bi)
        if bi >= 1:
            Abq[0], Abq[1] = Abq2[(bi - 1) % 2]
            sqq[0], sqq[1] = sqq2[(bi - 1) % 2]
            stage2(bi - 1)
    Abq[0], Abq[1] = Abq2[(len(blocks) - 1) % 2]
    sqq[0], sqq[1] = sqq2[(len(blocks) - 1) % 2]
    stage2(len(blocks) - 1)
    kb.phase_end()

    kb.phase_begin()
    Gv = G.rearrange("p c (d g h) -> p c d g h", d=2, g=2, h=2)
    LFn = kb.sbt("LFn", [128, 2, NTK, 2]); bn = kb.sbt("bn", [128, 2, NTK, 2]); A_all = kb.sbt("A_all", [128, 2, NTK, 2])
    tmpg = kb.sbt("tmpg", [128, NTK, 2])
    AmaxCol = kb.sbt("AmaxCol", [68, 1]); BtotCol = kb.sbt("BtotCol", [68, 1]); rep = kb.sbt("rep", [68, 2, 128])
    AB = kb.sbt("AB", [128, 2, NTK, 2]); Mn = kb.sbt("Mn", [128, NTK, 2]); Mprev = kb.sbt("Mprev", [128, NTK, 2])
    Mu = kb.sbt("Mu", [128, 2, NTK, 2]); Ee = kb.sbt("Ee", [128, 2, NTK, 2]); Ecol = kb.sbt("Ecol", [128, 2, NTK])
    Wt = kb.sbt("Wt", [128, 2, NTK, 2]); NBt = kb.sbt("NBt", [128, 2, NTK, 2])
    Wk = kb.sbt("Wk", [128, 2, NTK, 2]); LOW = kb.sbt("LOW", [128, 2, NTK, 2])
    for d in range(2):
        kb.act(tmpg, Gv[:, :, d, 1, :], AF.Exp, r=["G"], w=["tmpg"], scale=-1.0)
        kb.act(LFn[:, d], tmpg, AF.Ln, r=["tmpg"], w=["LFn"], bias=1.0)
        pX = pA[:, 0:68]
        kb.mm(pX, tri[:, d, :], LFn[:, d].rearrange("p c h -> p (c h)"), True, True, r=["tri", "LFn"], w=["pA"])
        kb.copy("dve", bn[:, d].rearrange("p c h -> p (c h)"), pX, r=["pA"], w=["bn"])
        kb.tt("dve", A_all[:, d], Gv[:, :, d, 0, :], pX.rearrange("p (c h) -> p c h", h=2), ALU.add, r=["G", "pA"], w=["A_all"])
        kb.tr(pB[0:68, 0:128], A_all[:, d].rearrange("p c h -> p (c h)"), identf, r=["A_all", "identf"], w=["pB"])
        kb.S.op("dve", lambda e: e.tensor_reduce(out=AmaxCol, in_=pB[0:68, 0:128], axis=AX.X, op=ALU.max), r=["pB"], w=["AmaxCol"])
        kb.tr(pC[0:68, 0:128], bn[:, d].rearrange("p c h -> p (c h)"), identf, r=["bn", "identf"], w=["pC"])
        col = 127 if d == 0 else 0
        kb.copy("dve", BtotCol, pC[0:68, col:col + 1], r=["pC"], w=["BtotCol"])
        kb.ts("dve", rep[:, 0, :], onesf[0:68, :], AmaxCol, None, ALU.mult, None, r=["onesf", "AmaxCol"], w=["rep"])
        kb.ts("dve", rep[:, 1, :], onesf[0:68, :], BtotCol, -1.0, ALU.mult, ALU.mult, r=["onesf", "BtotCol"], w=["rep"])
        pm = identf[0:68, 0:68] if d == 0 else permB
        kb.mm(pD[:, 0:68], rep[:, 0, :], pm, True, True, r=["rep", "permB", "identf"], w=["pD"])
        kb.mm(pD[:, 68:136], rep[:, 1, :], pm, True, True, r=["rep", "permB", "identf"], w=["pD"])
        kb.copy("act", AB.rearrange("p a c h -> p (a c h)"), pD[:, 0:136], r=["pD"], w=["AB"])
        for h in range(2):
            kb.S.op("dve", lambda e, h=h: e.tensor_tensor_scan(out=Mn[:, :, h], data0=AB[:, 0, :, h], data1=AB[:, 1, :, h],
                                                           initial=0.0, op0=ALU.max, op1=ALU.add), r=["AB"], w=["Mn"])
        kb.memset("pool", Mprev[:, 0, :], 0.0, w=["Mprev"])
        kb.copy("dve", Mprev[:, 1:NTK, :], Mn[:, 0:NTK - 1, :], r=["Mn"], w=["Mprev"])
        kb.tt("dve", Mu[:, d], Mprev, AB[:, 0], ALU.max, r=["Mprev", "AB"], w=["Mu"])
        kb.tt("dve", Ee[:, d], Mprev, Mu[:, d], ALU.subtract, r=["Mprev", "Mu"], w=["Ee"])
        kb.act(Ee[:, d], Ee[:, d], AF.Exp, r=["Ee"], w=["Ee"])
        kb.copy("dve", Ecol[0:64, d, :], Ee[0:64, d, :, 0], r=["Ee"], w=["Ecol"])
        kb.copy("dve", Ecol[64:128, d, :], Ee[64:128, d, :, 1], r=["Ee"], w=["Ecol"])
        if d == 0:
            kb.tt("dve", Wt[:, 0], A_all[:, 0], Mu[:, 0], ALU.subtract, r=["A_all", "Mu"], w=["Wt"])
            kb.tt("dve", NBt[:, 0], bn[:, 0], Mu[:, 0], ALU.subtract, r=["bn", "Mu"], w=["NBt"])
        else:
            for pos in range(NTK):
                c = c_of_pos(1, pos)
                kb.tt("dve", Wt[:, 1, c, :], A_all[:, 1, c, :], Mu[:, 1, pos, :], ALU.subtract, r=["A_all", "Mu"], w=["Wt"])
                kb.tt("pool", NBt[:, 1, c, :], bn[:, 1, c, :], Mu[:, 1, pos, :], ALU.subtract, r=["bn", "Mu"], w=["NBt"])
        kb.act(Wk[:, d], Wt[:, d], AF.Exp, r=["Wt"], w=["Wk"], bias=math.log(0.125))
        kb.act(LOW[:, d], NBt[:, d], AF.Exp, r=["NBt"], w=["LOW"])

    hdir = [kb.sbt("hdir%d" % d, [128, NTK, 2, 64]) for d in range(2)]
    ysm = kb.sbt("ysm", [128, TTK], BF16)
    St = [kb.sbt("St%d" % d, [64, 2, 65]) for d in range(2)]
    Stb = [kb.sbt("Stb%d" % d, [64, 2, 72], BF16) for d in range(2)]
    Stmp = [kb.sbt("Stmp%d" % d, [64, 2, 65]) for d in range(2)]
    Pb = [[kb.sbt("Pb%d_%d" % (d, i), [128, 2, 128], BF16) for i in range(2)] for d in range(2)]
    kpb = [[kb.sbt("kpb%d_%d" % (d, i), [128, 2, 64], BF16) for i in range(2)] for d in range(2)]
    absd = [kb.sbt("absd%d" % d, [128, 2]) for d in range(2)]
    rc = [kb.sbt("rc%d" % d, [128, 2]) for d in range(2)]
    pS_ = [pA, pD]; pN_ = [pB, pE]; pU_ = [pC, pF]
    pSk = ["pA", "pD"]; pNk = ["pB", "pE"]; pUk = ["pC", "pF"]
    for d in range(2):
        kb.memset("pool", St[d], 0.0, w=["St%d" % d])
        kb.memset("pool", Stb[d], 0.0, w=["Stb%d" % d])
    for pos in range(NTK):
        for d in range(2):
            c = c_of_pos(d, pos)
            cs = slice(c * 128, (c + 1) * 128)
            i2 = kb.rot("Pb%d" % d, 2)
            P, kp = Pb[d][i2], kpb[d][i2]
            Pk, kpk = "Pb%d_%d" % (d, i2), "kpb%d_%d" % (d, i2)
            pSv = pS_[d].rearrange("p (h t) -> p h t", h=4)[:, 0:2, :]
            pNv = pN_[d].rearrange("p (h t) -> p h t", h=4)[:, 0:2, :]
            pUv = pU_[d].rearrange("p (h t) -> p h t", h=4)[:, 0:2, :]
            for h in range(2):
                kb.mm(pSv[:, h, :], kT[:, h, cs], qT[:, h, cs], True, True, r=["kT", "qT"], w=[pSk[d]])
            for h in range(2):
                kb.stt("dve", P[:, h, :], pSv[:, h, :], Wk[:, d, c, h:h + 1], mask[:, d, :], ALU.mult, ALU.mult,
                       r=[pSk[d], "Wk", "mask"], w=[Pk])
                kb.act(kp[:, h, :], k_tm[:, c, 64 * h:64 * h + 64], AF.Copy, r=["k_tm", "Wk"], w=[kpk], scale=Wk[:, d, c, h:h + 1])
            for h in range(2):
                kb.mm(pNv[:, h, 0:65], P[:, h, :], vext[:, c, h, 0:65], True, False, r=[Pk, "vext"], w=[pNk[d]])
                kb.mm(pNv[:, h, 0:65], qT[:, h, cs], Stb[d][:, h, 0:65], False, True, r=["qT", "Stb%d" % d], w=[pNk[d]])
            if pos < NTK - 1:
                for h in range(2):
                    kb.mm(pUv[0:64, h, 0:65], kp[:, h, :], vext[:, c, h, 0:65], True, True, r=[kpk, "vext"], w=[pUk[d]])
            kb.act(absd[d], pNv[:, :, 64], AF.Abs, r=[pNk[d]], w=["absd%d" % d])
            kb.tt("dve", absd[d], absd[d], LOW[:, d, c, :], ALU.max, r=["absd%d" % d, "LOW"], w=["absd%d" % d])
            kb.recip(rc[d], absd[d], r=["absd%d" % d], w=["rc%d" % d])
            kb.tt("dve", hdir[d][:, c], pNv[:, :, 0:64], rc[d].unsqueeze(2).to_broadcast([128, 2, 64]), ALU.mult,
                  r=[pNk[d], "rc%d" % d], w=["hdir%d_%d" % (d, c)])
            if pos < NTK - 1:
                ebc = Ee[0:64, d, pos + 1, :].unsqueeze(2).to_broadcast([64, 2, 65])
                kb.tt("dve", Stmp[d], St[d], pUv[0:64, :, 0:65], ALU.add, r=["St%d" % d, pUk[d]], w=["Stmp%d" % d])
                kb.tt("dve", St[d], Stmp[d], ebc, ALU.mult, r=["Stmp%d" % d, "Ee"], w=["St%d" % d])
                kb.tt("pool", Stb[d][:, :, 0:65], Stmp[d], ebc, ALU.mult, r=["Stmp%d" % d, "Ee"], w=["Stb%d" % d])
    hs = [kb.sbt("hs%d" % i, [128, 2, 64]) for i in range(2)]
    hsq = [kb.sbt("hsq%d" % i, [128, 2, 64]) for i in range(2)]
    ssq = [kb.sbt("ssq%d" % i, [128, 2]) for i in range(2)]
    mob = [kb.sbt("mob%d" % i, [128, 128], BF16) for i in range(2)]
    for c in range(NTK):
        i2 = c % 2
        cs = slice(c * 128, (c + 1) * 128)
        kb.tt("dve", hs[i2], hdir[0][:, c], hdir[1][:, c], ALU.add, r=["hdir0_%d" % c, "hdir1_%d" % c], w=["hs%d" % i2])
        kb.tt("pool", hsq[i2], hs[i2], hs[i2], ALU.mult, r=["hs%d" % i2], w=["hsq%d" % i2])
        kb.S.op("dve", lambda e, i2=i2: e.tensor_reduce(out=ssq[i2], in_=hsq[i2], axis=AX.X, op=ALU.add), r=["hsq%d" % i2], w=["ssq%d" % i2])
        kb.act(ssq[i2], ssq[i2], AF.Sqrt, r=["ssq%d" % i2], w=["ssq%d" % i2], bias=EPS_RMS, scale=1.0 / 64)
        kb.recip(ssq[i2], ssq[i2], r=["ssq%d" % i2], w=["ssq%d" % i2])
        kb.tt("dve", hs[i2], hs[i2], ssq[i2].unsqueeze(2).to_broadcast([128, 2, 64]), ALU.mult, r=["hs%d" % i2, "ssq%d" % i2], w=["hs%d" % i2])
        kb.tt("pool", hs[i2], hs[i2], mngb.unsqueeze(1).to_broadcast([128, 2, 64]), ALU.mult, r=["hs%d" % i2, "mngb"], w=["hs%d" % i2])
        kb.tt("dve", mob[i2], hs[i2].rearrange("p h v -> p (h v)"), sigo[:, c, :], ALU.mult, r=["hs%d" % i2, "sigo"], w=["mob%d" % i2])
        kb.tr(pT[:, 0:128], mob[i2], ident_b, r=["mob%d" % i2, "ident_b"], w=["pT"])
        kb.copy("act", ysm[:, cs], pT[:, 0:128], r=["pT"], w=["ysm"])
    kb.dma("sp", yint[p * 128:(p + 1) * 128, :], ysm, r=["ysm"], w=[ykey])
    kb.phase_end()

    kb.phase_begin()
    ysa = kb.sbt("ysa", [64, 2, TTK], BF16)
    Eb2 = [kb.sbt("Eb2_%d" % i, [128, 1024], BF16) for i in range(2)]
    Eb = [[Eb2[i][:, m * 512:(m + 1) * 512] for i in range(2)] for m in range(2)]
    Osb = [kb.sbt("Osb%d" % m, [128, 512]) for m in range(2)]
    for m in range(2):
        kb.memset("pool", Osb[m], 0.0, w=["Osb%d" % m])
    rz = [kb.sbt("rz%d" % m, [64, 512]) for m in range(2)]
    oT = kb.sbt("oT", [64, 512]); sqo = kb.sbt("sqo", [64, 512]); rso = kb.sbt("rso", [64, 512])
    scale = 32 ** -0.5
    qblocks = ([(0, 256, [0, 1])] if layer0 else []) + [(256 + 512 * i, 512, list(range(NTK))) for i in range(8)]
    psS = [[(pA, "pA"), (pC, "pC")], [(pB, "pB"), (pD, "pD")]]
    psO = [(pF, "pF"), (pG, "pG")]
    pending = None
    for h in range(2):
        for (q0, nq, kts) in qblocks:
            def emit_S(i):
                par = i % 2
                ks = slice(kts[i] * 128, (kts[i] + 1) * 128)
                for m in range(2):
                    pp, pk = psS[m][par]
                    kb.mm(pp[:, 0:nq], KTm[:, 2 * h + m, ks], QT[:, q0:q0 + nq], True, True, r=["KT", "QT"], w=[pk])

            def emit_E(i):
                par = i % 2
                src = kb.psums["pAB" if par == 0 else "pCD"].rearrange("p (m n) -> p m n", m=2)[:, :, 0:nq]
                dst = Eb2[par].rearrange("p (m n) -> p m n", m=2)[:, :, 0:nq]
                kb.act(dst, src, AF.Exp, r=[psS[0][par][1], psS[1][par][1]], w=["Eb0_%d" % par, "Eb1_%d" % par], scale=scale)

            def emit_PV(i):
                par = i % 2
                for m in range(2):
                    po, pok = psO[m]
                    kb.mm(po[:, 0:nq], Vx[:, kts[i], h, :], Eb[m][par][:, 0:nq], i == 0, i == len(kts) - 1,
                          r=["Vx", "Eb%d_%d" % (m, par)], w=[pok])

            emit_S(0)
            for i in range(len(kts)):
                emit_E(i)
                if i + 1 < len(kts):
                    emit_S(i + 1)
                emit_PV(i)
                if pending and i == min(1, len(kts) - 1):
                    pending[0]()
                if pending and i == min(8, len(kts) - 1):
                    pending[1]()
                    pending = None
            for m in range(2):
                po, pok = psO[m]
                kb.copy("act" if m == 0 else "dve", Osb[m][0:65, 0:nq], po[0:65, 0:nq], r=[pok], w=["Osb%d" % m])

            def fin_a(h=h, q0=q0, nq=nq):
                for m in range(2):
                    kb.mm(pE[0:64, 0:nq], selZ, Osb[m][:, 0:nq], True, True, r=["selZ", "Osb%d" % m], w=["pE"])
                    kb.recip(rz[m][:, 0:nq], pE[0:64, 0:nq], r=["pE"], w=["rz%d" % m])
                    kb.tt("pool", rz[m][:, 0:nq], Osb[m][0:64, 0:nq], rz[m][:, 0:nq], ALU.mult, r=["Osb%d" % m, "rz%d" % m], w=["rz%d" % m])
                kb.stt("dve", oT[:, 0:nq], rz[1][:, 0:nq], negLam, rz[0][:, 0:nq], ALU.mult, ALU.add, r=["rz0", "rz1", "negLam"], w=["oT"])
                kb.tt("pool", sqo[:, 0:nq], oT[:, 0:nq], oT[:, 0:nq], ALU.mult, r=["oT"], w=["sqo"])

            def fin_b(h=h, q0=q0, nq=nq):
                kb.mm(pE[0:64, 0:nq], onesf[0:64, 0:64], sqo[:, 0:nq], True, True, r=["onesf", "sqo"], w=["pE"])
                kb.act(rso[:, 0:nq], pE[0:64, 0:nq], AF.Sqrt, r=["pE"], w=["rso"], bias=EPS_RMS, scale=1.0 / 64)
                kb.recip(rso[:, 0:nq], rso[:, 0:nq], r=["rso"], w=["rso"])
                kb.tt("dve", oT[:, 0:nq], oT[:, 0:nq], rso[:, 0:nq], ALU.mult, r=["oT", "rso"], w=["oT"])
                kb.ts("dve", ysa[:, h, q0:q0 + nq], oT[:, 0:nq], sublg, 1.0 - float(lam_init), ALU.mult, ALU.mult, r=["oT", "sublg"], w=["ysa"])

            pending = (fin_a, fin_b)
    if pending:
        pending[0]()
        pending[1]()
    if not layer0:
        kb.memset("pool", ysa[:, :, 0:256], 0.0, w=["ysa"])
    for h in range(2):
        kb.dma("sp", yint[256 + p * 128 + 64 * h:256 + p * 128 + 64 * h + 64, :], ysa[:, h, :], r=["ysa"], w=[ykey])
    kb.phase_end()
    kb.scope_end()


def build_fused():
    kb = KB()
    hfull = kb.inp("hfull", [TTK, D], lvl="global")
    hful1 = kb.internal("hful1", [TTK, D])
    hout = kb.outp("hout", [4096, D])
    yint = [kb.internal("yint%d" % l, [512, TTK], BF16) for l in range(2)]
    kb.ps("pT", [128, 1024], BF16)
    for pair in ("AB", "CD"):
        t = kb.ps("p" + pair, [128, 1024])
        kb.psums["p" + pair[0]] = t[:, 0:512]
        kb.psums["p" + pair[1]] = t[:, 512:1024]
    for nm in "EFG":
        kb.ps("p" + nm, [128, 512])
    modc_d = [kb.internal("modc%d" % l, [128, 48, 2]) for l in range(2)]
    grow_d = [kb.internal("grow%d" % l, [2, 2048]) for l in range(2)]
    uTd = [kb.internal("uTd%d" % l, [128, 8, TTK], BF16) for l in range(2)]
    wb16 = []
    for l in range(2):
        d = {}
        for nm, shape in (("w_pc", [D, 768]), ("w_out", [D, D]), ("fwo", [DFF, D]), ("fwi", [D, 2 * DFF])):
            d[nm] = (kb.internal("L%d_%s_b16" % (l, nm), shape, BF16), "L%d_%s_b16" % (l, nm))
        wb16.append(d)

    def convert_weights(l):
        save = dict(kb.pfx)
        if True:
            kb.pfx = {"inst": "", "layer": "L%d_" % l, "global": ""}
            for nm, shape, rows in (("w_pc", [D, 768], 256), ("w_out", [D, D], 256), ("fwo", [DFF, D], 256), ("fwi", [D, 2 * DFF], 128)):
                src = kb.inp(nm, shape, lvl="layer")
                dst, key = wb16[l][nm]
                for r0 in range(0, shape[0], rows):
                    kb.dma("pool", dst[r0:r0 + rows, :], src[r0:r0 + rows, :], w=[key], bg=True)
        kb.pfx = save

    for l in range(2):
        hsrc, hkey = (hfull, "hfull") if l == 0 else (hful1, "hful1")
        convert_weights(l)
        emit_mod(kb, l, modc_d[l], grow_d[l], "mod%d" % l)
        for p in range(2):
            emit_M(kb, l, p, hsrc, hkey, yint[l], "yint%d" % l, modc_d[l], "mod%d" % l, uTd[l],
                   hook=None)
        for g in range(2):
            emit_F(kb, l, g, hsrc, hkey, yint[l], "yint%d" % l, hful1 if l == 0 else hout, "hful1" if l == 0 else "hout", l == 1,
                   wb16[l], modc_d[l], grow_d[l], "mod%d" % l, uTd[l])
    kb.S.emit()
    return kb

import numpy as np, ml_dtypes
BF = ml_dtypes.bfloat16
S_, CT, D, DFF = 4096, 256, 1024, 2816
POOL_WINDOWS = (2, 4, 8, 16)

def col_layout(v, n):
    return np.ascontiguousarray(v.reshape(n, 128).T)

def band_mats(base, p0, L, w):
    B = np.zeros((256, 128), np.float32)
    for t in range(128):
        p = p0 + t
        lo = min(max(p - w // 2, 0), L - 1); hi = min(max(p + (w - 1 - w // 2), 0), L - 1)
        inv = np.float32(1.0) / np.float32(hi - lo + 1)
        for q in range(lo, hi + 1):
            B[q - base, t] += inv
        B[p - base, t] -= 1.0
    return B

def prep_F(inp, l, h, hc, yma_lat, yma_ctx, b, g):
    layer0 = (l == 0)
    start, cstart = g * 2048, g * 128
    m = {}
    rows = [h[b, start:start + 2048]]
    if layer0: rows.append(hc[b, cstart:cstart + 128])
    m["hrows"] = np.ascontiguousarray(np.concatenate(rows, 0))
    def padded(src, s0, n, L):
        out = np.zeros((n, D), np.float32); msk = np.zeros((n,), np.float32)
        lo, hi = max(s0, 0), min(s0 + n, L)
        out[lo - s0:hi - s0] = src[lo:hi]; msk[lo - s0:hi - s0] = 1.0
        return out, msk
    hp, mk = padded(h[b], start - 16, 2176, S_)
    hps, mks = [hp], [mk]
    if layer0:
        cp, cm = padded(hc[b], cstart - 16, 256, CT)
        hps.append(cp); mks.append(cm)
    m["hpad"] = np.ascontiguousarray(np.concatenate(hps, 0)); m["maskp"] = np.concatenate(mks, 0)
    ys = [yma_lat[b, start:start + 2048]]
    if layer0: ys.append(yma_ctx[b, cstart:cstart + 128])
    m["yTma"] = np.ascontiguousarray(np.concatenate(ys, 0).T).astype(BF)
    m["ccol"] = np.ascontiguousarray(np.stack([col_layout(inp["c"][b], 8), col_layout(inp["c_ctx"], 8)], axis=-1))
    m["modw"] = inp["mod_w"][l]
    m["modb_col"] = col_layout(inp["mod_b"][l], 48)
    m["modb_row"] = inp["mod_b"][l]
    m["n1g"] = col_layout(inp["norm1_g"][l], 8); m["n2g"] = col_layout(inp["norm2_g"][l], 8)
    m["w_pc"] = np.ascontiguousarray(inp["w_in"][l][:, 1808:2576])
    bands = np.zeros((128, 32, 128), np.float32)
    cls_list = [(0, S_, start), (1, S_, start), (15, S_, start)]
    for cls in range(4):
        for gi, w in enumerate(POOL_WINDOWS):
            if cls < 3:
                ot = cls_list[cls][0]
                base = start - 16 + ot * 128; p0 = start + ot * 128; L = S_
            else:
                base = cstart - 16; p0 = cstart; L = CT
            B = band_mats(base, p0, L, w)
            bands[:, (cls * 4 + gi) * 2 + 0, :] = B[0:128]; bands[:, (cls * 4 + gi) * 2 + 1, :] = B[128:256]
    m["bands"] = bands
    pw2 = np.zeros((64, 4, 128), np.float32)
    for gi in range(4):
        pw2[:, gi, (gi % 2) * 64:(gi % 2) * 64 + 64] = inp["pool_w"][l][gi]
    m["poolw2"] = pw2
    m["pscale"] = col_layout(inp["pool_scale"][l], 2)
    m["dww"] = np.ascontiguousarray(inp["conv_dw_w"][l].T.reshape(2, 128, 31).transpose(1, 0, 2))
    m["dwb"] = col_layout(inp["conv_dw_b"][l], 2); m["lng"] = col_layout(inp["conv_ln_g"][l], 2); m["lnb"] = col_layout(inp["conv_ln_b"][l], 2)
    m["pww"] = inp["conv_pw_w"][l]; m["w_out"] = inp["w_out"][l]; m["fwi"] = inp["ffn_w_in"][l]; m["fwo"] = inp["ffn_w_out"][l]
    m["identf"] = np.eye(128, dtype=np.float32)
    sel2 = np.zeros((2, 2, 128), np.float32); sel2[0, 0] = 1; sel2[1, 1] = 1
    m["sel2"] = sel2
    return {k: np.ascontiguousarray(v) for k, v in m.items()}

NTK = 34
def rope_tables():
    TT = NTK * 128
    cos = np.ones((64, TT), np.float32); sin = np.zeros((64, TT), np.float32)
    tok = np.arange(S_)
    rows = (tok // 64).astype(np.float32); cols = (tok % 64).astype(np.float32)
    freqs = (np.float32(10000.0) ** (-np.arange(8, dtype=np.float32) / np.float32(8))).astype(np.float32)
    for m in range(2):
        for d in range(32):
            axis, half, f = d // 16, (d % 16) // 8, d % 8
            ang = (rows if axis == 0 else cols) * freqs[f]
            cos[m * 32 + d, 256:] = np.cos(ang.astype(np.float32))
            sin[m * 32 + d, 256:] = np.sin(ang.astype(np.float32)) * (-1.0 if half == 0 else 1.0)
    return cos, sin

_CONST = {}
def consts_M():
    if _CONST: return _CONST
    c = {}
    c["identf"] = np.eye(128, dtype=np.float32)
    r = np.arange(128)
    tri = np.zeros((128, 2, 128), np.float32)
    tri[:, 0, :] = (r[:, None] <= r[None, :]); tri[:, 1, :] = (r[:, None] >= r[None, :])
    c["tri"] = tri; c["mask"] = tri.copy()
    bo = np.zeros((128, 128), np.float32)
    for j in range(4):
        bo[32 * j:32 * j + 32, 32 * j:32 * j + 32] = 1
    c["bones"] = bo
    mc = np.zeros((128, 4), np.float32)
    for j in range(4):
        mc[32 * j:32 * j + 32, j] = 1
    c["mcol"] = mc
    pm = np.zeros((128, 128), np.float32)
    for d in range(128):
        dd = d % 32
        partner = d + 8 if (dd % 16) < 8 else d - 8
        pm[partner, d] = 1.0
    c["perm"] = pm
    pB = np.zeros((68, 68), np.float32)
    for pos in range(34):
        cc = 1 - pos if pos < 2 else 35 - pos
        for h in range(2):
            pB[cc * 2 + h, pos * 2 + h] = 1.0
    c["permB"] = pB
    sg = np.zeros((2, 64), np.float32); sg[0] = -1; sg[1] = 1
    c["sgn"] = sg
    sz = np.zeros((128, 64), np.float32); sz[64, :] = 1.0
    c["selZ"] = sz
    ct_, st_ = rope_tables()
    c["cost"], c["sint"] = np.concatenate([ct_, ct_], 0), np.concatenate([st_, st_], 0)
    _CONST.update(c)
    return _CONST

def prep_M(inp, l, h, hc, b, g):
    m = dict(consts_M())
    m["hfull"] = np.ascontiguousarray(np.concatenate([hc[b], h[b]], 0))
    m["ccol"] = np.ascontiguousarray(np.stack([col_layout(inp["c"][b], 8), col_layout(inp["c_ctx"], 8)], axis=-1))
    m["modw"] = np.ascontiguousarray(inp["mod_w"][l][:, :2048])
    m["modb_col"] = col_layout(inp["mod_b"][l][:2048], 16)
    m["n1g"] = col_layout(inp["norm1_g"][l], 8)
    W = inp["w_in"][l]
    hp = slice(g * 128, (g + 1) * 128)
    mq, mk, mv, mo = W[:, 0:256][:, hp], W[:, 256:512][:, hp], W[:, 512:768][:, hp], W[:, 768:1024][:, hp]
    gates = W[:, 1024:1040].reshape(D, 2, 2, 4)[:, :, :, 2 * g:2 * g + 2].reshape(D, 8)
    aq, ak, av = W[:, 1040:1296][:, hp], W[:, 1296:1552][:, hp], W[:, 1552:1808][:, hp]
    m["w_fm"] = np.concatenate([mq, mk, aq, ak], 1)
    m["w_tm"] = np.concatenate([mk, mv, mo, av, gates, np.zeros((D, 8), np.float32)], 1)
    m["gate_b"] = inp["mlstm_gate_b"][l][:, :, 2 * g:2 * g + 2].reshape(8)
    m["mng"] = inp["mlstm_norm_g"][l]
    m["qkng"] = np.stack([np.tile(inp["attn_q_norm_g"][l], 2), np.tile(inp["attn_k_norm_g"][l], 2)], 1)
    m["lamqk"] = np.stack([inp["lambda_q"][l], inp["lambda_k"][l]], 1)
    m["sublng"] = inp["attn_subln_g"][l].reshape(64, 1)
    return {k: np.ascontiguousarray(v, dtype=np.float32) for k, v in m.items()}


def prep_fused(inp, b):
    x, ctx = inp["x"], inp["ctx"]
    m = dict(consts_M())
    m["hfull"] = np.concatenate([ctx[b], x[b]], 0)
    m["ccol"] = np.stack([col_layout(inp["c"][b], 8), col_layout(inp["c_ctx"], 8)], axis=-1)
    sel2 = np.zeros((2, 2, 128), np.float32); sel2[0, 0] = 1; sel2[1, 1] = 1
    m["sel2"] = sel2
    for g in range(2):
        start, cstart = g * 2048, g * 128
        mk = np.zeros((19 * 128,), np.float32)
        pos = start - 16 + np.arange(2176); mk[:2176] = ((pos >= 0) & (pos < S_))
        cpos = cstart - 16 + np.arange(256); mk[2176:] = ((cpos >= 0) & (cpos < CT))
        m["maskp%d" % g] = mk
        bands = np.zeros((128, 32, 128), np.float32)
        for cls in range(4):
            for gi, w in enumerate(POOL_WINDOWS):
                if cls < 3:
                    ot = (0, 1, 15)[cls]
                    base = start - 16 + ot * 128; p0 = start + ot * 128; L = S_
                else:
                    base = cstart - 16; p0 = cstart; L = CT
                B = band_mats(base, p0, L, w)
                bands[:, (cls * 4 + gi) * 2 + 0, :] = B[0:128]; bands[:, (cls * 4 + gi) * 2 + 1, :] = B[128:256]
        m["bands%d" % g] = bands
    for l in range(2):
        L_ = "L%d_" % l
        m[L_ + "modw"] = inp["mod_w"][l]
        m[L_ + "modb_col"] = col_layout(inp["mod_b"][l], 48)
        m[L_ + "modb_row"] = inp["mod_b"][l]
        m[L_ + "n1g"] = col_layout(inp["norm1_g"][l], 8); m[L_ + "n2g"] = col_layout(inp["norm2_g"][l], 8)
        W = inp["w_in"][l]
        m[L_ + "w_pc"] = W[:, 1808:2576]
        pw2 = np.zeros((64, 4, 128), np.float32)
        for gi in range(4):
            pw2[:, gi, (gi % 2) * 64:(gi % 2) * 64 + 64] = inp["pool_w"][l][gi]
        m[L_ + "poolw2"] = pw2
        m[L_ + "pscale"] = col_layout(inp["pool_scale"][l], 2)
        m[L_ + "dww"] = inp["conv_dw_w"][l].T.reshape(2, 128, 31).transpose(1, 0, 2)
        m[L_ + "dwb"] = col_layout(inp["conv_dw_b"][l], 2); m[L_ + "lng"] = col_layout(inp["conv_ln_g"][l], 2)
        m[L_ + "lnb"] = col_layout(inp["conv_ln_b"][l], 2)
        m[L_ + "pww"] = inp["conv_pw_w"][l]; m[L_ + "w_out"] = inp["w_out"][l]
        m[L_ + "fwi"] = inp["ffn_w_in"][l]; m[L_ + "fwo"] = inp["ffn_w_out"][l]
        m[L_ + "mng"] = inp["mlstm_norm_g"][l]
        m[L_ + "qkng"] = np.stack([np.tile(inp["attn_q_norm_g"][l], 4), np.tile(inp["attn_k_norm_g"][l], 4)], 1)
        m[L_ + "lamqk"] = np.stack([inp["lambda_q"][l], inp["lambda_k"][l]], 1)
        m[L_ + "sublng"] = inp["attn_subln_g"][l].reshape(64, 1)
        for p in range(2):
            P_ = "M%d%d_" % (l, p)
            hp = slice(p * 128, (p + 1) * 128)
            mq, mk_, mv, mo = W[:, 0:256][:, hp], W[:, 256:512][:, hp], W[:, 512:768][:, hp], W[:, 768:1024][:, hp]
            gates = W[:, 1024:1040].reshape(D, 2, 2, 4)[:, :, :, 2 * p:2 * p + 2].reshape(D, 8)
            aq, ak, av = W[:, 1040:1296][:, hp], W[:, 1296:1552][:, hp], W[:, 1552:1808][:, hp]
            m[P_ + "w_fm"] = np.concatenate([mq, mk_, aq, ak], 1)
            m[P_ + "w_tm"] = np.concatenate([mk_, mv, mo, av, gates, np.zeros((D, 8), np.float32)], 1)
            m[P_ + "gate_b"] = inp["mlstm_gate_b"][l][:, :, 2 * p:2 * p + 2].reshape(8)
    return {k: np.ascontiguousarray(v, dtype=np.float32) for k, v in m.items()}

_PROG = {}


def kernel(**inputs):
    inp = {k: np.asarray(v) for k, v in inputs.items()}
    if "kb" not in _PROG:
        _PROG["kb"] = build_fused()
    kb = _PROG["kb"]
    maps = [prep_fused(inp, b) for b in range(4)]
    res = run_bass_kernel_spmd(kb.nc, maps, core_ids=list(range(4)))
    out = np.stack([np.asarray(res.results[b]["hout"]) for b in range(4)], 0)
    return out.astype(np.float32)
```

```python
import math
import numpy as np
import concourse.bass as bass
import concourse.mybir as mybir
from concourse.bass_utils import run_bass_kernel_spmd

F32 = mybir.dt.float32
BF16 = mybir.dt.bfloat16
ALU = mybir.AluOpType
AF = mybir.ActivationFunctionType
AX = mybir.AxisListType

COMPUTE = ("pe", "act", "dve", "pool")
N_DMA_SEMS = 12
N_BG_SEMS = 24


class _Op:
    __slots__ = ("eng", "fn", "deps", "is_dma", "eidx", "marked", "sem", "semval", "prev_on_sem", "bg")

    def __init__(self, eng, fn, deps, is_dma):
        self.eng, self.fn, self.deps, self.is_dma = eng, fn, deps, is_dma
        self.marked = False
        self.bg = False
        self.sem = None
        self.semval = 0
        self.prev_on_sem = None


class Sched:
    def __init__(self, nc, same_engine_sync=True):
        self.nc = nc
        self.ops = []
        self.last_w = {}
        self.readers = {}
        self.same_engine_sync = same_engine_sync
        self.engs = {"pe": nc.tensor, "act": nc.scalar, "dve": nc.vector, "pool": nc.gpsimd, "sp": nc.sync}
        self.ecount = {e: 0 for e in self.engs}
        self._bar_tile = nc.alloc_sbuf_tensor("bar_tile", [128, 8], F32).ap()

    def barrier(self):
        self.op("pool", lambda e: e.memset(self._bar_tile, 0.0), r=(), w=["__phase__"])

    def op(self, eng, fn, r=(), w=(), dma=False):
        r = list(r) + ["__phase__"]
        deps = set()
        for k in r:
            if k in self.last_w:
                deps.add(self.last_w[k])
        for k in w:
            if k in self.last_w:
                deps.add(self.last_w[k])
            deps.update(self.readers.get(k, ()))
        idx = len(self.ops)
        o = _Op(eng, fn, deps, dma)
        o.eidx = self.ecount[eng]
        self.ecount[eng] += 1
        self.ops.append(o)
        for k in r:
            self.readers.setdefault(k, []).append(idx)
        for k in w:
            self.last_w[k] = idx
            self.readers[k] = []
        return idx

    def dma(self, eng, out, in_, r=(), w=(), bg=False):
        i = self.op(eng, lambda e: e.dma_start(out=out, in_=in_), r, w, dma=True)
        self.ops[i].bg = bg
        return i

    def emit(self, final_wait_eng="sp"):
        nc = self.nc
        import os
        if os.environ.get("TRUNC"):
            self.ops = self.ops[:int(os.environ["TRUNC"])]
        ops = self.ops
        known = {e: {f: -1 for f in COMPUTE} for e in self.engs}
        dma_known = {e: {} for e in self.engs}
        waits = [None] * len(ops)
        dma_pool = {e: [] for e in self.engs}
        dma_rr = {e: 0 for e in self.engs}
        dma_last = {}
        for i, o in enumerate(ops):
            wl = []
            for d in sorted(o.deps):
                p = ops[d]
                if p.is_dma:
                    if dma_known[o.eng].get(p.sem, 0) < p.semval:
                        dma_known[o.eng][p.sem] = p.semval
                        wl.append(("dma", d))
                else:
                    if p.eng == o.eng and (o.eng == 'pe' or not self.same_engine_sync) and not o.is_dma:
                        continue
                    if known[o.eng][p.eng] < p.eidx:
                        known[o.eng][p.eng] = p.eidx
                        wl.append(("eng", d))
            best = {}
            for kind, d in wl:
                if kind == "eng":
                    e = ops[d].eng
                    if e not in best or ops[best[e]].eidx < ops[d].eidx:
                        best[e] = d
            bestd = {}
            for kind, d in wl:
                if kind == "dma":
                    sl = ops[d].sem
                    if sl not in bestd or ops[bestd[sl]].semval < ops[d].semval:
                        bestd[sl] = d
            wl = [("dma", d) for d in bestd.values()] + [("eng", d) for d in best.values()]
            if o.is_dma:
                qn = o.eng + ("_bg" if o.bg else "")
                dma_rr.setdefault(qn, 0)
                slot = dma_rr[qn] % (N_BG_SEMS if o.bg else N_DMA_SEMS)
                dma_rr[qn] += 1
                o.sem = (qn, slot)
                prev = dma_last.get(o.sem)
                o.prev_on_sem = prev
                o.semval = (ops[prev].semval if prev is not None else 0) + 16
                dma_last[o.sem] = i
                if prev is not None and dma_known[o.eng].get(o.sem, 0) < ops[prev].semval:
                    wl.append(("dma", prev))
                    dma_known[o.eng][o.sem] = ops[prev].semval
            for kind, d in wl:
                if kind == "eng":
                    ops[d].marked = True
            waits[i] = wl
        cnt = {e: 0 for e in COMPUTE}
        ordinal = {}
        for i, o in enumerate(ops):
            if not o.is_dma and o.marked:
                cnt[o.eng] += 1
                ordinal[i] = cnt[o.eng]
        esem = {e: nc.alloc_semaphore("sem_" + e) for e in COMPUTE}
        dsem = {}
        for o in ops:
            if o.is_dma and o.sem not in dsem:
                dsem[o.sem] = nc.alloc_semaphore("dsem_%s_%d" % o.sem)
        for i, o in enumerate(ops):
            eng = self.engs[o.eng]
            for kind, d in waits[i]:
                p = ops[d]
                if kind == "dma":
                    eng.wait_ge(dsem[p.sem], p.semval)
                else:
                    eng.wait_ge(esem[p.eng], ordinal[d])
            ins = o.fn(eng)
            if o.is_dma:
                ins.then_inc(dsem[o.sem], 16)
            elif o.marked:
                ins.then_inc(esem[o.eng], 1)
        fe = self.engs[final_wait_eng]
        for key, i in dma_last.items():
            fe.wait_ge(dsem[key], ops[i].semval)
        self.stats = {e: self.ecount[e] for e in self.engs}
        self.stats["marked"] = dict(cnt)


EPS_RMS = 1e-6
EPS_LN = 1e-5
D = 1024
DFF = 2816


class KB:
    def __init__(self):
        self.nc = bass.Bass("TRN2", target_bir_lowering=False)
        self.S = Sched(self.nc)
        self.rr = {}
        self.dram = {}
        self.pfx = {"inst": "", "layer": "", "global": ""}
        self.uid = 0
        self.scope = None
        self.psums = {}

    def inp(self, name, shape, dt=F32, lvl="inst"):
        full = self.pfx[lvl] + name
        if full not in self.dram:
            self.dram[full] = self.nc.dram_tensor(full, list(shape), dt, kind="ExternalInput").ap()
        return self.dram[full]

    def outp(self, name, shape, dt=F32):
        if name not in self.dram:
            self.dram[name] = self.nc.dram_tensor(name, list(shape), dt, kind="ExternalOutput").ap()
        return self.dram[name]

    def internal(self, name, shape, dt=F32):
        if name not in self.dram:
            self.dram[name] = self.nc.dram_tensor(name, list(shape), dt).ap()
        return self.dram[name]

    def scope_begin(self, inst, layer):
        from contextlib import ExitStack
        self.scope = ExitStack()
        self.pfx = {"inst": inst + "_", "layer": layer + "_", "global": ""}
        self.rr = {}
        self._bufs = {}

    def scope_end(self):
        self.S.barrier()
        self.scope.close()
        self.scope = None

    def sb(self, name, shape, dt=F32):
        self.uid += 1
        return self.scope.enter_context(self.nc.sbuf_tensor("s%d_%s" % (self.uid, name), list(shape), dt)).ap()

    def phase_begin(self):
        from contextlib import ExitStack
        self.stack = ExitStack()

    def sbt(self, name, shape, dt=F32):
        self.uid += 1
        return self.stack.enter_context(self.nc.sbuf_tensor("s%d_%s" % (self.uid, name), list(shape), dt)).ap()

    def phase_end(self):
        self.S.barrier()
        self.stack.close()
        self.stack = None

    def ps(self, name, shape, dt=F32):
        if name not in self.psums:
            self.psums[name] = self.nc.alloc_psum_tensor("p_" + name, list(shape), dt).ap()
        return self.psums[name]

    def rot(self, name, n):
        i = self.rr.get(name, 0)
        self.rr[name] = i + 1
        return i % n

    def dma(self, eng, out, in_, r=(), w=(), bg=False):
        self.S.dma(eng, out, in_, r=r, w=w, bg=bg)

    def mm(self, out, lhsT, rhs, start, stop, r, w):
        self.S.op("pe", lambda e: e.matmul(out, lhsT=lhsT, rhs=rhs, start=start, stop=stop), r=r, w=w)

    def tr(self, out, in_, ident, r, w):
        self.S.op("pe", lambda e: e.transpose(out, in_, ident), r=r, w=w)

    def act(self, out, in_, func, r, w, bias=0.0, scale=1.0, accum=None, eng="act"):
        if accum is None:
            self.S.op(eng, lambda e: e.activation(out=out, in_=in_, func=func, bias=bias, scale=scale), r=r, w=w)
        else:
            self.S.op(eng, lambda e: e.activation(out=out, in_=in_, func=func, bias=bias, scale=scale, accum_out=accum), r=r, w=w)

    def tt(self, eng, out, in0, in1, op, r, w):
        self.S.op(eng, lambda e: e.tensor_tensor(out=out, in0=in0, in1=in1, op=op), r=r, w=w)

    def ts(self, eng, out, in0, s1, s2, op0, op1, r, w):
        if s2 is None:
            self.S.op(eng, lambda e: e.tensor_scalar(out=out, in0=in0, scalar1=s1, scalar2=None, op0=op0), r=r, w=w)
        else:
            self.S.op(eng, lambda e: e.tensor_scalar(out=out, in0=in0, scalar1=s1, scalar2=s2, op0=op0, op1=op1), r=r, w=w)

    def stt(self, eng, out, in0, scalar, in1, op0, op1, r, w):
        self.S.op(eng, lambda e: e.scalar_tensor_tensor(out=out, in0=in0, scalar=scalar, in1=in1, op0=op0, op1=op1), r=r, w=w)

    def copy(self, eng, out, in_, r, w):
        if eng == "act":
            self.S.op("act", lambda e: e.copy(out=out, in_=in_), r=r, w=w)
        else:
            self.S.op(eng, lambda e: e.tensor_copy(out=out, in_=in_), r=r, w=w)

    def recip(self, out, in_, r, w):
        self.S.op("dve", lambda e: e.reciprocal(out=out, in_=in_), r=r, w=w)

    def memset(self, eng, ap, val, w):
        self.S.op(eng, lambda e: e.memset(ap, val), w=w)


def mod_columns(kb, modw, modb_col, ccol, col0, nchunks, name, stg, psm):
    sc = kb.sbt(name + "_sc", [128, 8, 2])
    kb.dma("sp", sc, ccol, w=[name + "_sc"])
    kb.act(sc, sc, AF.Silu, r=[name + "_sc"], w=[name + "_sc"])
    mb = kb.sbt(name + "_mb", [128, nchunks])
    kb.dma("sp", mb, modb_col, w=[name + "_mb"])
    for jg in range(nchunks // 4):
        b = kb.rot("mstg", 2)
        kb.dma("sp", stg[b].rearrange("p (kt n) -> p kt n", kt=8),
               modw[:, col0 + jg * 512:col0 + (jg + 1) * 512].rearrange("(kt p) n -> p kt n", p=128), w=["mstg%d" % b])
        for jj in range(4):
            j = jg * 4 + jj
            for kt in range(8):
                kb.mm(psm[:, j, :], stg[b][:, kt * 512 + jj * 128:kt * 512 + (jj + 1) * 128], sc[:, kt, :], kt == 0, kt == 7,
                      r=["mstg%d" % b, name + "_sc"], w=["psm"])
    modc = kb.sbt(name + "_modc", [128, nchunks, 2])
    for r_ in range(2):
        kb.tt("dve", modc[:, :, r_], psm[:, :, r_], mb, ALU.add, r=["psm", name + "_mb"], w=[name + "_modc"])
    return modc


def emit_mod(kb, l, modc_d, grow_d, mkey):
    kb.scope_begin("MOD%d" % l, "L%d" % l)
    ccol = kb.inp("ccol", [128, 8, 2], lvl="global")
    modw = kb.inp("modw", [D, 6 * D], lvl="layer")
    modb_col = kb.inp("modb_col", [128, 48], lvl="layer")
    modb_row = kb.inp("modb_row", [6 * D], lvl="layer")
    pA = kb.ps("pA", [128, 512]); pB = kb.ps("pB", [128, 512]); pC = kb.ps("pC", [128, 512])
    pD = kb.ps("pD", [128, 512]); pE = kb.ps("pE", [128, 512])
    kb.phase_begin()
    mstg = [kb.sbt("mstg%d" % i, [128, 4096]) for i in range(2)]
    psm = pE.rearrange("p (c r) -> p c r", r=2)[:, 0:16, :]
    for i in range(3):
        mc = mod_columns(kb, modw, modb_col[:, 16 * i:16 * (i + 1)], ccol, 2048 * i, 16, "mc%d" % i, mstg, psm)
        kb.dma("sp", modc_d[:, 16 * i:16 * (i + 1), :], mc, r=["mc%d_modc" % i], w=[mkey])
    scr = kb.sbt("scr", [128, 8, 2])
    kb.dma("sp", scr, ccol, w=["scr"])
    kb.act(scr, scr, AF.Silu, r=["scr"], w=["scr"])
    grow = kb.sbt("grow", [2, 2048])
    gb2 = kb.sbt("gb2", [2, 2048])
    kb.dma("sp", gb2[:, 0:1024], modb_row[2 * D:3 * D].partition_broadcast(2), w=["gb2"])
    kb.dma("sp", gb2[:, 1024:2048], modb_row[5 * D:6 * D].partition_broadcast(2), w=["gb2"])
    psr_l = [pA, pB, pC, pD]; psr_k = ["pA", "pB", "pC", "pD"]
    for kt in range(8):
        b = kt % 2
        kb.dma("sp", mstg[b][:, 0:1024], modw[kt * 128:(kt + 1) * 128, 2 * D:3 * D], w=["mstg%d" % b])
        kb.dma("sp", mstg[b][:, 1024:2048], modw[kt * 128:(kt + 1) * 128, 5 * D:6 * D], w=["mstg%d" % b])
        for j in range(4):
            kb.mm(psr_l[j][0:2, :], scr[:, kt, :], mstg[b][:, j * 512:(j + 1) * 512], kt == 0, kt == 7,
                  r=["mstg%d" % b, "scr"], w=[psr_k[j]])
    for j in range(4):
        kb.tt("dve", grow[:, j * 512:(j + 1) * 512], psr_l[j][0:2, :], gb2[:, j * 512:(j + 1) * 512], ALU.add,
              r=[psr_k[j], "gb2"], w=["grow"])
    kb.dma("sp", grow_d, grow, r=["grow"], w=[mkey])
    kb.phase_end()
    kb.scope_end()


def norm_transpose_tile(kb, x_dram_rows, uT_out, SC, SH, r_idx, ident_b, pst, tag, okey):
    i = kb.rot(tag + "_x", 3)
    xk = "%s_x%d" % (tag, i)
    x = kb._bufs[xk]
    kb.dma("sp", x, x_dram_rows, w=[xk])
    return norm_transpose_sb(kb, x, xk, uT_out, SC, SH, r_idx, ident_b, pst, tag, okey)


NXH = 4


def norm_part(kb, x, xk, tag):
    j = kb.rot(tag + "_s", NXH)
    junk, ss, xh = kb._bufs[tag + "_junk"], kb._bufs["%s_ss%d" % (tag, j)], kb._bufs["%s_xh%d" % (tag, j)]
    ssk, xhk = "%s_ss%d" % (tag, j), "%s_xh%d" % (tag, j)
    kb.memset("pool", ss, 0.0, w=[ssk])
    kb.act(junk, x, AF.Square, r=[xk, ssk], w=[tag + "_junk", ssk], accum=ss)
    kb.act(ss, ss, AF.Sqrt, r=[ssk], w=[ssk], bias=EPS_RMS, scale=1.0 / D)
    kb.recip(ss, ss, r=[ssk], w=[ssk])
    kb.ts("dve", xh, x, ss, None, ALU.mult, None, r=[xk, ssk], w=[xhk])
    return xh, xhk


def tr_part(kb, xh, xhk, uT_out, SC, SH, r_idx, ident_b, pst, okey):
    pk, psT = pst
    for kt in range(8):
        kb.tr(psT[:, kt * 128:(kt + 1) * 128], xh[:, kt * 128:(kt + 1) * 128], ident_b, r=[xhk, "ident_b"], w=[pk])
    for kt in range(8):
        eng = "act" if kt % 2 == 0 else "dve"
        if eng == "act":
            kb.act(uT_out[:, kt, :], psT[:, kt * 128:(kt + 1) * 128], AF.Identity, r=[pk, "SCSH"], w=[okey],
                   bias=SH[:, kt, r_idx:r_idx + 1], scale=SC[:, kt, r_idx:r_idx + 1])
        else:
            kb.ts("dve", uT_out[:, kt, :], psT[:, kt * 128:(kt + 1) * 128], SC[:, kt, r_idx:r_idx + 1],
                  SH[:, kt, r_idx:r_idx + 1], ALU.mult, ALU.add, r=[pk, "SCSH"], w=[okey])


def norm_transpose_sb(kb, x, xk, uT_out, SC, SH, r_idx, ident_b, pst, tag, okey):
    xh, xhk = norm_part(kb, x, xk, tag)
    tr_part(kb, xh, xhk, uT_out, SC, SH, r_idx, ident_b, pst, okey)


def alloc_norm_xbufs(kb, tag):
    for i in range(3):
        kb._bufs["%s_x%d" % (tag, i)] = kb.sbt("%s_x%d" % (tag, i), [128, D])


def alloc_norm_bufs(kb, tag):
    kb._bufs[tag + "_junk"] = kb.sb(tag + "_junk", [128, D], BF16)
    for j in range(NXH):
        kb._bufs["%s_ss%d" % (tag, j)] = kb.sb("%s_ss%d" % (tag, j), [128, 1])
        kb._bufs["%s_xh%d" % (tag, j)] = kb.sb("%s_xh%d" % (tag, j), [128, D], BF16)


def load_rows(kb, x, xk, hsrc, hkey, row0, nrows_total):
    lo, hi = max(row0, 0), min(row0 + 128, nrows_total)
    if lo > row0 or hi < row0 + 128:
        kb.memset("pool", x, 0.0, w=[xk])
    if hi > lo:
        kb.dma("sp", x[lo - row0:hi - row0, :], hsrc[lo:hi, :], r=[hkey], w=[xk])


def emit_F(kb, l, g, hsrc, hkey, yint, ykey, hdst, hdkey, dst_is_final, wb16, modc_d, grow_d, mkey, uTd):
    layer0 = (l == 0)
    kb.scope_begin("F%d%d" % (l, g), "L%d" % l)
    nc = kb.nc
    NT = 17 if layer0 else 16
    NP = 19 if layer0 else 17
    TP = NP * 128
    TT = NT * 128
    lat0 = 256 + g * 2048
    ctx0 = g * 128
    maskp = kb.inp("maskp%d" % g, [19 * 128], lvl="global")
    ccol = kb.inp("ccol", [128, 8, 2], lvl="global")
    modw = kb.inp("modw", [D, 6 * D], lvl="layer")
    modb_col = kb.inp("modb_col", [128, 48], lvl="layer")
    modb_row = kb.inp("modb_row", [6 * D], lvl="layer")
    n1g = kb.inp("n1g", [128, 8], lvl="layer")
    n2g = kb.inp("n2g", [128, 8], lvl="layer")
    w_pc, wpc_key = wb16["w_pc"]
    bands = kb.inp("bands%d" % g, [128, 32, 128], lvl="global")
    poolw2 = kb.inp("poolw2", [64, 4, 128], lvl="layer")
    pscale = kb.inp("pscale", [128, 2], lvl="layer")
    dww = kb.inp("dww", [128, 2, 31], lvl="layer")
    dwb = kb.inp("dwb", [128, 2], lvl="layer")
    lng = kb.inp("lng", [128, 2], lvl="layer")
    lnb = kb.inp("lnb", [128, 2], lvl="layer")
    pww = kb.inp("pww", [256, 256], lvl="layer")
    w_out, wout_key = wb16["w_out"]
    fwi, fwi_key = wb16["fwi"]
    fwo, fwo_key = wb16["fwo"]
    identf_d = kb.inp("identf", [128, 128], lvl="global")
    sel2_d = kb.inp("sel2", [2, 2, 128], lvl="global")

    identf = kb.sb("identf", [128, 128])
    kb.dma("sp", identf, identf_d, w=["identf"])
    ident_b = kb.sb("ident_b", [128, 128], BF16)
    kb.copy("dve", ident_b, identf, r=["identf"], w=["ident_b"])
    onesf = kb.sb("onesf", [128, 128])
    kb.memset("pool", onesf, 1.0, w=["onesf"])
    sel2 = kb.sb("sel2", [2, 2, 128])
    kb.dma("sp", sel2, sel2_d, w=["sel2"])
    SC1 = kb.sb("SC1", [128, 8, 2]); SH1 = kb.sb("SH1", [128, 8, 2])
    SC2 = kb.sb("SC2", [128, 8, 2]); SH2 = kb.sb("SH2", [128, 8, 2])
    Gbc = [kb.sb("Gbc%d" % r_, [128, 2048]) for r_ in range(2 if layer0 else 1)]
    yT = kb.sb("yT", [128, 8, TT], BF16)
    alloc_norm_bufs(kb, "n1")
    pT = kb.ps("pT", [128, 1024], BF16)
    pA = kb.ps("pA", [128, 512]); pB = kb.ps("pB", [128, 512]); pC = kb.ps("pC", [128, 512])
    pD = kb.ps("pD", [128, 512]); pE = kb.ps("pE", [128, 512])
    for kt in range(4):
        kb.dma("sp", yT[:, kt, 0:2048], yint[kt * 128:(kt + 1) * 128, lat0:lat0 + 2048], r=[ykey], w=["yT%d" % kt])
        if layer0:
            kb.dma("sp", yT[:, kt, 2048:2176], yint[kt * 128:(kt + 1) * 128, ctx0:ctx0 + 128], r=[ykey], w=["yT%d" % kt])

    kb.phase_begin()
    mc1 = kb.sbt("mc1", [128, 16, 2]); mc2 = kb.sbt("mc2", [128, 16, 2])
    kb.dma("sp", mc1, modc_d[:, 0:16, :], r=[mkey], w=["mc1_modc"])
    kb.dma("sp", mc2, modc_d[:, 24:40, :], r=[mkey], w=["mc2_modc"])
    g1n = kb.sbt("g1n", [128, 8]); g2n = kb.sbt("g2n", [128, 8])
    kb.dma("sp", g1n, n1g, w=["g1n"]); kb.dma("sp", g2n, n2g, w=["g2n"])
    for (mc, gn, SC, SH, nm) in ((mc1, g1n, SC1, SH1, "1"), (mc2, g2n, SC2, SH2, "2")):
        for r_ in range(2):
            kb.stt("dve", SC[:, :, r_], mc[:, 8:16, r_], 1.0, gn, ALU.add, ALU.mult,
                   r=["mc%s_modc" % nm, "g%sn" % nm], w=["SCSH"])
            kb.copy("dve", SH[:, :, r_], mc[:, 0:8, r_], r=["mc%s_modc" % nm], w=["SCSH"])
    grow = kb.sbt("grow", [2, 2048])
    kb.dma("sp", grow, grow_d, r=[mkey], w=["grow"])
    psr_l = [pA, pB, pC, pD]; psr_k = ["pA", "pB", "pC", "pD"]
    for r_ in range(len(Gbc)):
        for j in range(4):
            kb.mm(psr_l[j], sel2[:, r_, :], grow[:, j * 512:(j + 1) * 512], True, True, r=["sel2", "grow"], w=[psr_k[j]])
        for j in range(4):
            kb.copy("act", Gbc[r_][:, j * 512:(j + 1) * 512], psr_l[j], r=[psr_k[j]], w=["Gbc%d" % r_])

    wpc_sb = kb.sbt("wpc", [128, 8, 768], BF16)
    kb.dma("sp", wpc_sb, w_pc.rearrange("(kt p) n -> p kt n", p=128), r=[wpc_key], w=["wpc"])
    pww_sb = kb.sbt("pww", [128, 2, 256], BF16)
    kb.dma("pool", pww_sb, pww.rearrange("(ct p) n -> p ct n", p=128), w=["pww"])
    poolw_sb = kb.sbt("poolw", [64, 4, 128], BF16)
    kb.dma("pool", poolw_sb, poolw2, w=["poolw"])
    bands_sb = kb.sbt("bands", [128, 32, 128])
    kb.dma("sp", bands_sb, bands, w=["bands"])
    small = {}
    for nm, src, shp in (("pscale", pscale, [128, 2]), ("dww", dww, [128, 2, 31]), ("dwb", dwb, [128, 2]),
                         ("lng", lng, [128, 2]), ("lnb", lnb, [128, 2])):
        small[nm] = kb.sbt(nm, shp)
        kb.dma("sp", small[nm], src, w=[nm])
    maskb = kb.sbt("maskb", [128, TP], BF16)
    kb.dma("pool", maskb, maskp[0:TP].partition_broadcast(128), w=["maskb"])
    Dg = kb.sbt("Dg", [128, 2, 31, 128], BF16)
    for ct in range(2):
        for k in range(31):
            if k % 3 == 0:
                kb.act(Dg[:, ct, k, :], identf, AF.Copy, r=["identf", "dww"], w=["Dg"], scale=small["dww"][:, ct, k:k + 1])
            else:
                kb.ts("pool" if k % 3 == 1 else "dve", Dg[:, ct, k, :], identf, small["dww"][:, ct, k:k + 1], None, ALU.mult, None,
                      r=["identf", "dww"], w=["Dg"])
    alloc_norm_xbufs(kb, "n1")
    u_p = kb.sbt("u_p", [128, NP, 256])
    GLU = kb.sbt("GLU", [128, 2, TP], BF16)
    uTt = [kb.sbt("uTt%d" % i, [128, 8, 128], BF16) for i in range(4)]
    sg = kb.sbt("sg", [128, 2, 128]); gl = kb.sbt("gl", [128, 2, 128])
    for i in range(NP):
        r_idx = 0 if i < 17 else 1
        b = i % 4
        if i < 17:
            t0_, tbase, tlen = g * 2048 - 16 + i * 128, 256, 4096
        else:
            t0_, tbase, tlen = g * 128 - 16 + (i - 17) * 128, 0, 256
        lo_, hi_ = max(t0_, 0), min(t0_ + 128, tlen)
        if lo_ > t0_ or hi_ < t0_ + 128:
            kb.memset("pool", uTt[b], 0.0, w=["uTt%d" % b])
        if hi_ > lo_:
            kb.dma("sp", uTt[b][:, :, lo_ - t0_:hi_ - t0_], uTd[:, :, tbase + lo_:tbase + hi_], r=["uTd%d" % l], w=["uTt%d" % b])
        pcv = pA.rearrange("p (c t) -> p c t", c=4)
        for cti in range(4):
            for kt in range(8):
                kb.mm(pcv[:, cti, :], wpc_sb[:, kt, 256 + cti * 128:256 + (cti + 1) * 128], uTt[b][:, kt, :], kt == 0, kt == 7,
                      r=["wpc", "uTt%d" % b], w=["pA"])
        for kt in range(8):
            kb.mm(pB[:, 0:256], uTt[b][:, kt, :], wpc_sb[:, kt, 0:256], kt == 0, kt == 7, r=["wpc", "uTt%d" % b], w=["pB"])
        kb.act(sg, pcv[:, 2:4, :], AF.Sigmoid, r=["pA"], w=["sg"])
        kb.tt("dve", gl, pcv[:, 0:2, :], sg, ALU.mult, r=["pA", "sg"], w=["gl"])
        kb.tt("pool", GLU[:, :, i * 128:(i + 1) * 128], gl, maskb[:, i * 128:(i + 1) * 128].unsqueeze(1).to_broadcast([128, 2, 128]),
              ALU.mult, r=["gl", "maskb"], w=["GLU%d" % i])
        kb.copy("act", u_p[:, i, :], pB[:, 0:256], r=["pB"], w=["u_p%d" % i])

    pooled = kb.sbt("pooled", [64, 4, 128], BF16)
    blocks = [(j, j, (0 if j == 0 else (2 if j == 15 else 1))) for j in range(16)]
    if layer0:
        blocks.append((16, 17, 3))
    for (ot, pt, cls) in blocks:
        ppv = pC.rearrange("p (g t) -> p g t", g=4)
        for gi in range(4):
            for half in range(2):
                kb.mm(ppv[0:64, gi, :], u_p[:, pt + half, gi * 64:(gi + 1) * 64], bands_sb[:, (cls * 4 + gi) * 2 + half, :],
                      half == 0, half == 1, r=["u_p%d" % (pt + half), "bands"], w=["pC"])
        kb.copy("act", pooled, ppv[0:64, :, :], r=["pC"], w=["pooled"])
        pqv = pD.rearrange("p (g t) -> p g t", g=4)
        for pr in range(2):
            for gl_ in range(2):
                gi = pr * 2 + gl_
                kb.mm(pqv[:, pr, :], poolw_sb[:, gi, :], pooled[:, gi, :], gl_ == 0, gl_ == 1, r=["poolw", "pooled"], w=["pD"])
        for pr in range(2):
            kb.ts("dve", yT[:, 4 + pr, ot * 128:(ot + 1) * 128], pqv[:, pr, :], small["pscale"][:, pr:pr + 1], None, ALU.mult, None,
                  r=["pD", "pscale"], w=["yT%d" % (4 + pr)])

    xc = kb.sbt("xc", [128, 2, 512]); sq = kb.sbt("sq", [128, 2, 512])
    mean = kb.sbt("mean", [128, 512]); var = kb.sbt("var", [128, 512]); t1 = kb.sbt("t1", [128, 2, 512])
    zs = kb.sbt("zs", [128, 2, 512], BF16)
    cblocks = [(jb * 512, jb * 512 + 1, 512) for jb in range(4)]
    if layer0:
        cblocks.append((16 * 128, 17 * 128 + 1, 128))
    for (oc, gc, n) in cblocks:
        for ct in range(2):
            pc_ = pA if ct == 0 else pB
            for k in range(31):
                kb.mm(pc_[:, 0:n], Dg[:, ct, k, :], GLU[:, ct, gc + k:gc + k + n], k == 0, k == 30,
                      r=["Dg"] + ["GLU%d" % i for i in range((gc + k) // 128, (gc + k + n - 1) // 128 + 1)], w=["pA" if ct == 0 else "pB"])
            kb.act(xc[:, ct, 0:n], pc_[:, 0:n], AF.Identity, r=["pA" if ct == 0 else "pB", "dwb"], w=["xc"], bias=small["dwb"][:, ct:ct + 1])
        kb.tt("pool", sq[:, :, 0:n], xc[:, :, 0:n], xc[:, :, 0:n], ALU.mult, r=["xc"], w=["sq"])
        for ct in range(2):
            kb.mm(pC[:, 0:n], onesf, xc[:, ct, 0:n], ct == 0, ct == 1, r=["onesf", "xc"], w=["pC"])
        for ct in range(2):
            kb.mm(pD[:, 0:n], onesf, sq[:, ct, 0:n], ct == 0, ct == 1, r=["onesf", "sq"], w=["pD"])
        kb.act(mean[:, 0:n], pC[:, 0:n], AF.Copy, r=["pC"], w=["mean"], scale=1.0 / 256)
        kb.tt("dve", var[:, 0:n], mean[:, 0:n], mean[:, 0:n], ALU.mult, r=["mean"], w=["var"])
        kb.stt("dve", var[:, 0:n], pD[:, 0:n], 1.0 / 256, var[:, 0:n], ALU.mult, ALU.subtract, r=["pD", "var"], w=["var"])
        kb.act(var[:, 0:n], var[:, 0:n], AF.Sqrt, r=["var"], w=["var"], bias=EPS_LN)
        kb.recip(var[:, 0:n], var[:, 0:n], r=["var"], w=["var"])
        for ct in range(2):
            kb.tt("dve", t1[:, ct, 0:n], xc[:, ct, 0:n], mean[:, 0:n], ALU.subtract, r=["xc", "mean"], w=["t1"])
            kb.tt("pool", t1[:, ct, 0:n], t1[:, ct, 0:n], var[:, 0:n], ALU.mult, r=["t1", "var"], w=["t1"])
            kb.act(zs[:, ct, 0:n], t1[:, ct, 0:n], AF.Silu, r=["t1", "lng", "lnb"], w=["zs"],
                   bias=small["lnb"][:, ct:ct + 1], scale=small["lng"][:, ct:ct + 1])
        for dt_ in range(2):
            for ct in range(2):
                kb.mm(pE[:, 0:n], pww_sb[:, ct, dt_ * 128:(dt_ + 1) * 128], zs[:, ct, 0:n], ct == 0, ct == 1, r=["pww", "zs"], w=["pE"])
            kb.copy("act", yT[:, 6 + dt_, oc:oc + n], pE[:, 0:n], r=["pE"], w=["yT%d" % (6 + dt_)])
    kb.phase_end()

    kb.phase_begin()
    wout_sb = kb.sbt("wout", [128, 8, D], BF16)
    for kt in range(8):
        kb.dma("sp", wout_sb[:, kt, :], w_out[kt * 128:(kt + 1) * 128, :], r=[wout_key], w=["wout"])
    fwo_sb = kb.sbt("fwo", [128, 22, D], BF16)
    for fc in range(22):
        kb.dma("act", fwo_sb[:, fc, :], fwo[fc * 128:(fc + 1) * 128, :], r=[fwo_key], w=["fwo"])
    h0 = [kb.sbt("h0_%d" % i, [128, D]) for i in range(1)]
    h1 = kb.sbt("h1", [128, 4, D])
    u2T = kb.sbt("u2T", [128, 8, 512], BF16)
    actT = kb.sbt("actT", [128, 22, 512], BF16)
    wch = [kb.sbt("wch%d" % i, [128, 8, 256], BF16) for i in range(5)]
    sgf = [kb.sbt("sgf%d" % i, [128, 512]) for i in range(2)]
    tmpo = [kb.sbt("tmpo%d" % i, [128, 512]) for i in range(2)]
    obuf = [kb.sbt("obuf%d" % i, [128, D]) for i in range(1)]
    fwi_v = fwi.rearrange("(kt p) n -> p kt n", p=128)
    ykeys = ["yT%d" % k for k in range(8)]
    groups = [list(range(g * 4, min(g * 4 + 4, NT))) for g in range((NT + 3) // 4)]
    for grp in groups:
        n = len(grp) * 128
        for li, ti in enumerate(grp):
            r_idx = 1 if ti == 16 else 0
            g_bc = Gbc[r_idx]
            hb = kb.rot("h0", 1)
            hr0 = (lat0 + ti * 128) if ti < 16 else ctx0
            kb.dma("sp", h0[hb], hsrc[hr0:hr0 + 128, :], r=[hkey], w=["h0_%d" % hb])
            for nh in range(2):
                pp = pA if nh == 0 else pB
                pk = "pA" if nh == 0 else "pB"
                for kt in range(8):
                    kb.mm(pp, yT[:, kt, ti * 128:(ti + 1) * 128], wout_sb[:, kt, nh * 512:(nh + 1) * 512], kt == 0, kt == 7,
                          r=["wout", ykeys[kt]], w=[pk])
                tb = kb.rot("tmpo", 2)
                kb.tt("dve", tmpo[tb], pp, g_bc[:, nh * 512:(nh + 1) * 512], ALU.mult, r=[pk, "Gbc%d" % r_idx], w=["tmpo%d" % tb])
                kb.tt("pool", h1[:, li, nh * 512:(nh + 1) * 512], tmpo[tb], h0[hb][:, nh * 512:(nh + 1) * 512], ALU.add,
                      r=["tmpo%d" % tb, "h0_%d" % hb], w=["h1_%d" % li])
            norm_transpose_sb(kb, h1[:, li, :], "h1_%d" % li, u2T[:, :, li * 128:(li + 1) * 128], SC2, SH2, r_idx, ident_b,
                              ("pT", pT), "n1", "u2T")
        pF = kb.ps("pF", [128, 512]); pG = kb.ps("pG", [128, 512])
        for fc in range(22):
            wb = kb.rot("wch", 5)
            (pg_, pgk), (pu_, puk) = ((pC, "pC"), (pD, "pD")) if fc % 2 == 0 else ((pF, "pF"), (pG, "pG"))
            kb.dma("sp", wch[wb][:, :, 0:128], fwi_v[:, :, fc * 128:(fc + 1) * 128], r=[fwi_key], w=["wch%d" % wb])
            kb.dma("sp", wch[wb][:, :, 128:256], fwi_v[:, :, DFF + fc * 128:DFF + (fc + 1) * 128], r=[fwi_key], w=["wch%d" % wb])
            for kt in range(8):
                kb.mm(pg_[:, 0:n], wch[wb][:, kt, 0:128], u2T[:, kt, 0:n], kt == 0, kt == 7, r=["wch%d" % wb, "u2T"], w=[pgk])
            for kt in range(8):
                kb.mm(pu_[:, 0:n], wch[wb][:, kt, 128:256], u2T[:, kt, 0:n], kt == 0, kt == 7, r=["wch%d" % wb, "u2T"], w=[puk])
            sb_ = kb.rot("sgf", 2)
            kb.act(sgf[sb_][:, 0:n], pg_[:, 0:n], AF.Silu, r=[pgk], w=["sgf%d" % sb_])
            kb.tt("dve", actT[:, fc, 0:n], sgf[sb_][:, 0:n], pu_[:, 0:n], ALU.mult, r=["sgf%d" % sb_, puk], w=["actT"])
        for li, ti in enumerate(grp):
            r_idx = 1 if ti == 16 else 0
            g_bc = Gbc[r_idx]
            ob = kb.rot("obuf", 1)
            for nh in range(2):
                pp = pA if nh == 0 else pB
                pk = "pA" if nh == 0 else "pB"
                for fc in range(22):
                    kb.mm(pp, actT[:, fc, li * 128:(li + 1) * 128], fwo_sb[:, fc, nh * 512:(nh + 1) * 512], fc == 0, fc == 21,
                          r=["actT", "fwo"], w=[pk])
                tb = kb.rot("tmpo", 2)
                kb.tt("dve", tmpo[tb], pp, g_bc[:, 1024 + nh * 512:1024 + (nh + 1) * 512], ALU.mult,
                      r=[pk, "Gbc%d" % r_idx], w=["tmpo%d" % tb])
                kb.tt("pool", obuf[ob][:, nh * 512:(nh + 1) * 512], tmpo[tb], h1[:, li, nh * 512:(nh + 1) * 512], ALU.add,
                      r=["tmpo%d" % tb, "h1_%d" % li], w=["obuf%d" % ob])
            if dst_is_final:
                od0 = g * 2048 + ti * 128
            else:
                od0 = (lat0 + ti * 128) if ti < 16 else ctx0
            kb.dma("sp", hdst[od0:od0 + 128, :], obuf[ob], r=["obuf%d" % ob], w=[hdkey])
    kb.phase_end()
    kb.scope_end()

import math

NTK = 34
TTK = NTK * 128


def c_of_pos(d, pos):
    if d == 0:
        return pos
    return 1 - pos if pos < 2 else 35 - pos


def emit_M(kb, l, p, hsrc, hkey, yint, ykey, modc_d, mkey, uTd, hook=None):
    layer0 = (l == 0)
    lam_init = 0.8 - 0.6 * math.exp(-0.3 * l)
    stop_after = None
    kb.scope_begin("M%d%d" % (l, p), "L%d" % l)
    nc = kb.nc
    ccol = kb.inp("ccol", [128, 8, 2], lvl="global")
    modw = kb.inp("modw", [D, 6 * D], lvl="layer")
    modb_col = kb.inp("modb_col", [128, 48], lvl="layer")[:, 0:16]
    n1g = kb.inp("n1g", [128, 8], lvl="layer")
    w_fm = kb.inp("w_fm", [D, 512])
    w_tm = kb.inp("w_tm", [D, 528])
    gate_b = kb.inp("gate_b", [8])
    mng = kb.inp("mng", [64], lvl="layer")
    qkng = kb.inp("qkng", [128, 2], lvl="layer")
    lamqk = kb.inp("lamqk", [2, 2, 32], lvl="layer")
    sublng = kb.inp("sublng", [64, 1], lvl="layer")
    identf_d = kb.inp("identf", [128, 128], lvl="global")
    tri_d = kb.inp("tri", [128, 2, 128], lvl="global")
    mask_d = kb.inp("mask", [128, 2, 128], lvl="global")
    bones_d = kb.inp("bones", [128, 128], lvl="global")
    perm_d = kb.inp("perm", [128, 128], lvl="global")
    mcol_d = kb.inp("mcol", [128, 4], lvl="global")
    permB_d = kb.inp("permB", [68, 68], lvl="global")
    sgn_d = kb.inp("sgn", [2, 64], lvl="global")
    cos_d = kb.inp("cost", [128, TTK], lvl="global")
    sin_d = kb.inp("sint", [128, TTK], lvl="global")
    selZ_d0 = kb.inp("selZ", [128, 64], lvl="global")

    identf = kb.sb("identf", [128, 128]); kb.dma("sp", identf, identf_d, w=["identf"])
    ident_b = kb.sb("ident_b", [128, 128], BF16); kb.copy("dve", ident_b, identf, r=["identf"], w=["ident_b"])
    onesf = kb.sb("onesf", [128, 128]); kb.memset("pool", onesf, 1.0, w=["onesf"])
    tri = kb.sb("tri", [128, 2, 128]); kb.dma("sp", tri, tri_d, w=["tri"])
    mask = kb.sb("mask", [128, 2, 128]); kb.dma("sp", mask, mask_d, w=["mask"])
    bones = kb.sb("bones", [128, 128], BF16); kb.dma("pool", bones, bones_d, w=["bones"])
    perm = kb.sb("perm", [128, 128], BF16); kb.dma("pool", perm, perm_d, w=["perm"])
    mcol = kb.sb("mcol", [128, 4]); kb.dma("sp", mcol, mcol_d, w=["mcol"])
    permB = kb.sb("permB", [68, 68]); kb.dma("sp", permB, permB_d, w=["permB"])
    sgn = kb.sb("sgn", [2, 64]); kb.dma("sp", sgn, sgn_d, w=["sgn"])
    qkg = kb.sb("qkg", [128, 2]); kb.dma("sp", qkg, qkng, w=["qkg"])
    sublg = kb.sb("sublg", [64, 1]); kb.dma("sp", sublg, sublng, w=["sublg"])
    gbb = kb.sb("gbb", [128, 8]); kb.dma("sp", gbb, gate_b.partition_broadcast(128), w=["gbb"])
    mngb = kb.sb("mngb", [128, 64]); kb.dma("sp", mngb, mng.partition_broadcast(128), w=["mngb"])
    SC1 = kb.sb("SC1", [128, 8, 2]); SH1 = kb.sb("SH1", [128, 8, 2])
    negLam = kb.sb("negLam", [64, 1])
    qT = kb.sb("qT", [64, 2, TTK], BF16); kT = kb.sb("kT", [64, 2, TTK], BF16)
    k_tm = kb.sb("k_tm", [128, NTK, 128], BF16)
    vext = kb.sb("vext", [128, NTK, 2, 72], BF16)
    Vx = kb.sb("Vx", [128, NTK, 2, 128], BF16)
    sigo = kb.sb("sigo", [128, NTK, 128], BF16)
    G = kb.sb("G", [128, NTK, 8])
    QT = kb.sb("QT", [128, TTK], BF16)
    KTm = kb.sb("KTm", [128, 4, TTK], BF16)
    selZ_d = selZ_d0
    selZ = kb.sb("selZ", [128, 64]); kb.dma("sp", selZ, selZ_d, w=["selZ"])
    kb.memset("pool", vext, 1.0, w=["vext"]); kb.memset("pool", Vx, 0.0, w=["Vx"])
    kb.memset("pool", Vx[:, :, :, 64:65], 1.0, w=["Vx"])
    alloc_norm_bufs(kb, "n1")
    pT = kb.ps("pT", [128, 1024], BF16)
    pA = kb.ps("pA", [128, 512]); pB = kb.ps("pB", [128, 512]); pC = kb.ps("pC", [128, 512])
    pD = kb.ps("pD", [128, 512]); pE = kb.ps("pE", [128, 512]); pF = kb.ps("pF", [128, 512]); pG = kb.ps("pG", [128, 512])

    kb.phase_begin()
    mc1 = kb.sbt("mc1", [128, 16, 2])
    kb.dma("sp", mc1, modc_d[:, 0:16, :], r=[mkey], w=["mc1_modc"])
    g1n = kb.sbt("g1n", [128, 8]); kb.dma("sp", g1n, n1g, w=["g1n"])
    for r_ in range(2):
        kb.stt("dve", SC1[:, :, r_], mc1[:, 8:16, r_], 1.0, g1n, ALU.add, ALU.mult, r=["mc1_modc", "g1n"], w=["SCSH"])
        kb.copy("dve", SH1[:, :, r_], mc1[:, 0:8, r_], r=["mc1_modc"], w=["SCSH"])
    lqk = kb.sbt("lqk", [2, 2, 32]); kb.dma("sp", lqk, lamqk, w=["lqk"])
    lpr = kb.sbt("lpr", [2, 2, 32]); lsm = kb.sbt("lsm", [2, 2])
    for j_ in range(2):
        kb.tt("dve", lpr[:, j_, :], lqk[:, 0, :], lqk[:, 1, :], ALU.mult, r=["lqk"], w=["lpr"])
    kb.S.op("dve", lambda e: e.tensor_reduce(out=lsm, in_=lpr, axis=AX.X, op=ALU.add), r=["lpr"], w=["lsm"])
    kb.act(lsm, lsm, AF.Exp, r=["lsm"], w=["lsm"])
    kb.mm(pF[0:64, 0:2], sgn, lsm, True, True, r=["sgn", "lsm"], w=["pF"])
    kb.ts("dve", negLam, pF[0:64, 0:1], -float(lam_init), None, ALU.add, None, r=["pF"], w=["negLam"])

    wfm = kb.sbt("wfm", [128, 8, 512], BF16)
    kb.dma("pool", wfm, w_fm.rearrange("(kt p) n -> p kt n", p=128), w=["wfm"])
    wtm = kb.sbt("wtm", [128, 8, 528], BF16)
    kb.dma("pool", wtm, w_tm.rearrange("(kt p) n -> p kt n", p=128), w=["wtm"])
    if hook is not None:
        hook()
    alloc_norm_xbufs(kb, "n1")
    uTb = [kb.sbt("uTb%d" % i, [128, 8, 512], BF16) for i in range(2)]
    cosb = [kb.sbt("cosb%d" % i, [128, 512]) for i in range(2)]
    sinb = [kb.sbt("sinb%d" % i, [128, 512]) for i in range(2)]
    Ab = kb.sbt("Ab", [128, 512], BF16); sqb = kb.sbt("sqb", [128, 512], BF16)
    rs = kb.sbt("rs", [128, 512]); t1 = kb.sbt("t1", [128, 512]); t2 = kb.sbt("t2", [128, 512])
    blocks = [(0, 256)] + [(256 + 512 * i, 512) for i in range(8)]
    Abq = [kb.sbt("Abq%d" % i, [128, 512], BF16) for i in range(2)]
    sqq = [kb.sbt("sqq%d" % i, [128, 512], BF16) for i in range(2)]

    def stage1(bi):
        c0, n = blocks[bi]
        ub = bi % 2
        uk = "uTb%d" % ub
        kb.dma("sp", cosb[ub][:, 0:n], cos_d[:, c0:c0 + n], w=["cosb%d" % ub])
        kb.dma("sp", sinb[ub][:, 0:n], sin_d[:, c0:c0 + n], w=["sinb%d" % ub])
        if p == 0:
            nq_ = []
            for li in range(n // 128):
                c = c0 // 128 + li
                xi = kb.rot("n1_x", 3)
                xk_ = "n1_x%d" % xi
                kb.dma("sp", kb._bufs[xk_], hsrc[c * 128:(c + 1) * 128, :], r=[hkey], w=[xk_])
                nq_.append(norm_part(kb, kb._bufs[xk_], xk_, "n1"))
            for li in range(n // 128):
                c = c0 // 128 + li
                tr_part(kb, nq_[li][0], nq_[li][1], uTb[ub][:, :, li * 128:(li + 1) * 128], SC1, SH1,
                        1 if c < 2 else 0, ident_b, ("pT", pT), uk)
            kb.dma("sp", uTd[:, :, c0:c0 + n], uTb[ub][:, :, 0:n], r=[uk], w=["uTd%d" % l])
        else:
            kb.dma("sp", uTb[ub][:, :, 0:n], uTd[:, :, c0:c0 + n], r=["uTd%d" % l], w=[uk])
        for which, dst, dk in ((0, qT, "qT"), (1, kT, "kT")):
            for h in range(2):
                pp, pk = (pA, "pA") if h == 0 else (pB, "pB")
                cs = which * 128 + h * 64
                for kt in range(8):
                    kb.mm(pp[0:64, 0:n], wfm[:, kt, cs:cs + 64], uTb[ub][:, kt, 0:n], kt == 0, kt == 7, r=["wfm", uk], w=[pk])
                kb.copy("act" if h == 0 else "dve", dst[:, h, c0:c0 + n], pp[0:64, 0:n], r=[pk], w=[dk])
        for which in (0, 1):
            cs = 256 + which * 128
            for kt in range(8):
                kb.mm(pC[:, 0:n], wfm[:, kt, cs:cs + 128], uTb[ub][:, kt, 0:n], kt == 0, kt == 7, r=["wfm", uk], w=["pC"])
            kb.act(Abq[which][:, 0:n], pC[:, 0:n], AF.Copy, r=["pC", "qkg"], w=["Abq%d_%d" % (which, bi % 2)], scale=qkg[:, which:which + 1])
            kb.act(sqq[which][:, 0:n], pC[:, 0:n], AF.Square, r=["pC"], w=["sqq%d_%d" % (which, bi % 2)])
        for li in range(n // 128):
            c = c0 // 128 + li
            lt = uTb[ub][:, :, li * 128:(li + 1) * 128]
            for kt in range(8):
                kb.mm(pF, lt[:, kt, :], wtm[:, kt, 0:512], kt == 0, kt == 7, r=["wtm", uk], w=["pF"])
            for kt in range(8):
                kb.mm(pG[:, 0:16], lt[:, kt, :], wtm[:, kt, 512:528], kt == 0, kt == 7, r=["wtm", uk], w=["pG"])
            kb.copy("act", k_tm[:, c, :], pF[:, 0:128], r=["pF"], w=["k_tm"])
            kb.copy("act", vext[:, c, :, 0:64], pF[:, 128:256].rearrange("p (h v) -> p h v", h=2), r=["pF"], w=["vext"])
            kb.act(sigo[:, c, :], pF[:, 256:384], AF.Sigmoid, r=["pF"], w=["sigo"])
            kb.copy("act", Vx[:, c, :, 0:64], pF[:, 384:512].rearrange("p (h v) -> p h v", h=2), r=["pF"], w=["Vx"])
            kb.tt("dve", G[:, c, :], pG[:, 0:8], gbb, ALU.add, r=["pG", "gbb"], w=["G"])

    def stage2(bi):
        c0, n = blocks[bi]
        ub = bi % 2
        for which in (0, 1):
            Ab, sqb = Abq[which], sqq[which]
            kb.mm(pD[:, 0:n], bones, sqb[:, 0:n], True, True, r=["bones", "sqq%d_%d" % (which, bi % 2)], w=["pD"])
            kb.mm(pE[:, 0:n], perm, Ab[:, 0:n], True, True, r=["perm", "Abq%d_%d" % (which, bi % 2)], w=["pE"])
            kb.act(rs[:, 0:n], pD[:, 0:n], AF.Sqrt, r=["pD"], w=["rs"], bias=EPS_RMS, scale=1.0 / 32)
            kb.recip(rs[:, 0:n], rs[:, 0:n], r=["rs"], w=["rs"])
            kb.tt("dve", t1[:, 0:n], Ab[:, 0:n], cosb[ub][:, 0:n], ALU.mult, r=["Abq%d_%d" % (which, bi % 2), "cosb%d" % ub], w=["t1"])
            kb.tt("dve", t2[:, 0:n], pE[:, 0:n], sinb[ub][:, 0:n], ALU.mult, r=["pE", "sinb%d" % ub], w=["t2"])
            kb.tt("pool", t1[:, 0:n], t1[:, 0:n], t2[:, 0:n], ALU.add, r=["t1", "t2"], w=["t1"])
            if which == 0:
                kb.tt("pool", QT[:, c0:c0 + n], t1[:, 0:n], rs[:, 0:n], ALU.mult, r=["t1", "rs"], w=["QT"])
            else:
                kb.tt("pool", t2[:, 0:n], t1[:, 0:n], rs[:, 0:n], ALU.mult, r=["t1", "rs", "t2"], w=["t2"])
                for j in range(4):
                    kb.ts("dve" if j % 2 == 0 else "pool", KTm[:, j, c0:c0 + n], t2[:, 0:n], mcol[:, j:j + 1], None, ALU.mult, None,
                          r=["t2", "mcol"], w=["KT"])

    Abq2 = [[Abq[0], Abq[1]], [kb.sbt("Abq2", [128, 512], BF16), kb.sbt("Abq3", [128, 512], BF16)]]
    sqq2 = [[sqq[0], sqq[1]], [kb.sbt("sqq2", [128, 512], BF16), kb.sbt("sqq3", [128, 512], BF16)]]
    for bi in range(len(blocks)):
        Abq[0], Abq[1] = Abq2[bi % 2]
        sqq[0], sqq[1] = sqq2[bi % 2]
        _names = {0: ("Abq0", "Abq1", "sqq0", "sqq1"), 1: ("Abq2", "Abq3", "sqq2", "sqq3")}
        stage1(bi)
        if bi >= 1:
            Abq[0], Abq[1] = Abq2[(bi - 1) % 2]
            sqq[0], sqq[1] = sqq2[(bi - 1) % 2]
            stage2(bi - 1)
    Abq[0], Abq[1] = Abq2[(len(blocks) - 1) % 2]
    sqq[0], sqq[1] = sqq2[(len(blocks) - 1) % 2]
    stage2(len(blocks) - 1)
    kb.phase_end()

    kb.phase_begin()
    Gv = G.rearrange("p c (d g h) -> p c d g h", d=2, g=2, h=2)
    LFn = kb.sbt("LFn", [128, 2, NTK, 2]); bn = kb.sbt("bn", [128, 2, NTK, 2]); A_all = kb.sbt("A_all", [128, 2, NTK, 2])
    tmpg = kb.sbt("tmpg", [128, NTK, 2])
    AmaxCol = kb.sbt("AmaxCol", [68, 1]); BtotCol = kb.sbt("BtotCol", [68, 1]); rep = kb.sbt("rep", [68, 2, 128])
    AB = kb.sbt("AB", [128, 2, NTK, 2]); Mn = kb.sbt("Mn", [128, NTK, 2]); Mprev = kb.sbt("Mprev", [128, NTK, 2])
    Mu = kb.sbt("Mu", [128, 2, NTK, 2]); Ee = kb.sbt("Ee", [128, 2, NTK, 2]); Ecol = kb.sbt("Ecol", [128, 2, NTK])
    Wt = kb.sbt("Wt", [128, 2, NTK, 2]); NBt = kb.sbt("NBt", [128, 2, NTK, 2])
    Wk = kb.sbt("Wk", [128, 2, NTK, 2]); LOW = kb.sbt("LOW", [128, 2, NTK, 2])
    for d in range(2):
        kb.act(tmpg, Gv[:, :, d, 1, :], AF.Exp, r=["G"], w=["tmpg"], scale=-1.0)
        kb.act(LFn[:, d], tmpg, AF.Ln, r=["tmpg"], w=["LFn"], bias=1.0)
        pX = pA[:, 0:68]
        kb.mm(pX, tri[:, d, :], LFn[:, d].rearrange("p c h -> p (c h)"), True, True, r=["tri", "LFn"], w=["pA"])
        kb.copy("dve", bn[:, d].rearrange("p c h -> p (c h)"), pX, r=["pA"], w=["bn"])
        kb.tt("dve", A_all[:, d], Gv[:, :, d, 0, :], pX.rearrange("p (c h) -> p c h", h=2), ALU.add, r=["G", "pA"], w=["A_all"])
        kb.tr(pB[0:68, 0:128], A_all[:, d].rearrange("p c h -> p (c h)"), identf, r=["A_all", "identf"], w=["pB"])
        kb.S.op("dve", lambda e: e.tensor_reduce(out=AmaxCol, in_=pB[0:68, 0:128], axis=AX.X, op=ALU.max), r=["pB"], w=["AmaxCol"])
        kb.tr(pC[0:68, 0:128], bn[:, d].rearrange("p c h -> p (c h)"), identf, r=["bn", "identf"], w=["pC"])
        col = 127 if d == 0 else 0
        kb.copy("dve", BtotCol, pC[0:68, col:col + 1], r=["pC"], w=["BtotCol"])
        kb.ts("dve", rep[:, 0, :], onesf[0:68, :], AmaxCol, None, ALU.mult, None, r=["onesf", "AmaxCol"], w=["rep"])
        kb.ts("dve", rep[:, 1, :], onesf[0:68, :], BtotCol, -1.0, ALU.mult, ALU.mult, r=["onesf", "BtotCol"], w=["rep"])
        pm = identf[0:68, 0:68] if d == 0 else permB
        kb.mm(pD[:, 0:68], rep[:, 0, :], pm, True, True, r=["rep", "permB", "identf"], w=["pD"])
        kb.mm(pD[:, 68:136], rep[:, 1, :], pm, True, True, r=["rep", "permB", "identf"], w=["pD"])
        kb.copy("act", AB.rearrange("p a c h -> p (a c h)"), pD[:, 0:136], r=["pD"], w=["AB"])
        for h in range(2):
            kb.S.op("dve", lambda e, h=h: e.tensor_tensor_scan(out=Mn[:, :, h], data0=AB[:, 0, :, h], data1=AB[:, 1, :, h],
                                                           initial=0.0, op0=ALU.max, op1=ALU.add), r=["AB"], w=["Mn"])
        kb.memset("pool", Mprev[:, 0, :], 0.0, w=["Mprev"])
        kb.copy("dve", Mprev[:, 1:NTK, :], Mn[:, 0:NTK - 1, :], r=["Mn"], w=["Mprev"])
        kb.tt("dve", Mu[:, d], Mprev, AB[:, 0], ALU.max, r=["Mprev", "AB"], w=["Mu"])
        kb.tt("dve", Ee[:, d], Mprev, Mu[:, d], ALU.subtract, r=["Mprev", "Mu"], w=["Ee"])
        kb.act(Ee[:, d], Ee[:, d], AF.Exp, r=["Ee"], w=["Ee"])
        kb.copy("dve", Ecol[0:64, d, :], Ee[0:64, d, :, 0], r=["Ee"], w=["Ecol"])
        kb.copy("dve", Ecol[64:128, d, :], Ee[64:128, d, :, 1], r=["Ee"], w=["Ecol"])
        if d == 0:
            kb.tt("dve", Wt[:, 0], A_all[:, 0], Mu[:, 0], ALU.subtract, r=["A_all", "Mu"], w=["Wt"])
            kb.tt("dve", NBt[:, 0], bn[:, 0], Mu[:, 0], ALU.subtract, r=["bn", "Mu"], w=["NBt"])
        else:
            for pos in range(NTK):
                c = c_of_pos(1, pos)
                kb.tt("dve", Wt[:, 1, c, :], A_all[:, 1, c, :], Mu[:, 1, pos, :], ALU.subtract, r=["A_all", "Mu"], w=["Wt"])
                kb.tt("pool", NBt[:, 1, c, :], bn[:, 1, c, :], Mu[:, 1, pos, :], ALU.subtract, r=["bn", "Mu"], w=["NBt"])
        kb.act(Wk[:, d], Wt[:, d], AF.Exp, r=["Wt"], w=["Wk"], bias=math.log(0.125))
        kb.act(LOW[:, d], NBt[:, d], AF.Exp, r=["NBt"], w=["LOW"])

    hdir = [kb.sbt("hdir%d" % d, [128, NTK, 2, 64]) for d in range(2)]
    ysm = kb.sbt("ysm", [128, TTK], BF16)
    St = [kb.sbt("St%d" % d, [64, 2, 65]) for d in range(2)]
    Stb = [kb.sbt("Stb%d" % d, [64, 2, 72], BF16) for d in range(2)]
    Stmp = [kb.sbt("Stmp%d" % d, [64, 2, 65]) for d in range(2)]
    Pb = [[kb.sbt("Pb%d_%d" % (d, i), [128, 2, 128], BF16) for i in range(2)] for d in range(2)]
    kpb = [[kb.sbt("kpb%d_%d" % (d, i), [128, 2, 64], BF16) for i in range(2)] for d in range(2)]
    absd = [kb.sbt("absd%d" % d, [128, 2]) for d in range(2)]
    rc = [kb.sbt("rc%d" % d, [128, 2]) for d in range(2)]
    pS_ = [pA, pD]; pN_ = [pB, pE]; pU_ = [pC, pF]
    pSk = ["pA", "pD"]; pNk = ["pB", "pE"]; pUk = ["pC", "pF"]
    for d in range(2):
        kb.memset("pool", St[d], 0.0, w=["St%d" % d])
        kb.memset("pool", Stb[d], 0.0, w=["Stb%d" % d])
    for pos in range(NTK):
        for d in range(2):
            c = c_of_pos(d, pos)
            cs = slice(c * 128, (c + 1) * 128)
            i2 = kb.rot("Pb%d" % d, 2)
            P, kp = Pb[d][i2], kpb[d][i2]
            Pk, kpk = "Pb%d_%d" % (d, i2), "kpb%d_%d" % (d, i2)
            pSv = pS_[d].rearrange("p (h t) -> p h t", h=4)[:, 0:2, :]
            pNv = pN_[d].rearrange("p (h t) -> p h t", h=4)[:, 0:2, :]
            pUv = pU_[d].rearrange("p (h t) -> p h t", h=4)[:, 0:2, :]
            for h in range(2):
                kb.mm(pSv[:, h, :], kT[:, h, cs], qT[:, h, cs], True, True, r=["kT", "qT"], w=[pSk[d]])
            for h in range(2):
                kb.stt("dve", P[:, h, :], pSv[:, h, :], Wk[:, d, c, h:h + 1], mask[:, d, :], ALU.mult, ALU.mult,
                       r=[pSk[d], "Wk", "mask"], w=[Pk])
                kb.act(kp[:, h, :], k_tm[:, c, 64 * h:64 * h + 64], AF.Copy, r=["k_tm", "Wk"], w=[kpk], scale=Wk[:, d, c, h:h + 1])
            for h in range(2):
                kb.mm(pNv[:, h, 0:65], P[:, h, :], vext[:, c, h, 0:65], True, False, r=[Pk, "vext"], w=[pNk[d]])
                kb.mm(pNv[:, h, 0:65], qT[:, h, cs], Stb[d][:, h, 0:65], False, True, r=["qT", "Stb%d" % d], w=[pNk[d]])
            if pos < NTK - 1:
                for h in range(2):
                    kb.mm(pUv[0:64, h, 0:65], kp[:, h, :], vext[:, c, h, 0:65], True, True, r=[kpk, "vext"], w=[pUk[d]])
            kb.act(absd[d], pNv[:, :, 64], AF.Abs, r=[pNk[d]], w=["absd%d" % d])
            kb.tt("dve", absd[d], absd[d], LOW[:, d, c, :], ALU.max, r=["absd%d" % d, "LOW"], w=["absd%d" % d])
            kb.recip(rc[d], absd[d], r=["absd%d" % d], w=["rc%d" % d])
            kb.tt("dve", hdir[d][:, c], pNv[:, :, 0:64], rc[d].unsqueeze(2).to_broadcast([128, 2, 64]), ALU.mult,
                  r=[pNk[d], "rc%d" % d], w=["hdir%d_%d" % (d, c)])
            if pos < NTK - 1:
                ebc = Ee[0:64, d, pos + 1, :].unsqueeze(2).to_broadcast([64, 2, 65])
                kb.tt("dve", Stmp[d], St[d], pUv[0:64, :, 0:65], ALU.add, r=["St%d" % d, pUk[d]], w=["Stmp%d" % d])
                kb.tt("dve", St[d], Stmp[d], ebc, ALU.mult, r=["Stmp%d" % d, "Ee"], w=["St%d" % d])
                kb.tt("pool", Stb[d][:, :, 0:65], Stmp[d], ebc, ALU.mult, r=["Stmp%d" % d, "Ee"], w=["Stb%d" % d])
    hs = [kb.sbt("hs%d" % i, [128, 2, 64]) for i in range(2)]
    hsq = [kb.sbt("hsq%d" % i, [128, 2, 64]) for i in range(2)]
    ssq = [kb.sbt("ssq%d" % i, [128, 2]) for i in range(2)]
    mob = [kb.sbt("mob%d" % i, [128, 128], BF16) for i in range(2)]
    for c in range(NTK):
        i2 = c % 2
        cs = slice(c * 128, (c + 1) * 128)
        kb.tt("dve", hs[i2], hdir[0][:, c], hdir[1][:, c], ALU.add, r=["hdir0_%d" % c, "hdir1_%d" % c], w=["hs%d" % i2])
        kb.tt("pool", hsq[i2], hs[i2], hs[i2], ALU.mult, r=["hs%d" % i2], w=["hsq%d" % i2])
        kb.S.op("dve", lambda e, i2=i2: e.tensor_reduce(out=ssq[i2], in_=hsq[i2], axis=AX.X, op=ALU.add), r=["hsq%d" % i2], w=["ssq%d" % i2])
        kb.act(ssq[i2], ssq[i2], AF.Sqrt, r=["ssq%d" % i2], w=["ssq%d" % i2], bias=EPS_RMS, scale=1.0 / 64)
        kb.recip(ssq[i2], ssq[i2], r=["ssq%d" % i2], w=["ssq%d" % i2])
        kb.tt("dve", hs[i2], hs[i2], ssq[i2].unsqueeze(2).to_broadcast([128, 2, 64]), ALU.mult, r=["hs%d" % i2, "ssq%d" % i2], w=["hs%d" % i2])
        kb.tt("pool", hs[i2], hs[i2], mngb.unsqueeze(1).to_broadcast([128, 2, 64]), ALU.mult, r=["hs%d" % i2, "mngb"], w=["hs%d" % i2])
        kb.tt("dve", mob[i2], hs[i2].rearrange("p h v -> p (h v)"), sigo[:, c, :], ALU.mult, r=["hs%d" % i2, "sigo"], w=["mob%d" % i2])
        kb.tr(pT[:, 0:128], mob[i2], ident_b, r=["mob%d" % i2, "ident_b"], w=["pT"])
        kb.copy("act", ysm[:, cs], pT[:, 0:128], r=["pT"], w=["ysm"])
    kb.dma("sp", yint[p * 128:(p + 1) * 128, :], ysm, r=["ysm"], w=[ykey])
    kb.phase_end()

    kb.phase_begin()
    ysa = kb.sbt("ysa", [64, 2, TTK], BF16)
    Eb = [[kb.sbt("Eb%d_%d" % (m, i), [128, 512], BF16) for i in range(2)] for m in range(2)]
    Osb = [kb.sbt("Osb%d" % m, [128, 512]) for m in range(2)]
    for m in range(2):
        kb.memset("pool", Osb[m], 0.0, w=["Osb%d" % m])
    rz = [kb.sbt("rz%d" % m, [64, 512]) for m in range(2)]
    oT = kb.sbt("oT", [64, 512]); sqo = kb.sbt("sqo", [64, 512]); rso = kb.sbt("rso", [64, 512])
    scale = 32 ** -0.5
    qblocks = ([(0, 256, [0, 1])] if layer0 else []) + [(256 + 512 * i, 512, list(range(NTK))) for i in range(8)]
    psS = [[(pA, "pA"), (pC, "pC")], [(pB, "pB"), (pD, "pD")]]
    psO = [(pF, "pF"), (pG, "pG")]
    pending = None
    for h in range(2):
        for (q0, nq, kts) in qblocks:
            def emit_S(i):
                par = i % 2
                ks = slice(kts[i] * 128, (kts[i] + 1) * 128)
                for m in range(2):
                    pp, pk = psS[m][par]
                    kb.mm(pp[:, 0:nq], KTm[:, 2 * h + m, ks], QT[:, q0:q0 + nq], True, True, r=["KT", "QT"], w=[pk])

            def emit_E(i):
                par = i % 2
                for m in range(2):
                    pp, pk = psS[m][par]
                    kb.act(Eb[m][par][:, 0:nq], pp[:, 0:nq], AF.Exp, r=[pk], w=["Eb%d_%d" % (m, par)], scale=scale)

            def emit_PV(i):
                par = i % 2
                for m in range(2):
                    po, pok = psO[m]
                    kb.mm(po[:, 0:nq], Vx[:, kts[i], h, :], Eb[m][par][:, 0:nq], i == 0, i == len(kts) - 1,
                          r=["Vx", "Eb%d_%d" % (m, par)], w=[pok])

            emit_S(0)
            for i in range(len(kts)):
                emit_E(i)
                if i + 1 < len(kts):
                    emit_S(i + 1)
                emit_PV(i)
                if pending and i == min(1, len(kts) - 1):
                    pending[0]()
                if pending and i == min(8, len(kts) - 1):
                    pending[1]()
                    pending = None
            for m in range(2):
                po, pok = psO[m]
                kb.copy("act" if m == 0 else "dve", Osb[m][0:65, 0:nq], po[0:65, 0:nq], r=[pok], w=["Osb%d" % m])

            def fin_a(h=h, q0=q0, nq=nq):
                for m in range(2):
                    kb.mm(pE[0:64, 0:nq], selZ, Osb[m][:, 0:nq], True, True, r=["selZ", "Osb%d" % m], w=["pE"])
                    kb.recip(rz[m][:, 0:nq], pE[0:64, 0:nq], r=["pE"], w=["rz%d" % m])
                    kb.tt("pool", rz[m][:, 0:nq], Osb[m][0:64, 0:nq], rz[m][:, 0:nq], ALU.mult, r=["Osb%d" % m, "rz%d" % m], w=["rz%d" % m])
                kb.stt("dve", oT[:, 0:nq], rz[1][:, 0:nq], negLam, rz[0][:, 0:nq], ALU.mult, ALU.add, r=["rz0", "rz1", "negLam"], w=["oT"])
                kb.tt("pool", sqo[:, 0:nq], oT[:, 0:nq], oT[:, 0:nq], ALU.mult, r=["oT"], w=["sqo"])

            def fin_b(h=h, q0=q0, nq=nq):
                kb.mm(pE[0:64, 0:nq], onesf[0:64, 0:64], sqo[:, 0:nq], True, True, r=["onesf", "sqo"], w=["pE"])
                kb.act(rso[:, 0:nq], pE[0:64, 0:nq], AF.Sqrt, r=["pE"], w=["rso"], bias=EPS_RMS, scale=1.0 / 64)
                kb.recip(rso[:, 0:nq], rso[:, 0:nq], r=["rso"], w=["rso"])
                kb.tt("dve", oT[:, 0:nq], oT[:, 0:nq], rso[:, 0:nq], ALU.mult, r=["oT", "rso"], w=["oT"])
                kb.ts("dve", ysa[:, h, q0:q0 + nq], oT[:, 0:nq], sublg, 1.0 - float(lam_init), ALU.mult, ALU.mult, r=["oT", "sublg"], w=["ysa"])

            pending = (fin_a, fin_b)
    if pending:
        pending[0]()
        pending[1]()
    if not layer0:
        kb.memset("pool", ysa[:, :, 0:256], 0.0, w=["ysa"])
    for h in range(2):
        kb.dma("sp", yint[256 + p * 128 + 64 * h:256 + p * 128 + 64 * h + 64, :], ysa[:, h, :], r=["ysa"], w=[ykey])
    kb.phase_end()
    kb.scope_end()


def build_fused():
    kb = KB()
    hfull = kb.inp("hfull", [TTK, D], lvl="global")
    hful1 = kb.internal("hful1", [TTK, D])
    hout = kb.outp("hout", [4096, D])
    yint = [kb.internal("yint%d" % l, [512, TTK], BF16) for l in range(2)]
    kb.ps("pT", [128, 1024], BF16)
    for nm in "ABCDEFG":
        kb.ps("p" + nm, [128, 512])
    modc_d = [kb.internal("modc%d" % l, [128, 48, 2]) for l in range(2)]
    grow_d = [kb.internal("grow%d" % l, [2, 2048]) for l in range(2)]
    uTd = [kb.internal("uTd%d" % l, [128, 8, TTK], BF16) for l in range(2)]
    wb16 = []
    for l in range(2):
        d = {}
        for nm, shape in (("w_pc", [D, 768]), ("w_out", [D, D]), ("fwo", [DFF, D]), ("fwi", [D, 2 * DFF])):
            d[nm] = (kb.internal("L%d_%s_b16" % (l, nm), shape, BF16), "L%d_%s_b16" % (l, nm))
        wb16.append(d)

    def convert_weights(l):
        save = dict(kb.pfx)
        if True:
            kb.pfx = {"inst": "", "layer": "L%d_" % l, "global": ""}
            for nm, shape, rows in (("w_pc", [D, 768], 256), ("w_out", [D, D], 256), ("fwo", [DFF, D], 256), ("fwi", [D, 2 * DFF], 128)):
                src = kb.inp(nm, shape, lvl="layer")
                dst, key = wb16[l][nm]
                for r0 in range(0, shape[0], rows):
                    kb.dma("pool", dst[r0:r0 + rows, :], src[r0:r0 + rows, :], w=[key], bg=True)
        kb.pfx = save

    convert_weights(0)
    for l in range(2):
        emit_mod(kb, l, modc_d[l], grow_d[l], "mod%d" % l)
    for l in range(2):
        hsrc, hkey = (hfull, "hfull") if l == 0 else (hful1, "hful1")
        for p in range(2):
            emit_M(kb, l, p, hsrc, hkey, yint[l], "yint%d" % l, modc_d[l], "mod%d" % l, uTd[l],
                   hook=None)
        if l == 0:
            convert_weights(1)
        for g in range(2):
            emit_F(kb, l, g, hsrc, hkey, yint[l], "yint%d" % l, hful1 if l == 0 else hout, "hful1" if l == 0 else "hout", l == 1,
                   wb16[l], modc_d[l], grow_d[l], "mod%d" % l, uTd[l])
    kb.S.emit()
    return kb

import numpy as np, ml_dtypes
BF = ml_dtypes.bfloat16
S_, CT, D, DFF = 4096, 256, 1024, 2816
POOL_WINDOWS = (2, 4, 8, 16)

def col_layout(v, n):
    return np.ascontiguousarray(v.reshape(n, 128).T)

def band_mats(base, p0, L, w):
    B = np.zeros((256, 128), np.float32)
    for t in range(128):
        p = p0 + t
        lo = min(max(p - w // 2, 0), L - 1); hi = min(max(p + (w - 1 - w // 2), 0), L - 1)
        inv = np.float32(1.0) / np.float32(hi - lo + 1)
        for q in range(lo, hi + 1):
            B[q - base, t] += inv
        B[p - base, t] -= 1.0
    return B

def prep_F(inp, l, h, hc, yma_lat, yma_ctx, b, g):
    layer0 = (l == 0)
    start, cstart = g * 2048, g * 128
    m = {}
    rows = [h[b, start:start + 2048]]
    if layer0: rows.append(hc[b, cstart:cstart + 128])
    m["hrows"] = np.ascontiguousarray(np.concatenate(rows, 0))
    def padded(src, s0, n, L):
        out = np.zeros((n, D), np.float32); msk = np.zeros((n,), np.float32)
        lo, hi = max(s0, 0), min(s0 + n, L)
        out[lo - s0:hi - s0] = src[lo:hi]; msk[lo - s0:hi - s0] = 1.0
        return out, msk
    hp, mk = padded(h[b], start - 16, 2176, S_)
    hps, mks = [hp], [mk]
    if layer0:
        cp, cm = padded(hc[b], cstart - 16, 256, CT)
        hps.append(cp); mks.append(cm)
    m["hpad"] = np.ascontiguousarray(np.concatenate(hps, 0)); m["maskp"] = np.concatenate(mks, 0)
    ys = [yma_lat[b, start:start + 2048]]
    if layer0: ys.append(yma_ctx[b, cstart:cstart + 128])
    m["yTma"] = np.ascontiguousarray(np.concatenate(ys, 0).T).astype(BF)
    m["ccol"] = np.ascontiguousarray(np.stack([col_layout(inp["c"][b], 8), col_layout(inp["c_ctx"], 8)], axis=-1))
    m["modw"] = inp["mod_w"][l]
    m["modb_col"] = col_layout(inp["mod_b"][l], 48)
    m["modb_row"] = inp["mod_b"][l]
    m["n1g"] = col_layout(inp["norm1_g"][l], 8); m["n2g"] = col_layout(inp["norm2_g"][l], 8)
    m["w_pc"] = np.ascontiguousarray(inp["w_in"][l][:, 1808:2576])
    bands = np.zeros((128, 32, 128), np.float32)
    cls_list = [(0, S_, start), (1, S_, start), (15, S_, start)]
    for cls in range(4):
        for gi, w in enumerate(POOL_WINDOWS):
            if cls < 3:
                ot = cls_list[cls][0]
                base = start - 16 + ot * 128; p0 = start + ot * 128; L = S_
            else:
                base = cstart - 16; p0 = cstart; L = CT
            B = band_mats(base, p0, L, w)
            bands[:, (cls * 4 + gi) * 2 + 0, :] = B[0:128]; bands[:, (cls * 4 + gi) * 2 + 1, :] = B[128:256]
    m["bands"] = bands
    pw2 = np.zeros((64, 4, 128), np.float32)
    for gi in range(4):
        pw2[:, gi, (gi % 2) * 64:(gi % 2) * 64 + 64] = inp["pool_w"][l][gi]
    m["poolw2"] = pw2
    m["pscale"] = col_layout(inp["pool_scale"][l], 2)
    m["dww"] = np.ascontiguousarray(inp["conv_dw_w"][l].T.reshape(2, 128, 31).transpose(1, 0, 2))
    m["dwb"] = col_layout(inp["conv_dw_b"][l], 2); m["lng"] = col_layout(inp["conv_ln_g"][l], 2); m["lnb"] = col_layout(inp["conv_ln_b"][l], 2)
    m["pww"] = inp["conv_pw_w"][l]; m["w_out"] = inp["w_out"][l]; m["fwi"] = inp["ffn_w_in"][l]; m["fwo"] = inp["ffn_w_out"][l]
    m["identf"] = np.eye(128, dtype=np.float32)
    sel2 = np.zeros((2, 2, 128), np.float32); sel2[0, 0] = 1; sel2[1, 1] = 1
    m["sel2"] = sel2
    return {k: np.ascontiguousarray(v) for k, v in m.items()}

NTK = 34
def rope_tables():
    TT = NTK * 128
    cos = np.ones((64, TT), np.float32); sin = np.zeros((64, TT), np.float32)
    tok = np.arange(S_)
    rows = (tok // 64).astype(np.float32); cols = (tok % 64).astype(np.float32)
    freqs = (np.float32(10000.0) ** (-np.arange(8, dtype=np.float32) / np.float32(8))).astype(np.float32)
    for m in range(2):
        for d in range(32):
            axis, half, f = d // 16, (d % 16) // 8, d % 8
            ang = (rows if axis == 0 else cols) * freqs[f]
            cos[m * 32 + d, 256:] = np.cos(ang.astype(np.float32))
            sin[m * 32 + d, 256:] = np.sin(ang.astype(np.float32)) * (-1.0 if half == 0 else 1.0)
    return cos, sin

_CONST = {}
def consts_M():
    if _CONST: return _CONST
    c = {}
    c["identf"] = np.eye(128, dtype=np.float32)
    r = np.arange(128)
    tri = np.zeros((128, 2, 128), np.float32)
    tri[:, 0, :] = (r[:, None] <= r[None, :]); tri[:, 1, :] = (r[:, None] >= r[None, :])
    c["tri"] = tri; c["mask"] = tri.copy()
    bo = np.zeros((128, 128), np.float32)
    for j in range(4):
        bo[32 * j:32 * j + 32, 32 * j:32 * j + 32] = 1
    c["bones"] = bo
    mc = np.zeros((128, 4), np.float32)
    for j in range(4):
        mc[32 * j:32 * j + 32, j] = 1
    c["mcol"] = mc
    pm = np.zeros((128, 128), np.float32)
    for d in range(128):
        dd = d % 32
        partner = d + 8 if (dd % 16) < 8 else d - 8
        pm[partner, d] = 1.0
    c["perm"] = pm
    pB = np.zeros((68, 68), np.float32)
    for pos in range(34):
        cc = 1 - pos if pos < 2 else 35 - pos
        for h in range(2):
            pB[cc * 2 + h, pos * 2 + h] = 1.0
    c["permB"] = pB
    sg = np.zeros((2, 64), np.float32); sg[0] = -1; sg[1] = 1
    c["sgn"] = sg
    sz = np.zeros((128, 64), np.float32); sz[64, :] = 1.0
    c["selZ"] = sz
    ct_, st_ = rope_tables()
    c["cost"], c["sint"] = np.concatenate([ct_, ct_], 0), np.concatenate([st_, st_], 0)
    _CONST.update(c)
    return _CONST

def prep_M(inp, l, h, hc, b, g):
    m = dict(consts_M())
    m["hfull"] = np.ascontiguousarray(np.concatenate([hc[b], h[b]], 0))
    m["ccol"] = np.ascontiguousarray(np.stack([col_layout(inp["c"][b], 8), col_layout(inp["c_ctx"], 8)], axis=-1))
    m["modw"] = np.ascontiguousarray(inp["mod_w"][l][:, :2048])
    m["modb_col"] = col_layout(inp["mod_b"][l][:2048], 16)
    m["n1g"] = col_layout(inp["norm1_g"][l], 8)
    W = inp["w_in"][l]
    hp = slice(g * 128, (g + 1) * 128)
    mq, mk, mv, mo = W[:, 0:256][:, hp], W[:, 256:512][:, hp], W[:, 512:768][:, hp], W[:, 768:1024][:, hp]
    gates = W[:, 1024:1040].reshape(D, 2, 2, 4)[:, :, :, 2 * g:2 * g + 2].reshape(D, 8)
    aq, ak, av = W[:, 1040:1296][:, hp], W[:, 1296:1552][:, hp], W[:, 1552:1808][:, hp]
    m["w_fm"] = np.concatenate([mq, mk, aq, ak], 1)
    m["w_tm"] = np.concatenate([mk, mv, mo, av, gates, np.zeros((D, 8), np.float32)], 1)
    m["gate_b"] = inp["mlstm_gate_b"][l][:, :, 2 * g:2 * g + 2].reshape(8)
    m["mng"] = inp["mlstm_norm_g"][l]
    m["qkng"] = np.stack([np.tile(inp["attn_q_norm_g"][l], 2), np.tile(inp["attn_k_norm_g"][l], 2)], 1)
    m["lamqk"] = np.stack([inp["lambda_q"][l], inp["lambda_k"][l]], 1)
    m["sublng"] = inp["attn_subln_g"][l].reshape(64, 1)
    return {k: np.ascontiguousarray(v, dtype=np.float32) for k, v in m.items()}


def prep_fused(inp, b):
    x, ctx = inp["x"], inp["ctx"]
    m = dict(consts_M())
    m["hfull"] = np.concatenate([ctx[b], x[b]], 0)
    m["ccol"] = np.stack([col_layout(inp["c"][b], 8), col_layout(inp["c_ctx"], 8)], axis=-1)
    sel2 = np.zeros((2, 2, 128), np.float32); sel2[0, 0] = 1; sel2[1, 1] = 1
    m["sel2"] = sel2
    for g in range(2):
        start, cstart = g * 2048, g * 128
        mk = np.zeros((19 * 128,), np.float32)
        pos = start - 16 + np.arange(2176); mk[:2176] = ((pos >= 0) & (pos < S_))
        cpos = cstart - 16 + np.arange(256); mk[2176:] = ((cpos >= 0) & (cpos < CT))
        m["maskp%d" % g] = mk
        bands = np.zeros((128, 32, 128), np.float32)
        for cls in range(4):
            for gi, w in enumerate(POOL_WINDOWS):
                if cls < 3:
                    ot = (0, 1, 15)[cls]
                    base = start - 16 + ot * 128; p0 = start + ot * 128; L = S_
                else:
                    base = cstart - 16; p0 = cstart; L = CT
                B = band_mats(base, p0, L, w)
                bands[:, (cls * 4 + gi) * 2 + 0, :] = B[0:128]; bands[:, (cls * 4 + gi) * 2 + 1, :] = B[128:256]
        m["bands%d" % g] = bands
    for l in range(2):
        L_ = "L%d_" % l
        m[L_ + "modw"] = inp["mod_w"][l]
        m[L_ + "modb_col"] = col_layout(inp["mod_b"][l], 48)
        m[L_ + "modb_row"] = inp["mod_b"][l]
        m[L_ + "n1g"] = col_layout(inp["norm1_g"][l], 8); m[L_ + "n2g"] = col_layout(inp["norm2_g"][l], 8)
        W = inp["w_in"][l]
        m[L_ + "w_pc"] = W[:, 1808:2576]
        pw2 = np.zeros((64, 4, 128), np.float32)
        for gi in range(4):
            pw2[:, gi, (gi % 2) * 64:(gi % 2) * 64 + 64] = inp["pool_w"][l][gi]
        m[L_ + "poolw2"] = pw2
        m[L_ + "pscale"] = col_layout(inp["pool_scale"][l], 2)
        m[L_ + "dww"] = inp["conv_dw_w"][l].T.reshape(2, 128, 31).transpose(1, 0, 2)
        m[L_ + "dwb"] = col_layout(inp["conv_dw_b"][l], 2); m[L_ + "lng"] = col_layout(inp["conv_ln_g"][l], 2)
        m[L_ + "lnb"] = col_layout(inp["conv_ln_b"][l], 2)
        m[L_ + "pww"] = inp["conv_pw_w"][l]; m[L_ + "w_out"] = inp["w_out"][l]
        m[L_ + "fwi"] = inp["ffn_w_in"][l]; m[L_ + "fwo"] = inp["ffn_w_out"][l]
        m[L_ + "mng"] = inp["mlstm_norm_g"][l]
        m[L_ + "qkng"] = np.stack([np.tile(inp["attn_q_norm_g"][l], 4), np.tile(inp["attn_k_norm_g"][l], 4)], 1)
        m[L_ + "lamqk"] = np.stack([inp["lambda_q"][l], inp["lambda_k"][l]], 1)
        m[L_ + "sublng"] = inp["attn_subln_g"][l].reshape(64, 1)
        for p in range(2):
            P_ = "M%d%d_" % (l, p)
            hp = slice(p * 128, (p + 1) * 128)
            mq, mk_, mv, mo = W[:, 0:256][:, hp], W[:, 256:512][:, hp], W[:, 512:768][:, hp], W[:, 768:1024][:, hp]
            gates = W[:, 1024:1040].reshape(D, 2, 2, 4)[:, :, :, 2 * p:2 * p + 2].reshape(D, 8)
            aq, ak, av = W[:, 1040:1296][:, hp], W[:, 1296:1552][:, hp], W[:, 1552:1808][:, hp]
            m[P_ + "w_fm"] = np.concatenate([mq, mk_, aq, ak], 1)
            m[P_ + "w_tm"] = np.concatenate([mk_, mv, mo, av, gates, np.zeros((D, 8), np.float32)], 1)
            m[P_ + "gate_b"] = inp["mlstm_gate_b"][l][:, :, 2 * p:2 * p + 2].reshape(8)
    return {k: np.ascontiguousarray(v, dtype=np.float32) for k, v in m.items()}

_PROG = {}


def kernel(**inputs):
    inp = {k: np.asarray(v) for k, v in inputs.items()}
    if "kb" not in _PROG:
        _PROG["kb"] = build_fused()
    kb = _PROG["kb"]
    maps = [prep_fused(inp, b) for b in range(4)]
    res = run_bass_kernel_spmd(kb.nc, maps, core_ids=list(range(4)))
    out = np.stack([np.asarray(res.results[b]["hout"]) for b in range(4)], 0)
    return out.astype(np.float32)
```

```python
import math
import numpy as np
import concourse.bass as bass
import concourse.mybir as mybir
from concourse.bass_utils import run_bass_kernel_spmd

F32 = mybir.dt.float32
BF16 = mybir.dt.bfloat16
ALU = mybir.AluOpType
AF = mybir.ActivationFunctionType
AX = mybir.AxisListType

COMPUTE = ("pe", "act", "dve", "pool")
N_DMA_SEMS = 12
N_BG_SEMS = 24


class _Op:
    __slots__ = ("eng", "fn", "deps", "is_dma", "eidx", "marked", "sem", "semval", "prev_on_sem", "bg")

    def __init__(self, eng, fn, deps, is_dma):
        self.eng, self.fn, self.deps, self.is_dma = eng, fn, deps, is_dma
        self.marked = False
        self.bg = False
        self.sem = None
        self.semval = 0
        self.prev_on_sem = None


class Sched:
    def __init__(self, nc, same_engine_sync=True):
        self.nc = nc
        self.ops = []
        self.last_w = {}
        self.readers = {}
        self.same_engine_sync = same_engine_sync
        self.engs = {"pe": nc.tensor, "act": nc.scalar, "dve": nc.vector, "pool": nc.gpsimd, "sp": nc.sync}
        self.ecount = {e: 0 for e in self.engs}
        self._bar_tile = nc.alloc_sbuf_tensor("bar_tile", [128, 8], F32).ap()

    def barrier(self):
        self.op("pool", lambda e: e.memset(self._bar_tile, 0.0), r=(), w=["__phase__"])

    def op(self, eng, fn, r=(), w=(), dma=False):
        r = list(r) + ["__phase__"]
        deps = set()
        for k in r:
            if k in self.last_w:
                deps.add(self.last_w[k])
        for k in w:
            if k in self.last_w:
                deps.add(self.last_w[k])
            deps.update(self.readers.get(k, ()))
        idx = len(self.ops)
        o = _Op(eng, fn, deps, dma)
        o.eidx = self.ecount[eng]
        self.ecount[eng] += 1
        self.ops.append(o)
        for k in r:
            self.readers.setdefault(k, []).append(idx)
        for k in w:
            self.last_w[k] = idx
            self.readers[k] = []
        return idx

    def dma(self, eng, out, in_, r=(), w=(), bg=False):
        i = self.op(eng, lambda e: e.dma_start(out=out, in_=in_), r, w, dma=True)
        self.ops[i].bg = bg
        return i

    def emit(self, final_wait_eng="sp"):
        nc = self.nc
        import os
        if os.environ.get("TRUNC"):
            self.ops = self.ops[:int(os.environ["TRUNC"])]
        ops = self.ops
        known = {e: {f: -1 for f in COMPUTE} for e in self.engs}
        dma_known = {e: {} for e in self.engs}
        waits = [None] * len(ops)
        dma_pool = {e: [] for e in self.engs}
        dma_rr = {e: 0 for e in self.engs}
        dma_last = {}
        for i, o in enumerate(ops):
            wl = []
            for d in sorted(o.deps):
                p = ops[d]
                if p.is_dma:
                    if dma_known[o.eng].get(p.sem, 0) < p.semval:
                        dma_known[o.eng][p.sem] = p.semval
                        wl.append(("dma", d))
                else:
                    if p.eng == o.eng and (o.eng == 'pe' or not self.same_engine_sync) and not o.is_dma:
                        continue
                    if known[o.eng][p.eng] < p.eidx:
                        known[o.eng][p.eng] = p.eidx
                        wl.append(("eng", d))
            best = {}
            for kind, d in wl:
                if kind == "eng":
                    e = ops[d].eng
                    if e not in best or ops[best[e]].eidx < ops[d].eidx:
                        best[e] = d
            bestd = {}
            for kind, d in wl:
                if kind == "dma":
                    sl = ops[d].sem
                    if sl not in bestd or ops[bestd[sl]].semval < ops[d].semval:
                        bestd[sl] = d
            wl = [("dma", d) for d in bestd.values()] + [("eng", d) for d in best.values()]
            if o.is_dma:
                qn = o.eng + ("_bg" if o.bg else "")
                dma_rr.setdefault(qn, 0)
                slot = dma_rr[qn] % (N_BG_SEMS if o.bg else N_DMA_SEMS)
                dma_rr[qn] += 1
                o.sem = (qn, slot)
                prev = dma_last.get(o.sem)
                o.prev_on_sem = prev
                o.semval = (ops[prev].semval if prev is not None else 0) + 16
                dma_last[o.sem] = i
                if prev is not None and dma_known[o.eng].get(o.sem, 0) < ops[prev].semval:
                    wl.append(("dma", prev))
                    dma_known[o.eng][o.sem] = ops[prev].semval
            for kind, d in wl:
                if kind == "eng":
                    ops[d].marked = True
            waits[i] = wl
        cnt = {e: 0 for e in COMPUTE}
        ordinal = {}
        for i, o in enumerate(ops):
            if not o.is_dma and o.marked:
                cnt[o.eng] += 1
                ordinal[i] = cnt[o.eng]
        esem = {e: nc.alloc_semaphore("sem_" + e) for e in COMPUTE}
        dsem = {}
        for o in ops:
            if o.is_dma and o.sem not in dsem:
                dsem[o.sem] = nc.alloc_semaphore("dsem_%s_%d" % o.sem)
        for i, o in enumerate(ops):
            eng = self.engs[o.eng]
            for kind, d in waits[i]:
                p = ops[d]
                if kind == "dma":
                    eng.wait_ge(dsem[p.sem], p.semval)
                else:
                    eng.wait_ge(esem[p.eng], ordinal[d])
            ins = o.fn(eng)
            if o.is_dma:
                ins.then_inc(dsem[o.sem], 16)
            elif o.marked:
                ins.then_inc(esem[o.eng], 1)
        fe = self.engs[final_wait_eng]
        for key, i in dma_last.items():
            fe.wait_ge(dsem[key], ops[i].semval)
        self.stats = {e: self.ecount[e] for e in self.engs}
        self.stats["marked"] = dict(cnt)


EPS_RMS = 1e-6
EPS_LN = 1e-5
D = 1024
DFF = 2816


class KB:
    def __init__(self):
        self.nc = bass.Bass("TRN2", target_bir_lowering=False)
        self.S = Sched(self.nc)
        self.rr = {}
        self.dram = {}
        self.pfx = {"inst": "", "layer": "", "global": ""}
        self.uid = 0
        self.scope = None
        self.psums = {}

    def inp(self, name, shape, dt=F32, lvl="inst"):
        full = self.pfx[lvl] + name
        if full not in self.dram:
            self.dram[full] = self.nc.dram_tensor(full, list(shape), dt, kind="ExternalInput").ap()
        return self.dram[full]

    def outp(self, name, shape, dt=F32):
        if name not in self.dram:
            self.dram[name] = self.nc.dram_tensor(name, list(shape), dt, kind="ExternalOutput").ap()
        return self.dram[name]

    def internal(self, name, shape, dt=F32):
        if name not in self.dram:
            self.dram[name] = self.nc.dram_tensor(name, list(shape), dt).ap()
        return self.dram[name]

    def scope_begin(self, inst, layer):
        from contextlib import ExitStack
        self.scope = ExitStack()
        self.pfx = {"inst": inst + "_", "layer": layer + "_", "global": ""}
        self.rr = {}
        self._bufs = {}

    def scope_end(self):
        self.S.barrier()
        self.scope.close()
        self.scope = None

    def sb(self, name, shape, dt=F32):
        self.uid += 1
        return self.scope.enter_context(self.nc.sbuf_tensor("s%d_%s" % (self.uid, name), list(shape), dt)).ap()

    def phase_begin(self):
        from contextlib import ExitStack
        self.stack = ExitStack()

    def sbt(self, name, shape, dt=F32):
        self.uid += 1
        return self.stack.enter_context(self.nc.sbuf_tensor("s%d_%s" % (self.uid, name), list(shape), dt)).ap()

    def phase_end(self):
        self.S.barrier()
        self.stack.close()
        self.stack = None

    def ps(self, name, shape, dt=F32):
        if name not in self.psums:
            self.psums[name] = self.nc.alloc_psum_tensor("p_" + name, list(shape), dt).ap()
        return self.psums[name]

    def rot(self, name, n):
        i = self.rr.get(name, 0)
        self.rr[name] = i + 1
        return i % n

    def dma(self, eng, out, in_, r=(), w=(), bg=False):
        self.S.dma(eng, out, in_, r=r, w=w, bg=bg)

    def mm(self, out, lhsT, rhs, start, stop, r, w):
        self.S.op("pe", lambda e: e.matmul(out, lhsT=lhsT, rhs=rhs, start=start, stop=stop), r=r, w=w)

    def tr(self, out, in_, ident, r, w):
        self.S.op("pe", lambda e: e.transpose(out, in_, ident), r=r, w=w)

    def act(self, out, in_, func, r, w, bias=0.0, scale=1.0, accum=None, eng="act"):
        if accum is None:
            self.S.op(eng, lambda e: e.activation(out=out, in_=in_, func=func, bias=bias, scale=scale), r=r, w=w)
        else:
            self.S.op(eng, lambda e: e.activation(out=out, in_=in_, func=func, bias=bias, scale=scale, accum_out=accum), r=r, w=w)

    def tt(self, eng, out, in0, in1, op, r, w):
        self.S.op(eng, lambda e: e.tensor_tensor(out=out, in0=in0, in1=in1, op=op), r=r, w=w)

    def ts(self, eng, out, in0, s1, s2, op0, op1, r, w):
        if s2 is None:
            self.S.op(eng, lambda e: e.tensor_scalar(out=out, in0=in0, scalar1=s1, scalar2=None, op0=op0), r=r, w=w)
        else:
            self.S.op(eng, lambda e: e.tensor_scalar(out=out, in0=in0, scalar1=s1, scalar2=s2, op0=op0, op1=op1), r=r, w=w)

    def stt(self, eng, out, in0, scalar, in1, op0, op1, r, w):
        self.S.op(eng, lambda e: e.scalar_tensor_tensor(out=out, in0=in0, scalar=scalar, in1=in1, op0=op0, op1=op1), r=r, w=w)

    def copy(self, eng, out, in_, r, w):
        if eng == "act":
            self.S.op("act", lambda e: e.copy(out=out, in_=in_), r=r, w=w)
        else:
            self.S.op(eng, lambda e: e.tensor_copy(out=out, in_=in_), r=r, w=w)

    def recip(self, out, in_, r, w):
        self.S.op("dve", lambda e: e.reciprocal(out=out, in_=in_), r=r, w=w)

    def memset(self, eng, ap, val, w):
        self.S.op(eng, lambda e: e.memset(ap, val), w=w)


def mod_columns(kb, modw, modb_col, ccol, col0, nchunks, name, stg, psm):
    sc = kb.sbt(name + "_sc", [128, 8, 2])
    kb.dma("sp", sc, ccol, w=[name + "_sc"])
    kb.act(sc, sc, AF.Silu, r=[name + "_sc"], w=[name + "_sc"])
    mb = kb.sbt(name + "_mb", [128, nchunks])
    kb.dma("sp", mb, modb_col, w=[name + "_mb"])
    for jg in range(nchunks // 4):
        b = kb.rot("mstg", 2)
        kb.dma("sp", stg[b].rearrange("p (kt n) -> p kt n", kt=8),
               modw[:, col0 + jg * 512:col0 + (jg + 1) * 512].rearrange("(kt p) n -> p kt n", p=128), w=["mstg%d" % b])
        for jj in range(4):
            j = jg * 4 + jj
            for kt in range(8):
                kb.mm(psm[:, j, :], stg[b][:, kt * 512 + jj * 128:kt * 512 + (jj + 1) * 128], sc[:, kt, :], kt == 0, kt == 7,
                      r=["mstg%d" % b, name + "_sc"], w=["psm"])
    modc = kb.sbt(name + "_modc", [128, nchunks, 2])
    for r_ in range(2):
        kb.tt("dve", modc[:, :, r_], psm[:, :, r_], mb, ALU.add, r=["psm", name + "_mb"], w=[name + "_modc"])
    return modc


def emit_mod(kb, l, modc_d, grow_d, mkey):
    kb.scope_begin("MOD%d" % l, "L%d" % l)
    ccol = kb.inp("ccol", [128, 8, 2], lvl="global")
    modw = kb.inp("modw", [D, 6 * D], lvl="layer")
    modb_col = kb.inp("modb_col", [128, 48], lvl="layer")
    modb_row = kb.inp("modb_row", [6 * D], lvl="layer")
    pA = kb.ps("pA", [128, 512]); pB = kb.ps("pB", [128, 512]); pC = kb.ps("pC", [128, 512])
    pD = kb.ps("pD", [128, 512]); pE = kb.ps("pE", [128, 512])
    kb.phase_begin()
    mstg = [kb.sbt("mstg%d" % i, [128, 4096]) for i in range(2)]
    psm = pE.rearrange("p (c r) -> p c r", r=2)[:, 0:16, :]
    for i in range(3):
        mc = mod_columns(kb, modw, modb_col[:, 16 * i:16 * (i + 1)], ccol, 2048 * i, 16, "mc%d" % i, mstg, psm)
        kb.dma("sp", modc_d[:, 16 * i:16 * (i + 1), :], mc, r=["mc%d_modc" % i], w=[mkey])
    scr = kb.sbt("scr", [128, 8, 2])
    kb.dma("sp", scr, ccol, w=["scr"])
    kb.act(scr, scr, AF.Silu, r=["scr"], w=["scr"])
    grow = kb.sbt("grow", [2, 2048])
    gb2 = kb.sbt("gb2", [2, 2048])
    kb.dma("sp", gb2[:, 0:1024], modb_row[2 * D:3 * D].partition_broadcast(2), w=["gb2"])
    kb.dma("sp", gb2[:, 1024:2048], modb_row[5 * D:6 * D].partition_broadcast(2), w=["gb2"])
    psr_l = [pA, pB, pC, pD]; psr_k = ["pA", "pB", "pC", "pD"]
    for kt in range(8):
        b = kt % 2
        kb.dma("sp", mstg[b][:, 0:1024], modw[kt * 128:(kt + 1) * 128, 2 * D:3 * D], w=["mstg%d" % b])
        kb.dma("sp", mstg[b][:, 1024:2048], modw[kt * 128:(kt + 1) * 128, 5 * D:6 * D], w=["mstg%d" % b])
        for j in range(4):
            kb.mm(psr_l[j][0:2, :], scr[:, kt, :], mstg[b][:, j * 512:(j + 1) * 512], kt == 0, kt == 7,
                  r=["mstg%d" % b, "scr"], w=[psr_k[j]])
    for j in range(4):
        kb.tt("dve", grow[:, j * 512:(j + 1) * 512], psr_l[j][0:2, :], gb2[:, j * 512:(j + 1) * 512], ALU.add,
              r=[psr_k[j], "gb2"], w=["grow"])
    kb.dma("sp", grow_d, grow, r=["grow"], w=[mkey])
    kb.phase_end()
    kb.scope_end()


def norm_transpose_tile(kb, x_dram_rows, uT_out, SC, SH, r_idx, ident_b, pst, tag, okey):
    i = kb.rot(tag + "_x", 3)
    xk = "%s_x%d" % (tag, i)
    x = kb._bufs[xk]
    kb.dma("sp", x, x_dram_rows, w=[xk])
    return norm_transpose_sb(kb, x, xk, uT_out, SC, SH, r_idx, ident_b, pst, tag, okey)


NXH = 4


def norm_part(kb, x, xk, tag):
    j = kb.rot(tag + "_s", NXH)
    junk, ss, xh = kb._bufs[tag + "_junk"], kb._bufs["%s_ss%d" % (tag, j)], kb._bufs["%s_xh%d" % (tag, j)]
    ssk, xhk = "%s_ss%d" % (tag, j), "%s_xh%d" % (tag, j)
    kb.memset("pool", ss, 0.0, w=[ssk])
    kb.act(junk, x, AF.Square, r=[xk, ssk], w=[tag + "_junk", ssk], accum=ss)
    kb.act(ss, ss, AF.Sqrt, r=[ssk], w=[ssk], bias=EPS_RMS, scale=1.0 / D)
    kb.recip(ss, ss, r=[ssk], w=[ssk])
    kb.ts("dve", xh, x, ss, None, ALU.mult, None, r=[xk, ssk], w=[xhk])
    return xh, xhk


def tr_part(kb, xh, xhk, uT_out, SC, SH, r_idx, ident_b, pst, okey):
    pk, psT = pst
    for kt in range(8):
        kb.tr(psT[:, kt * 128:(kt + 1) * 128], xh[:, kt * 128:(kt + 1) * 128], ident_b, r=[xhk, "ident_b"], w=[pk])
    for kt in range(8):
        eng = "act" if kt % 2 == 0 else "dve"
        if eng == "act":
            kb.act(uT_out[:, kt, :], psT[:, kt * 128:(kt + 1) * 128], AF.Identity, r=[pk, "SCSH"], w=[okey],
                   bias=SH[:, kt, r_idx:r_idx + 1], scale=SC[:, kt, r_idx:r_idx + 1])
        else:
            kb.ts("dve", uT_out[:, kt, :], psT[:, kt * 128:(kt + 1) * 128], SC[:, kt, r_idx:r_idx + 1],
                  SH[:, kt, r_idx:r_idx + 1], ALU.mult, ALU.add, r=[pk, "SCSH"], w=[okey])


def norm_transpose_sb(kb, x, xk, uT_out, SC, SH, r_idx, ident_b, pst, tag, okey):
    xh, xhk = norm_part(kb, x, xk, tag)
    tr_part(kb, xh, xhk, uT_out, SC, SH, r_idx, ident_b, pst, okey)


def alloc_norm_xbufs(kb, tag):
    for i in range(3):
        kb._bufs["%s_x%d" % (tag, i)] = kb.sbt("%s_x%d" % (tag, i), [128, D])


def alloc_norm_bufs(kb, tag):
    kb._bufs[tag + "_junk"] = kb.sb(tag + "_junk", [128, D], BF16)
    for j in range(NXH):
        kb._bufs["%s_ss%d" % (tag, j)] = kb.sb("%s_ss%d" % (tag, j), [128, 1])
        kb._bufs["%s_xh%d" % (tag, j)] = kb.sb("%s_xh%d" % (tag, j), [128, D], BF16)


def load_rows(kb, x, xk, hsrc, hkey, row0, nrows_total):
    lo, hi = max(row0, 0), min(row0 + 128, nrows_total)
    if lo > row0 or hi < row0 + 128:
        kb.memset("pool", x, 0.0, w=[xk])
    if hi > lo:
        kb.dma("sp", x[lo - row0:hi - row0, :], hsrc[lo:hi, :], r=[hkey], w=[xk])


def emit_F(kb, l, g, hsrc, hkey, yint, ykey, hdst, hdkey, dst_is_final, wb16, modc_d, grow_d, mkey, uTd):
    layer0 = (l == 0)
    kb.scope_begin("F%d%d" % (l, g), "L%d" % l)
    nc = kb.nc
    NT = 17 if layer0 else 16
    NP = 19 if layer0 else 17
    TP = NP * 128
    TT = NT * 128
    lat0 = 256 + g * 2048
    ctx0 = g * 128
    maskp = kb.inp("maskp%d" % g, [19 * 128], lvl="global")
    ccol = kb.inp("ccol", [128, 8, 2], lvl="global")
    modw = kb.inp("modw", [D, 6 * D], lvl="layer")
    modb_col = kb.inp("modb_col", [128, 48], lvl="layer")
    modb_row = kb.inp("modb_row", [6 * D], lvl="layer")
    n1g = kb.inp("n1g", [128, 8], lvl="layer")
    n2g = kb.inp("n2g", [128, 8], lvl="layer")
    w_pc, wpc_key = wb16["w_pc"]
    bands = kb.inp("bands%d" % g, [128, 32, 128], lvl="global")
    poolw2 = kb.inp("poolw2", [64, 4, 128], lvl="layer")
    pscale = kb.inp("pscale", [128, 2], lvl="layer")
    dww = kb.inp("dww", [128, 2, 31], lvl="layer")
    dwb = kb.inp("dwb", [128, 2], lvl="layer")
    lng = kb.inp("lng", [128, 2], lvl="layer")
    lnb = kb.inp("lnb", [128, 2], lvl="layer")
    pww = kb.inp("pww", [256, 256], lvl="layer")
    w_out, wout_key = wb16["w_out"]
    fwi, fwi_key = wb16["fwi"]
    fwo, fwo_key = wb16["fwo"]
    identf_d = kb.inp("identf", [128, 128], lvl="global")
    sel2_d = kb.inp("sel2", [2, 2, 128], lvl="global")

    identf = kb.sb("identf", [128, 128])
    kb.dma("sp", identf, identf_d, w=["identf"])
    ident_b = kb.sb("ident_b", [128, 128], BF16)
    kb.copy("dve", ident_b, identf, r=["identf"], w=["ident_b"])
    onesf = kb.sb("onesf", [128, 128])
    kb.memset("pool", onesf, 1.0, w=["onesf"])
    sel2 = kb.sb("sel2", [2, 2, 128])
    kb.dma("sp", sel2, sel2_d, w=["sel2"])
    SC1 = kb.sb("SC1", [128, 8, 2]); SH1 = kb.sb("SH1", [128, 8, 2])
    SC2 = kb.sb("SC2", [128, 8, 2]); SH2 = kb.sb("SH2", [128, 8, 2])
    Gbc = [kb.sb("Gbc%d" % r_, [128, 2048]) for r_ in range(2 if layer0 else 1)]
    yT = kb.sb("yT", [128, 8, TT], BF16)
    alloc_norm_bufs(kb, "n1")
    pT = kb.ps("pT", [128, 1024], BF16)
    pA = kb.ps("pA", [128, 512]); pB = kb.ps("pB", [128, 512]); pC = kb.ps("pC", [128, 512])
    pD = kb.ps("pD", [128, 512]); pE = kb.ps("pE", [128, 512])
    for kt in range(4):
        kb.dma("sp", yT[:, kt, 0:2048], yint[kt * 128:(kt + 1) * 128, lat0:lat0 + 2048], r=[ykey], w=["yT%d" % kt])
        if layer0:
            kb.dma("sp", yT[:, kt, 2048:2176], yint[kt * 128:(kt + 1) * 128, ctx0:ctx0 + 128], r=[ykey], w=["yT%d" % kt])

    kb.phase_begin()
    mc1 = kb.sbt("mc1", [128, 16, 2]); mc2 = kb.sbt("mc2", [128, 16, 2])
    kb.dma("sp", mc1, modc_d[:, 0:16, :], r=[mkey], w=["mc1_modc"])
    kb.dma("sp", mc2, modc_d[:, 24:40, :], r=[mkey], w=["mc2_modc"])
    g1n = kb.sbt("g1n", [128, 8]); g2n = kb.sbt("g2n", [128, 8])
    kb.dma("sp", g1n, n1g, w=["g1n"]); kb.dma("sp", g2n, n2g, w=["g2n"])
    for (mc, gn, SC, SH, nm) in ((mc1, g1n, SC1, SH1, "1"), (mc2, g2n, SC2, SH2, "2")):
        for r_ in range(2):
            kb.stt("dve", SC[:, :, r_], mc[:, 8:16, r_], 1.0, gn, ALU.add, ALU.mult,
                   r=["mc%s_modc" % nm, "g%sn" % nm], w=["SCSH"])
            kb.copy("dve", SH[:, :, r_], mc[:, 0:8, r_], r=["mc%s_modc" % nm], w=["SCSH"])
    grow = kb.sbt("grow", [2, 2048])
    kb.dma("sp", grow, grow_d, r=[mkey], w=["grow"])
    psr_l = [pA, pB, pC, pD]; psr_k = ["pA", "pB", "pC", "pD"]
    for r_ in range(len(Gbc)):
        for j in range(4):
            kb.mm(psr_l[j], sel2[:, r_, :], grow[:, j * 512:(j + 1) * 512], True, True, r=["sel2", "grow"], w=[psr_k[j]])
        for j in range(4):
            kb.copy("act", Gbc[r_][:, j * 512:(j + 1) * 512], psr_l[j], r=[psr_k[j]], w=["Gbc%d" % r_])

    wpc_sb = kb.sbt("wpc", [128, 8, 768], BF16)
    kb.dma("sp", wpc_sb, w_pc.rearrange("(kt p) n -> p kt n", p=128), r=[wpc_key], w=["wpc"])
    pww_sb = kb.sbt("pww", [128, 2, 256], BF16)
    kb.dma("pool", pww_sb, pww.rearrange("(ct p) n -> p ct n", p=128), w=["pww"])
    poolw_sb = kb.sbt("poolw", [64, 4, 128], BF16)
    kb.dma("pool", poolw_sb, poolw2, w=["poolw"])
    bands_sb = kb.sbt("bands", [128, 32, 128])
    kb.dma("sp", bands_sb, bands, w=["bands"])
    small = {}
    for nm, src, shp in (("pscale", pscale, [128, 2]), ("dww", dww, [128, 2, 31]), ("dwb", dwb, [128, 2]),
                         ("lng", lng, [128, 2]), ("lnb", lnb, [128, 2])):
        small[nm] = kb.sbt(nm, shp)
        kb.dma("sp", small[nm], src, w=[nm])
    maskb = kb.sbt("maskb", [128, TP], BF16)
    kb.dma("pool", maskb, maskp[0:TP].partition_broadcast(128), w=["maskb"])
    Dg = kb.sbt("Dg", [128, 2, 31, 128], BF16)
    for ct in range(2):
        for k in range(31):
            if k % 3 == 0:
                kb.act(Dg[:, ct, k, :], identf, AF.Copy, r=["identf", "dww"], w=["Dg"], scale=small["dww"][:, ct, k:k + 1])
            else:
                kb.ts("pool" if k % 3 == 1 else "dve", Dg[:, ct, k, :], identf, small["dww"][:, ct, k:k + 1], None, ALU.mult, None,
                      r=["identf", "dww"], w=["Dg"])
    alloc_norm_xbufs(kb, "n1")
    u_p = kb.sbt("u_p", [128, NP, 256])
    GLU = kb.sbt("GLU", [128, 2, TP], BF16)
    uTt = [kb.sbt("uTt%d" % i, [128, 8, 128], BF16) for i in range(4)]
    sg = kb.sbt("sg", [128, 2, 128]); gl = kb.sbt("gl", [128, 2, 128])
    for i in range(NP):
        r_idx = 0 if i < 17 else 1
        b = i % 4
        if i < 17:
            t0_, tbase, tlen = g * 2048 - 16 + i * 128, 256, 4096
        else:
            t0_, tbase, tlen = g * 128 - 16 + (i - 17) * 128, 0, 256
        lo_, hi_ = max(t0_, 0), min(t0_ + 128, tlen)
        if lo_ > t0_ or hi_ < t0_ + 128:
            kb.memset("pool", uTt[b], 0.0, w=["uTt%d" % b])
        if hi_ > lo_:
            kb.dma("sp", uTt[b][:, :, lo_ - t0_:hi_ - t0_], uTd[:, :, tbase + lo_:tbase + hi_], r=["uTd%d" % l], w=["uTt%d" % b])
        pcv = pA.rearrange("p (c t) -> p c t", c=4)
        for cti in range(4):
            for kt in range(8):
                kb.mm(pcv[:, cti, :], wpc_sb[:, kt, 256 + cti * 128:256 + (cti + 1) * 128], uTt[b][:, kt, :], kt == 0, kt == 7,
                      r=["wpc", "uTt%d" % b], w=["pA"])
        for kt in range(8):
            kb.mm(pB[:, 0:256], uTt[b][:, kt, :], wpc_sb[:, kt, 0:256], kt == 0, kt == 7, r=["wpc", "uTt%d" % b], w=["pB"])
        kb.act(sg, pcv[:, 2:4, :], AF.Sigmoid, r=["pA"], w=["sg"])
        kb.tt("dve", gl, pcv[:, 0:2, :], sg, ALU.mult, r=["pA", "sg"], w=["gl"])
        kb.tt("pool", GLU[:, :, i * 128:(i + 1) * 128], gl, maskb[:, i * 128:(i + 1) * 128].unsqueeze(1).to_broadcast([128, 2, 128]),
              ALU.mult, r=["gl", "maskb"], w=["GLU%d" % i])
        kb.copy("act", u_p[:, i, :], pB[:, 0:256], r=["pB"], w=["u_p%d" % i])

    pooled = kb.sbt("pooled", [64, 4, 128], BF16)
    blocks = [(j, j, (0 if j == 0 else (2 if j == 15 else 1))) for j in range(16)]
    if layer0:
        blocks.append((16, 17, 3))
    for (ot, pt, cls) in blocks:
        ppv = pC.rearrange("p (g t) -> p g t", g=4)
        for gi in range(4):
            for half in range(2):
                kb.mm(ppv[0:64, gi, :], u_p[:, pt + half, gi * 64:(gi + 1) * 64], bands_sb[:, (cls * 4 + gi) * 2 + half, :],
                      half == 0, half == 1, r=["u_p%d" % (pt + half), "bands"], w=["pC"])
        kb.copy("act", pooled, ppv[0:64, :, :], r=["pC"], w=["pooled"])
        pqv = pD.rearrange("p (g t) -> p g t", g=4)
        for pr in range(2):
            for gl_ in range(2):
                gi = pr * 2 + gl_
                kb.mm(pqv[:, pr, :], poolw_sb[:, gi, :], pooled[:, gi, :], gl_ == 0, gl_ == 1, r=["poolw", "pooled"], w=["pD"])
        for pr in range(2):
            kb.ts("dve", yT[:, 4 + pr, ot * 128:(ot + 1) * 128], pqv[:, pr, :], small["pscale"][:, pr:pr + 1], None, ALU.mult, None,
                  r=["pD", "pscale"], w=["yT%d" % (4 + pr)])

    xc = kb.sbt("xc", [128, 2, 512]); sq = kb.sbt("sq", [128, 2, 512])
    mean = kb.sbt("mean", [128, 512]); var = kb.sbt("var", [128, 512]); t1 = kb.sbt("t1", [128, 2, 512])
    zs = kb.sbt("zs", [128, 2, 512], BF16)
    cblocks = [(jb * 512, jb * 512 + 1, 512) for jb in range(4)]
    if layer0:
        cblocks.append((16 * 128, 17 * 128 + 1, 128))
    for (oc, gc, n) in cblocks:
        for ct in range(2):
            pc_ = pA if ct == 0 else pB
            for k in range(31):
                kb.mm(pc_[:, 0:n], Dg[:, ct, k, :], GLU[:, ct, gc + k:gc + k + n], k == 0, k == 30,
                      r=["Dg"] + ["GLU%d" % i for i in range((gc + k) // 128, (gc + k + n - 1) // 128 + 1)], w=["pA" if ct == 0 else "pB"])
            kb.act(xc[:, ct, 0:n], pc_[:, 0:n], AF.Identity, r=["pA" if ct == 0 else "pB", "dwb"], w=["xc"], bias=small["dwb"][:, ct:ct + 1])
        kb.tt("pool", sq[:, :, 0:n], xc[:, :, 0:n], xc[:, :, 0:n], ALU.mult, r=["xc"], w=["sq"])
        for ct in range(2):
            kb.mm(pC[:, 0:n], onesf, xc[:, ct, 0:n], ct == 0, ct == 1, r=["onesf", "xc"], w=["pC"])
        for ct in range(2):
            kb.mm(pD[:, 0:n], onesf, sq[:, ct, 0:n], ct == 0, ct == 1, r=["onesf", "sq"], w=["pD"])
        kb.act(mean[:, 0:n], pC[:, 0:n], AF.Copy, r=["pC"], w=["mean"], scale=1.0 / 256)
        kb.tt("dve", var[:, 0:n], mean[:, 0:n], mean[:, 0:n], ALU.mult, r=["mean"], w=["var"])
        kb.stt("dve", var[:, 0:n], pD[:, 0:n], 1.0 / 256, var[:, 0:n], ALU.mult, ALU.subtract, r=["pD", "var"], w=["var"])
        kb.act(var[:, 0:n], var[:, 0:n], AF.Sqrt, r=["var"], w=["var"], bias=EPS_LN)
        kb.recip(var[:, 0:n], var[:, 0:n], r=["var"], w=["var"])
        for ct in range(2):
            kb.tt("dve", t1[:, ct, 0:n], xc[:, ct, 0:n], mean[:, 0:n], ALU.subtract, r=["xc", "mean"], w=["t1"])
            kb.tt("pool", t1[:, ct, 0:n], t1[:, ct, 0:n], var[:, 0:n], ALU.mult, r=["t1", "var"], w=["t1"])
            kb.act(zs[:, ct, 0:n], t1[:, ct, 0:n], AF.Silu, r=["t1", "lng", "lnb"], w=["zs"],
                   bias=small["lnb"][:, ct:ct + 1], scale=small["lng"][:, ct:ct + 1])
        for dt_ in range(2):
            for ct in range(2):
                kb.mm(pE[:, 0:n], pww_sb[:, ct, dt_ * 128:(dt_ + 1) * 128], zs[:, ct, 0:n], ct == 0, ct == 1, r=["pww", "zs"], w=["pE"])
            kb.copy("act", yT[:, 6 + dt_, oc:oc + n], pE[:, 0:n], r=["pE"], w=["yT%d" % (6 + dt_)])
    kb.phase_end()

    kb.phase_begin()
    wout_sb = kb.sbt("wout", [128, 8, D], BF16)
    for kt in range(8):
        kb.dma("sp", wout_sb[:, kt, :], w_out[kt * 128:(kt + 1) * 128, :], r=[wout_key], w=["wout"])
    fwo_sb = kb.sbt("fwo", [128, 22, D], BF16)
    for fc in range(22):
        kb.dma("act", fwo_sb[:, fc, :], fwo[fc * 128:(fc + 1) * 128, :], r=[fwo_key], w=["fwo"])
    h0 = [kb.sbt("h0_%d" % i, [128, D]) for i in range(1)]
    h1 = kb.sbt("h1", [128, 4, D])
    u2T = kb.sbt("u2T", [128, 8, 512], BF16)
    actT = kb.sbt("actT", [128, 22, 512], BF16)
    wch = [kb.sbt("wch%d" % i, [128, 8, 256], BF16) for i in range(5)]
    sgf = [kb.sbt("sgf%d" % i, [128, 512]) for i in range(2)]
    tmpo = [kb.sbt("tmpo%d" % i, [128, 512]) for i in range(2)]
    obuf = [kb.sbt("obuf%d" % i, [128, D]) for i in range(1)]
    fwi_v = fwi.rearrange("(kt p) n -> p kt n", p=128)
    ykeys = ["yT%d" % k for k in range(8)]
    groups = [list(range(g * 4, min(g * 4 + 4, NT))) for g in range((NT + 3) // 4)]
    for grp in groups:
        n = len(grp) * 128
        for li, ti in enumerate(grp):
            r_idx = 1 if ti == 16 else 0
            g_bc = Gbc[r_idx]
            hb = kb.rot("h0", 1)
            hr0 = (lat0 + ti * 128) if ti < 16 else ctx0
            kb.dma("sp", h0[hb], hsrc[hr0:hr0 + 128, :], r=[hkey], w=["h0_%d" % hb])
            for nh in range(2):
                pp = pA if nh == 0 else pB
                pk = "pA" if nh == 0 else "pB"
                for kt in range(8):
                    kb.mm(pp, yT[:, kt, ti * 128:(ti + 1) * 128], wout_sb[:, kt, nh * 512:(nh + 1) * 512], kt == 0, kt == 7,
                          r=["wout", ykeys[kt]], w=[pk])
                tb = kb.rot("tmpo", 2)
                kb.tt("dve", tmpo[tb], pp, g_bc[:, nh * 512:(nh + 1) * 512], ALU.mult, r=[pk, "Gbc%d" % r_idx], w=["tmpo%d" % tb])
                kb.tt("pool", h1[:, li, nh * 512:(nh + 1) * 512], tmpo[tb], h0[hb][:, nh * 512:(nh + 1) * 512], ALU.add,
                      r=["tmpo%d" % tb, "h0_%d" % hb], w=["h1_%d" % li])
            norm_transpose_sb(kb, h1[:, li, :], "h1_%d" % li, u2T[:, :, li * 128:(li + 1) * 128], SC2, SH2, r_idx, ident_b,
                              ("pT", pT), "n1", "u2T")
        pF = kb.ps("pF", [128, 512]); pG = kb.ps("pG", [128, 512])
        for fc in range(22):
            wb = kb.rot("wch", 5)
            (pg_, pgk), (pu_, puk) = ((pC, "pC"), (pD, "pD")) if fc % 2 == 0 else ((pF, "pF"), (pG, "pG"))
            kb.dma("sp", wch[wb][:, :, 0:128], fwi_v[:, :, fc * 128:(fc + 1) * 128], r=[fwi_key], w=["wch%d" % wb])
            kb.dma("sp", wch[wb][:, :, 128:256], fwi_v[:, :, DFF + fc * 128:DFF + (fc + 1) * 128], r=[fwi_key], w=["wch%d" % wb])
            for kt in range(8):
                kb.mm(pg_[:, 0:n], wch[wb][:, kt, 0:128], u2T[:, kt, 0:n], kt == 0, kt == 7, r=["wch%d" % wb, "u2T"], w=[pgk])
            for kt in range(8):
                kb.mm(pu_[:, 0:n], wch[wb][:, kt, 128:256], u2T[:, kt, 0:n], kt == 0, kt == 7, r=["wch%d" % wb, "u2T"], w=[puk])
            sb_ = kb.rot("sgf", 2)
            kb.act(sgf[sb_][:, 0:n], pg_[:, 0:n], AF.Silu, r=[pgk], w=["sgf%d" % sb_])
            kb.tt("dve", actT[:, fc, 0:n], sgf[sb_][:, 0:n], pu_[:, 0:n], ALU.mult, r=["sgf%d" % sb_, puk], w=["actT"])
        for li, ti in enumerate(grp):
            r_idx = 1 if ti == 16 else 0
            g_bc = Gbc[r_idx]
            ob = kb.rot("obuf", 1)
            for nh in range(2):
                pp = pA if nh == 0 else pB
                pk = "pA" if nh == 0 else "pB"
                for fc in range(22):
                    kb.mm(pp, actT[:, fc, li * 128:(li + 1) * 128], fwo_sb[:, fc, nh * 512:(nh + 1) * 512], fc == 0, fc == 21,
                          r=["actT", "fwo"], w=[pk])
                tb = kb.rot("tmpo", 2)
                kb.tt("dve", tmpo[tb], pp, g_bc[:, 1024 + nh * 512:1024 + (nh + 1) * 512], ALU.mult,
                      r=[pk, "Gbc%d" % r_idx], w=["tmpo%d" % tb])
                kb.tt("pool", obuf[ob][:, nh * 512:(nh + 1) * 512], tmpo[tb], h1[:, li, nh * 512:(nh + 1) * 512], ALU.add,
                      r=["tmpo%d" % tb, "h1_%d" % li], w=["obuf%d" % ob])
            if dst_is_final:
                od0 = g * 2048 + ti * 128
            else:
                od0 = (lat0 + ti * 128) if ti < 16 else ctx0
            kb.dma("sp", hdst[od0:od0 + 128, :], obuf[ob], r=["obuf%d" % ob], w=[hdkey])
    kb.phase_end()
    kb.scope_end()

import math

NTK = 34
TTK = NTK * 128


def c_of_pos(d, pos):
    if d == 0:
        return pos
    return 1 - pos if pos < 2 else 35 - pos


def emit_M(kb, l, p, hsrc, hkey, yint, ykey, modc_d, mkey, uTd, hook=None):
    layer0 = (l == 0)
    lam_init = 0.8 - 0.6 * math.exp(-0.3 * l)
    stop_after = None
    kb.scope_begin("M%d%d" % (l, p), "L%d" % l)
    nc = kb.nc
    ccol = kb.inp("ccol", [128, 8, 2], lvl="global")
    modw = kb.inp("modw", [D, 6 * D], lvl="layer")
    modb_col = kb.inp("modb_col", [128, 48], lvl="layer")[:, 0:16]
    n1g = kb.inp("n1g", [128, 8], lvl="layer")
    w_fm = kb.inp("w_fm", [D, 512])
    w_tm = kb.inp("w_tm", [D, 528])
    gate_b = kb.inp("gate_b", [8])
    mng = kb.inp("mng", [64], lvl="layer")
    qkng = kb.inp("qkng", [128, 2], lvl="layer")
    lamqk = kb.inp("lamqk", [2, 2, 32], lvl="layer")
    sublng = kb.inp("sublng", [64, 1], lvl="layer")
    identf_d = kb.inp("identf", [128, 128], lvl="global")
    tri_d = kb.inp("tri", [128, 2, 128], lvl="global")
    mask_d = kb.inp("mask", [128, 2, 128], lvl="global")
    bones_d = kb.inp("bones", [128, 128], lvl="global")
    perm_d = kb.inp("perm", [128, 128], lvl="global")
    mcol_d = kb.inp("mcol", [128, 4], lvl="global")
    permB_d = kb.inp("permB", [68, 68], lvl="global")
    sgn_d = kb.inp("sgn", [2, 64], lvl="global")
    cos_d = kb.inp("cost", [128, TTK], lvl="global")
    sin_d = kb.inp("sint", [128, TTK], lvl="global")
    selZ_d0 = kb.inp("selZ", [128, 64], lvl="global")

    identf = kb.sb("identf", [128, 128]); kb.dma("sp", identf, identf_d, w=["identf"])
    ident_b = kb.sb("ident_b", [128, 128], BF16); kb.copy("dve", ident_b, identf, r=["identf"], w=["ident_b"])
    onesf = kb.sb("onesf", [128, 128]); kb.memset("pool", onesf, 1.0, w=["onesf"])
    tri = kb.sb("tri", [128, 2, 128]); kb.dma("sp", tri, tri_d, w=["tri"])
    mask = kb.sb("mask", [128, 2, 128]); kb.dma("sp", mask, mask_d, w=["mask"])
    bones = kb.sb("bones", [128, 128], BF16); kb.dma("pool", bones, bones_d, w=["bones"])
    perm = kb.sb("perm", [128, 128], BF16); kb.dma("pool", perm, perm_d, w=["perm"])
    mcol = kb.sb("mcol", [128, 4]); kb.dma("sp", mcol, mcol_d, w=["mcol"])
    permB = kb.sb("permB", [68, 68]); kb.dma("sp", permB, permB_d, w=["permB"])
    sgn = kb.sb("sgn", [2, 64]); kb.dma("sp", sgn, sgn_d, w=["sgn"])
    qkg = kb.sb("qkg", [128, 2]); kb.dma("sp", qkg, qkng, w=["qkg"])
    sublg = kb.sb("sublg", [64, 1]); kb.dma("sp", sublg, sublng, w=["sublg"])
    gbb = kb.sb("gbb", [128, 8]); kb.dma("sp", gbb, gate_b.partition_broadcast(128), w=["gbb"])
    mngb = kb.sb("mngb", [128, 64]); kb.dma("sp", mngb, mng.partition_broadcast(128), w=["mngb"])
    SC1 = kb.sb("SC1", [128, 8, 2]); SH1 = kb.sb("SH1", [128, 8, 2])
    negLam = kb.sb("negLam", [64, 1])
    qT = kb.sb("qT", [64, 2, TTK], BF16); kT = kb.sb("kT", [64, 2, TTK], BF16)
    k_tm = kb.sb("k_tm", [128, NTK, 128], BF16)
    vext = kb.sb("vext", [128, NTK, 2, 72], BF16)
    Vx = kb.sb("Vx", [128, NTK, 2, 128], BF16)
    sigo = kb.sb("sigo", [128, NTK, 128], BF16)
    G = kb.sb("G", [128, NTK, 8])
    QT = kb.sb("QT", [128, TTK], BF16)
    KTm = kb.sb("KTm", [128, 4, TTK], BF16)
    selZ_d = selZ_d0
    selZ = kb.sb("selZ", [128, 64]); kb.dma("sp", selZ, selZ_d, w=["selZ"])
    kb.memset("pool", vext, 1.0, w=["vext"]); kb.memset("pool", Vx, 0.0, w=["Vx"])
    kb.memset("pool", Vx[:, :, :, 64:65], 1.0, w=["Vx"])
    alloc_norm_bufs(kb, "n1")
    pT = kb.ps("pT", [128, 1024], BF16)
    pA = kb.ps("pA", [128, 512]); pB = kb.ps("pB", [128, 512]); pC = kb.ps("pC", [128, 512])
    pD = kb.ps("pD", [128, 512]); pE = kb.ps("pE", [128, 512]); pF = kb.ps("pF", [128, 512]); pG = kb.ps("pG", [128, 512])

    kb.phase_begin()
    mc1 = kb.sbt("mc1", [128, 16, 2])
    kb.dma("sp", mc1, modc_d[:, 0:16, :], r=[mkey], w=["mc1_modc"])
    g1n = kb.sbt("g1n", [128, 8]); kb.dma("sp", g1n, n1g, w=["g1n"])
    for r_ in range(2):
        kb.stt("dve", SC1[:, :, r_], mc1[:, 8:16, r_], 1.0, g1n, ALU.add, ALU.mult, r=["mc1_modc", "g1n"], w=["SCSH"])
        kb.copy("dve", SH1[:, :, r_], mc1[:, 0:8, r_], r=["mc1_modc"], w=["SCSH"])
    lqk = kb.sbt("lqk", [2, 2, 32]); kb.dma("sp", lqk, lamqk, w=["lqk"])
    lpr = kb.sbt("lpr", [2, 2, 32]); lsm = kb.sbt("lsm", [2, 2])
    for j_ in range(2):
        kb.tt("dve", lpr[:, j_, :], lqk[:, 0, :], lqk[:, 1, :], ALU.mult, r=["lqk"], w=["lpr"])
    kb.S.op("dve", lambda e: e.tensor_reduce(out=lsm, in_=lpr, axis=AX.X, op=ALU.add), r=["lpr"], w=["lsm"])
    kb.act(lsm, lsm, AF.Exp, r=["lsm"], w=["lsm"])
    kb.mm(pF[0:64, 0:2], sgn, lsm, True, True, r=["sgn", "lsm"], w=["pF"])
    kb.ts("dve", negLam, pF[0:64, 0:1], -float(lam_init), None, ALU.add, None, r=["pF"], w=["negLam"])

    wfm = kb.sbt("wfm", [128, 8, 512], BF16)
    kb.dma("pool", wfm, w_fm.rearrange("(kt p) n -> p kt n", p=128), w=["wfm"])
    wtm = kb.sbt("wtm", [128, 8, 528], BF16)
    kb.dma("pool", wtm, w_tm.rearrange("(kt p) n -> p kt n", p=128), w=["wtm"])
    if hook is not None:
        hook()
    alloc_norm_xbufs(kb, "n1")
    uTb = [kb.sbt("uTb%d" % i, [128, 8, 512], BF16) for i in range(2)]
    cosb = [kb.sbt("cosb%d" % i, [128, 512]) for i in range(2)]
    sinb = [kb.sbt("sinb%d" % i, [128, 512]) for i in range(2)]
    Ab = kb.sbt("Ab", [128, 512], BF16); sqb = kb.sbt("sqb", [128, 512], BF16)
    rs = kb.sbt("rs", [128, 512]); t1 = kb.sbt("t1", [128, 512]); t2 = kb.sbt("t2", [128, 512])
    blocks = [(0, 256)] + [(256 + 512 * i, 512) for i in range(8)]
    Abq = [kb.sbt("Abq%d" % i, [128, 512], BF16) for i in range(2)]
    sqq = [kb.sbt("sqq%d" % i, [128, 512], BF16) for i in range(2)]

    def stage1(bi):
        c0, n = blocks[bi]
        ub = bi % 2
        uk = "uTb%d" % ub
        kb.dma("sp", cosb[ub][:, 0:n], cos_d[:, c0:c0 + n], w=["cosb%d" % ub])
        kb.dma("sp", sinb[ub][:, 0:n], sin_d[:, c0:c0 + n], w=["sinb%d" % ub])
        if p == 0:
            nq_ = []
            for li in range(n // 128):
                c = c0 // 128 + li
                xi = kb.rot("n1_x", 3)
                xk_ = "n1_x%d" % xi
                kb.dma("sp", kb._bufs[xk_], hsrc[c * 128:(c + 1) * 128, :], r=[hkey], w=[xk_])
                nq_.append(norm_part(kb, kb._bufs[xk_], xk_, "n1"))
            for li in range(n // 128):
                c = c0 // 128 + li
                tr_part(kb, nq_[li][0], nq_[li][1], uTb[ub][:, :, li * 128:(li + 1) * 128], SC1, SH1,
                        1 if c < 2 else 0, ident_b, ("pT", pT), uk)
            kb.dma("sp", uTd[:, :, c0:c0 + n], uTb[ub][:, :, 0:n], r=[uk], w=["uTd%d" % l])
        else:
            kb.dma("sp", uTb[ub][:, :, 0:n], uTd[:, :, c0:c0 + n], r=["uTd%d" % l], w=[uk])
        for which, dst, dk in ((0, qT, "qT"), (1, kT, "kT")):
            for h in range(2):
                pp, pk = (pA, "pA") if h == 0 else (pB, "pB")
                cs = which * 128 + h * 64
                for kt in range(8):
                    kb.mm(pp[0:64, 0:n], wfm[:, kt, cs:cs + 64], uTb[ub][:, kt, 0:n], kt == 0, kt == 7, r=["wfm", uk], w=[pk])
                kb.copy("act" if h == 0 else "dve", dst[:, h, c0:c0 + n], pp[0:64, 0:n], r=[pk], w=[dk])
        for which in (0, 1):
            cs = 256 + which * 128
            for kt in range(8):
                kb.mm(pC[:, 0:n], wfm[:, kt, cs:cs + 128], uTb[ub][:, kt, 0:n], kt == 0, kt == 7, r=["wfm", uk], w=["pC"])
            kb.act(Abq[which][:, 0:n], pC[:, 0:n], AF.Copy, r=["pC", "qkg"], w=["Abq%d_%d" % (which, bi % 2)], scale=qkg[:, which:which + 1])
            kb.act(sqq[which][:, 0:n], pC[:, 0:n], AF.Square, r=["pC"], w=["sqq%d_%d" % (which, bi % 2)])
        for li in range(n // 128):
            c = c0 // 128 + li
            lt = uTb[ub][:, :, li * 128:(li + 1) * 128]
            for kt in range(8):
                kb.mm(pF, lt[:, kt, :], wtm[:, kt, 0:512], kt == 0, kt == 7, r=["wtm", uk], w=["pF"])
            for kt in range(8):
                kb.mm(pG[:, 0:16], lt[:, kt, :], wtm[:, kt, 512:528], kt == 0, kt == 7, r=["wtm", uk], w=["pG"])
            kb.copy("act", k_tm[:, c, :], pF[:, 0:128], r=["pF"], w=["k_tm"])
            kb.copy("act", vext[:, c, :, 0:64], pF[:, 128:256].rearrange("p (h v) -> p h v", h=2), r=["pF"], w=["vext"])
            kb.act(sigo[:, c, :], pF[:, 256:384], AF.Sigmoid, r=["pF"], w=["sigo"])
            kb.copy("act", Vx[:, c, :, 0:64], pF[:, 384:512].rearrange("p (h v) -> p h v", h=2), r=["pF"], w=["Vx"])
            kb.tt("dve", G[:, c, :], pG[:, 0:8], gbb, ALU.add, r=["pG", "gbb"], w=["G"])

    def stage2(bi):
        c0, n = blocks[bi]
        ub = bi % 2
        for which in (0, 1):
            Ab, sqb = Abq[which], sqq[which]
            kb.mm(pD[:, 0:n], bones, sqb[:, 0:n], True, True, r=["bones", "sqq%d_%d" % (which, bi % 2)], w=["pD"])
            kb.mm(pE[:, 0:n], perm, Ab[:, 0:n], True, True, r=["perm", "Abq%d_%d" % (which, bi % 2)], w=["pE"])
            kb.act(rs[:, 0:n], pD[:, 0:n], AF.Sqrt, r=["pD"], w=["rs"], bias=EPS_RMS, scale=1.0 / 32)
            kb.recip(rs[:, 0:n], rs[:, 0:n], r=["rs"], w=["rs"])
            kb.tt("dve", t1[:, 0:n], Ab[:, 0:n], cosb[ub][:, 0:n], ALU.mult, r=["Abq%d_%d" % (which, bi % 2), "cosb%d" % ub], w=["t1"])
            kb.tt("dve", t2[:, 0:n], pE[:, 0:n], sinb[ub][:, 0:n], ALU.mult, r=["pE", "sinb%d" % ub], w=["t2"])
            kb.tt("pool", t1[:, 0:n], t1[:, 0:n], t2[:, 0:n], ALU.add, r=["t1", "t2"], w=["t1"])
            if which == 0:
                kb.tt("pool", QT[:, c0:c0 + n], t1[:, 0:n], rs[:, 0:n], ALU.mult, r=["t1", "rs"], w=["QT"])
            else:
                kb.tt("pool", t2[:, 0:n], t1[:, 0:n], rs[:, 0:n], ALU.mult, r=["t1", "rs", "t2"], w=["t2"])
                for j in range(4):
                    kb.ts("dve" if j % 2 == 0 else "pool", KTm[:, j, c0:c0 + n], t2[:, 0:n], mcol[:, j:j + 1], None, ALU.mult, None,
                          r=["t2", "mcol"], w=["KT"])

    Abq2 = [[Abq[0], Abq[1]], [kb.sbt("Abq2", [128, 512], BF16), kb.sbt("Abq3", [128, 512], BF16)]]
    sqq2 = [[sqq[0], sqq[1]], [kb.sbt("sqq2", [128, 512], BF16), kb.sbt("sqq3", [128, 512], BF16)]]
    for bi in range(len(blocks)):
        Abq[0], Abq[1] = Abq2[bi % 2]
        sqq[0], sqq[1] = sqq2[bi % 2]
        _names = {0: ("Abq0", "Abq1", "sqq0", "sqq1"), 1: ("Abq2", "Abq3", "sqq2", "sqq3")}
        stage1(bi)
        if bi >= 1:
            Abq[0], Abq[1] = Abq2[(bi - 1) % 2]
            sqq[0], sqq[1] = sqq2[(bi - 1) % 2]
            stage2(bi - 1)
    Abq[0], Abq[1] = Abq2[(len(blocks) - 1) % 2]
    sqq[0], sqq[1] = sqq2[(len(blocks) - 1) % 2]
    stage2(len(blocks) - 1)
    kb.phase_end()

    kb.phase_begin()
    Gv = G.rearrange("p c (d g h) -> p c d g h", d=2, g=2, h=2)
    LFn = kb.sbt("LFn", [128, 2, NTK, 2]); bn = kb.sbt("bn", [128, 2, NTK, 2]); A_all = kb.sbt("A_all", [128, 2, NTK, 2])
    tmpg = kb.sbt("tmpg", [128, NTK, 2])
    AmaxCol = kb.sbt("AmaxCol", [68, 1]); BtotCol = kb.sbt("BtotCol", [68, 1]); rep = kb.sbt("rep", [68, 2, 128])
    AB = kb.sbt("AB", [128, 2, NTK, 2]); Mn = kb.sbt("Mn", [128, NTK, 2]); Mprev = kb.sbt("Mprev", [128, NTK, 2])
    Mu = kb.sbt("Mu", [128, 2, NTK, 2]); Ee = kb.sbt("Ee", [128, 2, NTK, 2]); Ecol = kb.sbt("Ecol", [128, 2, NTK])
    Wt = kb.sbt("Wt", [128, 2, NTK, 2]); NBt = kb.sbt("NBt", [128, 2, NTK, 2])
    Wk = kb.sbt("Wk", [128, 2, NTK, 2]); LOW = kb.sbt("LOW", [128, 2, NTK, 2])
    for d in range(2):
        kb.act(tmpg, Gv[:, :, d, 1, :], AF.Exp, r=["G"], w=["tmpg"], scale=-1.0)
        kb.act(LFn[:, d], tmpg, AF.Ln, r=["tmpg"], w=["LFn"], bias=1.0)
        pX = pA[:, 0:68]
        kb.mm(pX, tri[:, d, :], LFn[:, d].rearrange("p c h -> p (c h)"), True, True, r=["tri", "LFn"], w=["pA"])
        kb.copy("dve", bn[:, d].rearrange("p c h -> p (c h)"), pX, r=["pA"], w=["bn"])
        kb.tt("dve", A_all[:, d], Gv[:, :, d, 0, :], pX.rearrange("p (c h) -> p c h", h=2), ALU.add, r=["G", "pA"], w=["A_all"])
        kb.tr(pB[0:68, 0:128], A_all[:, d].rearrange("p c h -> p (c h)"), identf, r=["A_all", "identf"], w=["pB"])
        kb.S.op("dve", lambda e: e.tensor_reduce(out=AmaxCol, in_=pB[0:68, 0:128], axis=AX.X, op=ALU.max), r=["pB"], w=["AmaxCol"])
        kb.tr(pC[0:68, 0:128], bn[:, d].rearrange("p c h -> p (c h)"), identf, r=["bn", "identf"], w=["pC"])
        col = 127 if d == 0 else 0
        kb.copy("dve", BtotCol, pC[0:68, col:col + 1], r=["pC"], w=["BtotCol"])
        kb.ts("dve", rep[:, 0, :], onesf[0:68, :], AmaxCol, None, ALU.mult, None, r=["onesf", "AmaxCol"], w=["rep"])
        kb.ts("dve", rep[:, 1, :], onesf[0:68, :], BtotCol, -1.0, ALU.mult, ALU.mult, r=["onesf", "BtotCol"], w=["rep"])
        pm = identf[0:68, 0:68] if d == 0 else permB
        kb.mm(pD[:, 0:68], rep[:, 0, :], pm, True, True, r=["rep", "permB", "identf"], w=["pD"])
        kb.mm(pD[:, 68:136], rep[:, 1, :], pm, True, True, r=["rep", "permB", "identf"], w=["pD"])
        kb.copy("act", AB.rearrange("p a c h -> p (a c h)"), pD[:, 0:136], r=["pD"], w=["AB"])
        for h in range(2):
            kb.S.op("dve", lambda e, h=h: e.tensor_tensor_scan(out=Mn[:, :, h], data0=AB[:, 0, :, h], data1=AB[:, 1, :, h],
                                                           initial=0.0, op0=ALU.max, op1=ALU.add), r=["AB"], w=["Mn"])
        kb.memset("pool", Mprev[:, 0, :], 0.0, w=["Mprev"])
        kb.copy("dve", Mprev[:, 1:NTK, :], Mn[:, 0:NTK - 1, :], r=["Mn"], w=["Mprev"])
        kb.tt("dve", Mu[:, d], Mprev, AB[:, 0], ALU.max, r=["Mprev", "AB"], w=["Mu"])
        kb.tt("dve", Ee[:, d], Mprev, Mu[:, d], ALU.subtract, r=["Mprev", "Mu"], w=["Ee"])
        kb.act(Ee[:, d], Ee[:, d], AF.Exp, r=["Ee"], w=["Ee"])
        kb.copy("dve", Ecol[0:64, d, :], Ee[0:64, d, :, 0], r=["Ee"], w=["Ecol"])
        kb.copy("dve", Ecol[64:128, d, :], Ee[64:128, d, :, 1], r=["Ee"], w=["Ecol"])
        if d == 0:
            kb.tt("dve", Wt[:, 0], A_all[:, 0], Mu[:, 0], ALU.subtract, r=["A_all", "Mu"], w=["Wt"])
            kb.tt("dve", NBt[:, 0], bn[:, 0], Mu[:, 0], ALU.subtract, r=["bn", "Mu"], w=["NBt"])
        else:
            for pos in range(NTK):
                c = c_of_pos(1, pos)
                kb.tt("dve", Wt[:, 1, c, :], A_all[:, 1, c, :], Mu[:, 1, pos, :], ALU.subtract, r=["A_all", "Mu"], w=["Wt"])
                kb.tt("pool", NBt[:, 1, c, :], bn[:, 1, c, :], Mu[:, 1, pos, :], ALU.subtract, r=["bn", "Mu"], w=["NBt"])
        kb.act(Wk[:, d], Wt[:, d], AF.Exp, r=["Wt"], w=["Wk"], bias=math.log(0.125))
        kb.act(LOW[:, d], NBt[:, d], AF.Exp, r=["NBt"], w=["LOW"])

    hdir = [kb.sbt("hdir%d" % d, [128, NTK, 2, 64]) for d in range(2)]
    ysm = kb.sbt("ysm", [128, TTK], BF16)
    St = [kb.sbt("St%d" % d, [64, 2, 65]) for d in range(2)]
    Stb = [kb.sbt("Stb%d" % d, [64, 2, 72], BF16) for d in range(2)]
    Stmp = [kb.sbt("Stmp%d" % d, [64, 2, 65]) for d in range(2)]
    Pb = [[kb.sbt("Pb%d_%d" % (d, i), [128, 2, 128], BF16) for i in range(2)] for d in range(2)]
    kpb = [[kb.sbt("kpb%d_%d" % (d, i), [128, 2, 64], BF16) for i in range(2)] for d in range(2)]
    absd = [kb.sbt("absd%d" % d, [128, 2]) for d in range(2)]
    rc = [kb.sbt("rc%d" % d, [128, 2]) for d in range(2)]
    pS_ = [pA, pD]; pN_ = [pB, pE]; pU_ = [pC, pF]
    pSk = ["pA", "pD"]; pNk = ["pB", "pE"]; pUk = ["pC", "pF"]
    for d in range(2):
        kb.memset("pool", St[d], 0.0, w=["St%d" % d])
        kb.memset("pool", Stb[d], 0.0, w=["Stb%d" % d])
    for pos in range(NTK):
        for d in range(2):
            c = c_of_pos(d, pos)
            cs = slice(c * 128, (c + 1) * 128)
            i2 = kb.rot("Pb%d" % d, 2)
            P, kp = Pb[d][i2], kpb[d][i2]
            Pk, kpk = "Pb%d_%d" % (d, i2), "kpb%d_%d" % (d, i2)
            pSv = pS_[d].rearrange("p (h t) -> p h t", h=4)[:, 0:2, :]
            pNv = pN_[d].rearrange("p (h t) -> p h t", h=4)[:, 0:2, :]
            pUv = pU_[d].rearrange("p (h t) -> p h t", h=4)[:, 0:2, :]
            for h in range(2):
                kb.mm(pSv[:, h, :], kT[:, h, cs], qT[:, h, cs], True, True, r=["kT", "qT"], w=[pSk[d]])
            for h in range(2):
                kb.stt("dve", P[:, h, :], pSv[:, h, :], Wk[:, d, c, h:h + 1], mask[:, d, :], ALU.mult, ALU.mult,
                       r=[pSk[d], "Wk", "mask"], w=[Pk])
                kb.act(kp[:, h, :], k_tm[:, c, 64 * h:64 * h + 64], AF.Copy, r=["k_tm", "Wk"], w=[kpk], scale=Wk[:, d, c, h:h + 1])
            for h in range(2):
                kb.mm(pNv[:, h, 0:65], P[:, h, :], vext[:, c, h, 0:65], True, False, r=[Pk, "vext"], w=[pNk[d]])
                kb.mm(pNv[:, h, 0:65], qT[:, h, cs], Stb[d][:, h, 0:65], False, True, r=["qT", "Stb%d" % d], w=[pNk[d]])
            if pos < NTK - 1:
                for h in range(2):
                    kb.mm(pUv[0:64, h, 0:65], kp[:, h, :], vext[:, c, h, 0:65], True, True, r=[kpk, "vext"], w=[pUk[d]])
            kb.act(absd[d], pNv[:, :, 64], AF.Abs, r=[pNk[d]], w=["absd%d" % d])
            kb.tt("dve", absd[d], absd[d], LOW[:, d, c, :], ALU.max, r=["absd%d" % d, "LOW"], w=["absd%d" % d])
            kb.recip(rc[d], absd[d], r=["absd%d" % d], w=["rc%d" % d])
            kb.tt("dve", hdir[d][:, c], pNv[:, :, 0:64], rc[d].unsqueeze(2).to_broadcast([128, 2, 64]), ALU.mult,
                  r=[pNk[d], "rc%d" % d], w=["hdir%d_%d" % (d, c)])
            if pos < NTK - 1:
                ebc = Ee[0:64, d, pos + 1, :].unsqueeze(2).to_broadcast([64, 2, 65])
                kb.tt("dve", Stmp[d], St[d], pUv[0:64, :, 0:65], ALU.add, r=["St%d" % d, pUk[d]], w=["Stmp%d" % d])
                kb.tt("dve", St[d], Stmp[d], ebc, ALU.mult, r=["Stmp%d" % d, "Ee"], w=["St%d" % d])
                kb.tt("pool", Stb[d][:, :, 0:65], Stmp[d], ebc, ALU.mult, r=["Stmp%d" % d, "Ee"], w=["Stb%d" % d])
    hs = [kb.sbt("hs%d" % i, [128, 2, 64]) for i in range(2)]
    hsq = [kb.sbt("hsq%d" % i, [128, 2, 64]) for i in range(2)]
    ssq = [kb.sbt("ssq%d" % i, [128, 2]) for i in range(2)]
    mob = [kb.sbt("mob%d" % i, [128, 128], BF16) for i in range(2)]
    for c in range(NTK):
        i2 = c % 2
        cs = slice(c * 128, (c + 1) * 128)
        kb.tt("dve", hs[i2], hdir[0][:, c], hdir[1][:, c], ALU.add, r=["hdir0_%d" % c, "hdir1_%d" % c], w=["hs%d" % i2])
        kb.tt("pool", hsq[i2], hs[i2], hs[i2], ALU.mult, r=["hs%d" % i2], w=["hsq%d" % i2])
        kb.S.op("dve", lambda e, i2=i2: e.tensor_reduce(out=ssq[i2], in_=hsq[i2], axis=AX.X, op=ALU.add), r=["hsq%d" % i2], w=["ssq%d" % i2])
        kb.act(ssq[i2], ssq[i2], AF.Sqrt, r=["ssq%d" % i2], w=["ssq%d" % i2], bias=EPS_RMS, scale=1.0 / 64)
        kb.recip(ssq[i2], ssq[i2], r=["ssq%d" % i2], w=["ssq%d" % i2])
        kb.tt("dve", hs[i2], hs[i2], ssq[i2].unsqueeze(2).to_broadcast([128, 2, 64]), ALU.mult, r=["hs%d" % i2, "ssq%d" % i2], w=["hs%d" % i2])
        kb.tt("pool", hs[i2], hs[i2], mngb.unsqueeze(1).to_broadcast([128, 2, 64]), ALU.mult, r=["hs%d" % i2, "mngb"], w=["hs%d" % i2])
        kb.tt("dve", mob[i2], hs[i2].rearrange("p h v -> p (h v)"), sigo[:, c, :], ALU.mult, r=["hs%d" % i2, "sigo"], w=["mob%d" % i2])
        kb.tr(pT[:, 0:128], mob[i2], ident_b, r=["mob%d" % i2, "ident_b"], w=["pT"])
        kb.copy("act", ysm[:, cs], pT[:, 0:128], r=["pT"], w=["ysm"])
    kb.dma("sp", yint[p * 128:(p + 1) * 128, :], ysm, r=["ysm"], w=[ykey])
    kb.phase_end()

    kb.phase_begin()
    ysa = kb.sbt("ysa", [64, 2, TTK], BF16)
    Eb = [[kb.sbt("Eb%d_%d" % (m, i), [128, 512], BF16) for i in range(2)] for m in range(2)]
    Osb = [kb.sbt("Osb%d" % m, [128, 512]) for m in range(2)]
    for m in range(2):
        kb.memset("pool", Osb[m], 0.0, w=["Osb%d" % m])
    rz = [kb.sbt("rz%d" % m, [64, 512]) for m in range(2)]
    oT = kb.sbt("oT", [64, 512]); sqo = kb.sbt("sqo", [64, 512]); rso = kb.sbt("rso", [64, 512])
    scale = 32 ** -0.5
    qblocks = ([(0, 256, [0, 1])] if layer0 else []) + [(256 + 512 * i, 512, list(range(NTK))) for i in range(8)]
    psS = [[(pA, "pA"), (pC, "pC")], [(pB, "pB"), (pD, "pD")]]
    psO = [(pF, "pF"), (pG, "pG")]
    pending = None
    for h in range(2):
        for (q0, nq, kts) in qblocks:
            def emit_S(i):
                par = i % 2
                ks = slice(kts[i] * 128, (kts[i] + 1) * 128)
                for m in range(2):
                    pp, pk = psS[m][par]
                    kb.mm(pp[:, 0:nq], KTm[:, 2 * h + m, ks], QT[:, q0:q0 + nq], True, True, r=["KT", "QT"], w=[pk])

            def emit_E(i):
                par = i % 2
                for m in range(2):
                    pp, pk = psS[m][par]
                    kb.act(Eb[m][par][:, 0:nq], pp[:, 0:nq], AF.Exp, r=[pk], w=["Eb%d_%d" % (m, par)], scale=scale)

            def emit_PV(i):
                par = i % 2
                for m in range(2):
                    po, pok = psO[m]
                    kb.mm(po[:, 0:nq], Vx[:, kts[i], h, :], Eb[m][par][:, 0:nq], i == 0, i == len(kts) - 1,
                          r=["Vx", "Eb%d_%d" % (m, par)], w=[pok])

            emit_S(0)
            for i in range(len(kts)):
                emit_E(i)
                if i + 1 < len(kts):
                    emit_S(i + 1)
                emit_PV(i)
                if pending and i == min(1, len(kts) - 1):
                    pending[0]()
                if pending and i == min(8, len(kts) - 1):
                    pending[1]()
                    pending = None
            for m in range(2):
                po, pok = psO[m]
                kb.copy("dve", Osb[m][0:65, 0:nq], po[0:65, 0:nq], r=[pok], w=["Osb%d" % m])

            def fin_a(h=h, q0=q0, nq=nq):
                for m in range(2):
                    kb.mm(pE[0:64, 0:nq], selZ, Osb[m][:, 0:nq], True, True, r=["selZ", "Osb%d" % m], w=["pE"])
                    kb.recip(rz[m][:, 0:nq], pE[0:64, 0:nq], r=["pE"], w=["rz%d" % m])
                    kb.tt("pool", rz[m][:, 0:nq], Osb[m][0:64, 0:nq], rz[m][:, 0:nq], ALU.mult, r=["Osb%d" % m, "rz%d" % m], w=["rz%d" % m])
                kb.stt("dve", oT[:, 0:nq], rz[1][:, 0:nq], negLam, rz[0][:, 0:nq], ALU.mult, ALU.add, r=["rz0", "rz1", "negLam"], w=["oT"])
                kb.tt("pool", sqo[:, 0:nq], oT[:, 0:nq], oT[:, 0:nq], ALU.mult, r=["oT"], w=["sqo"])

            def fin_b(h=h, q0=q0, nq=nq):
                kb.mm(pE[0:64, 0:nq], onesf[0:64, 0:64], sqo[:, 0:nq], True, True, r=["onesf", "sqo"], w=["pE"])
                kb.act(rso[:, 0:nq], pE[0:64, 0:nq], AF.Ln, r=["pE"], w=["rso"], bias=EPS_RMS, scale=1.0 / 64)
                kb.act(rso[:, 0:nq], rso[:, 0:nq], AF.Exp, r=["rso"], w=["rso"], scale=-0.5)
                kb.tt("dve", oT[:, 0:nq], oT[:, 0:nq], rso[:, 0:nq], ALU.mult, r=["oT", "rso"], w=["oT"])
                kb.ts("dve", ysa[:, h, q0:q0 + nq], oT[:, 0:nq], sublg, 1.0 - float(lam_init), ALU.mult, ALU.mult, r=["oT", "sublg"], w=["ysa"])

            pending = (fin_a, fin_b)
    if pending:
        pending[0]()
        pending[1]()
    if not layer0:
        kb.memset("pool", ysa[:, :, 0:256], 0.0, w=["ysa"])
    for h in range(2):
        kb.dma("sp", yint[256 + p * 128 + 64 * h:256 + p * 128 + 64 * h + 64, :], ysa[:, h, :], r=["ysa"], w=[ykey])
    kb.phase_end()
    kb.scope_end()


def build_fused():
    kb = KB()
    hfull = kb.inp("hfull", [TTK, D], lvl="global")
    hful1 = kb.internal("hful1", [TTK, D])
    hout = kb.outp("hout", [4096, D])
    yint = [kb.internal("yint%d" % l, [512, TTK], BF16) for l in range(2)]
    kb.ps("pT", [128, 1024], BF16)
    for nm in "ABCDEFG":
        kb.ps("p" + nm, [128, 512])
    modc_d = [kb.internal("modc%d" % l, [128, 48, 2]) for l in range(2)]
    grow_d = [kb.internal("grow%d" % l, [2, 2048]) for l in range(2)]
    uTd = [kb.internal("uTd%d" % l, [128, 8, TTK], BF16) for l in range(2)]
    wb16 = []
    for l in range(2):
        d = {}
        for nm, shape in (("w_pc", [D, 768]), ("w_out", [D, D]), ("fwo", [DFF, D]), ("fwi", [D, 2 * DFF])):
            d[nm] = (kb.internal("L%d_%s_b16" % (l, nm), shape, BF16), "L%d_%s_b16" % (l, nm))
        wb16.append(d)

    def convert_weights(l):
        save = dict(kb.pfx)
        if True:
            kb.pfx = {"inst": "", "layer": "L%d_" % l, "global": ""}
            for nm, shape, rows in (("w_pc", [D, 768], 256), ("w_out", [D, D], 256), ("fwo", [DFF, D], 256), ("fwi", [D, 2 * DFF], 128)):
                src = kb.inp(nm, shape, lvl="layer")
                dst, key = wb16[l][nm]
                for r0 in range(0, shape[0], rows):
                    kb.dma("pool", dst[r0:r0 + rows, :], src[r0:r0 + rows, :], w=[key], bg=True)
        kb.pfx = save

    for l in range(2):
        hsrc, hkey = (hfull, "hfull") if l == 0 else (hful1, "hful1")
        convert_weights(l)
        emit_mod(kb, l, modc_d[l], grow_d[l], "mod%d" % l)
        for p in range(2):
            emit_M(kb, l, p, hsrc, hkey, yint[l], "yint%d" % l, modc_d[l], "mod%d" % l, uTd[l],
                   hook=None)
        for g in range(2):
            emit_F(kb, l, g, hsrc, hkey, yint[l], "yint%d" % l, hful1 if l == 0 else hout, "hful1" if l == 0 else "hout", l == 1,
                   wb16[l], modc_d[l], grow_d[l], "mod%d" % l, uTd[l])
    kb.S.emit()
    return kb

import numpy as np, ml_dtypes
BF = ml_dtypes.bfloat16
S_, CT, D, DFF = 4096, 256, 1024, 2816
POOL_WINDOWS = (2, 4, 8, 16)

def col_layout(v, n):
    return np.ascontiguousarray(v.reshape(n, 128).T)

def band_mats(base, p0, L, w):
    B = np.zeros((256, 128), np.float32)
    for t in range(128):
        p = p0 + t
        lo = min(max(p - w // 2, 0), L - 1); hi = min(max(p + (w - 1 - w // 2), 0), L - 1)
        inv = np.float32(1.0) / np.float32(hi - lo + 1)
        for q in range(lo, hi + 1):
            B[q - base, t] += inv
        B[p - base, t] -= 1.0
    return B

def prep_F(inp, l, h, hc, yma_lat, yma_ctx, b, g):
    layer0 = (l == 0)
    start, cstart = g * 2048, g * 128
    m = {}
    rows = [h[b, start:start + 2048]]
    if layer0: rows.append(hc[b, cstart:cstart + 128])
    m["hrows"] = np.ascontiguousarray(np.concatenate(rows, 0))
    def padded(src, s0, n, L):
        out = np.zeros((n, D), np.float32); msk = np.zeros((n,), np.float32)
        lo, hi = max(s0, 0), min(s0 + n, L)
        out[lo - s0:hi - s0] = src[lo:hi]; msk[lo - s0:hi - s0] = 1.0
        return out, msk
    hp, mk = padded(h[b], start - 16, 2176, S_)
    hps, mks = [hp], [mk]
    if layer0:
        cp, cm = padded(hc[b], cstart - 16, 256, CT)
        hps.append(cp); mks.append(cm)
    m["hpad"] = np.ascontiguousarray(np.concatenate(hps, 0)); m["maskp"] = np.concatenate(mks, 0)
    ys = [yma_lat[b, start:start + 2048]]
    if layer0: ys.append(yma_ctx[b, cstart:cstart + 128])
    m["yTma"] = np.ascontiguousarray(np.concatenate(ys, 0).T).astype(BF)
    m["ccol"] = np.ascontiguousarray(np.stack([col_layout(inp["c"][b], 8), col_layout(inp["c_ctx"], 8)], axis=-1))
    m["modw"] = inp["mod_w"][l]
    m["modb_col"] = col_layout(inp["mod_b"][l], 48)
    m["modb_row"] = inp["mod_b"][l]
    m["n1g"] = col_layout(inp["norm1_g"][l], 8); m["n2g"] = col_layout(inp["norm2_g"][l], 8)
    m["w_pc"] = np.ascontiguousarray(inp["w_in"][l][:, 1808:2576])
    bands = np.zeros((128, 32, 128), np.float32)
    cls_list = [(0, S_, start), (1, S_, start), (15, S_, start)]
    for cls in range(4):
        for gi, w in enumerate(POOL_WINDOWS):
            if cls < 3:
                ot = cls_list[cls][0]
                base = start - 16 + ot * 128; p0 = start + ot * 128; L = S_
            else:
                base = cstart - 16; p0 = cstart; L = CT
            B = band_mats(base, p0, L, w)
            bands[:, (cls * 4 + gi) * 2 + 0, :] = B[0:128]; bands[:, (cls * 4 + gi) * 2 + 1, :] = B[128:256]
    m["bands"] = bands
    pw2 = np.zeros((64, 4, 128), np.float32)
    for gi in range(4):
        pw2[:, gi, (gi % 2) * 64:(gi % 2) * 64 + 64] = inp["pool_w"][l][gi]
    m["poolw2"] = pw2
    m["pscale"] = col_layout(inp["pool_scale"][l], 2)
    m["dww"] = np.ascontiguousarray(inp["conv_dw_w"][l].T.reshape(2, 128, 31).transpose(1, 0, 2))
    m["dwb"] = col_layout(inp["conv_dw_b"][l], 2); m["lng"] = col_layout(inp["conv_ln_g"][l], 2); m["lnb"] = col_layout(inp["conv_ln_b"][l], 2)
    m["pww"] = inp["conv_pw_w"][l]; m["w_out"] = inp["w_out"][l]; m["fwi"] = inp["ffn_w_in"][l]; m["fwo"] = inp["ffn_w_out"][l]
    m["identf"] = np.eye(128, dtype=np.float32)
    sel2 = np.zeros((2, 2, 128), np.float32); sel2[0, 0] = 1; sel2[1, 1] = 1
    m["sel2"] = sel2
    return {k: np.ascontiguousarray(v) for k, v in m.items()}

NTK = 34
def rope_tables():
    TT = NTK * 128
    cos = np.ones((64, TT), np.float32); sin = np.zeros((64, TT), np.float32)
    tok = np.arange(S_)
    rows = (tok // 64).astype(np.float32); cols = (tok % 64).astype(np.float32)
    freqs = (np.float32(10000.0) ** (-np.arange(8, dtype=np.float32) / np.float32(8))).astype(np.float32)
    for m in range(2):
        for d in range(32):
            axis, half, f = d // 16, (d % 16) // 8, d % 8
            ang = (rows if axis == 0 else cols) * freqs[f]
            cos[m * 32 + d, 256:] = np.cos(ang.astype(np.float32))
            sin[m * 32 + d, 256:] = np.sin(ang.astype(np.float32)) * (-1.0 if half == 0 else 1.0)
    return cos, sin

_CONST = {}
def consts_M():
    if _CONST: return _CONST
    c = {}
    c["identf"] = np.eye(128, dtype=np.float32)
    r = np.arange(128)
    tri = np.zeros((128, 2, 128), np.float32)
    tri[:, 0, :] = (r[:, None] <= r[None, :]); tri[:, 1, :] = (r[:, None] >= r[None, :])
    c["tri"] = tri; c["mask"] = tri.copy()
    bo = np.zeros((128, 128), np.float32)
    for j in range(4):
        bo[32 * j:32 * j + 32, 32 * j:32 * j + 32] = 1
    c["bones"] = bo
    mc = np.zeros((128, 4), np.float32)
    for j in range(4):
        mc[32 * j:32 * j + 32, j] = 1
    c["mcol"] = mc
    pm = np.zeros((128, 128), np.float32)
    for d in range(128):
        dd = d % 32
        partner = d + 8 if (dd % 16) < 8 else d - 8
        pm[partner, d] = 1.0
    c["perm"] = pm
    pB = np.zeros((68, 68), np.float32)
    for pos in range(34):
        cc = 1 - pos if pos < 2 else 35 - pos
        for h in range(2):
            pB[cc * 2 + h, pos * 2 + h] = 1.0
    c["permB"] = pB
    sg = np.zeros((2, 64), np.float32); sg[0] = -1; sg[1] = 1
    c["sgn"] = sg
    sz = np.zeros((128, 64), np.float32); sz[64, :] = 1.0
    c["selZ"] = sz
    ct_, st_ = rope_tables()
    c["cost"], c["sint"] = np.concatenate([ct_, ct_], 0), np.concatenate([st_, st_], 0)
    _CONST.update(c)
    return _CONST

def prep_M(inp, l, h, hc, b, g):
    m = dict(consts_M())
    m["hfull"] = np.ascontiguousarray(np.concatenate([hc[b], h[b]], 0))
    m["ccol"] = np.ascontiguousarray(np.stack([col_layout(inp["c"][b], 8), col_layout(inp["c_ctx"], 8)], axis=-1))
    m["modw"] = np.ascontiguousarray(inp["mod_w"][l][:, :2048])
    m["modb_col"] = col_layout(inp["mod_b"][l][:2048], 16)
    m["n1g"] = col_layout(inp["norm1_g"][l], 8)
    W = inp["w_in"][l]
    hp = slice(g * 128, (g + 1) * 128)
    mq, mk, mv, mo = W[:, 0:256][:, hp], W[:, 256:512][:, hp], W[:, 512:768][:, hp], W[:, 768:1024][:, hp]
    gates = W[:, 1024:1040].reshape(D, 2, 2, 4)[:, :, :, 2 * g:2 * g + 2].reshape(D, 8)
    aq, ak, av = W[:, 1040:1296][:, hp], W[:, 1296:1552][:, hp], W[:, 1552:1808][:, hp]
    m["w_fm"] = np.concatenate([mq, mk, aq, ak], 1)
    m["w_tm"] = np.concatenate([mk, mv, mo, av, gates, np.zeros((D, 8), np.float32)], 1)
    m["gate_b"] = inp["mlstm_gate_b"][l][:, :, 2 * g:2 * g + 2].reshape(8)
    m["mng"] = inp["mlstm_norm_g"][l]
    m["qkng"] = np.stack([np.tile(inp["attn_q_norm_g"][l], 2), np.tile(inp["attn_k_norm_g"][l], 2)], 1)
    m["lamqk"] = np.stack([inp["lambda_q"][l], inp["lambda_k"][l]], 1)
    m["sublng"] = inp["attn_subln_g"][l].reshape(64, 1)
    return {k: np.ascontiguousarray(v, dtype=np.float32) for k, v in m.items()}


def prep_fused(inp, b):
    x, ctx = inp["x"], inp["ctx"]
    m = dict(consts_M())
    m["hfull"] = np.concatenate([ctx[b], x[b]], 0)
    m["ccol"] = np.stack([col_layout(inp["c"][b], 8), col_layout(inp["c_ctx"], 8)], axis=-1)
    sel2 = np.zeros((2, 2, 128), np.float32); sel2[0, 0] = 1; sel2[1, 1] = 1
    m["sel2"] = sel2
    for g in range(2):
        start, cstart = g * 2048, g * 128
        mk = np.zeros((19 * 128,), np.float32)
        pos = start - 16 + np.arange(2176); mk[:2176] = ((pos >= 0) & (pos < S_))
        cpos = cstart - 16 + np.arange(256); mk[2176:] = ((cpos >= 0) & (cpos < CT))
        m["maskp%d" % g] = mk
        bands = np.zeros((128, 32, 128), np.float32)
        for cls in range(4):
            for gi, w in enumerate(POOL_WINDOWS):
                if cls < 3:
                    ot = (0, 1, 15)[cls]
                    base = start - 16 + ot * 128; p0 = start + ot * 128; L = S_
                else:
                    base = cstart - 16; p0 = cstart; L = CT
                B = band_mats(base, p0, L, w)
                bands[:, (cls * 4 + gi) * 2 + 0, :] = B[0:128]; bands[:, (cls * 4 + gi) * 2 + 1, :] = B[128:256]
        m["bands%d" % g] = bands
    for l in range(2):
        L_ = "L%d_" % l
        m[L_ + "modw"] = inp["mod_w"][l]
        m[L_ + "modb_col"] = col_layout(inp["mod_b"][l], 48)
        m[L_ + "modb_row"] = inp["mod_b"][l]
        m[L_ + "n1g"] = col_layout(inp["norm1_g"][l], 8); m[L_ + "n2g"] = col_layout(inp["norm2_g"][l], 8)
        W = inp["w_in"][l]
        m[L_ + "w_pc"] = W[:, 1808:2576]
        pw2 = np.zeros((64, 4, 128), np.float32)
        for gi in range(4):
            pw2[:, gi, (gi % 2) * 64:(gi % 2) * 64 + 64] = inp["pool_w"][l][gi]
        m[L_ + "poolw2"] = pw2
        m[L_ + "pscale"] = col_layout(inp["pool_scale"][l], 2)
        m[L_ + "dww"] = inp["conv_dw_w"][l].T.reshape(2, 128, 31).transpose(1, 0, 2)
        m[L_ + "dwb"] = col_layout(inp["conv_dw_b"][l], 2); m[L_ + "lng"] = col_layout(inp["conv_ln_g"][l], 2)
        m[L_ + "lnb"] = col_layout(inp["conv_ln_b"][l], 2)
        m[L_ + "pww"] = inp["conv_pw_w"][l]; m[L_ + "w_out"] = inp["w_out"][l]
        m[L_ + "fwi"] = inp["ffn_w_in"][l]; m[L_ + "fwo"] = inp["ffn_w_out"][l]
        m[L_ + "mng"] = inp["mlstm_norm_g"][l]
        m[L_ + "qkng"] = np.stack([np.tile(inp["attn_q_norm_g"][l], 4), np.tile(inp["attn_k_norm_g"][l], 4)], 1)
        m[L_ + "lamqk"] = np.stack([inp["lambda_q"][l], inp["lambda_k"][l]], 1)
        m[L_ + "sublng"] = inp["attn_subln_g"][l].reshape(64, 1)
        for p in range(2):
            P_ = "M%d%d_" % (l, p)
            hp = slice(p * 128, (p + 1) * 128)
            mq, mk_, mv, mo = W[:, 0:256][:, hp], W[:, 256:512][:, hp], W[:, 512:768][:, hp], W[:, 768:1024][:, hp]
            gates = W[:, 1024:1040].reshape(D, 2, 2, 4)[:, :, :, 2 * p:2 * p + 2].reshape(D, 8)
            aq, ak, av = W[:, 1040:1296][:, hp], W[:, 1296:1552][:, hp], W[:, 1552:1808][:, hp]
            m[P_ + "w_fm"] = np.concatenate([mq, mk_, aq, ak], 1)
            m[P_ + "w_tm"] = np.concatenate([mk_, mv, mo, av, gates, np.zeros((D, 8), np.float32)], 1)
            m[P_ + "gate_b"] = inp["mlstm_gate_b"][l][:, :, 2 * p:2 * p + 2].reshape(8)
    return {k: np.ascontiguousarray(v, dtype=np.float32) for k, v in m.items()}

_PROG = {}


def kernel(**inputs):
    inp = {k: np.asarray(v) for k, v in inputs.items()}
    if "kb" not in _PROG:
        _PROG["kb"] = build_fused()
    kb = _PROG["kb"]
    maps = [prep_fused(inp, b) for b in range(4)]
    res = run_bass_kernel_spmd(kb.nc, maps, core_ids=list(range(4)))
    out = np.stack([np.asarray(res.results[b]["hout"]) for b in range(4)], 0)
    return out.astype(np.float32)
```

```python
import math
import numpy as np
import concourse.bass as bass
import concourse.mybir as mybir
from concourse.bass_utils import run_bass_kernel_spmd

F32 = mybir.dt.float32
BF16 = mybir.dt.bfloat16
ALU = mybir.AluOpType
AF = mybir.ActivationFunctionType
AX = mybir.AxisListType

COMPUTE = ("pe", "act", "dve", "pool")
N_DMA_SEMS = 12
N_BG_SEMS = 24


class _Op:
    __slots__ = ("eng", "fn", "deps", "is_dma", "eidx", "marked", "sem", "semval", "prev_on_sem", "bg")

    def __init__(self, eng, fn, deps, is_dma):
        self.eng, self.fn, self.deps, self.is_dma = eng, fn, deps, is_dma
        self.marked = False
        self.bg = False
        self.sem = None
        self.semval = 0
        self.prev_on_sem = None


class Sched:
    def __init__(self, nc, same_engine_sync=True):
        self.nc = nc
        self.ops = []
        self.last_w = {}
        self.readers = {}
        self.same_engine_sync = same_engine_sync
        self.engs = {"pe": nc.tensor, "act": nc.scalar, "dve": nc.vector, "pool": nc.gpsimd, "sp": nc.sync}
        self.ecount = {e: 0 for e in self.engs}
        self._bar_tile = nc.alloc_sbuf_tensor("bar_tile", [128, 8], F32).ap()

    def barrier(self):
        self.op("pool", lambda e: e.memset(self._bar_tile, 0.0), r=(), w=["__phase__"])

    def op(self, eng, fn, r=(), w=(), dma=False):
        r = list(r) + ["__phase__"]
        deps = set()
        for k in r:
            if k in self.last_w:
                deps.add(self.last_w[k])
        for k in w:
            if k in self.last_w:
                deps.add(self.last_w[k])
            deps.update(self.readers.get(k, ()))
        idx = len(self.ops)
        o = _Op(eng, fn, deps, dma)
        o.eidx = self.ecount[eng]
        self.ecount[eng] += 1
        self.ops.append(o)
        for k in r:
            self.readers.setdefault(k, []).append(idx)
        for k in w:
            self.last_w[k] = idx
            self.readers[k] = []
        return idx

    def dma(self, eng, out, in_, r=(), w=(), bg=False):
        i = self.op(eng, lambda e: e.dma_start(out=out, in_=in_), r, w, dma=True)
        self.ops[i].bg = bg
        return i

    def emit(self, final_wait_eng="sp"):
        nc = self.nc
        import os
        if os.environ.get("TRUNC"):
            self.ops = self.ops[:int(os.environ["TRUNC"])]
        ops = self.ops
        known = {e: {f: -1 for f in COMPUTE} for e in self.engs}
        dma_known = {e: {} for e in self.engs}
        waits = [None] * len(ops)
        dma_pool = {e: [] for e in self.engs}
        dma_rr = {e: 0 for e in self.engs}
        dma_last = {}
        for i, o in enumerate(ops):
            wl = []
            for d in sorted(o.deps):
                p = ops[d]
                if p.is_dma:
                    if dma_known[o.eng].get(p.sem, 0) < p.semval:
                        dma_known[o.eng][p.sem] = p.semval
                        wl.append(("dma", d))
                else:
                    if p.eng == o.eng and (o.eng == 'pe' or not self.same_engine_sync) and not o.is_dma:
                        continue
                    if known[o.eng][p.eng] < p.eidx:
                        known[o.eng][p.eng] = p.eidx
                        wl.append(("eng", d))
            best = {}
            for kind, d in wl:
                if kind == "eng":
                    e = ops[d].eng
                    if e not in best or ops[best[e]].eidx < ops[d].eidx:
                        best[e] = d
            bestd = {}
            for kind, d in wl:
                if kind == "dma":
                    sl = ops[d].sem
                    if sl not in bestd or ops[bestd[sl]].semval < ops[d].semval:
                        bestd[sl] = d
            wl = [("dma", d) for d in bestd.values()] + [("eng", d) for d in best.values()]
            if o.is_dma:
                qn = o.eng + ("_bg" if o.bg else "")
                dma_rr.setdefault(qn, 0)
                slot = dma_rr[qn] % (N_BG_SEMS if o.bg else N_DMA_SEMS)
                dma_rr[qn] += 1
                o.sem = (qn, slot)
                prev = dma_last.get(o.sem)
                o.prev_on_sem = prev
                o.semval = (ops[prev].semval if prev is not None else 0) + 16
                dma_last[o.sem] = i
                if prev is not None and dma_known[o.eng].get(o.sem, 0) < ops[prev].semval:
                    wl.append(("dma", prev))
                    dma_known[o.eng][o.sem] = ops[prev].semval
            for kind, d in wl:
                if kind == "eng":
                    ops[d].marked = True
            waits[i] = wl
        cnt = {e: 0 for e in COMPUTE}
        ordinal = {}
        for i, o in enumerate(ops):
            if not o.is_dma and o.marked:
                cnt[o.eng] += 1
                ordinal[i] = cnt[o.eng]
        esem = {e: nc.alloc_semaphore("sem_" + e) for e in COMPUTE}
        dsem = {}
        for o in ops:
            if o.is_dma and o.sem not in dsem:
                dsem[o.sem] = nc.alloc_semaphore("dsem_%s_%d" % o.sem)
        for i, o in enumerate(ops):
            eng = self.engs[o.eng]
            for kind, d in waits[i]:
                p = ops[d]
                if kind == "dma":
                    eng.wait_ge(dsem[p.sem], p.semval)
                else:
                    eng.wait_ge(esem[p.eng], ordinal[d])
            ins = o.fn(eng)
            if o.is_dma:
                ins.then_inc(dsem[o.sem], 16)
            elif o.marked:
                ins.then_inc(esem[o.eng], 1)
        fe = self.engs[final_wait_eng]
        for key, i in dma_last.items():
            fe.wait_ge(dsem[key], ops[i].semval)
        self.stats = {e: self.ecount[e] for e in self.engs}
        self.stats["marked"] = dict(cnt)


EPS_RMS = 1e-6
EPS_LN = 1e-5
D = 1024
DFF = 2816


class KB:
    def __init__(self):
        self.nc = bass.Bass("TRN2", target_bir_lowering=False)
        self.S = Sched(self.nc)
        self.rr = {}
        self.dram = {}
        self.pfx = {"inst": "", "layer": "", "global": ""}
        self.uid = 0
        self.scope = None
        self.psums = {}

    def inp(self, name, shape, dt=F32, lvl="inst"):
        full = self.pfx[lvl] + name
        if full not in self.dram:
            self.dram[full] = self.nc.dram_tensor(full, list(shape), dt, kind="ExternalInput").ap()
        return self.dram[full]

    def outp(self, name, shape, dt=F32):
        if name not in self.dram:
            self.dram[name] = self.nc.dram_tensor(name, list(shape), dt, kind="ExternalOutput").ap()
        return self.dram[name]

    def internal(self, name, shape, dt=F32):
        if name not in self.dram:
            self.dram[name] = self.nc.dram_tensor(name, list(shape), dt).ap()
        return self.dram[name]

    def scope_begin(self, inst, layer):
        from contextlib import ExitStack
        self.scope = ExitStack()
        self.pfx = {"inst": inst + "_", "layer": layer + "_", "global": ""}
        self.rr = {}
        self._bufs = {}

    def scope_end(self):
        self.S.barrier()
        self.scope.close()
        self.scope = None

    def sb(self, name, shape, dt=F32):
        self.uid += 1
        return self.scope.enter_context(self.nc.sbuf_tensor("s%d_%s" % (self.uid, name), list(shape), dt)).ap()

    def phase_begin(self):
        from contextlib import ExitStack
        self.stack = ExitStack()

    def sbt(self, name, shape, dt=F32):
        self.uid += 1
        return self.stack.enter_context(self.nc.sbuf_tensor("s%d_%s" % (self.uid, name), list(shape), dt)).ap()

    def phase_end(self):
        self.S.barrier()
        self.stack.close()
        self.stack = None

    def ps(self, name, shape, dt=F32):
        if name not in self.psums:
            self.psums[name] = self.nc.alloc_psum_tensor("p_" + name, list(shape), dt).ap()
        return self.psums[name]

    def rot(self, name, n):
        i = self.rr.get(name, 0)
        self.rr[name] = i + 1
        return i % n

    def dma(self, eng, out, in_, r=(), w=(), bg=False):
        self.S.dma(eng, out, in_, r=r, w=w, bg=bg)

    def mm(self, out, lhsT, rhs, start, stop, r, w):
        self.S.op("pe", lambda e: e.matmul(out, lhsT=lhsT, rhs=rhs, start=start, stop=stop), r=r, w=w)

    def tr(self, out, in_, ident, r, w):
        self.S.op("pe", lambda e: e.transpose(out, in_, ident), r=r, w=w)

    def act(self, out, in_, func, r, w, bias=0.0, scale=1.0, accum=None, eng="act"):
        if accum is None:
            self.S.op(eng, lambda e: e.activation(out=out, in_=in_, func=func, bias=bias, scale=scale), r=r, w=w)
        else:
            self.S.op(eng, lambda e: e.activation(out=out, in_=in_, func=func, bias=bias, scale=scale, accum_out=accum), r=r, w=w)

    def tt(self, eng, out, in0, in1, op, r, w):
        self.S.op(eng, lambda e: e.tensor_tensor(out=out, in0=in0, in1=in1, op=op), r=r, w=w)

    def ts(self, eng, out, in0, s1, s2, op0, op1, r, w):
        if s2 is None:
            self.S.op(eng, lambda e: e.tensor_scalar(out=out, in0=in0, scalar1=s1, scalar2=None, op0=op0), r=r, w=w)
        else:
            self.S.op(eng, lambda e: e.tensor_scalar(out=out, in0=in0, scalar1=s1, scalar2=s2, op0=op0, op1=op1), r=r, w=w)

    def stt(self, eng, out, in0, scalar, in1, op0, op1, r, w):
        self.S.op(eng, lambda e: e.scalar_tensor_tensor(out=out, in0=in0, scalar=scalar, in1=in1, op0=op0, op1=op1), r=r, w=w)

    def copy(self, eng, out, in_, r, w):
        if eng == "act":
            self.S.op("act", lambda e: e.copy(out=out, in_=in_), r=r, w=w)
        else:
            self.S.op(eng, lambda e: e.tensor_copy(out=out, in_=in_), r=r, w=w)

    def recip(self, out, in_, r, w):
        self.S.op("dve", lambda e: e.reciprocal(out=out, in_=in_), r=r, w=w)

    def memset(self, eng, ap, val, w):
        self.S.op(eng, lambda e: e.memset(ap, val), w=w)


def mod_columns(kb, modw, modb_col, ccol, col0, nchunks, name, stg, psm):
    sc = kb.sbt(name + "_sc", [128, 8, 2])
    kb.dma("sp", sc, ccol, w=[name + "_sc"])
    kb.act(sc, sc, AF.Silu, r=[name + "_sc"], w=[name + "_sc"])
    mb = kb.sbt(name + "_mb", [128, nchunks])
    kb.dma("sp", mb, modb_col, w=[name + "_mb"])
    for jg in range(nchunks // 4):
        b = kb.rot("mstg", 2)
        kb.dma("sp", stg[b].rearrange("p (kt n) -> p kt n", kt=8),
               modw[:, col0 + jg * 512:col0 + (jg + 1) * 512].rearrange("(kt p) n -> p kt n", p=128), w=["mstg%d" % b])
        for jj in range(4):
            j = jg * 4 + jj
            for kt in range(8):
                kb.mm(psm[:, j, :], stg[b][:, kt * 512 + jj * 128:kt * 512 + (jj + 1) * 128], sc[:, kt, :], kt == 0, kt == 7,
                      r=["mstg%d" % b, name + "_sc"], w=["psm"])
    modc = kb.sbt(name + "_modc", [128, nchunks, 2])
    for r_ in range(2):
        kb.tt("dve", modc[:, :, r_], psm[:, :, r_], mb, ALU.add, r=["psm", name + "_mb"], w=[name + "_modc"])
    return modc


def emit_mod(kb, l, modc_d, grow_d, mkey):
    kb.scope_begin("MOD%d" % l, "L%d" % l)
    ccol = kb.inp("ccol", [128, 8, 2], lvl="global")
    modw = kb.inp("modw", [D, 6 * D], lvl="layer")
    modb_col = kb.inp("modb_col", [128, 48], lvl="layer")
    modb_row = kb.inp("modb_row", [6 * D], lvl="layer")
    pA = kb.ps("pA", [128, 512]); pB = kb.ps("pB", [128, 512]); pC = kb.ps("pC", [128, 512])
    pD = kb.ps("pD", [128, 512]); pE = kb.ps("pE", [128, 512])
    kb.phase_begin()
    mstg = [kb.sbt("mstg%d" % i, [128, 4096]) for i in range(2)]
    psm = pE.rearrange("p (c r) -> p c r", r=2)[:, 0:16, :]
    for i in range(3):
        mc = mod_columns(kb, modw, modb_col[:, 16 * i:16 * (i + 1)], ccol, 2048 * i, 16, "mc%d" % i, mstg, psm)
        kb.dma("sp", modc_d[:, 16 * i:16 * (i + 1), :], mc, r=["mc%d_modc" % i], w=[mkey])
    scr = kb.sbt("scr", [128, 8, 2])
    kb.dma("sp", scr, ccol, w=["scr"])
    kb.act(scr, scr, AF.Silu, r=["scr"], w=["scr"])
    grow = kb.sbt("grow", [2, 2048])
    gb2 = kb.sbt("gb2", [2, 2048])
    kb.dma("sp", gb2[:, 0:1024], modb_row[2 * D:3 * D].partition_broadcast(2), w=["gb2"])
    kb.dma("sp", gb2[:, 1024:2048], modb_row[5 * D:6 * D].partition_broadcast(2), w=["gb2"])
    psr_l = [pA, pB, pC, pD]; psr_k = ["pA", "pB", "pC", "pD"]
    for kt in range(8):
        b = kt % 2
        kb.dma("sp", mstg[b][:, 0:1024], modw[kt * 128:(kt + 1) * 128, 2 * D:3 * D], w=["mstg%d" % b])
        kb.dma("sp", mstg[b][:, 1024:2048], modw[kt * 128:(kt + 1) * 128, 5 * D:6 * D], w=["mstg%d" % b])
        for j in range(4):
            kb.mm(psr_l[j][0:2, :], scr[:, kt, :], mstg[b][:, j * 512:(j + 1) * 512], kt == 0, kt == 7,
                  r=["mstg%d" % b, "scr"], w=[psr_k[j]])
    for j in range(4):
        kb.tt("dve", grow[:, j * 512:(j + 1) * 512], psr_l[j][0:2, :], gb2[:, j * 512:(j + 1) * 512], ALU.add,
              r=[psr_k[j], "gb2"], w=["grow"])
    kb.dma("sp", grow_d, grow, r=["grow"], w=[mkey])
    kb.phase_end()
    kb.scope_end()


def norm_transpose_tile(kb, x_dram_rows, uT_out, SC, SH, r_idx, ident_b, pst, tag, okey):
    i = kb.rot(tag + "_x", 3)
    xk = "%s_x%d" % (tag, i)
    x = kb._bufs[xk]
    kb.dma("sp", x, x_dram_rows, w=[xk])
    return norm_transpose_sb(kb, x, xk, uT_out, SC, SH, r_idx, ident_b, pst, tag, okey)


NXH = 4


def norm_part(kb, x, xk, tag):
    j = kb.rot(tag + "_s", NXH)
    junk, ss, xh = kb._bufs[tag + "_junk"], kb._bufs["%s_ss%d" % (tag, j)], kb._bufs["%s_xh%d" % (tag, j)]
    ssk, xhk = "%s_ss%d" % (tag, j), "%s_xh%d" % (tag, j)
    kb.memset("pool", ss, 0.0, w=[ssk])
    kb.act(junk, x, AF.Square, r=[xk, ssk], w=[tag + "_junk", ssk], accum=ss)
    kb.act(ss, ss, AF.Sqrt, r=[ssk], w=[ssk], bias=EPS_RMS, scale=1.0 / D)
    kb.recip(ss, ss, r=[ssk], w=[ssk])
    kb.ts("dve", xh, x, ss, None, ALU.mult, None, r=[xk, ssk], w=[xhk])
    return xh, xhk


def tr_part(kb, xh, xhk, uT_out, SC, SH, r_idx, ident_b, pst, okey):
    pk, psT = pst
    for kt in range(8):
        kb.tr(psT[:, kt * 128:(kt + 1) * 128], xh[:, kt * 128:(kt + 1) * 128], ident_b, r=[xhk, "ident_b"], w=[pk])
    for kt in range(8):
        eng = "act" if kt % 2 == 0 else "dve"
        if eng == "act":
            kb.act(uT_out[:, kt, :], psT[:, kt * 128:(kt + 1) * 128], AF.Identity, r=[pk, "SCSH"], w=[okey],
                   bias=SH[:, kt, r_idx:r_idx + 1], scale=SC[:, kt, r_idx:r_idx + 1])
        else:
            kb.ts("dve", uT_out[:, kt, :], psT[:, kt * 128:(kt + 1) * 128], SC[:, kt, r_idx:r_idx + 1],
                  SH[:, kt, r_idx:r_idx + 1], ALU.mult, ALU.add, r=[pk, "SCSH"], w=[okey])


def norm_transpose_sb(kb, x, xk, uT_out, SC, SH, r_idx, ident_b, pst, tag, okey):
    xh, xhk = norm_part(kb, x, xk, tag)
    tr_part(kb, xh, xhk, uT_out, SC, SH, r_idx, ident_b, pst, okey)


def alloc_norm_xbufs(kb, tag):
    for i in range(3):
        kb._bufs["%s_x%d" % (tag, i)] = kb.sbt("%s_x%d" % (tag, i), [128, D])


def alloc_norm_bufs(kb, tag):
    kb._bufs[tag + "_junk"] = kb.sb(tag + "_junk", [128, D], BF16)
    for j in range(NXH):
        kb._bufs["%s_ss%d" % (tag, j)] = kb.sb("%s_ss%d" % (tag, j), [128, 1])
        kb._bufs["%s_xh%d" % (tag, j)] = kb.sb("%s_xh%d" % (tag, j), [128, D], BF16)


def load_rows(kb, x, xk, hsrc, hkey, row0, nrows_total):
    lo, hi = max(row0, 0), min(row0 + 128, nrows_total)
    if lo > row0 or hi < row0 + 128:
        kb.memset("pool", x, 0.0, w=[xk])
    if hi > lo:
        kb.dma("sp", x[lo - row0:hi - row0, :], hsrc[lo:hi, :], r=[hkey], w=[xk])


def emit_F(kb, l, g, hsrc, hkey, yint, ykey, hdst, hdkey, dst_is_final, wb16, modc_d, grow_d, mkey, uTd):
    layer0 = (l == 0)
    kb.scope_begin("F%d%d" % (l, g), "L%d" % l)
    nc = kb.nc
    NT = 17 if layer0 else 16
    NP = 19 if layer0 else 17
    TP = NP * 128
    TT = NT * 128
    lat0 = 256 + g * 2048
    ctx0 = g * 128
    maskp = kb.inp("maskp%d" % g, [19 * 128], lvl="global")
    ccol = kb.inp("ccol", [128, 8, 2], lvl="global")
    modw = kb.inp("modw", [D, 6 * D], lvl="layer")
    modb_col = kb.inp("modb_col", [128, 48], lvl="layer")
    modb_row = kb.inp("modb_row", [6 * D], lvl="layer")
    n1g = kb.inp("n1g", [128, 8], lvl="layer")
    n2g = kb.inp("n2g", [128, 8], lvl="layer")
    w_pc, wpc_key = wb16["w_pc"]
    bands = kb.inp("bands%d" % g, [128, 32, 128], lvl="global")
    poolw2 = kb.inp("poolw2", [64, 4, 128], lvl="layer")
    pscale = kb.inp("pscale", [128, 2], lvl="layer")
    dww = kb.inp("dww", [128, 2, 31], lvl="layer")
    dwb = kb.inp("dwb", [128, 2], lvl="layer")
    lng = kb.inp("lng", [128, 2], lvl="layer")
    lnb = kb.inp("lnb", [128, 2], lvl="layer")
    pww = kb.inp("pww", [256, 256], lvl="layer")
    w_out, wout_key = wb16["w_out"]
    fwi, fwi_key = wb16["fwi"]
    fwo, fwo_key = wb16["fwo"]
    identf_d = kb.inp("identf", [128, 128], lvl="global")
    sel2_d = kb.inp("sel2", [2, 2, 128], lvl="global")

    identf = kb.sb("identf", [128, 128])
    kb.dma("sp", identf, identf_d, w=["identf"])
    ident_b = kb.sb("ident_b", [128, 128], BF16)
    kb.copy("dve", ident_b, identf, r=["identf"], w=["ident_b"])
    onesf = kb.sb("onesf", [128, 128])
    kb.memset("pool", onesf, 1.0, w=["onesf"])
    sel2 = kb.sb("sel2", [2, 2, 128])
    kb.dma("sp", sel2, sel2_d, w=["sel2"])
    SC1 = kb.sb("SC1", [128, 8, 2]); SH1 = kb.sb("SH1", [128, 8, 2])
    SC2 = kb.sb("SC2", [128, 8, 2]); SH2 = kb.sb("SH2", [128, 8, 2])
    Gbc = [kb.sb("Gbc%d" % r_, [128, 2048]) for r_ in range(2 if layer0 else 1)]
    yT = kb.sb("yT", [128, 8, TT], BF16)
    alloc_norm_bufs(kb, "n1")
    pT = kb.ps("pT", [128, 1024], BF16)
    pA = kb.ps("pA", [128, 512]); pB = kb.ps("pB", [128, 512]); pC = kb.ps("pC", [128, 512])
    pD = kb.ps("pD", [128, 512]); pE = kb.ps("pE", [128, 512])
    for kt in range(4):
        kb.dma("sp", yT[:, kt, 0:2048], yint[kt * 128:(kt + 1) * 128, lat0:lat0 + 2048], r=[ykey], w=["yT%d" % kt])
        if layer0:
            kb.dma("sp", yT[:, kt, 2048:2176], yint[kt * 128:(kt + 1) * 128, ctx0:ctx0 + 128], r=[ykey], w=["yT%d" % kt])

    kb.phase_begin()
    mc1 = kb.sbt("mc1", [128, 16, 2]); mc2 = kb.sbt("mc2", [128, 16, 2])
    kb.dma("sp", mc1, modc_d[:, 0:16, :], r=[mkey], w=["mc1_modc"])
    kb.dma("sp", mc2, modc_d[:, 24:40, :], r=[mkey], w=["mc2_modc"])
    g1n = kb.sbt("g1n", [128, 8]); g2n = kb.sbt("g2n", [128, 8])
    kb.dma("sp", g1n, n1g, w=["g1n"]); kb.dma("sp", g2n, n2g, w=["g2n"])
    for (mc, gn, SC, SH, nm) in ((mc1, g1n, SC1, SH1, "1"), (mc2, g2n, SC2, SH2, "2")):
        for r_ in range(2):
            kb.stt("dve", SC[:, :, r_], mc[:, 8:16, r_], 1.0, gn, ALU.add, ALU.mult,
                   r=["mc%s_modc" % nm, "g%sn" % nm], w=["SCSH"])
            kb.copy("dve", SH[:, :, r_], mc[:, 0:8, r_], r=["mc%s_modc" % nm], w=["SCSH"])
    grow = kb.sbt("grow", [2, 2048])
    kb.dma("sp", grow, grow_d, r=[mkey], w=["grow"])
    psr_l = [pA, pB, pC, pD]; psr_k = ["pA", "pB", "pC", "pD"]
    for r_ in range(len(Gbc)):
        for j in range(4):
            kb.mm(psr_l[j], sel2[:, r_, :], grow[:, j * 512:(j + 1) * 512], True, True, r=["sel2", "grow"], w=[psr_k[j]])
        for j in range(4):
            kb.copy("act", Gbc[r_][:, j * 512:(j + 1) * 512], psr_l[j], r=[psr_k[j]], w=["Gbc%d" % r_])

    wpc_sb = kb.sbt("wpc", [128, 8, 768], BF16)
    kb.dma("sp", wpc_sb, w_pc.rearrange("(kt p) n -> p kt n", p=128), r=[wpc_key], w=["wpc"])
    pww_sb = kb.sbt("pww", [128, 2, 256], BF16)
    kb.dma("pool", pww_sb, pww.rearrange("(ct p) n -> p ct n", p=128), w=["pww"])
    poolw_sb = kb.sbt("poolw", [64, 4, 128], BF16)
    kb.dma("pool", poolw_sb, poolw2, w=["poolw"])
    bands_sb = kb.sbt("bands", [128, 32, 128])
    kb.dma("sp", bands_sb, bands, w=["bands"])
    small = {}
    for nm, src, shp in (("pscale", pscale, [128, 2]), ("dww", dww, [128, 2, 31]), ("dwb", dwb, [128, 2]),
                         ("lng", lng, [128, 2]), ("lnb", lnb, [128, 2])):
        small[nm] = kb.sbt(nm, shp)
        kb.dma("sp", small[nm], src, w=[nm])
    maskb = kb.sbt("maskb", [128, TP], BF16)
    kb.dma("pool", maskb, maskp[0:TP].partition_broadcast(128), w=["maskb"])
    Dg = kb.sbt("Dg", [128, 2, 31, 128], BF16)
    for ct in range(2):
        for k in range(31):
            if k % 3 == 0:
                kb.act(Dg[:, ct, k, :], identf, AF.Copy, r=["identf", "dww"], w=["Dg"], scale=small["dww"][:, ct, k:k + 1])
            else:
                kb.ts("pool" if k % 3 == 1 else "dve", Dg[:, ct, k, :], identf, small["dww"][:, ct, k:k + 1], None, ALU.mult, None,
                      r=["identf", "dww"], w=["Dg"])
    alloc_norm_xbufs(kb, "n1")
    u_p = kb.sbt("u_p", [128, NP, 256])
    GLU = kb.sbt("GLU", [128, 2, TP], BF16)
    uTt = [kb.sbt("uTt%d" % i, [128, 8, 128], BF16) for i in range(4)]
    sg = kb.sbt("sg", [128, 2, 128]); gl = kb.sbt("gl", [128, 2, 128])
    for i in range(NP):
        r_idx = 0 if i < 17 else 1
        b = i % 4
        if i < 17:
            t0_, tbase, tlen = g * 2048 - 16 + i * 128, 256, 4096
        else:
            t0_, tbase, tlen = g * 128 - 16 + (i - 17) * 128, 0, 256
        lo_, hi_ = max(t0_, 0), min(t0_ + 128, tlen)
        if lo_ > t0_ or hi_ < t0_ + 128:
            kb.memset("pool", uTt[b], 0.0, w=["uTt%d" % b])
        if hi_ > lo_:
            kb.dma("sp", uTt[b][:, :, lo_ - t0_:hi_ - t0_], uTd[:, :, tbase + lo_:tbase + hi_], r=["uTd%d" % l], w=["uTt%d" % b])
        pcv = pA.rearrange("p (c t) -> p c t", c=4)
        for cti in range(4):
            for kt in range(8):
                kb.mm(pcv[:, cti, :], wpc_sb[:, kt, 256 + cti * 128:256 + (cti + 1) * 128], uTt[b][:, kt, :], kt == 0, kt == 7,
                      r=["wpc", "uTt%d" % b], w=["pA"])
        for kt in range(8):
            kb.mm(pB[:, 0:256], uTt[b][:, kt, :], wpc_sb[:, kt, 0:256], kt == 0, kt == 7, r=["wpc", "uTt%d" % b], w=["pB"])
        kb.act(sg, pcv[:, 2:4, :], AF.Sigmoid, r=["pA"], w=["sg"])
        kb.tt("dve", gl, pcv[:, 0:2, :], sg, ALU.mult, r=["pA", "sg"], w=["gl"])
        kb.tt("pool", GLU[:, :, i * 128:(i + 1) * 128], gl, maskb[:, i * 128:(i + 1) * 128].unsqueeze(1).to_broadcast([128, 2, 128]),
              ALU.mult, r=["gl", "maskb"], w=["GLU%d" % i])
        kb.copy("act", u_p[:, i, :], pB[:, 0:256], r=["pB"], w=["u_p%d" % i])

    pF = kb.ps("pF", [128, 512]); pG = kb.ps("pG", [128, 512])
    pooled = [kb.sbt("pooled%d" % i, [64, 4, 128], BF16) for i in range(2)]
    blocks = [(j, j, (0 if j == 0 else (2 if j == 15 else 1))) for j in range(16)]
    if layer0:
        blocks.append((16, 17, 3))
    for bi_, (ot, pt, cls) in enumerate(blocks):
        par = bi_ % 2
        (pp1, pk1), (pp2, pk2) = ((pC, "pC"), (pD, "pD")) if par == 0 else ((pF, "pF"), (pG, "pG"))
        ppv = pp1.rearrange("p (g t) -> p g t", g=4)
        for gi in range(4):
            for half in range(2):
                kb.mm(ppv[0:64, gi, :], u_p[:, pt + half, gi * 64:(gi + 1) * 64], bands_sb[:, (cls * 4 + gi) * 2 + half, :],
                      half == 0, half == 1, r=["u_p%d" % (pt + half), "bands"], w=[pk1])
        kb.copy("act", pooled[par], ppv[0:64, :, :], r=[pk1], w=["pooled%d" % par])
        pqv = pp2.rearrange("p (g t) -> p g t", g=4)
        for pr in range(2):
            for gl_ in range(2):
                gi = pr * 2 + gl_
                kb.mm(pqv[:, pr, :], poolw_sb[:, gi, :], pooled[par][:, gi, :], gl_ == 0, gl_ == 1, r=["poolw", "pooled%d" % par], w=[pk2])
        for pr in range(2):
            kb.ts("dve", yT[:, 4 + pr, ot * 128:(ot + 1) * 128], pqv[:, pr, :], small["pscale"][:, pr:pr + 1], None, ALU.mult, None,
                  r=[pk2, "pscale"], w=["yT%d" % (4 + pr)])

    xc = [kb.sbt("xc%d" % i, [128, 2, 512]) for i in range(2)]
    sq = kb.sbt("sq", [128, 2, 512])
    mean = kb.sbt("mean", [128, 512]); var = kb.sbt("var", [128, 512]); t1 = kb.sbt("t1", [128, 2, 512])
    zs = kb.sbt("zs", [128, 2, 512], BF16)
    cblocks = [(jb * 512, jb * 512 + 1, 512) for jb in range(4)]
    if layer0:
        cblocks.append((16 * 128, 17 * 128 + 1, 128))

    def conv_acc(j):
        oc, gc, n = cblocks[j]
        par = j % 2
        for ct in range(2):
            pc_, pck = ((pA, "pA"), (pB, "pB"))[ct] if par == 0 else ((pF, "pF"), (pG, "pG"))[ct]
            for k in range(31):
                kb.mm(pc_[:, 0:n], Dg[:, ct, k, :], GLU[:, ct, gc + k:gc + k + n], k == 0, k == 30,
                      r=["Dg"] + ["GLU%d" % i for i in range((gc + k) // 128, (gc + k + n - 1) // 128 + 1)], w=[pck])
            kb.act(xc[par][:, ct, 0:n], pc_[:, 0:n], AF.Identity, r=[pck, "dwb"], w=["xc%d" % par], bias=small["dwb"][:, ct:ct + 1])

    def conv_ln(j):
        oc, gc, n = cblocks[j]
        par = j % 2
        x_, xk = xc[par], "xc%d" % par
        kb.tt("pool", sq[:, :, 0:n], x_[:, :, 0:n], x_[:, :, 0:n], ALU.mult, r=[xk], w=["sq"])
        for ct in range(2):
            kb.mm(pC[:, 0:n], onesf, x_[:, ct, 0:n], ct == 0, ct == 1, r=["onesf", xk], w=["pC"])
        for ct in range(2):
            kb.mm(pD[:, 0:n], onesf, sq[:, ct, 0:n], ct == 0, ct == 1, r=["onesf", "sq"], w=["pD"])
        kb.act(mean[:, 0:n], pC[:, 0:n], AF.Copy, r=["pC"], w=["mean"], scale=1.0 / 256)
        kb.tt("dve", var[:, 0:n], mean[:, 0:n], mean[:, 0:n], ALU.mult, r=["mean"], w=["var"])
        kb.stt("dve", var[:, 0:n], pD[:, 0:n], 1.0 / 256, var[:, 0:n], ALU.mult, ALU.subtract, r=["pD", "var"], w=["var"])
        kb.act(var[:, 0:n], var[:, 0:n], AF.Sqrt, r=["var"], w=["var"], bias=EPS_LN)
        kb.recip(var[:, 0:n], var[:, 0:n], r=["var"], w=["var"])
        for ct in range(2):
            kb.tt("dve", t1[:, ct, 0:n], x_[:, ct, 0:n], mean[:, 0:n], ALU.subtract, r=[xk, "mean"], w=["t1"])
            kb.tt("pool", t1[:, ct, 0:n], t1[:, ct, 0:n], var[:, 0:n], ALU.mult, r=["t1", "var"], w=["t1"])
            kb.act(zs[:, ct, 0:n], t1[:, ct, 0:n], AF.Silu, r=["t1", "lng", "lnb"], w=["zs"],
                   bias=small["lnb"][:, ct:ct + 1], scale=small["lng"][:, ct:ct + 1])

    def conv_pw(j):
        oc, gc, n = cblocks[j]
        for dt_ in range(2):
            for ct in range(2):
                kb.mm(pE[:, 0:n], pww_sb[:, ct, dt_ * 128:(dt_ + 1) * 128], zs[:, ct, 0:n], ct == 0, ct == 1, r=["pww", "zs"], w=["pE"])
            kb.copy("act", yT[:, 6 + dt_, oc:oc + n], pE[:, 0:n], r=["pE"], w=["yT%d" % (6 + dt_)])

    ncb = len(cblocks)
    conv_acc(0)
    for j in range(ncb):
        if j + 1 < ncb:
            conv_acc(j + 1)
        conv_ln(j)
        conv_pw(j)
    kb.phase_end()

    kb.phase_begin()
    wout_sb = kb.sbt("wout", [128, 8, D], BF16)
    for kt in range(8):
        kb.dma("sp", wout_sb[:, kt, :], w_out[kt * 128:(kt + 1) * 128, :], r=[wout_key], w=["wout"])
    fwo_sb = kb.sbt("fwo", [128, 22, D], BF16)
    for fc in range(22):
        kb.dma("act", fwo_sb[:, fc, :], fwo[fc * 128:(fc + 1) * 128, :], r=[fwo_key], w=["fwo"])
    h0 = [kb.sbt("h0_%d" % i, [128, D]) for i in range(1)]
    h1 = kb.sbt("h1", [128, 4, D])
    u2T = kb.sbt("u2T", [128, 8, 512], BF16)
    actT = kb.sbt("actT", [128, 22, 512], BF16)
    wch = [kb.sbt("wch%d" % i, [128, 8, 256], BF16) for i in range(5)]
    sgf = [kb.sbt("sgf%d" % i, [128, 512]) for i in range(2)]
    tmpo = [kb.sbt("tmpo%d" % i, [128, 512]) for i in range(2)]
    obuf = [kb.sbt("obuf%d" % i, [128, D]) for i in range(1)]
    fwi_v = fwi.rearrange("(kt p) n -> p kt n", p=128)
    ykeys = ["yT%d" % k for k in range(8)]
    groups = [list(range(g * 4, min(g * 4 + 4, NT))) for g in range((NT + 3) // 4)]
    for grp in groups:
        n = len(grp) * 128
        for li, ti in enumerate(grp):
            r_idx = 1 if ti == 16 else 0
            g_bc = Gbc[r_idx]
            hb = kb.rot("h0", 1)
            hr0 = (lat0 + ti * 128) if ti < 16 else ctx0
            kb.dma("sp", h0[hb], hsrc[hr0:hr0 + 128, :], r=[hkey], w=["h0_%d" % hb])
            for nh in range(2):
                pp = pA if nh == 0 else pB
                pk = "pA" if nh == 0 else "pB"
                for kt in range(8):
                    kb.mm(pp, yT[:, kt, ti * 128:(ti + 1) * 128], wout_sb[:, kt, nh * 512:(nh + 1) * 512], kt == 0, kt == 7,
                          r=["wout", ykeys[kt]], w=[pk])
                tb = kb.rot("tmpo", 2)
                kb.tt("dve", tmpo[tb], pp, g_bc[:, nh * 512:(nh + 1) * 512], ALU.mult, r=[pk, "Gbc%d" % r_idx], w=["tmpo%d" % tb])
                kb.tt("pool", h1[:, li, nh * 512:(nh + 1) * 512], tmpo[tb], h0[hb][:, nh * 512:(nh + 1) * 512], ALU.add,
                      r=["tmpo%d" % tb, "h0_%d" % hb], w=["h1_%d" % li])
            norm_transpose_sb(kb, h1[:, li, :], "h1_%d" % li, u2T[:, :, li * 128:(li + 1) * 128], SC2, SH2, r_idx, ident_b,
                              ("pT", pT), "n1", "u2T")
        pF = kb.ps("pF", [128, 512]); pG = kb.ps("pG", [128, 512])
        for fc in range(22):
            wb = kb.rot("wch", 5)
            (pg_, pgk), (pu_, puk) = ((pC, "pC"), (pD, "pD")) if fc % 2 == 0 else ((pF, "pF"), (pG, "pG"))
            kb.dma("sp", wch[wb][:, :, 0:128], fwi_v[:, :, fc * 128:(fc + 1) * 128], r=[fwi_key], w=["wch%d" % wb])
            kb.dma("sp", wch[wb][:, :, 128:256], fwi_v[:, :, DFF + fc * 128:DFF + (fc + 1) * 128], r=[fwi_key], w=["wch%d" % wb])
            for kt in range(8):
                kb.mm(pg_[:, 0:n], wch[wb][:, kt, 0:128], u2T[:, kt, 0:n], kt == 0, kt == 7, r=["wch%d" % wb, "u2T"], w=[pgk])
            for kt in range(8):
                kb.mm(pu_[:, 0:n], wch[wb][:, kt, 128:256], u2T[:, kt, 0:n], kt == 0, kt == 7, r=["wch%d" % wb, "u2T"], w=[puk])
            sb_ = kb.rot("sgf", 2)
            kb.act(sgf[sb_][:, 0:n], pg_[:, 0:n], AF.Silu, r=[pgk], w=["sgf%d" % sb_])
            kb.tt("dve", actT[:, fc, 0:n], sgf[sb_][:, 0:n], pu_[:, 0:n], ALU.mult, r=["sgf%d" % sb_, puk], w=["actT"])
        for li, ti in enumerate(grp):
            r_idx = 1 if ti == 16 else 0
            g_bc = Gbc[r_idx]
            ob = kb.rot("obuf", 1)
            for nh in range(2):
                pp = pA if nh == 0 else pB
                pk = "pA" if nh == 0 else "pB"
                for fc in range(22):
                    kb.mm(pp, actT[:, fc, li * 128:(li + 1) * 128], fwo_sb[:, fc, nh * 512:(nh + 1) * 512], fc == 0, fc == 21,
                          r=["actT", "fwo"], w=[pk])
                tb = kb.rot("tmpo", 2)
                kb.tt("dve", tmpo[tb], pp, g_bc[:, 1024 + nh * 512:1024 + (nh + 1) * 512], ALU.mult,
                      r=[pk, "Gbc%d" % r_idx], w=["tmpo%d" % tb])
                kb.tt("pool", obuf[ob][:, nh * 512:(nh + 1) * 512], tmpo[tb], h1[:, li, nh * 512:(nh + 1) * 512], ALU.add,
                      r=["tmpo%d" % tb, "h1_%d" % li], w=["obuf%d" % ob])
            if dst_is_final:
                od0 = g * 2048 + ti * 128
            else:
                od0 = (lat0 + ti * 128) if ti < 16 else ctx0
            kb.dma("sp", hdst[od0:od0 + 128, :], obuf[ob], r=["obuf%d" % ob], w=[hdkey])
    kb.phase_end()
    kb.scope_end()

import math

NTK = 34
TTK = NTK * 128


def c_of_pos(d, pos):
    if d == 0:
        return pos
    return 1 - pos if pos < 2 else 35 - pos


def emit_M(kb, l, p, hsrc, hkey, yint, ykey, modc_d, mkey, uTd, hook=None):
    layer0 = (l == 0)
    lam_init = 0.8 - 0.6 * math.exp(-0.3 * l)
    stop_after = None
    kb.scope_begin("M%d%d" % (l, p), "L%d" % l)
    nc = kb.nc
    ccol = kb.inp("ccol", [128, 8, 2], lvl="global")
    modw = kb.inp("modw", [D, 6 * D], lvl="layer")
    modb_col = kb.inp("modb_col", [128, 48], lvl="layer")[:, 0:16]
    n1g = kb.inp("n1g", [128, 8], lvl="layer")
    w_fm = kb.inp("w_fm", [D, 512])
    w_tm = kb.inp("w_tm", [D, 528])
    gate_b = kb.inp("gate_b", [8])
    mng = kb.inp("mng", [64], lvl="layer")
    qkng = kb.inp("qkng", [128, 2], lvl="layer")
    lamqk = kb.inp("lamqk", [2, 2, 32], lvl="layer")
    sublng = kb.inp("sublng", [64, 1], lvl="layer")
    identf_d = kb.inp("identf", [128, 128], lvl="global")
    tri_d = kb.inp("tri", [128, 2, 128], lvl="global")
    mask_d = kb.inp("mask", [128, 2, 128], lvl="global")
    bones_d = kb.inp("bones", [128, 128], lvl="global")
    perm_d = kb.inp("perm", [128, 128], lvl="global")
    mcol_d = kb.inp("mcol", [128, 4], lvl="global")
    permB_d = kb.inp("permB", [68, 68], lvl="global")
    sgn_d = kb.inp("sgn", [2, 64], lvl="global")
    cos_d = kb.inp("cost", [128, TTK], lvl="global")
    sin_d = kb.inp("sint", [128, TTK], lvl="global")
    selZ_d0 = kb.inp("selZ", [128, 64], lvl="global")

    identf = kb.sb("identf", [128, 128]); kb.dma("sp", identf, identf_d, w=["identf"])
    ident_b = kb.sb("ident_b", [128, 128], BF16); kb.copy("dve", ident_b, identf, r=["identf"], w=["ident_b"])
    onesf = kb.sb("onesf", [128, 128]); kb.memset("pool", onesf, 1.0, w=["onesf"])
    tri = kb.sb("tri", [128, 2, 128]); kb.dma("sp", tri, tri_d, w=["tri"])
    mask = kb.sb("mask", [128, 2, 128]); kb.dma("sp", mask, mask_d, w=["mask"])
    bones = kb.sb("bones", [128, 128], BF16); kb.dma("pool", bones, bones_d, w=["bones"])
    perm = kb.sb("perm", [128, 128], BF16); kb.dma("pool", perm, perm_d, w=["perm"])
    mcol = kb.sb("mcol", [128, 4]); kb.dma("sp", mcol, mcol_d, w=["mcol"])
    permB = kb.sb("permB", [68, 68]); kb.dma("sp", permB, permB_d, w=["permB"])
    sgn = kb.sb("sgn", [2, 64]); kb.dma("sp", sgn, sgn_d, w=["sgn"])
    qkg = kb.sb("qkg", [128, 2]); kb.dma("sp", qkg, qkng, w=["qkg"])
    sublg = kb.sb("sublg", [64, 1]); kb.dma("sp", sublg, sublng, w=["sublg"])
    gbb = kb.sb("gbb", [128, 8]); kb.dma("sp", gbb, gate_b.partition_broadcast(128), w=["gbb"])
    mngb = kb.sb("mngb", [128, 64]); kb.dma("sp", mngb, mng.partition_broadcast(128), w=["mngb"])
    SC1 = kb.sb("SC1", [128, 8, 2]); SH1 = kb.sb("SH1", [128, 8, 2])
    negLam = kb.sb("negLam", [64, 1])
    qT = kb.sb("qT", [64, 2, TTK], BF16); kT = kb.sb("kT", [64, 2, TTK], BF16)
    k_tm = kb.sb("k_tm", [128, NTK, 128], BF16)
    vext = kb.sb("vext", [128, NTK, 2, 72], BF16)
    Vx = kb.sb("Vx", [128, NTK, 2, 128], BF16)
    sigo = kb.sb("sigo", [128, NTK, 128], BF16)
    G = kb.sb("G", [128, NTK, 8])
    QT = kb.sb("QT", [128, TTK], BF16)
    KTm = kb.sb("KTm", [128, 4, TTK], BF16)
    selZ_d = selZ_d0
    selZ = kb.sb("selZ", [128, 64]); kb.dma("sp", selZ, selZ_d, w=["selZ"])
    kb.memset("pool", vext, 1.0, w=["vext"]); kb.memset("pool", Vx, 0.0, w=["Vx"])
    kb.memset("pool", Vx[:, :, :, 64:65], 1.0, w=["Vx"])
    alloc_norm_bufs(kb, "n1")
    pT = kb.ps("pT", [128, 1024], BF16)
    pA = kb.ps("pA", [128, 512]); pB = kb.ps("pB", [128, 512]); pC = kb.ps("pC", [128, 512])
    pD = kb.ps("pD", [128, 512]); pE = kb.ps("pE", [128, 512]); pF = kb.ps("pF", [128, 512]); pG = kb.ps("pG", [128, 512])

    kb.phase_begin()
    mc1 = kb.sbt("mc1", [128, 16, 2])
    kb.dma("sp", mc1, modc_d[:, 0:16, :], r=[mkey], w=["mc1_modc"])
    g1n = kb.sbt("g1n", [128, 8]); kb.dma("sp", g1n, n1g, w=["g1n"])
    for r_ in range(2):
        kb.stt("dve", SC1[:, :, r_], mc1[:, 8:16, r_], 1.0, g1n, ALU.add, ALU.mult, r=["mc1_modc", "g1n"], w=["SCSH"])
        kb.copy("dve", SH1[:, :, r_], mc1[:, 0:8, r_], r=["mc1_modc"], w=["SCSH"])
    lqk = kb.sbt("lqk", [2, 2, 32]); kb.dma("sp", lqk, lamqk, w=["lqk"])
    lpr = kb.sbt("lpr", [2, 2, 32]); lsm = kb.sbt("lsm", [2, 2])
    for j_ in range(2):
        kb.tt("dve", lpr[:, j_, :], lqk[:, 0, :], lqk[:, 1, :], ALU.mult, r=["lqk"], w=["lpr"])
    kb.S.op("dve", lambda e: e.tensor_reduce(out=lsm, in_=lpr, axis=AX.X, op=ALU.add), r=["lpr"], w=["lsm"])
    kb.act(lsm, lsm, AF.Exp, r=["lsm"], w=["lsm"])
    kb.mm(pF[0:64, 0:2], sgn, lsm, True, True, r=["sgn", "lsm"], w=["pF"])
    kb.ts("dve", negLam, pF[0:64, 0:1], -float(lam_init), None, ALU.add, None, r=["pF"], w=["negLam"])

    wfm = kb.sbt("wfm", [128, 8, 512], BF16)
    kb.dma("pool", wfm, w_fm.rearrange("(kt p) n -> p kt n", p=128), w=["wfm"])
    wtm = kb.sbt("wtm", [128, 8, 528], BF16)
    kb.dma("pool", wtm, w_tm.rearrange("(kt p) n -> p kt n", p=128), w=["wtm"])
    if hook is not None:
        hook()
    alloc_norm_xbufs(kb, "n1")
    uTb = [kb.sbt("uTb%d" % i, [128, 8, 512], BF16) for i in range(2)]
    cosb = [kb.sbt("cosb%d" % i, [128, 512]) for i in range(2)]
    sinb = [kb.sbt("sinb%d" % i, [128, 512]) for i in range(2)]
    Ab = kb.sbt("Ab", [128, 512], BF16); sqb = kb.sbt("sqb", [128, 512], BF16)
    rs = kb.sbt("rs", [128, 512]); t1 = kb.sbt("t1", [128, 512]); t2 = kb.sbt("t2", [128, 512])
    blocks = [(0, 256)] + [(256 + 512 * i, 512) for i in range(8)]
    Abq = [kb.sbt("Abq%d" % i, [128, 512], BF16) for i in range(2)]
    sqq = [kb.sbt("sqq%d" % i, [128, 512], BF16) for i in range(2)]

    def stage1(bi):
        c0, n = blocks[bi]
        ub = bi % 2
        uk = "uTb%d" % ub
        kb.dma("sp", cosb[ub][:, 0:n], cos_d[:, c0:c0 + n], w=["cosb%d" % ub])
        kb.dma("sp", sinb[ub][:, 0:n], sin_d[:, c0:c0 + n], w=["sinb%d" % ub])
        if p == 0:
            nq_ = []
            for li in range(n // 128):
                c = c0 // 128 + li
                xi = kb.rot("n1_x", 3)
                xk_ = "n1_x%d" % xi
                kb.dma("sp", kb._bufs[xk_], hsrc[c * 128:(c + 1) * 128, :], r=[hkey], w=[xk_])
                nq_.append(norm_part(kb, kb._bufs[xk_], xk_, "n1"))
            for li in range(n // 128):
                c = c0 // 128 + li
                tr_part(kb, nq_[li][0], nq_[li][1], uTb[ub][:, :, li * 128:(li + 1) * 128], SC1, SH1,
                        1 if c < 2 else 0, ident_b, ("pT", pT), uk)
            kb.dma("sp", uTd[:, :, c0:c0 + n], uTb[ub][:, :, 0:n], r=[uk], w=["uTd%d" % l])
        else:
            kb.dma("sp", uTb[ub][:, :, 0:n], uTd[:, :, c0:c0 + n], r=["uTd%d" % l], w=[uk])
        for which, dst, dk in ((0, qT, "qT"), (1, kT, "kT")):
            for h in range(2):
                pp, pk = (pA, "pA") if h == 0 else (pB, "pB")
                cs = which * 128 + h * 64
                for kt in range(8):
                    kb.mm(pp[0:64, 0:n], wfm[:, kt, cs:cs + 64], uTb[ub][:, kt, 0:n], kt == 0, kt == 7, r=["wfm", uk], w=[pk])
                kb.copy("act" if h == 0 else "dve", dst[:, h, c0:c0 + n], pp[0:64, 0:n], r=[pk], w=[dk])
        for which in (0, 1):
            cs = 256 + which * 128
            for kt in range(8):
                kb.mm(pC[:, 0:n], wfm[:, kt, cs:cs + 128], uTb[ub][:, kt, 0:n], kt == 0, kt == 7, r=["wfm", uk], w=["pC"])
            kb.act(Abq[which][:, 0:n], pC[:, 0:n], AF.Copy, r=["pC", "qkg"], w=["Abq%d_%d" % (which, bi % 2)], scale=qkg[:, which:which + 1])
            kb.act(sqq[which][:, 0:n], pC[:, 0:n], AF.Square, r=["pC"], w=["sqq%d_%d" % (which, bi % 2)])
        for li in range(n // 128):
            c = c0 // 128 + li
            lt = uTb[ub][:, :, li * 128:(li + 1) * 128]
            for kt in range(8):
                kb.mm(pF, lt[:, kt, :], wtm[:, kt, 0:512], kt == 0, kt == 7, r=["wtm", uk], w=["pF"])
            for kt in range(8):
                kb.mm(pG[:, 0:16], lt[:, kt, :], wtm[:, kt, 512:528], kt == 0, kt == 7, r=["wtm", uk], w=["pG"])
            kb.copy("act", k_tm[:, c, :], pF[:, 0:128], r=["pF"], w=["k_tm"])
            kb.copy("act", vext[:, c, :, 0:64], pF[:, 128:256].rearrange("p (h v) -> p h v", h=2), r=["pF"], w=["vext"])
            kb.act(sigo[:, c, :], pF[:, 256:384], AF.Sigmoid, r=["pF"], w=["sigo"])
            kb.copy("act", Vx[:, c, :, 0:64], pF[:, 384:512].rearrange("p (h v) -> p h v", h=2), r=["pF"], w=["Vx"])
            kb.tt("dve", G[:, c, :], pG[:, 0:8], gbb, ALU.add, r=["pG", "gbb"], w=["G"])

    def stage2(bi):
        c0, n = blocks[bi]
        ub = bi % 2
        for which in (0, 1):
            Ab, sqb = Abq[which], sqq[which]
            kb.mm(pD[:, 0:n], bones, sqb[:, 0:n], True, True, r=["bones", "sqq%d_%d" % (which, bi % 2)], w=["pD"])
            kb.mm(pE[:, 0:n], perm, Ab[:, 0:n], True, True, r=["perm", "Abq%d_%d" % (which, bi % 2)], w=["pE"])
            kb.act(rs[:, 0:n], pD[:, 0:n], AF.Sqrt, r=["pD"], w=["rs"], bias=EPS_RMS, scale=1.0 / 32)
            kb.recip(rs[:, 0:n], rs[:, 0:n], r=["rs"], w=["rs"])
            kb.tt("dve", t1[:, 0:n], Ab[:, 0:n], cosb[ub][:, 0:n], ALU.mult, r=["Abq%d_%d" % (which, bi % 2), "cosb%d" % ub], w=["t1"])
            kb.tt("dve", t2[:, 0:n], pE[:, 0:n], sinb[ub][:, 0:n], ALU.mult, r=["pE", "sinb%d" % ub], w=["t2"])
            kb.tt("pool", t1[:, 0:n], t1[:, 0:n], t2[:, 0:n], ALU.add, r=["t1", "t2"], w=["t1"])
            if which == 0:
                kb.tt("pool", QT[:, c0:c0 + n], t1[:, 0:n], rs[:, 0:n], ALU.mult, r=["t1", "rs"], w=["QT"])
            else:
                kb.tt("pool", t2[:, 0:n], t1[:, 0:n], rs[:, 0:n], ALU.mult, r=["t1", "rs", "t2"], w=["t2"])
                for j in range(4):
                    kb.ts("dve" if j % 2 == 0 else "pool", KTm[:, j, c0:c0 + n], t2[:, 0:n], mcol[:, j:j + 1], None, ALU.mult, None,
                          r=["t2", "mcol"], w=["KT"])

    Abq2 = [[Abq[0], Abq[1]], [kb.sbt("Abq2", [128, 512], BF16), kb.sbt("Abq3", [128, 512], BF16)]]
    sqq2 = [[sqq[0], sqq[1]], [kb.sbt("sqq2", [128, 512], BF16), kb.sbt("sqq3", [128, 512], BF16)]]
    for bi in range(len(blocks)):
        Abq[0], Abq[1] = Abq2[bi % 2]
        sqq[0], sqq[1] = sqq2[bi % 2]
        _names = {0: ("Abq0", "Abq1", "sqq0", "sqq1"), 1: ("Abq2", "Abq3", "sqq2", "sqq3")}
        stage1(bi)
        if bi >= 1:
            Abq[0], Abq[1] = Abq2[(bi - 1) % 2]
            sqq[0], sqq[1] = sqq2[(bi - 1) % 2]
            stage2(bi - 1)
    Abq[0], Abq[1] = Abq2[(len(blocks) - 1) % 2]
    sqq[0], sqq[1] = sqq2[(len(blocks) - 1) % 2]
    stage2(len(blocks) - 1)
    kb.phase_end()

    kb.phase_begin()
    Gv = G.rearrange("p c (d g h) -> p c d g h", d=2, g=2, h=2)
    LFn = kb.sbt("LFn", [128, 2, NTK, 2]); bn = kb.sbt("bn", [128, 2, NTK, 2]); A_all = kb.sbt("A_all", [128, 2, NTK, 2])
    tmpg = kb.sbt("tmpg", [128, NTK, 2])
    AmaxCol = kb.sbt("AmaxCol", [68, 1]); BtotCol = kb.sbt("BtotCol", [68, 1]); rep = kb.sbt("rep", [68, 2, 128])
    AB = kb.sbt("AB", [128, 2, NTK, 2]); Mn = kb.sbt("Mn", [128, NTK, 2]); Mprev = kb.sbt("Mprev", [128, NTK, 2])
    Mu = kb.sbt("Mu", [128, 2, NTK, 2]); Ee = kb.sbt("Ee", [128, 2, NTK, 2]); Ecol = kb.sbt("Ecol", [128, 2, NTK])
    Wt = kb.sbt("Wt", [128, 2, NTK, 2]); NBt = kb.sbt("NBt", [128, 2, NTK, 2])
    Wk = kb.sbt("Wk", [128, 2, NTK, 2]); LOW = kb.sbt("LOW", [128, 2, NTK, 2])
    for d in range(2):
        kb.act(tmpg, Gv[:, :, d, 1, :], AF.Exp, r=["G"], w=["tmpg"], scale=-1.0)
        kb.act(LFn[:, d], tmpg, AF.Ln, r=["tmpg"], w=["LFn"], bias=1.0)
        pX = pA[:, 0:68]
        kb.mm(pX, tri[:, d, :], LFn[:, d].rearrange("p c h -> p (c h)"), True, True, r=["tri", "LFn"], w=["pA"])
        kb.copy("dve", bn[:, d].rearrange("p c h -> p (c h)"), pX, r=["pA"], w=["bn"])
        kb.tt("dve", A_all[:, d], Gv[:, :, d, 0, :], pX.rearrange("p (c h) -> p c h", h=2), ALU.add, r=["G", "pA"], w=["A_all"])
        kb.tr(pB[0:68, 0:128], A_all[:, d].rearrange("p c h -> p (c h)"), identf, r=["A_all", "identf"], w=["pB"])
        kb.S.op("dve", lambda e: e.tensor_reduce(out=AmaxCol, in_=pB[0:68, 0:128], axis=AX.X, op=ALU.max), r=["pB"], w=["AmaxCol"])
        kb.tr(pC[0:68, 0:128], bn[:, d].rearrange("p c h -> p (c h)"), identf, r=["bn", "identf"], w=["pC"])
        col = 127 if d == 0 else 0
        kb.copy("dve", BtotCol, pC[0:68, col:col + 1], r=["pC"], w=["BtotCol"])
        kb.ts("dve", rep[:, 0, :], onesf[0:68, :], AmaxCol, None, ALU.mult, None, r=["onesf", "AmaxCol"], w=["rep"])
        kb.ts("dve", rep[:, 1, :], onesf[0:68, :], BtotCol, -1.0, ALU.mult, ALU.mult, r=["onesf", "BtotCol"], w=["rep"])
        pm = identf[0:68, 0:68] if d == 0 else permB
        kb.mm(pD[:, 0:68], rep[:, 0, :], pm, True, True, r=["rep", "permB", "identf"], w=["pD"])
        kb.mm(pD[:, 68:136], rep[:, 1, :], pm, True, True, r=["rep", "permB", "identf"], w=["pD"])
        kb.copy("act", AB.rearrange("p a c h -> p (a c h)"), pD[:, 0:136], r=["pD"], w=["AB"])
        for h in range(2):
            kb.S.op("dve", lambda e, h=h: e.tensor_tensor_scan(out=Mn[:, :, h], data0=AB[:, 0, :, h], data1=AB[:, 1, :, h],
                                                           initial=0.0, op0=ALU.max, op1=ALU.add), r=["AB"], w=["Mn"])
        kb.memset("pool", Mprev[:, 0, :], 0.0, w=["Mprev"])
        kb.copy("dve", Mprev[:, 1:NTK, :], Mn[:, 0:NTK - 1, :], r=["Mn"], w=["Mprev"])
        kb.tt("dve", Mu[:, d], Mprev, AB[:, 0], ALU.max, r=["Mprev", "AB"], w=["Mu"])
        kb.tt("dve", Ee[:, d], Mprev, Mu[:, d], ALU.subtract, r=["Mprev", "Mu"], w=["Ee"])
        kb.act(Ee[:, d], Ee[:, d], AF.Exp, r=["Ee"], w=["Ee"])
        kb.copy("dve", Ecol[0:64, d, :], Ee[0:64, d, :, 0], r=["Ee"], w=["Ecol"])
        kb.copy("dve", Ecol[64:128, d, :], Ee[64:128, d, :, 1], r=["Ee"], w=["Ecol"])
        if d == 0:
            kb.tt("dve", Wt[:, 0], A_all[:, 0], Mu[:, 0], ALU.subtract, r=["A_all", "Mu"], w=["Wt"])
            kb.tt("dve", NBt[:, 0], bn[:, 0], Mu[:, 0], ALU.subtract, r=["bn", "Mu"], w=["NBt"])
        else:
            for pos in range(NTK):
                c = c_of_pos(1, pos)
                kb.tt("dve", Wt[:, 1, c, :], A_all[:, 1, c, :], Mu[:, 1, pos, :], ALU.subtract, r=["A_all", "Mu"], w=["Wt"])
                kb.tt("pool", NBt[:, 1, c, :], bn[:, 1, c, :], Mu[:, 1, pos, :], ALU.subtract, r=["bn", "Mu"], w=["NBt"])
        kb.act(Wk[:, d], Wt[:, d], AF.Exp, r=["Wt"], w=["Wk"], bias=math.log(0.125))
        kb.act(LOW[:, d], NBt[:, d], AF.Exp, r=["NBt"], w=["LOW"])

    hdir = [kb.sbt("hdir%d" % d, [128, NTK, 2, 64]) for d in range(2)]
    ysm = kb.sbt("ysm", [128, TTK], BF16)
    St = [kb.sbt("St%d" % d, [64, 2, 65]) for d in range(2)]
    Stb = [kb.sbt("Stb%d" % d, [64, 2, 72], BF16) for d in range(2)]
    Stmp = [kb.sbt("Stmp%d" % d, [64, 2, 65]) for d in range(2)]
    Pb = [[kb.sbt("Pb%d_%d" % (d, i), [128, 2, 128], BF16) for i in range(2)] for d in range(2)]
    kpb = [[kb.sbt("kpb%d_%d" % (d, i), [128, 2, 64], BF16) for i in range(2)] for d in range(2)]
    absd = [kb.sbt("absd%d" % d, [128, 2]) for d in range(2)]
    rc = [kb.sbt("rc%d" % d, [128, 2]) for d in range(2)]
    pS_ = [pA, pD]; pN_ = [pB, pE]; pU_ = [pC, pF]
    pSk = ["pA", "pD"]; pNk = ["pB", "pE"]; pUk = ["pC", "pF"]
    for d in range(2):
        kb.memset("pool", St[d], 0.0, w=["St%d" % d])
        kb.memset("pool", Stb[d], 0.0, w=["Stb%d" % d])
    for pos in range(NTK):
        for d in range(2):
            c = c_of_pos(d, pos)
            cs = slice(c * 128, (c + 1) * 128)
            i2 = kb.rot("Pb%d" % d, 2)
            P, kp = Pb[d][i2], kpb[d][i2]
            Pk, kpk = "Pb%d_%d" % (d, i2), "kpb%d_%d" % (d, i2)
            pSv = pS_[d].rearrange("p (h t) -> p h t", h=4)[:, 0:2, :]
            pNv = pN_[d].rearrange("p (h t) -> p h t", h=4)[:, 0:2, :]
            pUv = pU_[d].rearrange("p (h t) -> p h t", h=4)[:, 0:2, :]
            for h in range(2):
                kb.mm(pSv[:, h, :], kT[:, h, cs], qT[:, h, cs], True, True, r=["kT", "qT"], w=[pSk[d]])
            for h in range(2):
                kb.stt("dve", P[:, h, :], pSv[:, h, :], Wk[:, d, c, h:h + 1], mask[:, d, :], ALU.mult, ALU.mult,
                       r=[pSk[d], "Wk", "mask"], w=[Pk])
                kb.act(kp[:, h, :], k_tm[:, c, 64 * h:64 * h + 64], AF.Copy, r=["k_tm", "Wk"], w=[kpk], scale=Wk[:, d, c, h:h + 1])
            for h in range(2):
                kb.mm(pNv[:, h, 0:65], P[:, h, :], vext[:, c, h, 0:65], True, False, r=[Pk, "vext"], w=[pNk[d]])
                kb.mm(pNv[:, h, 0:65], qT[:, h, cs], Stb[d][:, h, 0:65], False, True, r=["qT", "Stb%d" % d], w=[pNk[d]])
            if pos < NTK - 1:
                for h in range(2):
                    kb.mm(pUv[0:64, h, 0:65], kp[:, h, :], vext[:, c, h, 0:65], True, True, r=[kpk, "vext"], w=[pUk[d]])
            kb.act(absd[d], pNv[:, :, 64], AF.Abs, r=[pNk[d]], w=["absd%d" % d])
            kb.tt("dve", absd[d], absd[d], LOW[:, d, c, :], ALU.max, r=["absd%d" % d, "LOW"], w=["absd%d" % d])
            kb.recip(rc[d], absd[d], r=["absd%d" % d], w=["rc%d" % d])
            kb.tt("dve", hdir[d][:, c], pNv[:, :, 0:64], rc[d].unsqueeze(2).to_broadcast([128, 2, 64]), ALU.mult,
                  r=[pNk[d], "rc%d" % d], w=["hdir%d_%d" % (d, c)])
            if pos < NTK - 1:
                ebc = Ee[0:64, d, pos + 1, :].unsqueeze(2).to_broadcast([64, 2, 65])
                kb.tt("dve", Stmp[d], St[d], pUv[0:64, :, 0:65], ALU.add, r=["St%d" % d, pUk[d]], w=["Stmp%d" % d])
                kb.tt("dve", St[d], Stmp[d], ebc, ALU.mult, r=["Stmp%d" % d, "Ee"], w=["St%d" % d])
                kb.tt("pool", Stb[d][:, :, 0:65], Stmp[d], ebc, ALU.mult, r=["Stmp%d" % d, "Ee"], w=["Stb%d" % d])
    hs = [kb.sbt("hs%d" % i, [128, 2, 64]) for i in range(2)]
    hsq = [kb.sbt("hsq%d" % i, [128, 2, 64]) for i in range(2)]
    ssq = [kb.sbt("ssq%d" % i, [128, 2]) for i in range(2)]
    mob = [kb.sbt("mob%d" % i, [128, 128], BF16) for i in range(2)]
    for c in range(NTK):
        i2 = c % 2
        cs = slice(c * 128, (c + 1) * 128)
        kb.tt("dve", hs[i2], hdir[0][:, c], hdir[1][:, c], ALU.add, r=["hdir0_%d" % c, "hdir1_%d" % c], w=["hs%d" % i2])
        kb.tt("pool", hsq[i2], hs[i2], hs[i2], ALU.mult, r=["hs%d" % i2], w=["hsq%d" % i2])
        kb.S.op("dve", lambda e, i2=i2: e.tensor_reduce(out=ssq[i2], in_=hsq[i2], axis=AX.X, op=ALU.add), r=["hsq%d" % i2], w=["ssq%d" % i2])
        kb.act(ssq[i2], ssq[i2], AF.Sqrt, r=["ssq%d" % i2], w=["ssq%d" % i2], bias=EPS_RMS, scale=1.0 / 64)
        kb.recip(ssq[i2], ssq[i2], r=["ssq%d" % i2], w=["ssq%d" % i2])
        kb.tt("dve", hs[i2], hs[i2], ssq[i2].unsqueeze(2).to_broadcast([128, 2, 64]), ALU.mult, r=["hs%d" % i2, "ssq%d" % i2], w=["hs%d" % i2])
        kb.tt("pool", hs[i2], hs[i2], mngb.unsqueeze(1).to_broadcast([128, 2, 64]), ALU.mult, r=["hs%d" % i2, "mngb"], w=["hs%d" % i2])
        kb.tt("dve", mob[i2], hs[i2].rearrange("p h v -> p (h v)"), sigo[:, c, :], ALU.mult, r=["hs%d" % i2, "sigo"], w=["mob%d" % i2])
        kb.tr(pT[:, 0:128], mob[i2], ident_b, r=["mob%d" % i2, "ident_b"], w=["pT"])
        kb.copy("act", ysm[:, cs], pT[:, 0:128], r=["pT"], w=["ysm"])
    kb.dma("sp", yint[p * 128:(p + 1) * 128, :], ysm, r=["ysm"], w=[ykey])
    kb.phase_end()

    kb.phase_begin()
    ysa = kb.sbt("ysa", [64, 2, TTK], BF16)
    Eb = [[kb.sbt("Eb%d_%d" % (m, i), [128, 512], BF16) for i in range(2)] for m in range(2)]
    Osb = [kb.sbt("Osb%d" % m, [128, 512]) for m in range(2)]
    for m in range(2):
        kb.memset("pool", Osb[m], 0.0, w=["Osb%d" % m])
    rz = [kb.sbt("rz%d" % m, [64, 512]) for m in range(2)]
    oT = kb.sbt("oT", [64, 512]); sqo = kb.sbt("sqo", [64, 512]); rso = kb.sbt("rso", [64, 512])
    scale = 32 ** -0.5
    qblocks = ([(0, 256, [0, 1])] if layer0 else []) + [(256 + 512 * i, 512, list(range(NTK))) for i in range(8)]
    psS = [[(pA, "pA"), (pC, "pC")], [(pB, "pB"), (pD, "pD")]]
    psO = [(pF, "pF"), (pG, "pG")]
    pending = None
    for h in range(2):
        for (q0, nq, kts) in qblocks:
            def emit_S(i):
                par = i % 2
                ks = slice(kts[i] * 128, (kts[i] + 1) * 128)
                for m in range(2):
                    pp, pk = psS[m][par]
                    kb.mm(pp[:, 0:nq], KTm[:, 2 * h + m, ks], QT[:, q0:q0 + nq], True, True, r=["KT", "QT"], w=[pk])

            def emit_E(i):
                par = i % 2
                for m in range(2):
                    pp, pk = psS[m][par]
                    kb.act(Eb[m][par][:, 0:nq], pp[:, 0:nq], AF.Exp, r=[pk], w=["Eb%d_%d" % (m, par)], scale=scale)

            def emit_PV(i):
                par = i % 2
                for m in range(2):
                    po, pok = psO[m]
                    kb.mm(po[:, 0:nq], Vx[:, kts[i], h, :], Eb[m][par][:, 0:nq], i == 0, i == len(kts) - 1,
                          r=["Vx", "Eb%d_%d" % (m, par)], w=[pok])

            emit_S(0)
            for i in range(len(kts)):
                emit_E(i)
                if i + 1 < len(kts):
                    emit_S(i + 1)
                emit_PV(i)
                if pending and i == min(1, len(kts) - 1):
                    pending[0]()
                if pending and i == min(8, len(kts) - 1):
                    pending[1]()
                    pending = None
            for m in range(2):
                po, pok = psO[m]
                kb.copy("dve", Osb[m][0:65, 0:nq], po[0:65, 0:nq], r=[pok], w=["Osb%d" % m])

            def fin_a(h=h, q0=q0, nq=nq):
                for m in range(2):
                    kb.mm(pE[0:64, 0:nq], selZ, Osb[m][:, 0:nq], True, True, r=["selZ", "Osb%d" % m], w=["pE"])
                    kb.recip(rz[m][:, 0:nq], pE[0:64, 0:nq], r=["pE"], w=["rz%d" % m])
                    kb.tt("pool", rz[m][:, 0:nq], Osb[m][0:64, 0:nq], rz[m][:, 0:nq], ALU.mult, r=["Osb%d" % m, "rz%d" % m], w=["rz%d" % m])
                kb.stt("dve", oT[:, 0:nq], rz[1][:, 0:nq], negLam, rz[0][:, 0:nq], ALU.mult, ALU.add, r=["rz0", "rz1", "negLam"], w=["oT"])
                kb.tt("pool", sqo[:, 0:nq], oT[:, 0:nq], oT[:, 0:nq], ALU.mult, r=["oT"], w=["sqo"])

            def fin_b(h=h, q0=q0, nq=nq):
                kb.mm(pE[0:64, 0:nq], onesf[0:64, 0:64], sqo[:, 0:nq], True, True, r=["onesf", "sqo"], w=["pE"])
                kb.act(rso[:, 0:nq], pE[0:64, 0:nq], AF.Ln, r=["pE"], w=["rso"], bias=EPS_RMS, scale=1.0 / 64)
                kb.act(rso[:, 0:nq], rso[:, 0:nq], AF.Exp, r=["rso"], w=["rso"], scale=-0.5)
                kb.tt("dve", oT[:, 0:nq], oT[:, 0:nq], rso[:, 0:nq], ALU.mult, r=["oT", "rso"], w=["oT"])
                kb.ts("dve", ysa[:, h, q0:q0 + nq], oT[:, 0:nq], sublg, 1.0 - float(lam_init), ALU.mult, ALU.mult, r=["oT", "sublg"], w=["ysa"])

            pending = (fin_a, fin_b)
    if pending:
        pending[0]()
        pending[1]()
    if not layer0:
        kb.memset("pool", ysa[:, :, 0:256], 0.0, w=["ysa"])
    for h in range(2):
        kb.dma("sp", yint[256 + p * 128 + 64 * h:256 + p * 128 + 64 * h + 64, :], ysa[:, h, :], r=["ysa"], w=[ykey])
    kb.phase_end()
    kb.scope_end()


def build_fused():
    kb = KB()
    hfull = kb.inp("hfull", [TTK, D], lvl="global")
    hful1 = kb.internal("hful1", [TTK, D])
    hout = kb.outp("hout", [4096, D])
    yint = [kb.internal("yint%d" % l, [512, TTK], BF16) for l in range(2)]
    kb.ps("pT", [128, 1024], BF16)
    for nm in "ABCDEFG":
        kb.ps("p" + nm, [128, 512])
    modc_d = [kb.internal("modc%d" % l, [128, 48, 2]) for l in range(2)]
    grow_d = [kb.internal("grow%d" % l, [2, 2048]) for l in range(2)]
    uTd = [kb.internal("uTd%d" % l, [128, 8, TTK], BF16) for l in range(2)]
    wb16 = []
    for l in range(2):
        d = {}
        for nm, shape in (("w_pc", [D, 768]), ("w_out", [D, D]), ("fwo", [DFF, D]), ("fwi", [D, 2 * DFF])):
            d[nm] = (kb.internal("L%d_%s_b16" % (l, nm), shape, BF16), "L%d_%s_b16" % (l, nm))
        wb16.append(d)

    def convert_weights(l):
        save = dict(kb.pfx)
        if True:
            kb.pfx = {"inst": "", "layer": "L%d_" % l, "global": ""}
            for nm, shape, rows in (("w_pc", [D, 768], 256), ("w_out", [D, D], 256), ("fwo", [DFF, D], 256), ("fwi", [D, 2 * DFF], 128)):
                src = kb.inp(nm, shape, lvl="layer")
                dst, key = wb16[l][nm]
                for r0 in range(0, shape[0], rows):
                    kb.dma("pool", dst[r0:r0 + rows, :], src[r0:r0 + rows, :], w=[key], bg=True)
        kb.pfx = save

    for l in range(2):
        hsrc, hkey = (hfull, "hfull") if l == 0 else (hful1, "hful1")
        convert_weights(l)
        emit_mod(kb, l, modc_d[l], grow_d[l], "mod%d" % l)
        for p in range(2):
            emit_M(kb, l, p, hsrc, hkey, yint[l], "yint%d" % l, modc_d[l], "mod%d" % l, uTd[l],
                   hook=None)
        for g in range(2):
            emit_F(kb, l, g, hsrc, hkey, yint[l], "yint%d" % l, hful1 if l == 0 else hout, "hful1" if l == 0 else "hout", l == 1,
                   wb16[l], modc_d[l], grow_d[l], "mod%d" % l, uTd[l])
    kb.S.emit()
    return kb

import numpy as np, ml_dtypes
BF = ml_dtypes.bfloat16
S_, CT, D, DFF = 4096, 256, 1024, 2816
POOL_WINDOWS = (2, 4, 8, 16)

def col_layout(v, n):
    return np.ascontiguousarray(v.reshape(n, 128).T)

def band_mats(base, p0, L, w):
    B = np.zeros((256, 128), np.float32)
    for t in range(128):
        p = p0 + t
        lo = min(max(p - w // 2, 0), L - 1); hi = min(max(p + (w - 1 - w // 2), 0), L - 1)
        inv = np.float32(1.0) / np.float32(hi - lo + 1)
        for q in range(lo, hi + 1):
            B[q - base, t] += inv
        B[p - base, t] -= 1.0
    return B

def prep_F(inp, l, h, hc, yma_lat, yma_ctx, b, g):
    layer0 = (l == 0)
    start, cstart = g * 2048, g * 128
    m = {}
    rows = [h[b, start:start + 2048]]
    if layer0: rows.append(hc[b, cstart:cstart + 128])
    m["hrows"] = np.ascontiguousarray(np.concatenate(rows, 0))
    def padded(src, s0, n, L):
        out = np.zeros((n, D), np.float32); msk = np.zeros((n,), np.float32)
        lo, hi = max(s0, 0), min(s0 + n, L)
        out[lo - s0:hi - s0] = src[lo:hi]; msk[lo - s0:hi - s0] = 1.0
        return out, msk
    hp, mk = padded(h[b], start - 16, 2176, S_)
    hps, mks = [hp], [mk]
    if layer0:
        cp, cm = padded(hc[b], cstart - 16, 256, CT)
        hps.append(cp); mks.append(cm)
    m["hpad"] = np.ascontiguousarray(np.concatenate(hps, 0)); m["maskp"] = np.concatenate(mks, 0)
    ys = [yma_lat[b, start:start + 2048]]
    if layer0: ys.append(yma_ctx[b, cstart:cstart + 128])
    m["yTma"] = np.ascontiguousarray(np.concatenate(ys, 0).T).astype(BF)
    m["ccol"] = np.ascontiguousarray(np.stack([col_layout(inp["c"][b], 8), col_layout(inp["c_ctx"], 8)], axis=-1))
    m["modw"] = inp["mod_w"][l]
    m["modb_col"] = col_layout(inp["mod_b"][l], 48)
    m["modb_row"] = inp["mod_b"][l]
    m["n1g"] = col_layout(inp["norm1_g"][l], 8); m["n2g"] = col_layout(inp["norm2_g"][l], 8)
    m["w_pc"] = np.ascontiguousarray(inp["w_in"][l][:, 1808:2576])
    bands = np.zeros((128, 32, 128), np.float32)
    cls_list = [(0, S_, start), (1, S_, start), (15, S_, start)]
    for cls in range(4):
        for gi, w in enumerate(POOL_WINDOWS):
            if cls < 3:
                ot = cls_list[cls][0]
                base = start - 16 + ot * 128; p0 = start + ot * 128; L = S_
            else:
                base = cstart - 16; p0 = cstart; L = CT
            B = band_mats(base, p0, L, w)
            bands[:, (cls * 4 + gi) * 2 + 0, :] = B[0:128]; bands[:, (cls * 4 + gi) * 2 + 1, :] = B[128:256]
    m["bands"] = bands
    pw2 = np.zeros((64, 4, 128), np.float32)
    for gi in range(4):
        pw2[:, gi, (gi % 2) * 64:(gi % 2) * 64 + 64] = inp["pool_w"][l][gi]
    m["poolw2"] = pw2
    m["pscale"] = col_layout(inp["pool_scale"][l], 2)
    m["dww"] = np.ascontiguousarray(inp["conv_dw_w"][l].T.reshape(2, 128, 31).transpose(1, 0, 2))
    m["dwb"] = col_layout(inp["conv_dw_b"][l], 2); m["lng"] = col_layout(inp["conv_ln_g"][l], 2); m["lnb"] = col_layout(inp["conv_ln_b"][l], 2)
    m["pww"] = inp["conv_pw_w"][l]; m["w_out"] = inp["w_out"][l]; m["fwi"] = inp["ffn_w_in"][l]; m["fwo"] = inp["ffn_w_out"][l]
    m["identf"] = np.eye(128, dtype=np.float32)
    sel2 = np.zeros((2, 2, 128), np.float32); sel2[0, 0] = 1; sel2[1, 1] = 1
    m["sel2"] = sel2
    return {k: np.ascontiguousarray(v) for k, v in m.items()}

NTK = 34
def rope_tables():
    TT = NTK * 128
    cos = np.ones((64, TT), np.float32); sin = np.zeros((64, TT), np.float32)
    tok = np.arange(S_)
    rows = (tok // 64).astype(np.float32); cols = (tok % 64).astype(np.float32)
    freqs = (np.float32(10000.0) ** (-np.arange(8, dtype=np.float32) / np.float32(8))).astype(np.float32)
    for m in range(2):
        for d in range(32):
            axis, half, f = d // 16, (d % 16) // 8, d % 8
            ang = (rows if axis == 0 else cols) * freqs[f]
            cos[m * 32 + d, 256:] = np.cos(ang.astype(np.float32))
            sin[m * 32 + d, 256:] = np.sin(ang.astype(np.float32)) * (-1.0 if half == 0 else 1.0)
    return cos, sin

_CONST = {}
def consts_M():
    if _CONST: return _CONST
    c = {}
    c["identf"] = np.eye(128, dtype=np.float32)
    r = np.arange(128)
    tri = np.zeros((128, 2, 128), np.float32)
    tri[:, 0, :] = (r[:, None] <= r[None, :]); tri[:, 1, :] = (r[:, None] >= r[None, :])
    c["tri"] = tri; c["mask"] = tri.copy()
    bo = np.zeros((128, 128), np.float32)
    for j in range(4):
        bo[32 * j:32 * j + 32, 32 * j:32 * j + 32] = 1
    c["bones"] = bo
    mc = np.zeros((128, 4), np.float32)
    for j in range(4):
        mc[32 * j:32 * j + 32, j] = 1
    c["mcol"] = mc
    pm = np.zeros((128, 128), np.float32)
    for d in range(128):
        dd = d % 32
        partner = d + 8 if (dd % 16) < 8 else d - 8
        pm[partner, d] = 1.0
    c["perm"] = pm
    pB = np.zeros((68, 68), np.float32)
    for pos in range(34):
        cc = 1 - pos if pos < 2 else 35 - pos
        for h in range(2):
            pB[cc * 2 + h, pos * 2 + h] = 1.0
    c["permB"] = pB
    sg = np.zeros((2, 64), np.float32); sg[0] = -1; sg[1] = 1
    c["sgn"] = sg
    sz = np.zeros((128, 64), np.float32); sz[64, :] = 1.0
    c["selZ"] = sz
    ct_, st_ = rope_tables()
    c["cost"], c["sint"] = np.concatenate([ct_, ct_], 0), np.concatenate([st_, st_], 0)
    _CONST.update(c)
    return _CONST

def prep_M(inp, l, h, hc, b, g):
    m = dict(consts_M())
    m["hfull"] = np.ascontiguousarray(np.concatenate([hc[b], h[b]], 0))
    m["ccol"] = np.ascontiguousarray(np.stack([col_layout(inp["c"][b], 8), col_layout(inp["c_ctx"], 8)], axis=-1))
    m["modw"] = np.ascontiguousarray(inp["mod_w"][l][:, :2048])
    m["modb_col"] = col_layout(inp["mod_b"][l][:2048], 16)
    m["n1g"] = col_layout(inp["norm1_g"][l], 8)
    W = inp["w_in"][l]
    hp = slice(g * 128, (g + 1) * 128)
    mq, mk, mv, mo = W[:, 0:256][:, hp], W[:, 256:512][:, hp], W[:, 512:768][:, hp], W[:, 768:1024][:, hp]
    gates = W[:, 1024:1040].reshape(D, 2, 2, 4)[:, :, :, 2 * g:2 * g + 2].reshape(D, 8)
    aq, ak, av = W[:, 1040:1296][:, hp], W[:, 1296:1552][:, hp], W[:, 1552:1808][:, hp]
    m["w_fm"] = np.concatenate([mq, mk, aq, ak], 1)
    m["w_tm"] = np.concatenate([mk, mv, mo, av, gates, np.zeros((D, 8), np.float32)], 1)
    m["gate_b"] = inp["mlstm_gate_b"][l][:, :, 2 * g:2 * g + 2].reshape(8)
    m["mng"] = inp["mlstm_norm_g"][l]
    m["qkng"] = np.stack([np.tile(inp["attn_q_norm_g"][l], 2), np.tile(inp["attn_k_norm_g"][l], 2)], 1)
    m["lamqk"] = np.stack([inp["lambda_q"][l], inp["lambda_k"][l]], 1)
    m["sublng"] = inp["attn_subln_g"][l].reshape(64, 1)
    return {k: np.ascontiguousarray(v, dtype=np.float32) for k, v in m.items()}


def prep_fused(inp, b):
    x, ctx = inp["x"], inp["ctx"]
    m = dict(consts_M())
    m["hfull"] = np.concatenate([ctx[b], x[b]], 0)
    m["ccol"] = np.stack([col_layout(inp["c"][b], 8), col_layout(inp["c_ctx"], 8)], axis=-1)
    sel2 = np.zeros((2, 2, 128), np.float32); sel2[0, 0] = 1; sel2[1, 1] = 1
    m["sel2"] = sel2
    for g in range(2):
        start, cstart = g * 2048, g * 128
        mk = np.zeros((19 * 128,), np.float32)
        pos = start - 16 + np.arange(2176); mk[:2176] = ((pos >= 0) & (pos < S_))
        cpos = cstart - 16 + np.arange(256); mk[2176:] = ((cpos >= 0) & (cpos < CT))
        m["maskp%d" % g] = mk
        bands = np.zeros((128, 32, 128), np.float32)
        for cls in range(4):
            for gi, w in enumerate(POOL_WINDOWS):
                if cls < 3:
                    ot = (0, 1, 15)[cls]
                    base = start - 16 + ot * 128; p0 = start + ot * 128; L = S_
                else:
                    base = cstart - 16; p0 = cstart; L = CT
                B = band_mats(base, p0, L, w)
                bands[:, (cls * 4 + gi) * 2 + 0, :] = B[0:128]; bands[:, (cls * 4 + gi) * 2 + 1, :] = B[128:256]
        m["bands%d" % g] = bands
    for l in range(2):
        L_ = "L%d_" % l
        m[L_ + "modw"] = inp["mod_w"][l]
        m[L_ + "modb_col"] = col_layout(inp["mod_b"][l], 48)
        m[L_ + "modb_row"] = inp["mod_b"][l]
        m[L_ + "n1g"] = col_layout(inp["norm1_g"][l], 8); m[L_ + "n2g"] = col_layout(inp["norm2_g"][l], 8)
        W = inp["w_in"][l]
        m[L_ + "w_pc"] = W[:, 1808:2576]
        pw2 = np.zeros((64, 4, 128), np.float32)
        for gi in range(4):
            pw2[:, gi, (gi % 2) * 64:(gi % 2) * 64 + 64] = inp["pool_w"][l][gi]
        m[L_ + "poolw2"] = pw2
        m[L_ + "pscale"] = col_layout(inp["pool_scale"][l], 2)
        m[L_ + "dww"] = inp["conv_dw_w"][l].T.reshape(2, 128, 31).transpose(1, 0, 2)
        m[L_ + "dwb"] = col_layout(inp["conv_dw_b"][l], 2); m[L_ + "lng"] = col_layout(inp["conv_ln_g"][l], 2)
        m[L_ + "lnb"] = col_layout(inp["conv_ln_b"][l], 2)
        m[L_ + "pww"] = inp["conv_pw_w"][l]; m[L_ + "w_out"] = inp["w_out"][l]
        m[L_ + "fwi"] = inp["ffn_w_in"][l]; m[L_ + "fwo"] = inp["ffn_w_out"][l]
        m[L_ + "mng"] = inp["mlstm_norm_g"][l]
        m[L_ + "qkng"] = np.stack([np.tile(inp["attn_q_norm_g"][l], 4), np.tile(inp["attn_k_norm_g"][l], 4)], 1)
        m[L_ + "lamqk"] = np.stack([inp["lambda_q"][l], inp["lambda_k"][l]], 1)
        m[L_ + "sublng"] = inp["attn_subln_g"][l].reshape(64, 1)
        for p in range(2):
            P_ = "M%d%d_" % (l, p)
            hp = slice(p * 128, (p + 1) * 128)
            mq, mk_, mv, mo = W[:, 0:256][:, hp], W[:, 256:512][:, hp], W[:, 512:768][:, hp], W[:, 768:1024][:, hp]
            gates = W[:, 1024:1040].reshape(D, 2, 2, 4)[:, :, :, 2 * p:2 * p + 2].reshape(D, 8)
            aq, ak, av = W[:, 1040:1296][:, hp], W[:, 1296:1552][:, hp], W[:, 1552:1808][:, hp]
            m[P_ + "w_fm"] = np.concatenate([mq, mk_, aq, ak], 1)
            m[P_ + "w_tm"] = np.concatenate([mk_, mv, mo, av, gates, np.zeros((D, 8), np.float32)], 1)
            m[P_ + "gate_b"] = inp["mlstm_gate_b"][l][:, :, 2 * p:2 * p + 2].reshape(8)
    return {k: np.ascontiguousarray(v, dtype=np.float32) for k, v in m.items()}

_PROG = {}


def kernel(**inputs):
    inp = {k: np.asarray(v) for k, v in inputs.items()}
    if "kb" not in _PROG:
        _PROG["kb"] = build_fused()
    kb = _PROG["kb"]
    real = {0: 0, 1: 1, 4: 2, 5: 3}
    maps = [None] * 8
    for c, b in real.items():
        maps[c] = prep_fused(inp, b)
    zero = {k: np.zeros_like(v) for k, v in maps[0].items()}
    for c in range(8):
        if maps[c] is None:
            maps[c] = zero
    res = run_bass_kernel_spmd(kb.nc, maps, core_ids=list(range(8)))
    out = np.stack([np.asarray(res.results[c]["hout"]) for c in (0, 1, 4, 5)], 0)
    return out.astype(np.float32)
```

```python
import math
import numpy as np
import concourse.bass as bass
import concourse.mybir as mybir
from concourse.bass_utils import run_bass_kernel_spmd

F32 = mybir.dt.float32
BF16 = mybir.dt.bfloat16
ALU = mybir.AluOpType
AF = mybir.ActivationFunctionType
AX = mybir.AxisListType

COMPUTE = ("pe", "act", "dve", "pool")
N_DMA_SEMS = 12
N_BG_SEMS = 24


class _Op:
    __slots__ = ("eng", "fn", "deps", "is_dma", "eidx", "marked", "sem", "semval", "prev_on_sem", "bg")

    def __init__(self, eng, fn, deps, is_dma):
        self.eng, self.fn, self.deps, self.is_dma = eng, fn, deps, is_dma
        self.marked = False
        self.bg = False
        self.sem = None
        self.semval = 0
        self.prev_on_sem = None


class Sched:
    def __init__(self, nc, same_engine_sync=True):
        self.nc = nc
        self.ops = []
        self.last_w = {}
        self.readers = {}
        self.same_engine_sync = same_engine_sync
        self.engs = {"pe": nc.tensor, "act": nc.scalar, "dve": nc.vector, "pool": nc.gpsimd, "sp": nc.sync}
        self.ecount = {e: 0 for e in self.engs}
        self._bar_tile = nc.alloc_sbuf_tensor("bar_tile", [128, 8], F32).ap()

    def barrier(self):
        self.op("pool", lambda e: e.memset(self._bar_tile, 0.0), r=(), w=["__phase__"])

    def op(self, eng, fn, r=(), w=(), dma=False):
        r = list(r) + ["__phase__"]
        deps = set()
        for k in r:
            if k in self.last_w:
                deps.add(self.last_w[k])
        for k in w:
            if k in self.last_w:
                deps.add(self.last_w[k])
            deps.update(self.readers.get(k, ()))
        idx = len(self.ops)
        o = _Op(eng, fn, deps, dma)
        o.eidx = self.ecount[eng]
        self.ecount[eng] += 1
        self.ops.append(o)
        for k in r:
            self.readers.setdefault(k, []).append(idx)
        for k in w:
            self.last_w[k] = idx
            self.readers[k] = []
        return idx

    def dma(self, eng, out, in_, r=(), w=(), bg=False):
        i = self.op(eng, lambda e: e.dma_start(out=out, in_=in_), r, w, dma=True)
        self.ops[i].bg = bg
        return i

    def emit(self, final_wait_eng="sp"):
        nc = self.nc
        import os
        if os.environ.get("TRUNC"):
            self.ops = self.ops[:int(os.environ["TRUNC"])]
        ops = self.ops
        known = {e: {f: -1 for f in COMPUTE} for e in self.engs}
        dma_known = {e: {} for e in self.engs}
        waits = [None] * len(ops)
        dma_pool = {e: [] for e in self.engs}
        dma_rr = {e: 0 for e in self.engs}
        dma_last = {}
        for i, o in enumerate(ops):
            wl = []
            for d in sorted(o.deps):
                p = ops[d]
                if p.is_dma:
                    if dma_known[o.eng].get(p.sem, 0) < p.semval:
                        dma_known[o.eng][p.sem] = p.semval
                        wl.append(("dma", d))
                else:
                    if p.eng == o.eng and (o.eng == 'pe' or not self.same_engine_sync) and not o.is_dma:
                        continue
                    if known[o.eng][p.eng] < p.eidx:
                        known[o.eng][p.eng] = p.eidx
                        wl.append(("eng", d))
            best = {}
            for kind, d in wl:
                if kind == "eng":
                    e = ops[d].eng
                    if e not in best or ops[best[e]].eidx < ops[d].eidx:
                        best[e] = d
            bestd = {}
            for kind, d in wl:
                if kind == "dma":
                    sl = ops[d].sem
                    if sl not in bestd or ops[bestd[sl]].semval < ops[d].semval:
                        bestd[sl] = d
            wl = [("dma", d) for d in bestd.values()] + [("eng", d) for d in best.values()]
            if o.is_dma:
                qn = o.eng + ("_bg" if o.bg else "")
                dma_rr.setdefault(qn, 0)
                slot = dma_rr[qn] % (N_BG_SEMS if o.bg else N_DMA_SEMS)
                dma_rr[qn] += 1
                o.sem = (qn, slot)
                prev = dma_last.get(o.sem)
                o.prev_on_sem = prev
                o.semval = (ops[prev].semval if prev is not None else 0) + 16
                dma_last[o.sem] = i
                if prev is not None and dma_known[o.eng].get(o.sem, 0) < ops[prev].semval:
                    wl.append(("dma", prev))
                    dma_known[o.eng][o.sem] = ops[prev].semval
            for kind, d in wl:
                if kind == "eng":
                    ops[d].marked = True
            waits[i] = wl
        cnt = {e: 0 for e in COMPUTE}
        ordinal = {}
        for i, o in enumerate(ops):
            if not o.is_dma and o.marked:
                cnt[o.eng] += 1
                ordinal[i] = cnt[o.eng]
        esem = {e: nc.alloc_semaphore("sem_" + e) for e in COMPUTE}
        dsem = {}
        for o in ops:
            if o.is_dma and o.sem not in dsem:
                dsem[o.sem] = nc.alloc_semaphore("dsem_%s_%d" % o.sem)
        for i, o in enumerate(ops):
            eng = self.engs[o.eng]
            for kind, d in waits[i]:
                p = ops[d]
                if kind == "dma":
                    eng.wait_ge(dsem[p.sem], p.semval)
                else:
                    eng.wait_ge(esem[p.eng], ordinal[d])
            ins = o.fn(eng)
            if o.is_dma:
                ins.then_inc(dsem[o.sem], 16)
            elif o.marked:
                ins.then_inc(esem[o.eng], 1)
        fe = self.engs[final_wait_eng]
        for key, i in dma_last.items():
            fe.wait_ge(dsem[key], ops[i].semval)
        self.stats = {e: self.ecount[e] for e in self.engs}
        self.stats["marked"] = dict(cnt)


EPS_RMS = 1e-6
EPS_LN = 1e-5
D = 1024
DFF = 2816


class KB:
    def __init__(self):
        self.nc = bass.Bass("TRN2", target_bir_lowering=False)
        self.S = Sched(self.nc)
        self.rr = {}
        self.dram = {}
        self.pfx = {"inst": "", "layer": "", "global": ""}
        self.uid = 0
        self.scope = None
        self.psums = {}

    def inp(self, name, shape, dt=F32, lvl="inst"):
        full = self.pfx[lvl] + name
        if full not in self.dram:
            self.dram[full] = self.nc.dram_tensor(full, list(shape), dt, kind="ExternalInput").ap()
        return self.dram[full]

    def outp(self, name, shape, dt=F32):
        if name not in self.dram:
            self.dram[name] = self.nc.dram_tensor(name, list(shape), dt, kind="ExternalOutput").ap()
        return self.dram[name]

    def internal(self, name, shape, dt=F32):
        if name not in self.dram:
            self.dram[name] = self.nc.dram_tensor(name, list(shape), dt).ap()
        return self.dram[name]

    def scope_begin(self, inst, layer):
        from contextlib import ExitStack
        self.scope = ExitStack()
        self.pfx = {"inst": inst + "_", "layer": layer + "_", "global": ""}
        self.rr = {}
        self._bufs = {}

    def scope_end(self):
        self.S.barrier()
        self.scope.close()
        self.scope = None

    def sb(self, name, shape, dt=F32):
        self.uid += 1
        return self.scope.enter_context(self.nc.sbuf_tensor("s%d_%s" % (self.uid, name), list(shape), dt)).ap()

    def phase_begin(self):
        from contextlib import ExitStack
        self.stack = ExitStack()

    def sbt(self, name, shape, dt=F32):
        self.uid += 1
        return self.stack.enter_context(self.nc.sbuf_tensor("s%d_%s" % (self.uid, name), list(shape), dt)).ap()

    def phase_end(self):
        self.S.barrier()
        self.stack.close()
        self.stack = None

    def ps(self, name, shape, dt=F32):
        if name not in self.psums:
            self.psums[name] = self.nc.alloc_psum_tensor("p_" + name, list(shape), dt).ap()
        return self.psums[name]

    def rot(self, name, n):
        i = self.rr.get(name, 0)
        self.rr[name] = i + 1
        return i % n

    def dma(self, eng, out, in_, r=(), w=(), bg=False):
        self.S.dma(eng, out, in_, r=r, w=w, bg=bg)

    def mm(self, out, lhsT, rhs, start, stop, r, w):
        self.S.op("pe", lambda e: e.matmul(out, lhsT=lhsT, rhs=rhs, start=start, stop=stop), r=r, w=w)

    def tr(self, out, in_, ident, r, w):
        self.S.op("pe", lambda e: e.transpose(out, in_, ident), r=r, w=w)

    def act(self, out, in_, func, r, w, bias=0.0, scale=1.0, accum=None, eng="act"):
        if accum is None:
            self.S.op(eng, lambda e: e.activation(out=out, in_=in_, func=func, bias=bias, scale=scale), r=r, w=w)
        else:
            self.S.op(eng, lambda e: e.activation(out=out, in_=in_, func=func, bias=bias, scale=scale, accum_out=accum), r=r, w=w)

    def tt(self, eng, out, in0, in1, op, r, w):
        self.S.op(eng, lambda e: e.tensor_tensor(out=out, in0=in0, in1=in1, op=op), r=r, w=w)

    def ts(self, eng, out, in0, s1, s2, op0, op1, r, w):
        if s2 is None:
            self.S.op(eng, lambda e: e.tensor_scalar(out=out, in0=in0, scalar1=s1, scalar2=None, op0=op0), r=r, w=w)
        else:
            self.S.op(eng, lambda e: e.tensor_scalar(out=out, in0=in0, scalar1=s1, scalar2=s2, op0=op0, op1=op1), r=r, w=w)

    def stt(self, eng, out, in0, scalar, in1, op0, op1, r, w):
        self.S.op(eng, lambda e: e.scalar_tensor_tensor(out=out, in0=in0, scalar=scalar, in1=in1, op0=op0, op1=op1), r=r, w=w)

    def copy(self, eng, out, in_, r, w):
        if eng == "act":
            self.S.op("act", lambda e: e.copy(out=out, in_=in_), r=r, w=w)
        else:
            self.S.op(eng, lambda e: e.tensor_copy(out=out, in_=in_), r=r, w=w)

    def recip(self, out, in_, r, w):
        self.S.op("dve", lambda e: e.reciprocal(out=out, in_=in_), r=r, w=w)

    def memset(self, eng, ap, val, w):
        self.S.op(eng, lambda e: e.memset(ap, val), w=w)


def mod_columns(kb, modw, modb_col, ccol, col0, nchunks, name, stg, psm):
    sc = kb.sbt(name + "_sc", [128, 8, 2])
    kb.dma("sp", sc, ccol, w=[name + "_sc"])
    kb.act(sc, sc, AF.Silu, r=[name + "_sc"], w=[name + "_sc"])
    mb = kb.sbt(name + "_mb", [128, nchunks])
    kb.dma("sp", mb, modb_col, w=[name + "_mb"])
    for jg in range(nchunks // 4):
        b = kb.rot("mstg", 2)
        kb.dma("sp", stg[b].rearrange("p (kt n) -> p kt n", kt=8),
               modw[:, col0 + jg * 512:col0 + (jg + 1) * 512].rearrange("(kt p) n -> p kt n", p=128), w=["mstg%d" % b])
        for jj in range(4):
            j = jg * 4 + jj
            for kt in range(8):
                kb.mm(psm[:, j, :], stg[b][:, kt * 512 + jj * 128:kt * 512 + (jj + 1) * 128], sc[:, kt, :], kt == 0, kt == 7,
                      r=["mstg%d" % b, name + "_sc"], w=["psm"])
    modc = kb.sbt(name + "_modc", [128, nchunks, 2])
    for r_ in range(2):
        kb.tt("dve", modc[:, :, r_], psm[:, :, r_], mb, ALU.add, r=["psm", name + "_mb"], w=[name + "_modc"])
    return modc


def emit_mod(kb, l, modc_d, grow_d, mkey):
    kb.scope_begin("MOD%d" % l, "L%d" % l)
    ccol = kb.inp("ccol", [128, 8, 2], lvl="global")
    modw = kb.inp("modw", [D, 6 * D], lvl="layer")
    modb_col = kb.inp("modb_col", [128, 48], lvl="layer")
    modb_row = kb.inp("modb_row", [6 * D], lvl="layer")
    pA = kb.ps("pA", [128, 512]); pB = kb.ps("pB", [128, 512]); pC = kb.ps("pC", [128, 512])
    pD = kb.ps("pD", [128, 512]); pE = kb.ps("pE", [128, 512])
    kb.phase_begin()
    mstg = [kb.sbt("mstg%d" % i, [128, 4096]) for i in range(2)]
    psm = pE.rearrange("p (c r) -> p c r", r=2)[:, 0:16, :]
    for i in range(3):
        mc = mod_columns(kb, modw, modb_col[:, 16 * i:16 * (i + 1)], ccol, 2048 * i, 16, "mc%d" % i, mstg, psm)
        kb.dma("sp", modc_d[:, 16 * i:16 * (i + 1), :], mc, r=["mc%d_modc" % i], w=[mkey])
    scr = kb.sbt("scr", [128, 8, 2])
    kb.dma("sp", scr, ccol, w=["scr"])
    kb.act(scr, scr, AF.Silu, r=["scr"], w=["scr"])
    grow = kb.sbt("grow", [2, 2048])
    gb2 = kb.sbt("gb2", [2, 2048])
    kb.dma("sp", gb2[:, 0:1024], modb_row[2 * D:3 * D].partition_broadcast(2), w=["gb2"])
    kb.dma("sp", gb2[:, 1024:2048], modb_row[5 * D:6 * D].partition_broadcast(2), w=["gb2"])
    psr_l = [pA, pB, pC, pD]; psr_k = ["pA", "pB", "pC", "pD"]
    for kt in range(8):
        b = kt % 2
        kb.dma("sp", mstg[b][:, 0:1024], modw[kt * 128:(kt + 1) * 128, 2 * D:3 * D], w=["mstg%d" % b])
        kb.dma("sp", mstg[b][:, 1024:2048], modw[kt * 128:(kt + 1) * 128, 5 * D:6 * D], w=["mstg%d" % b])
        for j in range(4):
            kb.mm(psr_l[j][0:2, :], scr[:, kt, :], mstg[b][:, j * 512:(j + 1) * 512], kt == 0, kt == 7,
                  r=["mstg%d" % b, "scr"], w=[psr_k[j]])
    for j in range(4):
        kb.tt("dve", grow[:, j * 512:(j + 1) * 512], psr_l[j][0:2, :], gb2[:, j * 512:(j + 1) * 512], ALU.add,
              r=[psr_k[j], "gb2"], w=["grow"])
    kb.dma("sp", grow_d, grow, r=["grow"], w=[mkey])
    kb.phase_end()
    kb.scope_end()


def norm_transpose_tile(kb, x_dram_rows, uT_out, SC, SH, r_idx, ident_b, pst, tag, okey):
    i = kb.rot(tag + "_x", 3)
    xk = "%s_x%d" % (tag, i)
    x = kb._bufs[xk]
    kb.dma("sp", x, x_dram_rows, w=[xk])
    return norm_transpose_sb(kb, x, xk, uT_out, SC, SH, r_idx, ident_b, pst, tag, okey)


NXH = 4


def norm_part(kb, x, xk, tag):
    j = kb.rot(tag + "_s", NXH)
    junk, ss, xh = kb._bufs[tag + "_junk"], kb._bufs["%s_ss%d" % (tag, j)], kb._bufs["%s_xh%d" % (tag, j)]
    ssk, xhk = "%s_ss%d" % (tag, j), "%s_xh%d" % (tag, j)
    kb.memset("pool", ss, 0.0, w=[ssk])
    kb.act(junk, x, AF.Square, r=[xk, ssk], w=[tag + "_junk", ssk], accum=ss)
    kb.act(ss, ss, AF.Sqrt, r=[ssk], w=[ssk], bias=EPS_RMS, scale=1.0 / D)
    kb.recip(ss, ss, r=[ssk], w=[ssk])
    kb.ts("dve", xh, x, ss, None, ALU.mult, None, r=[xk, ssk], w=[xhk])
    return xh, xhk


def tr_part(kb, xh, xhk, uT_out, SC, SH, r_idx, ident_b, pst, okey):
    pk, psT = pst
    for kt in range(8):
        kb.tr(psT[:, kt * 128:(kt + 1) * 128], xh[:, kt * 128:(kt + 1) * 128], ident_b, r=[xhk, "ident_b"], w=[pk])
    for kt in range(8):
        eng = "act" if kt % 2 == 0 else "dve"
        if eng == "act":
            kb.act(uT_out[:, kt, :], psT[:, kt * 128:(kt + 1) * 128], AF.Identity, r=[pk, "SCSH"], w=[okey],
                   bias=SH[:, kt, r_idx:r_idx + 1], scale=SC[:, kt, r_idx:r_idx + 1])
        else:
            kb.ts("dve", uT_out[:, kt, :], psT[:, kt * 128:(kt + 1) * 128], SC[:, kt, r_idx:r_idx + 1],
                  SH[:, kt, r_idx:r_idx + 1], ALU.mult, ALU.add, r=[pk, "SCSH"], w=[okey])


def norm_transpose_sb(kb, x, xk, uT_out, SC, SH, r_idx, ident_b, pst, tag, okey):
    xh, xhk = norm_part(kb, x, xk, tag)
    tr_part(kb, xh, xhk, uT_out, SC, SH, r_idx, ident_b, pst, okey)


def alloc_norm_xbufs(kb, tag):
    for i in range(3):
        kb._bufs["%s_x%d" % (tag, i)] = kb.sbt("%s_x%d" % (tag, i), [128, D])


def alloc_norm_bufs(kb, tag):
    kb._bufs[tag + "_junk"] = kb.sb(tag + "_junk", [128, D], BF16)
    for j in range(NXH):
        kb._bufs["%s_ss%d" % (tag, j)] = kb.sb("%s_ss%d" % (tag, j), [128, 1])
        kb._bufs["%s_xh%d" % (tag, j)] = kb.sb("%s_xh%d" % (tag, j), [128, D], BF16)


def load_rows(kb, x, xk, hsrc, hkey, row0, nrows_total):
    lo, hi = max(row0, 0), min(row0 + 128, nrows_total)
    if lo > row0 or hi < row0 + 128:
        kb.memset("pool", x, 0.0, w=[xk])
    if hi > lo:
        kb.dma("sp", x[lo - row0:hi - row0, :], hsrc[lo:hi, :], r=[hkey], w=[xk])


def emit_F(kb, l, g, hsrc, hkey, yint, ykey, hdst, hdkey, dst_is_final, wb16, modc_d, grow_d, mkey, uTd):
    layer0 = (l == 0)
    kb.scope_begin("F%d%d" % (l, g), "L%d" % l)
    nc = kb.nc
    NT = 17 if layer0 else 16
    NP = 19 if layer0 else 17
    TP = NP * 128
    TT = NT * 128
    lat0 = 256 + g * 2048
    ctx0 = g * 128
    maskp = kb.inp("maskp%d" % g, [19 * 128], lvl="global")
    ccol = kb.inp("ccol", [128, 8, 2], lvl="global")
    modw = kb.inp("modw", [D, 6 * D], lvl="layer")
    modb_col = kb.inp("modb_col", [128, 48], lvl="layer")
    modb_row = kb.inp("modb_row", [6 * D], lvl="layer")
    n1g = kb.inp("n1g", [128, 8], lvl="layer")
    n2g = kb.inp("n2g", [128, 8], lvl="layer")
    w_pc, wpc_key = wb16["w_pc"]
    bands = kb.inp("bands%d" % g, [128, 32, 128], lvl="global")
    poolw2 = kb.inp("poolw2", [64, 4, 128], lvl="layer")
    pscale = kb.inp("pscale", [128, 2], lvl="layer")
    dww = kb.inp("dww", [128, 2, 31], lvl="layer")
    dwb = kb.inp("dwb", [128, 2], lvl="layer")
    lng = kb.inp("lng", [128, 2], lvl="layer")
    lnb = kb.inp("lnb", [128, 2], lvl="layer")
    pww = kb.inp("pww", [256, 256], lvl="layer")
    w_out, wout_key = wb16["w_out"]
    fwi, fwi_key = wb16["fwi"]
    fwo, fwo_key = wb16["fwo"]
    identf_d = kb.inp("identf", [128, 128], lvl="global")
    sel2_d = kb.inp("sel2", [2, 2, 128], lvl="global")

    identf = kb.sb("identf", [128, 128])
    kb.dma("sp", identf, identf_d, w=["identf"])
    ident_b = kb.sb("ident_b", [128, 128], BF16)
    kb.copy("dve", ident_b, identf, r=["identf"], w=["ident_b"])
    onesf = kb.sb("onesf", [128, 128])
    kb.memset("pool", onesf, 1.0, w=["onesf"])
    sel2 = kb.sb("sel2", [2, 2, 128])
    kb.dma("sp", sel2, sel2_d, w=["sel2"])
    SC1 = kb.sb("SC1", [128, 8, 2]); SH1 = kb.sb("SH1", [128, 8, 2])
    SC2 = kb.sb("SC2", [128, 8, 2]); SH2 = kb.sb("SH2", [128, 8, 2])
    Gbc = [kb.sb("Gbc%d" % r_, [128, 2048]) for r_ in range(2 if layer0 else 1)]
    yT = kb.sb("yT", [128, 8, TT], BF16)
    alloc_norm_bufs(kb, "n1")
    pT = kb.ps("pT", [128, 1024], BF16)
    pA = kb.ps("pA", [128, 512]); pB = kb.ps("pB", [128, 512]); pC = kb.ps("pC", [128, 512])
    pD = kb.ps("pD", [128, 512]); pE = kb.ps("pE", [128, 512])
    for kt in range(4):
        kb.dma("sp", yT[:, kt, 0:2048], yint[kt * 128:(kt + 1) * 128, lat0:lat0 + 2048], r=[ykey], w=["yT%d" % kt])
        if layer0:
            kb.dma("sp", yT[:, kt, 2048:2176], yint[kt * 128:(kt + 1) * 128, ctx0:ctx0 + 128], r=[ykey], w=["yT%d" % kt])

    kb.phase_begin()
    mc1 = kb.sbt("mc1", [128, 16, 2]); mc2 = kb.sbt("mc2", [128, 16, 2])
    kb.dma("sp", mc1, modc_d[:, 0:16, :], r=[mkey], w=["mc1_modc"])
    kb.dma("sp", mc2, modc_d[:, 24:40, :], r=[mkey], w=["mc2_modc"])
    g1n = kb.sbt("g1n", [128, 8]); g2n = kb.sbt("g2n", [128, 8])
    kb.dma("sp", g1n, n1g, w=["g1n"]); kb.dma("sp", g2n, n2g, w=["g2n"])
    for (mc, gn, SC, SH, nm) in ((mc1, g1n, SC1, SH1, "1"), (mc2, g2n, SC2, SH2, "2")):
        for r_ in range(2):
            kb.stt("dve", SC[:, :, r_], mc[:, 8:16, r_], 1.0, gn, ALU.add, ALU.mult,
                   r=["mc%s_modc" % nm, "g%sn" % nm], w=["SCSH"])
            kb.copy("dve", SH[:, :, r_], mc[:, 0:8, r_], r=["mc%s_modc" % nm], w=["SCSH"])
    grow = kb.sbt("grow", [2, 2048])
    kb.dma("sp", grow, grow_d, r=[mkey], w=["grow"])
    psr_l = [pA, pB, pC, pD]; psr_k = ["pA", "pB", "pC", "pD"]
    for r_ in range(len(Gbc)):
        for j in range(4):
            kb.mm(psr_l[j], sel2[:, r_, :], grow[:, j * 512:(j + 1) * 512], True, True, r=["sel2", "grow"], w=[psr_k[j]])
        for j in range(4):
            kb.copy("act", Gbc[r_][:, j * 512:(j + 1) * 512], psr_l[j], r=[psr_k[j]], w=["Gbc%d" % r_])

    wpc_sb = kb.sbt("wpc", [128, 8, 768], BF16)
    kb.dma("sp", wpc_sb, w_pc.rearrange("(kt p) n -> p kt n", p=128), r=[wpc_key], w=["wpc"])
    pww_sb = kb.sbt("pww", [128, 2, 256], BF16)
    kb.dma("pool", pww_sb, pww.rearrange("(ct p) n -> p ct n", p=128), w=["pww"])
    poolw_sb = kb.sbt("poolw", [64, 4, 128], BF16)
    kb.dma("pool", poolw_sb, poolw2, w=["poolw"])
    bands_sb = kb.sbt("bands", [128, 32, 128])
    kb.dma("sp", bands_sb, bands, w=["bands"])
    small = {}
    for nm, src, shp in (("pscale", pscale, [128, 2]), ("dww", dww, [128, 2, 31]), ("dwb", dwb, [128, 2]),
                         ("lng", lng, [128, 2]), ("lnb", lnb, [128, 2])):
        small[nm] = kb.sbt(nm, shp)
        kb.dma("sp", small[nm], src, w=[nm])
    maskb = kb.sbt("maskb", [128, TP], BF16)
    kb.dma("pool", maskb, maskp[0:TP].partition_broadcast(128), w=["maskb"])
    Dg = kb.sbt("Dg", [128, 2, 31, 128], BF16)
    for ct in range(2):
        for k in range(31):
            if k % 3 == 0:
                kb.act(Dg[:, ct, k, :], identf, AF.Copy, r=["identf", "dww"], w=["Dg"], scale=small["dww"][:, ct, k:k + 1])
            else:
                kb.ts("pool" if k % 3 == 1 else "dve", Dg[:, ct, k, :], identf, small["dww"][:, ct, k:k + 1], None, ALU.mult, None,
                      r=["identf", "dww"], w=["Dg"])
    alloc_norm_xbufs(kb, "n1")
    u_p = kb.sbt("u_p", [128, NP, 256])
    GLU = kb.sbt("GLU", [128, 2, TP], BF16)
    uTt = [kb.sbt("uTt%d" % i, [128, 8, 128], BF16) for i in range(4)]
    sg = kb.sbt("sg", [128, 2, 128]); gl = kb.sbt("gl", [128, 2, 128])
    for i in range(NP):
        r_idx = 0 if i < 17 else 1
        b = i % 4
        if i < 17:
            t0_, tbase, tlen = g * 2048 - 16 + i * 128, 256, 4096
        else:
            t0_, tbase, tlen = g * 128 - 16 + (i - 17) * 128, 0, 256
        lo_, hi_ = max(t0_, 0), min(t0_ + 128, tlen)
        if lo_ > t0_ or hi_ < t0_ + 128:
            kb.memset("pool", uTt[b], 0.0, w=["uTt%d" % b])
        if hi_ > lo_:
            kb.dma("sp", uTt[b][:, :, lo_ - t0_:hi_ - t0_], uTd[:, :, tbase + lo_:tbase + hi_], r=["uTd%d" % l], w=["uTt%d" % b])
        pcv = pA.rearrange("p (c t) -> p c t", c=4)
        for cti in range(4):
            for kt in range(8):
                kb.mm(pcv[:, cti, :], wpc_sb[:, kt, 256 + cti * 128:256 + (cti + 1) * 128], uTt[b][:, kt, :], kt == 0, kt == 7,
                      r=["wpc", "uTt%d" % b], w=["pA"])
        for kt in range(8):
            kb.mm(pB[:, 0:256], uTt[b][:, kt, :], wpc_sb[:, kt, 0:256], kt == 0, kt == 7, r=["wpc", "uTt%d" % b], w=["pB"])
        kb.act(sg, pcv[:, 2:4, :], AF.Sigmoid, r=["pA"], w=["sg"])
        kb.tt("dve", gl, pcv[:, 0:2, :], sg, ALU.mult, r=["pA", "sg"], w=["gl"])
        kb.tt("pool", GLU[:, :, i * 128:(i + 1) * 128], gl, maskb[:, i * 128:(i + 1) * 128].unsqueeze(1).to_broadcast([128, 2, 128]),
              ALU.mult, r=["gl", "maskb"], w=["GLU%d" % i])
        kb.copy("act", u_p[:, i, :], pB[:, 0:256], r=["pB"], w=["u_p%d" % i])

    pF = kb.ps("pF", [128, 512]); pG = kb.ps("pG", [128, 512])
    pooled = [kb.sbt("pooled%d" % i, [64, 4, 128], BF16) for i in range(2)]
    blocks = [(j, j, (0 if j == 0 else (2 if j == 15 else 1))) for j in range(16)]
    if layer0:
        blocks.append((16, 17, 3))
    for bi_, (ot, pt, cls) in enumerate(blocks):
        par = bi_ % 2
        (pp1, pk1), (pp2, pk2) = ((pC, "pC"), (pD, "pD")) if par == 0 else ((pF, "pF"), (pG, "pG"))
        ppv = pp1.rearrange("p (g t) -> p g t", g=4)
        for gi in range(4):
            for half in range(2):
                kb.mm(ppv[0:64, gi, :], u_p[:, pt + half, gi * 64:(gi + 1) * 64], bands_sb[:, (cls * 4 + gi) * 2 + half, :],
                      half == 0, half == 1, r=["u_p%d" % (pt + half), "bands"], w=[pk1])
        kb.copy("act", pooled[par], ppv[0:64, :, :], r=[pk1], w=["pooled%d" % par])
        pqv = pp2.rearrange("p (g t) -> p g t", g=4)
        for pr in range(2):
            for gl_ in range(2):
                gi = pr * 2 + gl_
                kb.mm(pqv[:, pr, :], poolw_sb[:, gi, :], pooled[par][:, gi, :], gl_ == 0, gl_ == 1, r=["poolw", "pooled%d" % par], w=[pk2])
        for pr in range(2):
            kb.ts("dve", yT[:, 4 + pr, ot * 128:(ot + 1) * 128], pqv[:, pr, :], small["pscale"][:, pr:pr + 1], None, ALU.mult, None,
                  r=[pk2, "pscale"], w=["yT%d" % (4 + pr)])

    xc = [kb.sbt("xc%d" % i, [128, 2, 512]) for i in range(2)]
    sq = kb.sbt("sq", [128, 2, 512])
    mean = kb.sbt("mean", [128, 512]); var = kb.sbt("var", [128, 512]); t1 = kb.sbt("t1", [128, 2, 512])
    zs = kb.sbt("zs", [128, 2, 512], BF16)
    cblocks = [(jb * 512, jb * 512 + 1, 512) for jb in range(4)]
    if layer0:
        cblocks.append((16 * 128, 17 * 128 + 1, 128))

    def conv_acc(j):
        oc, gc, n = cblocks[j]
        par = j % 2
        for ct in range(2):
            pc_, pck = ((pA, "pA"), (pB, "pB"))[ct] if par == 0 else ((pF, "pF"), (pG, "pG"))[ct]
            for k in range(31):
                kb.mm(pc_[:, 0:n], Dg[:, ct, k, :], GLU[:, ct, gc + k:gc + k + n], k == 0, k == 30,
                      r=["Dg"] + ["GLU%d" % i for i in range((gc + k) // 128, (gc + k + n - 1) // 128 + 1)], w=[pck])
            kb.act(xc[par][:, ct, 0:n], pc_[:, 0:n], AF.Identity, r=[pck, "dwb"], w=["xc%d" % par], bias=small["dwb"][:, ct:ct + 1])

    def conv_ln(j):
        oc, gc, n = cblocks[j]
        par = j % 2
        x_, xk = xc[par], "xc%d" % par
        kb.tt("pool", sq[:, :, 0:n], x_[:, :, 0:n], x_[:, :, 0:n], ALU.mult, r=[xk], w=["sq"])
        for ct in range(2):
            kb.mm(pC[:, 0:n], onesf, x_[:, ct, 0:n], ct == 0, ct == 1, r=["onesf", xk], w=["pC"])
        for ct in range(2):
            kb.mm(pD[:, 0:n], onesf, sq[:, ct, 0:n], ct == 0, ct == 1, r=["onesf", "sq"], w=["pD"])
        kb.act(mean[:, 0:n], pC[:, 0:n], AF.Copy, r=["pC"], w=["mean"], scale=1.0 / 256)
        kb.tt("dve", var[:, 0:n], mean[:, 0:n], mean[:, 0:n], ALU.mult, r=["mean"], w=["var"])
        kb.stt("dve", var[:, 0:n], pD[:, 0:n], 1.0 / 256, var[:, 0:n], ALU.mult, ALU.subtract, r=["pD", "var"], w=["var"])
        kb.act(var[:, 0:n], var[:, 0:n], AF.Sqrt, r=["var"], w=["var"], bias=EPS_LN)
        kb.recip(var[:, 0:n], var[:, 0:n], r=["var"], w=["var"])
        for ct in range(2):
            kb.tt("dve", t1[:, ct, 0:n], x_[:, ct, 0:n], mean[:, 0:n], ALU.subtract, r=[xk, "mean"], w=["t1"])
            kb.tt("pool", t1[:, ct, 0:n], t1[:, ct, 0:n], var[:, 0:n], ALU.mult, r=["t1", "var"], w=["t1"])
            kb.act(zs[:, ct, 0:n], t1[:, ct, 0:n], AF.Silu, r=["t1", "lng", "lnb"], w=["zs"],
                   bias=small["lnb"][:, ct:ct + 1], scale=small["lng"][:, ct:ct + 1])

    def conv_pw(j):
        oc, gc, n = cblocks[j]
        for dt_ in range(2):
            for ct in range(2):
                kb.mm(pE[:, 0:n], pww_sb[:, ct, dt_ * 128:(dt_ + 1) * 128], zs[:, ct, 0:n], ct == 0, ct == 1, r=["pww", "zs"], w=["pE"])
            kb.copy("act", yT[:, 6 + dt_, oc:oc + n], pE[:, 0:n], r=["pE"], w=["yT%d" % (6 + dt_)])

    ncb = len(cblocks)
    conv_acc(0)
    for j in range(ncb):
        if j + 1 < ncb:
            conv_acc(j + 1)
        conv_ln(j)
        conv_pw(j)
    kb.phase_end()

    kb.phase_begin()
    wout_sb = kb.sbt("wout", [128, 8, D], BF16)
    for kt in range(8):
        kb.dma("sp", wout_sb[:, kt, :], w_out[kt * 128:(kt + 1) * 128, :], r=[wout_key], w=["wout"])
    fwo_sb = kb.sbt("fwo", [128, 22, D], BF16)
    for fc in range(22):
        kb.dma("act", fwo_sb[:, fc, :], fwo[fc * 128:(fc + 1) * 128, :], r=[fwo_key], w=["fwo"])
    h0 = [kb.sbt("h0_%d" % i, [128, D]) for i in range(1)]
    h1 = kb.sbt("h1", [128, 4, D])
    u2T = kb.sbt("u2T", [128, 8, 512], BF16)
    actT = kb.sbt("actT", [128, 22, 512], BF16)
    wch = [kb.sbt("wch%d" % i, [128, 8, 256], BF16) for i in range(5)]
    sgf = [kb.sbt("sgf%d" % i, [128, 512]) for i in range(2)]
    tmpo = [kb.sbt("tmpo%d" % i, [128, 512]) for i in range(2)]
    obuf = [kb.sbt("obuf%d" % i, [128, D]) for i in range(1)]
    fwi_v = fwi.rearrange("(kt p) n -> p kt n", p=128)
    ykeys = ["yT%d" % k for k in range(8)]
    groups = [list(range(g * 4, min(g * 4 + 4, NT))) for g in range((NT + 3) // 4)]
    for grp in groups:
        n = len(grp) * 128
        for li, ti in enumerate(grp):
            r_idx = 1 if ti == 16 else 0
            g_bc = Gbc[r_idx]
            hb = kb.rot("h0", 1)
            hr0 = (lat0 + ti * 128) if ti < 16 else ctx0
            kb.dma("sp", h0[hb], hsrc[hr0:hr0 + 128, :], r=[hkey], w=["h0_%d" % hb])
            for nh in range(2):
                pp = pA if nh == 0 else pB
                pk = "pA" if nh == 0 else "pB"
                for kt in range(8):
                    kb.mm(pp, yT[:, kt, ti * 128:(ti + 1) * 128], wout_sb[:, kt, nh * 512:(nh + 1) * 512], kt == 0, kt == 7,
                          r=["wout", ykeys[kt]], w=[pk])
                tb = kb.rot("tmpo", 2)
                kb.tt("dve", tmpo[tb], pp, g_bc[:, nh * 512:(nh + 1) * 512], ALU.mult, r=[pk, "Gbc%d" % r_idx], w=["tmpo%d" % tb])
                kb.tt("pool", h1[:, li, nh * 512:(nh + 1) * 512], tmpo[tb], h0[hb][:, nh * 512:(nh + 1) * 512], ALU.add,
                      r=["tmpo%d" % tb, "h0_%d" % hb], w=["h1_%d" % li])
            norm_transpose_sb(kb, h1[:, li, :], "h1_%d" % li, u2T[:, :, li * 128:(li + 1) * 128], SC2, SH2, r_idx, ident_b,
                              ("pT", pT), "n1", "u2T")
        pF = kb.ps("pF", [128, 512]); pG = kb.ps("pG", [128, 512])
        for fc in range(22):
            wb = kb.rot("wch", 5)
            (pg_, pgk), (pu_, puk) = ((pC, "pC"), (pD, "pD")) if fc % 2 == 0 else ((pF, "pF"), (pG, "pG"))
            kb.dma("sp", wch[wb][:, :, 0:128], fwi_v[:, :, fc * 128:(fc + 1) * 128], r=[fwi_key], w=["wch%d" % wb])
            kb.dma("sp", wch[wb][:, :, 128:256], fwi_v[:, :, DFF + fc * 128:DFF + (fc + 1) * 128], r=[fwi_key], w=["wch%d" % wb])
            for kt in range(8):
                kb.mm(pg_[:, 0:n], wch[wb][:, kt, 0:128], u2T[:, kt, 0:n], kt == 0, kt == 7, r=["wch%d" % wb, "u2T"], w=[pgk])
            for kt in range(8):
                kb.mm(pu_[:, 0:n], wch[wb][:, kt, 128:256], u2T[:, kt, 0:n], kt == 0, kt == 7, r=["wch%d" % wb, "u2T"], w=[puk])
            sb_ = kb.rot("sgf", 2)
            kb.act(sgf[sb_][:, 0:n], pg_[:, 0:n], AF.Silu, r=[pgk], w=["sgf%d" % sb_])
            kb.tt("dve", actT[:, fc, 0:n], sgf[sb_][:, 0:n], pu_[:, 0:n], ALU.mult, r=["sgf%d" % sb_, puk], w=["actT"])
        for li, ti in enumerate(grp):
            r_idx = 1 if ti == 16 else 0
            g_bc = Gbc[r_idx]
            ob = kb.rot("obuf", 1)
            for nh in range(2):
                pp = pA if nh == 0 else pB
                pk = "pA" if nh == 0 else "pB"
                for fc in range(22):
                    kb.mm(pp, actT[:, fc, li * 128:(li + 1) * 128], fwo_sb[:, fc, nh * 512:(nh + 1) * 512], fc == 0, fc == 21,
                          r=["actT", "fwo"], w=[pk])
                tb = kb.rot("tmpo", 2)
                kb.tt("dve", tmpo[tb], pp, g_bc[:, 1024 + nh * 512:1024 + (nh + 1) * 512], ALU.mult,
                      r=[pk, "Gbc%d" % r_idx], w=["tmpo%d" % tb])
                kb.tt("pool", obuf[ob][:, nh * 512:(nh + 1) * 512], tmpo[tb], h1[:, li, nh * 512:(nh + 1) * 512], ALU.add,
                      r=["tmpo%d" % tb, "h1_%d" % li], w=["obuf%d" % ob])
            if dst_is_final:
                od0 = g * 2048 + ti * 128
            else:
                od0 = (lat0 + ti * 128) if ti < 16 else ctx0
            kb.dma("sp", hdst[od0:od0 + 128, :], obuf[ob], r=["obuf%d" % ob], w=[hdkey])
    kb.phase_end()
    kb.scope_end()

import math

NTK = 34
TTK = NTK * 128


def c_of_pos(d, pos):
    if d == 0:
        return pos
    return 1 - pos if pos < 2 else 35 - pos


def emit_M(kb, l, p, hsrc, hkey, yint, ykey, modc_d, mkey, uTd, hook=None):
    layer0 = (l == 0)
    lam_init = 0.8 - 0.6 * math.exp(-0.3 * l)
    stop_after = None
    kb.scope_begin("M%d%d" % (l, p), "L%d" % l)
    nc = kb.nc
    ccol = kb.inp("ccol", [128, 8, 2], lvl="global")
    modw = kb.inp("modw", [D, 6 * D], lvl="layer")
    modb_col = kb.inp("modb_col", [128, 48], lvl="layer")[:, 0:16]
    n1g = kb.inp("n1g", [128, 8], lvl="layer")
    w_fm = kb.inp("w_fm", [D, 512])
    w_tm = kb.inp("w_tm", [D, 528])
    gate_b = kb.inp("gate_b", [8])
    mng = kb.inp("mng", [64], lvl="layer")
    qkng = kb.inp("qkng", [128, 2], lvl="layer")
    lamqk = kb.inp("lamqk", [2, 2, 32], lvl="layer")
    sublng = kb.inp("sublng", [64, 1], lvl="layer")
    identf_d = kb.inp("identf", [128, 128], lvl="global")
    tri_d = kb.inp("tri", [128, 2, 128], lvl="global")
    mask_d = kb.inp("mask", [128, 2, 128], lvl="global")
    bones_d = kb.inp("bones", [128, 128], lvl="global")
    perm_d = kb.inp("perm", [128, 128], lvl="global")
    mcol_d = kb.inp("mcol", [128, 4], lvl="global")
    permB_d = kb.inp("permB", [68, 68], lvl="global")
    sgn_d = kb.inp("sgn", [2, 64], lvl="global")
    cos_d = kb.inp("cost", [128, TTK], lvl="global")
    sin_d = kb.inp("sint", [128, TTK], lvl="global")
    selZ_d0 = kb.inp("selZ", [128, 64], lvl="global")

    identf = kb.sb("identf", [128, 128]); kb.dma("sp", identf, identf_d, w=["identf"])
    ident_b = kb.sb("ident_b", [128, 128], BF16); kb.copy("dve", ident_b, identf, r=["identf"], w=["ident_b"])
    onesf = kb.sb("onesf", [128, 128]); kb.memset("pool", onesf, 1.0, w=["onesf"])
    tri = kb.sb("tri", [128, 2, 128]); kb.dma("sp", tri, tri_d, w=["tri"])
    mask = kb.sb("mask", [128, 2, 128]); kb.dma("sp", mask, mask_d, w=["mask"])
    bones = kb.sb("bones", [128, 128], BF16); kb.dma("pool", bones, bones_d, w=["bones"])
    perm = kb.sb("perm", [128, 128], BF16); kb.dma("pool", perm, perm_d, w=["perm"])
    mcol = kb.sb("mcol", [128, 4]); kb.dma("sp", mcol, mcol_d, w=["mcol"])
    permB = kb.sb("permB", [68, 68]); kb.dma("sp", permB, permB_d, w=["permB"])
    sgn = kb.sb("sgn", [2, 64]); kb.dma("sp", sgn, sgn_d, w=["sgn"])
    qkg = kb.sb("qkg", [128, 2]); kb.dma("sp", qkg, qkng, w=["qkg"])
    sublg = kb.sb("sublg", [64, 1]); kb.dma("sp", sublg, sublng, w=["sublg"])
    gbb = kb.sb("gbb", [128, 8]); kb.dma("sp", gbb, gate_b.partition_broadcast(128), w=["gbb"])
    mngb = kb.sb("mngb", [128, 64]); kb.dma("sp", mngb, mng.partition_broadcast(128), w=["mngb"])
    SC1 = kb.sb("SC1", [128, 8, 2]); SH1 = kb.sb("SH1", [128, 8, 2])
    negLam = kb.sb("negLam", [64, 1])
    qT = kb.sb("qT", [64, 2, TTK], BF16); kT = kb.sb("kT", [64, 2, TTK], BF16)
    k_tm = kb.sb("k_tm", [128, NTK, 128], BF16)
    vext = kb.sb("vext", [128, NTK, 2, 72], BF16)
    Vx = kb.sb("Vx", [128, NTK, 2, 128], BF16)
    sigo = kb.sb("sigo", [128, NTK, 128], BF16)
    G = kb.sb("G", [128, NTK, 8])
    QT = kb.sb("QT", [128, TTK], BF16)
    KTm = kb.sb("KTm", [128, 4, TTK], BF16)
    selZ_d = selZ_d0
    selZ = kb.sb("selZ", [128, 64]); kb.dma("sp", selZ, selZ_d, w=["selZ"])
    kb.memset("pool", vext, 1.0, w=["vext"]); kb.memset("pool", Vx, 0.0, w=["Vx"])
    kb.memset("pool", Vx[:, :, :, 64:65], 1.0, w=["Vx"])
    alloc_norm_bufs(kb, "n1")
    pT = kb.ps("pT", [128, 1024], BF16)
    pA = kb.ps("pA", [128, 512]); pB = kb.ps("pB", [128, 512]); pC = kb.ps("pC", [128, 512])
    pD = kb.ps("pD", [128, 512]); pE = kb.ps("pE", [128, 512]); pF = kb.ps("pF", [128, 512]); pG = kb.ps("pG", [128, 512])

    kb.phase_begin()
    mc1 = kb.sbt("mc1", [128, 16, 2])
    kb.dma("sp", mc1, modc_d[:, 0:16, :], r=[mkey], w=["mc1_modc"])
    g1n = kb.sbt("g1n", [128, 8]); kb.dma("sp", g1n, n1g, w=["g1n"])
    for r_ in range(2):
        kb.stt("dve", SC1[:, :, r_], mc1[:, 8:16, r_], 1.0, g1n, ALU.add, ALU.mult, r=["mc1_modc", "g1n"], w=["SCSH"])
        kb.copy("dve", SH1[:, :, r_], mc1[:, 0:8, r_], r=["mc1_modc"], w=["SCSH"])
    lqk = kb.sbt("lqk", [2, 2, 32]); kb.dma("sp", lqk, lamqk, w=["lqk"])
    lpr = kb.sbt("lpr", [2, 2, 32]); lsm = kb.sbt("lsm", [2, 2])
    for j_ in range(2):
        kb.tt("dve", lpr[:, j_, :], lqk[:, 0, :], lqk[:, 1, :], ALU.mult, r=["lqk"], w=["lpr"])
    kb.S.op("dve", lambda e: e.tensor_reduce(out=lsm, in_=lpr, axis=AX.X, op=ALU.add), r=["lpr"], w=["lsm"])
    kb.act(lsm, lsm, AF.Exp, r=["lsm"], w=["lsm"])
    kb.mm(pF[0:64, 0:2], sgn, lsm, True, True, r=["sgn", "lsm"], w=["pF"])
    kb.ts("dve", negLam, pF[0:64, 0:1], -float(lam_init), None, ALU.add, None, r=["pF"], w=["negLam"])

    wfm = kb.sbt("wfm", [128, 8, 512], BF16)
    kb.dma("pool", wfm, w_fm.rearrange("(kt p) n -> p kt n", p=128), w=["wfm"])
    wtm = kb.sbt("wtm", [128, 8, 528], BF16)
    kb.dma("pool", wtm, w_tm.rearrange("(kt p) n -> p kt n", p=128), w=["wtm"])
    if hook is not None:
        hook()
    alloc_norm_xbufs(kb, "n1")
    uTb = [kb.sbt("uTb%d" % i, [128, 8, 512], BF16) for i in range(2)]
    cosb = [kb.sbt("cosb%d" % i, [128, 512]) for i in range(2)]
    sinb = [kb.sbt("sinb%d" % i, [128, 512]) for i in range(2)]
    Ab = kb.sbt("Ab", [128, 512], BF16); sqb = kb.sbt("sqb", [128, 512], BF16)
    rs = kb.sbt("rs", [128, 512]); t1 = kb.sbt("t1", [128, 512]); t2 = kb.sbt("t2", [128, 512])
    blocks = [(0, 256)] + [(256 + 512 * i, 512) for i in range(8)]
    Abq = [kb.sbt("Abq%d" % i, [128, 512], BF16) for i in range(2)]
    sqq = [kb.sbt("sqq%d" % i, [128, 512], BF16) for i in range(2)]

    def stage1(bi):
        c0, n = blocks[bi]
        ub = bi % 2
        uk = "uTb%d" % ub
        kb.dma("sp", cosb[ub][:, 0:n], cos_d[:, c0:c0 + n], w=["cosb%d" % ub])
        kb.dma("sp", sinb[ub][:, 0:n], sin_d[:, c0:c0 + n], w=["sinb%d" % ub])
        if p == 0:
            nq_ = []
            for li in range(n // 128):
                c = c0 // 128 + li
                xi = kb.rot("n1_x", 3)
                xk_ = "n1_x%d" % xi
                kb.dma("sp", kb._bufs[xk_], hsrc[c * 128:(c + 1) * 128, :], r=[hkey], w=[xk_])
                nq_.append(norm_part(kb, kb._bufs[xk_], xk_, "n1"))
            for li in range(n // 128):
                c = c0 // 128 + li
                tr_part(kb, nq_[li][0], nq_[li][1], uTb[ub][:, :, li * 128:(li + 1) * 128], SC1, SH1,
                        1 if c < 2 else 0, ident_b, ("pT", pT), uk)
            kb.dma("sp", uTd[:, :, c0:c0 + n], uTb[ub][:, :, 0:n], r=[uk], w=["uTd%d" % l])
        else:
            kb.dma("sp", uTb[ub][:, :, 0:n], uTd[:, :, c0:c0 + n], r=["uTd%d" % l], w=[uk])
        for which, dst, dk in ((0, qT, "qT"), (1, kT, "kT")):
            for h in range(2):
                pp, pk = (pA, "pA") if h == 0 else (pB, "pB")
                cs = which * 128 + h * 64
                for kt in range(8):
                    kb.mm(pp[0:64, 0:n], wfm[:, kt, cs:cs + 64], uTb[ub][:, kt, 0:n], kt == 0, kt == 7, r=["wfm", uk], w=[pk])
                kb.copy("act" if h == 0 else "dve", dst[:, h, c0:c0 + n], pp[0:64, 0:n], r=[pk], w=[dk])
        for which in (0, 1):
            cs = 256 + which * 128
            for kt in range(8):
                kb.mm(pC[:, 0:n], wfm[:, kt, cs:cs + 128], uTb[ub][:, kt, 0:n], kt == 0, kt == 7, r=["wfm", uk], w=["pC"])
            kb.act(Abq[which][:, 0:n], pC[:, 0:n], AF.Copy, r=["pC", "qkg"], w=["Abq%d_%d" % (which, bi % 2)], scale=qkg[:, which:which + 1])
            kb.act(sqq[which][:, 0:n], pC[:, 0:n], AF.Square, r=["pC"], w=["sqq%d_%d" % (which, bi % 2)])
        nt_ = n // 128
        for li in range(nt_):
            c = c0 // 128 + li
            lt = uTb[ub][:, :, li * 128:(li + 1) * 128]
            for kt in range(8):
                kb.mm(pG[:, 16 * li:16 * li + 16], lt[:, kt, :], wtm[:, kt, 512:528], kt == 0, kt == 7, r=["wtm", uk], w=["pG"])
        kb.tt("dve", G[:, c0 // 128:c0 // 128 + nt_, :], pG[:, 0:16 * nt_].rearrange("p (t g) -> p t g", g=16)[:, :, 0:8],
              gbb.unsqueeze(1).to_broadcast([128, nt_, 8]), ALU.add, r=["pG", "gbb"], w=["G"])
        for li in range(nt_):
            c = c0 // 128 + li
            lt = uTb[ub][:, :, li * 128:(li + 1) * 128]
            pf_, pfk = (pF, "pF") if li % 2 == 0 else (pC, "pC")
            for kt in range(8):
                kb.mm(pf_, lt[:, kt, :], wtm[:, kt, 0:512], kt == 0, kt == 7, r=["wtm", uk], w=[pfk])
            kb.copy("act", k_tm[:, c, :], pf_[:, 0:128], r=[pfk], w=["k_tm"])
            kb.copy("act", vext[:, c, :, 0:64], pf_[:, 128:256].rearrange("p (h v) -> p h v", h=2), r=[pfk], w=["vext"])
            kb.act(sigo[:, c, :], pf_[:, 256:384], AF.Sigmoid, r=[pfk], w=["sigo"])
            kb.copy("act", Vx[:, c, :, 0:64], pf_[:, 384:512].rearrange("p (h v) -> p h v", h=2), r=[pfk], w=["Vx"])

    def stage2(bi):
        c0, n = blocks[bi]
        ub = bi % 2
        for which in (0, 1):
            Ab, sqb = Abq[which], sqq[which]
            kb.mm(pD[:, 0:n], bones, sqb[:, 0:n], True, True, r=["bones", "sqq%d_%d" % (which, bi % 2)], w=["pD"])
            kb.mm(pE[:, 0:n], perm, Ab[:, 0:n], True, True, r=["perm", "Abq%d_%d" % (which, bi % 2)], w=["pE"])
            kb.act(rs[:, 0:n], pD[:, 0:n], AF.Sqrt, r=["pD"], w=["rs"], bias=EPS_RMS, scale=1.0 / 32)
            kb.recip(rs[:, 0:n], rs[:, 0:n], r=["rs"], w=["rs"])
            kb.tt("dve", t1[:, 0:n], Ab[:, 0:n], cosb[ub][:, 0:n], ALU.mult, r=["Abq%d_%d" % (which, bi % 2), "cosb%d" % ub], w=["t1"])
            kb.tt("dve", t2[:, 0:n], pE[:, 0:n], sinb[ub][:, 0:n], ALU.mult, r=["pE", "sinb%d" % ub], w=["t2"])
            kb.tt("pool", t1[:, 0:n], t1[:, 0:n], t2[:, 0:n], ALU.add, r=["t1", "t2"], w=["t1"])
            if which == 0:
                kb.tt("pool", QT[:, c0:c0 + n], t1[:, 0:n], rs[:, 0:n], ALU.mult, r=["t1", "rs"], w=["QT"])
            else:
                kb.tt("pool", t2[:, 0:n], t1[:, 0:n], rs[:, 0:n], ALU.mult, r=["t1", "rs", "t2"], w=["t2"])
                for j in range(4):
                    kb.ts("dve" if j % 2 == 0 else "pool", KTm[:, j, c0:c0 + n], t2[:, 0:n], mcol[:, j:j + 1], None, ALU.mult, None,
                          r=["t2", "mcol"], w=["KT"])

    Abq2 = [[Abq[0], Abq[1]], [kb.sbt("Abq2", [128, 512], BF16), kb.sbt("Abq3", [128, 512], BF16)]]
    sqq2 = [[sqq[0], sqq[1]], [kb.sbt("sqq2", [128, 512], BF16), kb.sbt("sqq3", [128, 512], BF16)]]
    for bi in range(len(blocks)):
        Abq[0], Abq[1] = Abq2[bi % 2]
        sqq[0], sqq[1] = sqq2[bi % 2]
        _names = {0: ("Abq0", "Abq1", "sqq0", "sqq1"), 1: ("Abq2", "Abq3", "sqq2", "sqq3")}
        stage1(bi)
        if bi >= 1:
            Abq[0], Abq[1] = Abq2[(bi - 1) % 2]
            sqq[0], sqq[1] = sqq2[(bi - 1) % 2]
            stage2(bi - 1)
    Abq[0], Abq[1] = Abq2[(len(blocks) - 1) % 2]
    sqq[0], sqq[1] = sqq2[(len(blocks) - 1) % 2]
    stage2(len(blocks) - 1)
    kb.phase_end()

    kb.phase_begin()
    Gv = G.rearrange("p c (d g h) -> p c d g h", d=2, g=2, h=2)
    LFn = kb.sbt("LFn", [128, 2, NTK, 2]); bn = kb.sbt("bn", [128, 2, NTK, 2]); A_all = kb.sbt("A_all", [128, 2, NTK, 2])
    tmpg = kb.sbt("tmpg", [128, NTK, 2])
    AmaxCol = kb.sbt("AmaxCol", [68, 1]); BtotCol = kb.sbt("BtotCol", [68, 1]); rep = kb.sbt("rep", [68, 2, 128])
    AB = kb.sbt("AB", [128, 2, NTK, 2]); Mn = kb.sbt("Mn", [128, NTK, 2]); Mprev = kb.sbt("Mprev", [128, NTK, 2])
    Mu = kb.sbt("Mu", [128, 2, NTK, 2]); Ee = kb.sbt("Ee", [128, 2, NTK, 2]); Ecol = kb.sbt("Ecol", [128, 2, NTK])
    Wt = kb.sbt("Wt", [128, 2, NTK, 2]); NBt = kb.sbt("NBt", [128, 2, NTK, 2])
    Wk = kb.sbt("Wk", [128, 2, NTK, 2]); LOW = kb.sbt("LOW", [128, 2, NTK, 2])
    for d in range(2):
        kb.act(tmpg, Gv[:, :, d, 1, :], AF.Exp, r=["G"], w=["tmpg"], scale=-1.0)
        kb.act(LFn[:, d], tmpg, AF.Ln, r=["tmpg"], w=["LFn"], bias=1.0)
        pX = pA[:, 0:68]
        kb.mm(pX, tri[:, d, :], LFn[:, d].rearrange("p c h -> p (c h)"), True, True, r=["tri", "LFn"], w=["pA"])
        kb.copy("dve", bn[:, d].rearrange("p c h -> p (c h)"), pX, r=["pA"], w=["bn"])
        kb.tt("dve", A_all[:, d], Gv[:, :, d, 0, :], pX.rearrange("p (c h) -> p c h", h=2), ALU.add, r=["G", "pA"], w=["A_all"])
        kb.tr(pB[0:68, 0:128], A_all[:, d].rearrange("p c h -> p (c h)"), identf, r=["A_all", "identf"], w=["pB"])
        kb.S.op("dve", lambda e: e.tensor_reduce(out=AmaxCol, in_=pB[0:68, 0:128], axis=AX.X, op=ALU.max), r=["pB"], w=["AmaxCol"])
        kb.tr(pC[0:68, 0:128], bn[:, d].rearrange("p c h -> p (c h)"), identf, r=["bn", "identf"], w=["pC"])
        col = 127 if d == 0 else 0
        kb.copy("dve", BtotCol, pC[0:68, col:col + 1], r=["pC"], w=["BtotCol"])
        kb.ts("dve", rep[:, 0, :], onesf[0:68, :], AmaxCol, None, ALU.mult, None, r=["onesf", "AmaxCol"], w=["rep"])
        kb.ts("dve", rep[:, 1, :], onesf[0:68, :], BtotCol, -1.0, ALU.mult, ALU.mult, r=["onesf", "BtotCol"], w=["rep"])
        pm = identf[0:68, 0:68] if d == 0 else permB
        kb.mm(pD[:, 0:68], rep[:, 0, :], pm, True, True, r=["rep", "permB", "identf"], w=["pD"])
        kb.mm(pD[:, 68:136], rep[:, 1, :], pm, True, True, r=["rep", "permB", "identf"], w=["pD"])
        kb.copy("act", AB.rearrange("p a c h -> p (a c h)"), pD[:, 0:136], r=["pD"], w=["AB"])
        for h in range(2):
            kb.S.op("dve", lambda e, h=h: e.tensor_tensor_scan(out=Mn[:, :, h], data0=AB[:, 0, :, h], data1=AB[:, 1, :, h],
                                                           initial=0.0, op0=ALU.max, op1=ALU.add), r=["AB"], w=["Mn"])
        kb.memset("pool", Mprev[:, 0, :], 0.0, w=["Mprev"])
        kb.copy("dve", Mprev[:, 1:NTK, :], Mn[:, 0:NTK - 1, :], r=["Mn"], w=["Mprev"])
        kb.tt("dve", Mu[:, d], Mprev, AB[:, 0], ALU.max, r=["Mprev", "AB"], w=["Mu"])
        kb.tt("dve", Ee[:, d], Mprev, Mu[:, d], ALU.subtract, r=["Mprev", "Mu"], w=["Ee"])
        kb.act(Ee[:, d], Ee[:, d], AF.Exp, r=["Ee"], w=["Ee"])
        kb.copy("dve", Ecol[0:64, d, :], Ee[0:64, d, :, 0], r=["Ee"], w=["Ecol"])
        kb.copy("dve", Ecol[64:128, d, :], Ee[64:128, d, :, 1], r=["Ee"], w=["Ecol"])
        if d == 0:
            kb.tt("dve", Wt[:, 0], A_all[:, 0], Mu[:, 0], ALU.subtract, r=["A_all", "Mu"], w=["Wt"])
            kb.tt("dve", NBt[:, 0], bn[:, 0], Mu[:, 0], ALU.subtract, r=["bn", "Mu"], w=["NBt"])
        else:
            for pos in range(NTK):
                c = c_of_pos(1, pos)
                kb.tt("dve", Wt[:, 1, c, :], A_all[:, 1, c, :], Mu[:, 1, pos, :], ALU.subtract, r=["A_all", "Mu"], w=["Wt"])
                kb.tt("pool", NBt[:, 1, c, :], bn[:, 1, c, :], Mu[:, 1, pos, :], ALU.subtract, r=["bn", "Mu"], w=["NBt"])
        kb.act(Wk[:, d], Wt[:, d], AF.Exp, r=["Wt"], w=["Wk"], bias=math.log(0.125))
        kb.act(LOW[:, d], NBt[:, d], AF.Exp, r=["NBt"], w=["LOW"])

    hdir = [kb.sbt("hdir%d" % d, [128, NTK, 2, 64]) for d in range(2)]
    ysm = kb.sbt("ysm", [128, TTK], BF16)
    St = [kb.sbt("St%d" % d, [64, 2, 65]) for d in range(2)]
    Stb = [kb.sbt("Stb%d" % d, [64, 2, 72], BF16) for d in range(2)]
    Stmp = [kb.sbt("Stmp%d" % d, [64, 2, 65]) for d in range(2)]
    Pb = [[kb.sbt("Pb%d_%d" % (d, i), [128, 2, 128], BF16) for i in range(2)] for d in range(2)]
    kpb = [[kb.sbt("kpb%d_%d" % (d, i), [128, 2, 64], BF16) for i in range(2)] for d in range(2)]
    absd = [kb.sbt("absd%d" % d, [128, 2]) for d in range(2)]
    rc = [kb.sbt("rc%d" % d, [128, 2]) for d in range(2)]
    pS_ = [pA, pD]; pN_ = [pB, pE]; pU_ = [pC, pF]
    pSk = ["pA", "pD"]; pNk = ["pB", "pE"]; pUk = ["pC", "pF"]
    for d in range(2):
        kb.memset("pool", St[d], 0.0, w=["St%d" % d])
        kb.memset("pool", Stb[d], 0.0, w=["Stb%d" % d])
    for pos in range(NTK):
        for d in range(2):
            c = c_of_pos(d, pos)
            cs = slice(c * 128, (c + 1) * 128)
            i2 = kb.rot("Pb%d" % d, 2)
            P, kp = Pb[d][i2], kpb[d][i2]
            Pk, kpk = "Pb%d_%d" % (d, i2), "kpb%d_%d" % (d, i2)
            pSv = pS_[d].rearrange("p (h t) -> p h t", h=4)[:, 0:2, :]
            pNv = pN_[d].rearrange("p (h t) -> p h t", h=4)[:, 0:2, :]
            pUv = pU_[d].rearrange("p (h t) -> p h t", h=4)[:, 0:2, :]
            for h in range(2):
                kb.mm(pSv[:, h, :], kT[:, h, cs], qT[:, h, cs], True, True, r=["kT", "qT"], w=[pSk[d]])
            for h in range(2):
                kb.stt("dve", P[:, h, :], pSv[:, h, :], Wk[:, d, c, h:h + 1], mask[:, d, :], ALU.mult, ALU.mult,
                       r=[pSk[d], "Wk", "mask"], w=[Pk])
                kb.act(kp[:, h, :], k_tm[:, c, 64 * h:64 * h + 64], AF.Copy, r=["k_tm", "Wk"], w=[kpk], scale=Wk[:, d, c, h:h + 1])
            for h in range(2):
                kb.mm(pNv[:, h, 0:65], P[:, h, :], vext[:, c, h, 0:65], True, False, r=[Pk, "vext"], w=[pNk[d]])
                kb.mm(pNv[:, h, 0:65], qT[:, h, cs], Stb[d][:, h, 0:65], False, True, r=["qT", "Stb%d" % d], w=[pNk[d]])
            if pos < NTK - 1:
                for h in range(2):
                    kb.mm(pUv[0:64, h, 0:65], kp[:, h, :], vext[:, c, h, 0:65], True, True, r=[kpk, "vext"], w=[pUk[d]])
            kb.act(absd[d], pNv[:, :, 64], AF.Abs, r=[pNk[d]], w=["absd%d" % d])
            kb.tt("dve", absd[d], absd[d], LOW[:, d, c, :], ALU.max, r=["absd%d" % d, "LOW"], w=["absd%d" % d])
            kb.recip(rc[d], absd[d], r=["absd%d" % d], w=["rc%d" % d])
            kb.tt("dve", hdir[d][:, c], pNv[:, :, 0:64], rc[d].unsqueeze(2).to_broadcast([128, 2, 64]), ALU.mult,
                  r=[pNk[d], "rc%d" % d], w=["hdir%d_%d" % (d, c)])
            if pos < NTK - 1:
                ebc = Ee[0:64, d, pos + 1, :].unsqueeze(2).to_broadcast([64, 2, 65])
                kb.tt("dve", Stmp[d], St[d], pUv[0:64, :, 0:65], ALU.add, r=["St%d" % d, pUk[d]], w=["Stmp%d" % d])
                kb.tt("dve", St[d], Stmp[d], ebc, ALU.mult, r=["Stmp%d" % d, "Ee"], w=["St%d" % d])
                kb.tt("pool", Stb[d][:, :, 0:65], Stmp[d], ebc, ALU.mult, r=["Stmp%d" % d, "Ee"], w=["Stb%d" % d])
    hs = [kb.sbt("hs%d" % i, [128, 2, 64]) for i in range(2)]
    hsq = [kb.sbt("hsq%d" % i, [128, 2, 64]) for i in range(2)]
    ssq = [kb.sbt("ssq%d" % i, [128, 2]) for i in range(2)]
    mob = [kb.sbt("mob%d" % i, [128, 128], BF16) for i in range(2)]
    for c in range(NTK):
        i2 = c % 2
        cs = slice(c * 128, (c + 1) * 128)
        kb.tt("dve", hs[i2], hdir[0][:, c], hdir[1][:, c], ALU.add, r=["hdir0_%d" % c, "hdir1_%d" % c], w=["hs%d" % i2])
        kb.tt("pool", hsq[i2], hs[i2], hs[i2], ALU.mult, r=["hs%d" % i2], w=["hsq%d" % i2])
        kb.S.op("dve", lambda e, i2=i2: e.tensor_reduce(out=ssq[i2], in_=hsq[i2], axis=AX.X, op=ALU.add), r=["hsq%d" % i2], w=["ssq%d" % i2])
        kb.act(ssq[i2], ssq[i2], AF.Sqrt, r=["ssq%d" % i2], w=["ssq%d" % i2], bias=EPS_RMS, scale=1.0 / 64)
        kb.recip(ssq[i2], ssq[i2], r=["ssq%d" % i2], w=["ssq%d" % i2])
        kb.tt("dve", hs[i2], hs[i2], ssq[i2].unsqueeze(2).to_broadcast([128, 2, 64]), ALU.mult, r=["hs%d" % i2, "ssq%d" % i2], w=["hs%d" % i2])
        kb.tt("pool", hs[i2], hs[i2], mngb.unsqueeze(1).to_broadcast([128, 2, 64]), ALU.mult, r=["hs%d" % i2, "mngb"], w=["hs%d" % i2])
        kb.tt("dve", mob[i2], hs[i2].rearrange("p h v -> p (h v)"), sigo[:, c, :], ALU.mult, r=["hs%d" % i2, "sigo"], w=["mob%d" % i2])
        kb.tr(pT[:, 0:128], mob[i2], ident_b, r=["mob%d" % i2, "ident_b"], w=["pT"])
        kb.copy("act", ysm[:, cs], pT[:, 0:128], r=["pT"], w=["ysm"])
    kb.dma("sp", yint[p * 128:(p + 1) * 128, :], ysm, r=["ysm"], w=[ykey])
    kb.phase_end()

    kb.phase_begin()
    ysa = kb.sbt("ysa", [64, 2, TTK], BF16)
    Eb = [[kb.sbt("Eb%d_%d" % (m, i), [128, 512], BF16) for i in range(2)] for m in range(2)]
    Osb = [kb.sbt("Osb%d" % m, [128, 512]) for m in range(2)]
    for m in range(2):
        kb.memset("pool", Osb[m], 0.0, w=["Osb%d" % m])
    rz = [kb.sbt("rz%d" % m, [64, 512]) for m in range(2)]
    oT = kb.sbt("oT", [64, 512]); sqo = kb.sbt("sqo", [64, 512]); rso = kb.sbt("rso", [64, 512])
    scale = 32 ** -0.5
    qblocks = ([(0, 256, [0, 1])] if layer0 else []) + [(256 + 512 * i, 512, list(range(NTK))) for i in range(8)]
    psS = [[(pA, "pA"), (pC, "pC")], [(pB, "pB"), (pD, "pD")]]
    psO = [(pF, "pF"), (pG, "pG")]
    pending = None
    for h in range(2):
        for (q0, nq, kts) in qblocks:
            def emit_S(i):
                par = i % 2
                ks = slice(kts[i] * 128, (kts[i] + 1) * 128)
                for m in range(2):
                    pp, pk = psS[m][par]
                    kb.mm(pp[:, 0:nq], KTm[:, 2 * h + m, ks], QT[:, q0:q0 + nq], True, True, r=["KT", "QT"], w=[pk])

            def emit_E(i):
                par = i % 2
                for m in range(2):
                    pp, pk = psS[m][par]
                    kb.act(Eb[m][par][:, 0:nq], pp[:, 0:nq], AF.Exp, r=[pk], w=["Eb%d_%d" % (m, par)], scale=scale)

            def emit_PV(i):
                par = i % 2
                for m in range(2):
                    po, pok = psO[m]
                    kb.mm(po[:, 0:nq], Vx[:, kts[i], h, :], Eb[m][par][:, 0:nq], i == 0, i == len(kts) - 1,
                          r=["Vx", "Eb%d_%d" % (m, par)], w=[pok])

            emit_S(0)
            for i in range(len(kts)):
                emit_E(i)
                if i + 1 < len(kts):
                    emit_S(i + 1)
                emit_PV(i)
                if pending and i == min(1, len(kts) - 1):
                    pending[0]()
                if pending and i == min(8, len(kts) - 1):
                    pending[1]()
                    pending = None
            for m in range(2):
                po, pok = psO[m]
                kb.copy("dve", Osb[m][0:65, 0:nq], po[0:65, 0:nq], r=[pok], w=["Osb%d" % m])

            def fin_a(h=h, q0=q0, nq=nq):
                for m in range(2):
                    kb.mm(pE[0:64, 0:nq], selZ, Osb[m][:, 0:nq], True, True, r=["selZ", "Osb%d" % m], w=["pE"])
                    kb.recip(rz[m][:, 0:nq], pE[0:64, 0:nq], r=["pE"], w=["rz%d" % m])
                    kb.tt("pool", rz[m][:, 0:nq], Osb[m][0:64, 0:nq], rz[m][:, 0:nq], ALU.mult, r=["Osb%d" % m, "rz%d" % m], w=["rz%d" % m])
                kb.stt("dve", oT[:, 0:nq], rz[1][:, 0:nq], negLam, rz[0][:, 0:nq], ALU.mult, ALU.add, r=["rz0", "rz1", "negLam"], w=["oT"])
                kb.tt("pool", sqo[:, 0:nq], oT[:, 0:nq], oT[:, 0:nq], ALU.mult, r=["oT"], w=["sqo"])

            def fin_b(h=h, q0=q0, nq=nq):
                kb.mm(pE[0:64, 0:nq], onesf[0:64, 0:64], sqo[:, 0:nq], True, True, r=["onesf", "sqo"], w=["pE"])
                kb.act(rso[:, 0:nq], pE[0:64, 0:nq], AF.Ln, r=["pE"], w=["rso"], bias=EPS_RMS, scale=1.0 / 64)
                kb.act(rso[:, 0:nq], rso[:, 0:nq], AF.Exp, r=["rso"], w=["rso"], scale=-0.5)
                kb.tt("dve", oT[:, 0:nq], oT[:, 0:nq], rso[:, 0:nq], ALU.mult, r=["oT", "rso"], w=["oT"])
                kb.ts("dve", ysa[:, h, q0:q0 + nq], oT[:, 0:nq], sublg, 1.0 - float(lam_init), ALU.mult, ALU.mult, r=["oT", "sublg"], w=["ysa"])

            pending = (fin_a, fin_b)
    if pending:
        pending[0]()
        pending[1]()
    if not layer0:
        kb.memset("pool", ysa[:, :, 0:256], 0.0, w=["ysa"])
    for h in range(2):
        kb.dma("sp", yint[256 + p * 128 + 64 * h:256 + p * 128 + 64 * h + 64, :], ysa[:, h, :], r=["ysa"], w=[ykey])
    kb.phase_end()
    kb.scope_end()


def build_fused():
    kb = KB()
    hfull = kb.inp("hfull", [TTK, D], lvl="global")
    hful1 = kb.internal("hful1", [TTK, D])
    hout = kb.outp("hout", [4096, D])
    yint = [kb.internal("yint%d" % l, [512, TTK], BF16) for l in range(2)]
    kb.ps("pT", [128, 1024], BF16)
    for nm in "ABCDEFG":
        kb.ps("p" + nm, [128, 512])
    modc_d = [kb.internal("modc%d" % l, [128, 48, 2]) for l in range(2)]
    grow_d = [kb.internal("grow%d" % l, [2, 2048]) for l in range(2)]
    uTd = [kb.internal("uTd%d" % l, [128, 8, TTK], BF16) for l in range(2)]
    wb16 = []
    for l in range(2):
        d = {}
        for nm, shape in (("w_pc", [D, 768]), ("w_out", [D, D]), ("fwo", [DFF, D]), ("fwi", [D, 2 * DFF])):
            d[nm] = (kb.internal("L%d_%s_b16" % (l, nm), shape, BF16), "L%d_%s_b16" % (l, nm))
        wb16.append(d)

    def convert_weights(l):
        save = dict(kb.pfx)
        if True:
            kb.pfx = {"inst": "", "layer": "L%d_" % l, "global": ""}
            for nm, shape, rows in (("w_pc", [D, 768], 256), ("w_out", [D, D], 256), ("fwo", [DFF, D], 256), ("fwi", [D, 2 * DFF], 128)):
                src = kb.inp(nm, shape, lvl="layer")
                dst, key = wb16[l][nm]
                for r0 in range(0, shape[0], rows):
                    kb.dma("pool", dst[r0:r0 + rows, :], src[r0:r0 + rows, :], w=[key], bg=True)
        kb.pfx = save

    for l in range(2):
        hsrc, hkey = (hfull, "hfull") if l == 0 else (hful1, "hful1")
        convert_weights(l)
        emit_mod(kb, l, modc_d[l], grow_d[l], "mod%d" % l)
        for p in range(2):
            emit_M(kb, l, p, hsrc, hkey, yint[l], "yint%d" % l, modc_d[l], "mod%d" % l, uTd[l],
                   hook=None)
        for g in range(2):
            emit_F(kb, l, g, hsrc, hkey, yint[l], "yint%d" % l, hful1 if l == 0 else hout, "hful1" if l == 0 else "hout", l == 1,
                   wb16[l], modc_d[l], grow_d[l], "mod%d" % l, uTd[l])
    kb.S.emit()
    return kb

import numpy as np, ml_dtypes
BF = ml_dtypes.bfloat16
S_, CT, D, DFF = 4096, 256, 1024, 2816
POOL_WINDOWS = (2, 4, 8, 16)

def col_layout(v, n):
    return np.ascontiguousarray(v.reshape(n, 128).T)

def band_mats(base, p0, L, w):
    B = np.zeros((256, 128), np.float32)
    for t in range(128):
        p = p0 + t
        lo = min(max(p - w // 2, 0), L - 1); hi = min(max(p + (w - 1 - w // 2), 0), L - 1)
        inv = np.float32(1.0) / np.float32(hi - lo + 1)
        for q in range(lo, hi + 1):
            B[q - base, t] += inv
        B[p - base, t] -= 1.0
    return B

def prep_F(inp, l, h, hc, yma_lat, yma_ctx, b, g):
    layer0 = (l == 0)
    start, cstart = g * 2048, g * 128
    m = {}
    rows = [h[b, start:start + 2048]]
    if layer0: rows.append(hc[b, cstart:cstart + 128])
    m["hrows"] = np.ascontiguousarray(np.concatenate(rows, 0))
    def padded(src, s0, n, L):
        out = np.zeros((n, D), np.float32); msk = np.zeros((n,), np.float32)
        lo, hi = max(s0, 0), min(s0 + n, L)
        out[lo - s0:hi - s0] = src[lo:hi]; msk[lo - s0:hi - s0] = 1.0
        return out, msk
    hp, mk = padded(h[b], start - 16, 2176, S_)
    hps, mks = [hp], [mk]
    if layer0:
        cp, cm = padded(hc[b], cstart - 16, 256, CT)
        hps.append(cp); mks.append(cm)
    m["hpad"] = np.ascontiguousarray(np.concatenate(hps, 0)); m["maskp"] = np.concatenate(mks, 0)
    ys = [yma_lat[b, start:start + 2048]]
    if layer0: ys.append(yma_ctx[b, cstart:cstart + 128])
    m["yTma"] = np.ascontiguousarray(np.concatenate(ys, 0).T).astype(BF)
    m["ccol"] = np.ascontiguousarray(np.stack([col_layout(inp["c"][b], 8), col_layout(inp["c_ctx"], 8)], axis=-1))
    m["modw"] = inp["mod_w"][l]
    m["modb_col"] = col_layout(inp["mod_b"][l], 48)
    m["modb_row"] = inp["mod_b"][l]
    m["n1g"] = col_layout(inp["norm1_g"][l], 8); m["n2g"] = col_layout(inp["norm2_g"][l], 8)
    m["w_pc"] = np.ascontiguousarray(inp["w_in"][l][:, 1808:2576])
    bands = np.zeros((128, 32, 128), np.float32)
    cls_list = [(0, S_, start), (1, S_, start), (15, S_, start)]
    for cls in range(4):
        for gi, w in enumerate(POOL_WINDOWS):
            if cls < 3:
                ot = cls_list[cls][0]
                base = start - 16 + ot * 128; p0 = start + ot * 128; L = S_
            else:
                base = cstart - 16; p0 = cstart; L = CT
            B = band_mats(base, p0, L, w)
            bands[:, (cls * 4 + gi) * 2 + 0, :] = B[0:128]; bands[:, (cls * 4 + gi) * 2 + 1, :] = B[128:256]
    m["bands"] = bands
    pw2 = np.zeros((64, 4, 128), np.float32)
    for gi in range(4):
        pw2[:, gi, (gi % 2) * 64:(gi % 2) * 64 + 64] = inp["pool_w"][l][gi]
    m["poolw2"] = pw2
    m["pscale"] = col_layout(inp["pool_scale"][l], 2)
    m["dww"] = np.ascontiguousarray(inp["conv_dw_w"][l].T.reshape(2, 128, 31).transpose(1, 0, 2))
    m["dwb"] = col_layout(inp["conv_dw_b"][l], 2); m["lng"] = col_layout(inp["conv_ln_g"][l], 2); m["lnb"] = col_layout(inp["conv_ln_b"][l], 2)
    m["pww"] = inp["conv_pw_w"][l]; m["w_out"] = inp["w_out"][l]; m["fwi"] = inp["ffn_w_in"][l]; m["fwo"] = inp["ffn_w_out"][l]
    m["identf"] = np.eye(128, dtype=np.float32)
    sel2 = np.zeros((2, 2, 128), np.float32); sel2[0, 0] = 1; sel2[1, 1] = 1
    m["sel2"] = sel2
    return {k: np.ascontiguousarray(v) for k, v in m.items()}

NTK = 34
def rope_tables():
    TT = NTK * 128
    cos = np.ones((64, TT), np.float32); sin = np.zeros((64, TT), np.float32)
    tok = np.arange(S_)
    rows = (tok // 64).astype(np.float32); cols = (tok % 64).astype(np.float32)
    freqs = (np.float32(10000.0) ** (-np.arange(8, dtype=np.float32) / np.float32(8))).astype(np.float32)
    for m in range(2):
        for d in range(32):
            axis, half, f = d // 16, (d % 16) // 8, d % 8
            ang = (rows if axis == 0 else cols) * freqs[f]
            cos[m * 32 + d, 256:] = np.cos(ang.astype(np.float32))
            sin[m * 32 + d, 256:] = np.sin(ang.astype(np.float32)) * (-1.0 if half == 0 else 1.0)
    return cos, sin

_CONST = {}
def consts_M():
    if _CONST: return _CONST
    c = {}
    c["identf"] = np.eye(128, dtype=np.float32)
    r = np.arange(128)
    tri = np.zeros((128, 2, 128), np.float32)
    tri[:, 0, :] = (r[:, None] <= r[None, :]); tri[:, 1, :] = (r[:, None] >= r[None, :])
    c["tri"] = tri; c["mask"] = tri.copy()
    bo = np.zeros((128, 128), np.float32)
    for j in range(4):
        bo[32 * j:32 * j + 32, 32 * j:32 * j + 32] = 1
    c["bones"] = bo
    mc = np.zeros((128, 4), np.float32)
    for j in range(4):
        mc[32 * j:32 * j + 32, j] = 1
    c["mcol"] = mc
    pm = np.zeros((128, 128), np.float32)
    for d in range(128):
        dd = d % 32
        partner = d + 8 if (dd % 16) < 8 else d - 8
        pm[partner, d] = 1.0
    c["perm"] = pm
    pB = np.zeros((68, 68), np.float32)
    for pos in range(34):
        cc = 1 - pos if pos < 2 else 35 - pos
        for h in range(2):
            pB[cc * 2 + h, pos * 2 + h] = 1.0
    c["permB"] = pB
    sg = np.zeros((2, 64), np.float32); sg[0] = -1; sg[1] = 1
    c["sgn"] = sg
    sz = np.zeros((128, 64), np.float32); sz[64, :] = 1.0
    c["selZ"] = sz
    ct_, st_ = rope_tables()
    c["cost"], c["sint"] = np.concatenate([ct_, ct_], 0), np.concatenate([st_, st_], 0)
    _CONST.update(c)
    return _CONST

def prep_M(inp, l, h, hc, b, g):
    m = dict(consts_M())
    m["hfull"] = np.ascontiguousarray(np.concatenate([hc[b], h[b]], 0))
    m["ccol"] = np.ascontiguousarray(np.stack([col_layout(inp["c"][b], 8), col_layout(inp["c_ctx"], 8)], axis=-1))
    m["modw"] = np.ascontiguousarray(inp["mod_w"][l][:, :2048])
    m["modb_col"] = col_layout(inp["mod_b"][l][:2048], 16)
    m["n1g"] = col_layout(inp["norm1_g"][l], 8)
    W = inp["w_in"][l]
    hp = slice(g * 128, (g + 1) * 128)
    mq, mk, mv, mo = W[:, 0:256][:, hp], W[:, 256:512][:, hp], W[:, 512:768][:, hp], W[:, 768:1024][:, hp]
    gates = W[:, 1024:1040].reshape(D, 2, 2, 4)[:, :, :, 2 * g:2 * g + 2].reshape(D, 8)
    aq, ak, av = W[:, 1040:1296][:, hp], W[:, 1296:1552][:, hp], W[:, 1552:1808][:, hp]
    m["w_fm"] = np.concatenate([mq, mk, aq, ak], 1)
    m["w_tm"] = np.concatenate([mk, mv, mo, av, gates, np.zeros((D, 8), np.float32)], 1)
    m["gate_b"] = inp["mlstm_gate_b"][l][:, :, 2 * g:2 * g + 2].reshape(8)
    m["mng"] = inp["mlstm_norm_g"][l]
    m["qkng"] = np.stack([np.tile(inp["attn_q_norm_g"][l], 2), np.tile(inp["attn_k_norm_g"][l], 2)], 1)
    m["lamqk"] = np.stack([inp["lambda_q"][l], inp["lambda_k"][l]], 1)
    m["sublng"] = inp["attn_subln_g"][l].reshape(64, 1)
    return {k: np.ascontiguousarray(v, dtype=np.float32) for k, v in m.items()}


def prep_fused(inp, b):
    x, ctx = inp["x"], inp["ctx"]
    m = dict(consts_M())
    m["hfull"] = np.concatenate([ctx[b], x[b]], 0)
    m["ccol"] = np.stack([col_layout(inp["c"][b], 8), col_layout(inp["c_ctx"], 8)], axis=-1)
    sel2 = np.zeros((2, 2, 128), np.float32); sel2[0, 0] = 1; sel2[1, 1] = 1
    m["sel2"] = sel2
    for g in range(2):
        start, cstart = g * 2048, g * 128
        mk = np.zeros((19 * 128,), np.float32)
        pos = start - 16 + np.arange(2176); mk[:2176] = ((pos >= 0) & (pos < S_))
        cpos = cstart - 16 + np.arange(256); mk[2176:] = ((cpos >= 0) & (cpos < CT))
        m["maskp%d" % g] = mk
        bands = np.zeros((128, 32, 128), np.float32)
        for cls in range(4):
            for gi, w in enumerate(POOL_WINDOWS):
                if cls < 3:
                    ot = (0, 1, 15)[cls]
                    base = start - 16 + ot * 128; p0 = start + ot * 128; L = S_
                else:
                    base = cstart - 16; p0 = cstart; L = CT
                B = band_mats(base, p0, L, w)
                bands[:, (cls * 4 + gi) * 2 + 0, :] = B[0:128]; bands[:, (cls * 4 + gi) * 2 + 1, :] = B[128:256]
        m["bands%d" % g] = bands
    for l in range(2):
        L_ = "L%d_" % l
        m[L_ + "modw"] = inp["mod_w"][l]
        m[L_ + "modb_col"] = col_layout(inp["mod_b"][l], 48)
        m[L_ + "modb_row"] = inp["mod_b"][l]
        m[L_ + "n1g"] = col_layout(inp["norm1_g"][l], 8); m[L_ + "n2g"] = col_layout(inp["norm2_g"][l], 8)
        W = inp["w_in"][l]
        m[L_ + "w_pc"] = W[:, 1808:2576]
        pw2 = np.zeros((64, 4, 128), np.float32)
        for gi in range(4):
            pw2[:, gi, (gi % 2) * 64:(gi % 2) * 64 + 64] = inp["pool_w"][l][gi]
        m[L_ + "poolw2"] = pw2
        m[L_ + "pscale"] = col_layout(inp["pool_scale"][l], 2)
        m[L_ + "dww"] = inp["conv_dw_w"][l].T.reshape(2, 128, 31).transpose(1, 0, 2)
        m[L_ + "dwb"] = col_layout(inp["conv_dw_b"][l], 2); m[L_ + "lng"] = col_layout(inp["conv_ln_g"][l], 2)
        m[L_ + "lnb"] = col_layout(inp["conv_ln_b"][l], 2)
        m[L_ + "pww"] = inp["conv_pw_w"][l]; m[L_ + "w_out"] = inp["w_out"][l]
        m[L_ + "fwi"] = inp["ffn_w_in"][l]; m[L_ + "fwo"] = inp["ffn_w_out"][l]
        m[L_ + "mng"] = inp["mlstm_norm_g"][l]
        m[L_ + "qkng"] = np.stack([np.tile(inp["attn_q_norm_g"][l], 4), np.tile(inp["attn_k_norm_g"][l], 4)], 1)
        m[L_ + "lamqk"] = np.stack([inp["lambda_q"][l], inp["lambda_k"][l]], 1)
        m[L_ + "sublng"] = inp["attn_subln_g"][l].reshape(64, 1)
        for p in range(2):
            P_ = "M%d%d_" % (l, p)
            hp = slice(p * 128, (p + 1) * 128)
            mq, mk_, mv, mo = W[:, 0:256][:, hp], W[:, 256:512][:, hp], W[:, 512:768][:, hp], W[:, 768:1024][:, hp]
            gates = W[:, 1024:1040].reshape(D, 2, 2, 4)[:, :, :, 2 * p:2 * p + 2].reshape(D, 8)
            aq, ak, av = W[:, 1040:1296][:, hp], W[:, 1296:1552][:, hp], W[:, 1552:1808][:, hp]
            m[P_ + "w_fm"] = np.concatenate([mq, mk_, aq, ak], 1)
            m[P_ + "w_tm"] = np.concatenate([mk_, mv, mo, av, gates, np.zeros((D, 8), np.float32)], 1)
            m[P_ + "gate_b"] = inp["mlstm_gate_b"][l][:, :, 2 * p:2 * p + 2].reshape(8)
    return {k: np.ascontiguousarray(v, dtype=np.float32) for k, v in m.items()}

_PROG = {}


def kernel(**inputs):
    inp = {k: np.asarray(v) for k, v in inputs.items()}
    if "kb" not in _PROG:
        _PROG["kb"] = build_fused()
    kb = _PROG["kb"]
    real = {0: 0, 1: 1, 4: 2, 5: 3}
    maps = [None] * 8
    for c, b in real.items():
        maps[c] = prep_fused(inp, b)
    zero = {k: np.zeros_like(v) for k, v in maps[0].items()}
    for c in range(8):
        if maps[c] is None:
            maps[c] = zero
    res = run_bass_kernel_spmd(kb.nc, maps, core_ids=list(range(8)))
    out = np.stack([np.asarray(res.results[c]["hout"]) for c in (0, 1, 4, 5)], 0)
    return out.astype(np.float32)
```
